# Optimizing a Trainium2 kernel written in Bass

```python
import math
import jax, jax.numpy as jnp
from jax import lax
import numpy as np

D_MODEL = 1024
BATCH = 8
SEQ = 2048
DEPTH = 4

N_MIXERS = 2
N_HEADS = 16
HEAD_DIM = D_MODEL // N_HEADS
ROPE_THETA = 10000.0
MOBA_BLOCK = 256
MOBA_TOPK = 3
Q_CHUNK = 16
CONV_WIDTH = 31
FFN_CONV_WIDTH = 3
D_FF = 2816
NORM_EPS = 1e-6
LN_EPS = 1e-5
N_ATTN_LAYERS = (DEPTH + 1) // 2
N_CONV_LAYERS = DEPTH // 2

kernel_name = "moba_conformer_convffn_hybrid"


def rmsnorm(x, g):
    xf = x.astype(jnp.float32)
    y = xf * lax.rsqrt(jnp.mean(xf * xf, axis=-1, keepdims=True) + NORM_EPS)
    return (y * g.astype(jnp.float32)).astype(x.dtype)


def layernorm(x, g, b):
    xf = x.astype(jnp.float32)
    mu = jnp.mean(xf, axis=-1, keepdims=True)
    var = jnp.mean(jnp.square(xf - mu), axis=-1, keepdims=True)
    y = (xf - mu) * lax.rsqrt(var + LN_EPS)
    return (y * g.astype(jnp.float32) + b.astype(jnp.float32)).astype(x.dtype)


def causal_dwconv(x, w, b):
    width, ch = w.shape
    y = lax.conv_general_dilated(x, w[:, None, :], window_strides=(1,), padding=[(width - 1, 0)],
                                 dimension_numbers=("NWC", "WIO", "NWC"), feature_group_count=ch)
    return y + b


def rotary(x, cos, sin):
    xf = x.astype(jnp.float32)
    x1, x2 = jnp.split(xf, 2, axis=-1)
    return jnp.concatenate([x1 * cos - x2 * sin, x2 * cos + x1 * sin], axis=-1).astype(x.dtype)


def moba_attention(h, w_qkv, w_o):
    B, S, _ = h.shape
    qkv = (h @ w_qkv).reshape(B, S, 3, N_HEADS, HEAD_DIM)
    q = jnp.transpose(qkv[:, :, 0], (0, 2, 1, 3))
    k = jnp.transpose(qkv[:, :, 1], (0, 2, 1, 3))
    v = jnp.transpose(qkv[:, :, 2], (0, 2, 1, 3))
    pos = jnp.arange(S, dtype=jnp.float32)
    inv_freq = 1.0 / (ROPE_THETA ** (jnp.arange(0, HEAD_DIM, 2, dtype=jnp.float32) / HEAD_DIM))
    ang = pos[:, None] * inv_freq[None, :]
    cos, sin = jnp.cos(ang), jnp.sin(ang)
    q = rotary(q, cos, sin) * (HEAD_DIM ** -0.5)
    k = rotary(k, cos, sin)

    nb = -(-S // MOBA_BLOCK)
    sp = nb * MOBA_BLOCK
    padw = ((0, 0), (0, 0), (0, sp - S), (0, 0))
    q, k, v = jnp.pad(q, padw), jnp.pad(k, padw), jnp.pad(v, padw)
    kb = k.reshape(B, N_HEADS, nb, MOBA_BLOCK, HEAD_DIM)
    vb = v.reshape(B, N_HEADS, nb, MOBA_BLOCK, HEAD_DIM)
    kmean = jnp.mean(kb.astype(jnp.float32), axis=3)
    n_sel = min(MOBA_TOPK, nb)
    n_chunks = sp // Q_CHUNK
    qc = jnp.moveaxis(q.reshape(B, N_HEADS, n_chunks, Q_CHUNK, HEAD_DIM), 2, 0)

    b_idx = jnp.arange(B)[:, None, None, None]
    h_idx = jnp.arange(N_HEADS)[None, :, None, None]
    blk_ids = jnp.arange(nb)
    slot_ids = jnp.arange(n_sel)
    q_off = jnp.arange(Q_CHUNK)
    k_off = jnp.arange(MOBA_BLOCK)

    def attend_chunk(args):
        q_c, c = args
        start = c * Q_CHUNK
        blk = start // MOBA_BLOCK
        gate = jnp.einsum("bhqd,bhnd->bhqn", q_c.astype(jnp.float32), kmean)
        gate = jnp.where(blk_ids < blk, gate, -jnp.inf)
        _, sel = lax.top_k(gate, n_sel)
        k_sel = kb[b_idx, h_idx, sel]
        v_sel = vb[b_idx, h_idx, sel]
        s_sel = jnp.einsum("bhqd,bhqnkd->bhqnk", q_c, k_sel).astype(jnp.float32)
        s_sel = jnp.where((slot_ids < blk)[:, None], s_sel, -jnp.inf)
        k_own = lax.dynamic_index_in_dim(kb, blk, axis=2, keepdims=False)
        v_own = lax.dynamic_index_in_dim(vb, blk, axis=2, keepdims=False)
        s_own = jnp.einsum("bhqd,bhkd->bhqk", q_c, k_own).astype(jnp.float32)
        causal = (blk * MOBA_BLOCK + k_off)[None, :] <= (start + q_off)[:, None]
        s_own = jnp.where(causal, s_own, -jnp.inf)
        scores = jnp.concatenate([s_sel.reshape(B, N_HEADS, Q_CHUNK, n_sel * MOBA_BLOCK), s_own], axis=-1)
        p = jax.nn.softmax(scores, axis=-1).astype(v.dtype)
        p_sel = p[..., : n_sel * MOBA_BLOCK].reshape(B, N_HEADS, Q_CHUNK, n_sel, MOBA_BLOCK)
        p_own = p[..., n_sel * MOBA_BLOCK:]
        return (jnp.einsum("bhqnk,bhqnkd->bhqd", p_sel, v_sel)
                + jnp.einsum("bhqk,bhkd->bhqd", p_own, v_own))

    out = lax.map(attend_chunk, (qc, jnp.arange(n_chunks)))
    out = jnp.moveaxis(out, 0, 2).reshape(B, N_HEADS, sp, HEAD_DIM)[:, :, :S]
    out = jnp.transpose(out, (0, 2, 1, 3)).reshape(B, S, N_HEADS * HEAD_DIM)
    return out @ w_o


def conformer_conv(h, w_pw1, b_pw1, w_dw, b_dw, ln_g, ln_b, w_pw2, b_pw2):
    u = h @ w_pw1 + b_pw1
    a, g = jnp.split(u, 2, axis=-1)
    u = a * jax.nn.sigmoid(g)
    u = causal_dwconv(u, w_dw, b_dw)
    u = layernorm(u, ln_g, ln_b)
    u = jax.nn.silu(u)
    return u @ w_pw2 + b_pw2


def conv_ffn(h, w_up, w_dw, b_dw, w_down):
    u = causal_dwconv(h @ w_up, w_dw, b_dw)
    gate, val = jnp.split(u, 2, axis=-1)
    return (jax.nn.silu(gate) * val) @ w_down


def setup_inputs(seed: int = 0) -> dict:
    key = jax.random.key(seed)
    ks = jax.random.split(key, 20)
    D, F = D_MODEL, D_FF
    na, nc = N_ATTN_LAYERS, N_CONV_LAYERS
    nrm = lambda k, shape, fan_in: jax.random.normal(k, shape, jnp.float32) * (fan_in ** -0.5)
    small = lambda k, shape, s: jax.random.normal(k, shape, jnp.float32) * s
    return {
        "x": jax.random.normal(ks[0], (BATCH, SEQ, D), jnp.float32),
        "norm_mix_g": 1.0 + small(ks[1], (DEPTH, D), 0.1),
        "norm_ffn_g": 1.0 + small(ks[2], (DEPTH, D), 0.1),
        "final_norm_g": 1.0 + small(ks[3], (D,), 0.1),
        "attn_w_qkv": nrm(ks[4], (na, D, 3 * D), D),
        "attn_w_o": nrm(ks[5], (na, D, D), D),
        "conv_w_pw1": nrm(ks[6], (nc, D, 2 * D), D),
        "conv_b_pw1": small(ks[7], (nc, 2 * D), 0.02),
        "conv_w_dw": nrm(ks[8], (nc, CONV_WIDTH, D), CONV_WIDTH),
        "conv_b_dw": small(ks[9], (nc, D), 0.02),
        "conv_ln_g": 1.0 + small(ks[10], (nc, D), 0.1),
        "conv_ln_b": small(ks[11], (nc, D), 0.02),
        "conv_w_pw2": nrm(ks[12], (nc, D, D), D),
        "conv_b_pw2": small(ks[13], (nc, D), 0.02),
        "ffn_w_up": nrm(ks[14], (DEPTH, D, 2 * F), D),
        "ffn_w_dw": nrm(ks[15], (DEPTH, FFN_CONV_WIDTH, 2 * F), FFN_CONV_WIDTH),
        "ffn_b_dw": small(ks[16], (DEPTH, 2 * F), 0.02),
        "ffn_w_down": nrm(ks[17], (DEPTH, F, D), F),
    }


def reference(x, norm_mix_g, norm_ffn_g, final_norm_g, attn_w_qkv, attn_w_o,
              conv_w_pw1, conv_b_pw1, conv_w_dw, conv_b_dw, conv_ln_g, conv_ln_b,
              conv_w_pw2, conv_b_pw2, ffn_w_up, ffn_w_dw, ffn_b_dw, ffn_w_down):
    h = x
    for i in range(DEPTH):
        j = i // N_MIXERS
        hn = rmsnorm(h, norm_mix_g[i])
        if i % N_MIXERS == 0:
            h = h + moba_attention(hn, attn_w_qkv[j], attn_w_o[j])
        else:
            h = h + conformer_conv(hn, conv_w_pw1[j], conv_b_pw1[j], conv_w_dw[j], conv_b_dw[j],
                                   conv_ln_g[j], conv_ln_b[j], conv_w_pw2[j], conv_b_pw2[j])
        h = h + conv_ffn(rmsnorm(h, norm_ffn_g[i]), ffn_w_up[i], ffn_w_dw[i], ffn_b_dw[i], ffn_w_down[i])
    return rmsnorm(h, final_norm_g)
```

```python
import math
import numpy as np
from contextlib import ExitStack
import concourse.bass as bass
import concourse.mybir as mybir
from concourse.bass_utils import run_bass_kernel_spmd

F32 = mybir.dt.float32
BF16 = mybir.dt.bfloat16
I32 = mybir.dt.int32
ALU = mybir.AluOpType
AF = mybir.ActivationFunctionType
AX = mybir.AxisListType

D = 1024
SEQ = 2048
NCH = 8
NT = 4
TT = 512
H = 16
DH = 64
DFF = 2816
NFP = 22
DEPTH = 4
NEG = -30000.0
NORM_EPS = 1e-6
LN_EPS = 1e-5
CW = 31

R_MIX = 0
R_FFN = 32
R_FIN = 64
R_BPW1 = 72
R_WDW = 104
R_BDW = 600
R_LNG = 616
R_LNB = 632
R_BPW2 = 648
R_FWDW = 664
R_FBDW = 1192
R_TOT = 1408


class _Op:
    __slots__ = ("eng", "fn", "deps", "needs_inc", "semval", "grp", "gen", "pre")

    def __init__(self, eng, fn):
        self.eng = eng
        self.fn = fn
        self.deps = []
        self.needs_inc = False
        self.semval = None
        self.grp = None
        self.gen = 0
        self.pre = None


class _Grp:
    __slots__ = ("sem", "gens", "closed")

    def __init__(self, sem):
        self.sem = sem
        self.gens = [0]
        self.closed = False


class Sched:
    ENGS = ("pe", "act", "dve", "pool", "sp")

    def __init__(self, nc, es):
        self.nc = nc
        self.es = es
        self.q = {e: [] for e in self.ENGS}
        self.lastw = {}
        self.readers = {}
        self.esem = {e: es.enter_context(nc.semaphore("sem_" + e)) for e in ("pe", "act", "dve", "pool")}
        self.ecnt = {e: 0 for e in ("pe", "act", "dve", "pool")}
        self.seen = {e: {} for e in self.ENGS}
        self.groups = {}
        self.lastreal = {e: None for e in self.ENGS}
        self.nops = 0

    def _group(self, name):
        g = self.groups.get(name)
        if g is None:
            g = _Grp(self.es.enter_context(self.nc.semaphore("dg_" + name)))
            self.groups[name] = g
        return g

    def add(self, eng, fn, reads=(), writes=(), dma=None):
        op = _Op(eng, fn)
        deps = {}
        for k in reads:
            w = self.lastw.get(k)
            if w is not None:
                deps[id(w)] = w
            if isinstance(k, tuple) and k[0] == "ps":
                rd = self.readers.get(k)
                if rd:
                    for rk_, r in rd.items():
                        if rk_ != eng:
                            deps[id(r)] = r
        for k in writes:
            w = self.lastw.get(k)
            if w is not None:
                deps[id(w)] = w
            rd = self.readers.get(k)
            if rd:
                for r in rd.values():
                    deps[id(r)] = r
        if dma is not None:
            g = self._group(dma)
            op.grp = g
            if g.closed:
                op.pre = [(g, len(g.gens) - 1)]
                g.gens.append(g.gens[-1])
                g.closed = False
            g.gens[-1] += 16
            op.gen = len(g.gens) - 1
        for d in deps.values():
            if eng == "pe" and d.eng == "pe" and d.grp is None:
                continue
            if op.grp is not None and d.grp is op.grp and d.gen == op.gen:
                continue
            op.deps.append(d)
            d.needs_inc = True
            if d.grp is not None:
                d.grp.closed = True
        rk = eng if dma is None else ("dma", id(op))
        for k in reads:
            self.readers.setdefault(k, {})[rk] = op
        for k in writes:
            self.lastw[k] = op
            self.readers[k] = {}
        self.q[eng].append(op)
        if dma is None:
            self.lastreal[eng] = op
        self.nops += 1
        return op

    def pe(self, fn, reads=(), writes=()):
        return self.add("pe", fn, reads, writes)

    def act(self, fn, reads=(), writes=()):
        return self.add("act", fn, reads, writes)

    def dve(self, fn, reads=(), writes=()):
        return self.add("dve", fn, reads, writes)

    def pool(self, fn, reads=(), writes=()):
        return self.add("pool", fn, reads, writes)

    def dma(self, queue, group, fn, reads=(), writes=()):
        return self.add(queue, fn, reads, writes, dma=group)

    def barrier(self):
        lasts = [op for op in self.lastreal.values() if op is not None and op.grp is None]
        gl = [(g, len(g.gens) - 1) for g in self.groups.values() if g.gens[-1] > 0]
        for g, _ in gl:
            g.closed = True
        for e in self.ENGS:
            op = _Op(e, None)
            for d in lasts:
                if not (e == "pe" and d.eng == "pe"):
                    op.deps.append(d)
                    d.needs_inc = True
            op.pre = list(gl)
            self.q[e].append(op)
        self.lastw = {}
        self.readers = {}

    def _assign(self):
        for e in ("pe", "act", "dve", "pool"):
            c = self.ecnt[e]
            for op in self.q[e]:
                if op.grp is None and op.fn is not None and op.needs_inc and op.semval is None:
                    c += 1
                    op.semval = c
            self.ecnt[e] = c

    def _emit_engine(self, e, h):
        seen = self.seen[e]

        def wait(sem, val):
            key = id(sem)
            if seen.get(key, 0) < val:
                h.wait_ge(sem, val)
                seen[key] = val

        for op in self.q[e]:
            if op.pre is not None:
                for g, gi in op.pre:
                    wait(g.sem, g.gens[gi])
            for d in op.deps:
                if d.grp is not None:
                    wait(d.grp.sem, d.grp.gens[d.gen])
                else:
                    wait(self.esem[d.eng], d.semval)
            if op.fn is not None:
                ins = op.fn(h)
                if op.grp is not None:
                    ins.then_inc(op.grp.sem, 16)
                elif op.needs_inc:
                    ins.then_inc(self.esem[e], 1)
        self.q[e] = []
        self.lastreal[e] = None

    def flush(self):
        self._assign()
        with self.nc.Block() as block:
            @block.tensor
            def _(h):
                self._emit_engine("pe", h)

            @block.scalar
            def _(h):
                self._emit_engine("act", h)

            @block.vector
            def _(h):
                self._emit_engine("dve", h)

            @block.gpsimd
            def _(h):
                self._emit_engine("pool", h)

            @block.sync
            def _(h):
                self._emit_engine("sp", h)


class Builder:
    def __init__(self, stop_after=None):
        self.stop_after = stop_after
        self.nc = bass.Bass("TRN2", target_bir_lowering=False)
        nc = self.nc
        dt = nc.dram_tensor
        self.x = dt("x", [SEQ, D], F32, kind="ExternalInput").ap()
        self.ptab_in = dt("ptab_in", [R_TOT, 128], F32, kind="ExternalInput").ap()
        self.w_qkv = dt("w_qkv", [2, D, 3 * D], F32, kind="ExternalInput").ap()
        self.w_o = dt("w_o", [2, D, D], F32, kind="ExternalInput").ap()
        self.w_pw1 = dt("w_pw1", [2, D, 2 * D], F32, kind="ExternalInput").ap()
        self.w_pw2 = dt("w_pw2", [2, D, D], F32, kind="ExternalInput").ap()
        self.w_up = dt("w_up", [DEPTH, D, 2 * DFF], F32, kind="ExternalInput").ap()
        self.w_down = dt("w_down", [DEPTH, DFF, D], F32, kind="ExternalInput").ap()
        self.out = dt("out", [SEQ, D], F32, kind="ExternalOutput").ap()
        self._uid = 0

    def sb(self, es, shape, dtype, name=None):
        self._uid += 1
        return es.enter_context(self.nc.sbuf_tensor("%s_%d" % (name or "t", self._uid), shape, dtype))

    def mm(self, out, lhsT, rhs, start, stop, reads, writes):
        self.S.pe(lambda h: h.matmul(out, lhsT=lhsT, rhs=rhs, start=start, stop=stop), reads, writes)

    def prow(self, r):
        return self.ptab[:, r:r + 1]

    def build(self):
        nc = self.nc
        with ExitStack() as es:
            self.S = S = Sched(nc, es)
            self.ps = [es.enter_context(nc.psum_tensor("ps%d" % i, [128, TT], F32)) for i in range(8)]
            self.h = self.sb(es, [128, NCH, SEQ], F32, "h")
            self.hn = self.sb(es, [128, NCH, SEQ], BF16, "hn")
            self.cosT = self.sb(es, [128, SEQ], BF16, "cosT")
            self.sinT = self.sb(es, [128, SEQ], BF16, "sinT")
            self.ptab = self.sb(es, [128, R_TOT], F32, "ptab")
            self.identf = self.sb(es, [128, 128], F32, "identf")
            self.identb = self.sb(es, [128, 128], BF16, "identb")
            self.onesb = self.sb(es, [128, 128], BF16, "onesb")
            self.maskT = self.sb(es, [128, 128], BF16, "maskT")
            self.prot = self.sb(es, [128, 128], BF16, "prot")
            self.ind = [self.sb(es, [128, 8 * 128], BF16, "ind") for _ in range(2)]
            self.nsq = self.sb(es, [128, 2, TT], BF16, "nsq")
            self.nstd = self.sb(es, [128, 2, TT], F32, "nstd")

            self.phase_setup()
            stop = self.stop_after
            done = stop == "load"
            for i in range(DEPTH):
                if done:
                    break
                j = i // 2
                self.rmsnorm(R_MIX + i * 8)
                if i % 2 == 0:
                    self.phase_attention(j)
                else:
                    self.phase_conformer(j)
                if stop is not None and stop.split(":")[0] == "mix%d" % i:
                    done = True
                    break
                self.rmsnorm(R_FFN + i * 8)
                fsub = stop.split(":")[1] if (stop and ":" in stop and stop.startswith("ffn")) else None
                if fsub != "norm":
                    self.phase_ffn(i)
                if stop is not None and stop.split(":")[0] == "ffn%d" % i:
                    done = True
                    break
            self.phase_final(raw=(stop is not None))
        return nc

    def phase_setup(self):
        nc, S = self.nc, self.S
        ps = self.ps
        with ExitStack() as es:
            xs = [self.sb(es, [128, D], F32, "xs") for _ in range(3)]
            pst = [self.sb(es, [128, 128], F32, "pst") for _ in range(2)]
            pi_i = self.sb(es, [128, 1], I32, "pi_i")
            pm_i = self.sb(es, [128, 2], I32, "pm_i")
            sgn = self.sb(es, [128, 2], F32, "sgn")
            invrow = self.sb(es, [1, 128], F32, "invrow")
            invf = self.sb(es, [128, 1], F32, "invf")
            pos_i = self.sb(es, [128, SEQ], I32, "pos_i")
            ang = self.sb(es, [128, SEQ], F32, "ang")
            ta = self.sb(es, [128, SEQ], F32, "ta")
            tb = self.sb(es, [128, SEQ], F32, "tb")
            tc = self.sb(es, [128, SEQ], F32, "tc")

            identf, identb, onesb, maskT, prot, ind = self.identf, self.identb, self.onesb, self.maskT, self.prot, self.ind
            S.pool(lambda h: h.memset(identf[:], 1.0), writes=["identf"])
            S.pool(lambda h: h.affine_select(out=identf[:], in_=identf[:], pattern=[[-1, 128]], compare_op=ALU.is_equal,
                                             fill=0.0, base=0, channel_multiplier=1), reads=["identf"], writes=["identf"])
            S.dve(lambda h: h.tensor_copy(out=identb[:], in_=identf[:]), reads=["identf"], writes=["identb"])
            S.pool(lambda h: h.memset(onesb[:], 1.0), writes=["onesb"])
            S.pool(lambda h: h.memset(maskT[:], 0.0), writes=["maskT"])
            S.pool(lambda h: h.affine_select(out=maskT[:], in_=maskT[:], pattern=[[1, 128]], compare_op=ALU.is_ge,
                                             fill=NEG, base=0, channel_multiplier=-1), reads=["maskT"], writes=["maskT"])
            for (dst, src) in ((0, 32), (32, 0), (64, 96), (96, 64)):
                S.dve(lambda h, dst=dst, src=src: h.tensor_copy(out=prot[:, dst:dst + 32], in_=identb[:, src:src + 32]),
                      reads=["identb"], writes=["prot"])
            for jj in range(2):
                S.pool(lambda h, jj=jj: h.memset(ind[jj][:], 0.0), writes=["ind"])
                S.pool(lambda h, jj=jj: h.memset(ind[jj][64 * jj:64 * jj + 64, :], 1.0), reads=["ind"], writes=["ind"])
                S.pool(lambda h, jj=jj: h.affine_select(out=ind[jj][64 * jj:64 * jj + 64, :], in_=ind[jj][64 * jj:64 * jj + 64, :],
                                                        pattern=[[1, 8], [0, 128]], compare_op=ALU.is_equal, fill=0.0,
                                                        base=0, channel_multiplier=-1), reads=["ind"], writes=["ind"])
            for r in range(R_TOT // 128):
                st = pst[r % 2]
                S.dma("sp", "pst%d" % (r % 2), lambda h, r=r, st=st: h.dma_start(out=st[:], in_=self.ptab_in[r * 128:(r + 1) * 128, :]),
                      writes=[("pst", r % 2)])
                bank = ps[r % 2]
                S.pe(lambda h, st=st, bank=bank: h.transpose(out=bank[:, 0:128], in_=st[:], identity=identf[:]),
                     reads=[("pst", r % 2), "identf"], writes=[("ps", r % 2)])
                S.act(lambda h, r=r, bank=bank: h.activation(out=self.ptab[:, r * 128:(r + 1) * 128], in_=bank[:, 0:128], func=AF.Copy),
                      reads=[("ps", r % 2)], writes=["ptab"])
            inv = (np.float32(1.0) / np.power(np.float32(10000.0), np.arange(0, DH, 2, dtype=np.float32) / np.float32(DH))).astype(np.float32)
            irv = invrow[:].rearrange("o (a b) -> o a b", b=32)
            for i in range(32):
                S.dve(lambda h, i=i: h.memset(irv[:, :, i:i + 1], float(inv[i])), writes=["invrow"])
            S.pe(lambda h: h.transpose(out=ps[2][:, 0:1], in_=invrow[:], identity=identf[0:1, 0:1]),
                 reads=["invrow", "identf"], writes=[("ps", 2)])
            S.act(lambda h: h.activation(out=invf[:], in_=ps[2][:, 0:1], func=AF.Copy), reads=[("ps", 2)], writes=["invf"])
            S.pool(lambda h: h.iota(pi_i[:], pattern=[[0, 1]], base=0, channel_multiplier=1), writes=["pi_i"])
            S.dve(lambda h: h.tensor_scalar(out=pm_i[:, 0:1], in0=pi_i[:], scalar1=32, scalar2=None, op0=ALU.bitwise_and),
                  reads=["pi_i"], writes=["pm_i"])
            S.dve(lambda h: h.tensor_copy(out=sgn[:, 0:1], in_=pm_i[:, 0:1]), reads=["pm_i"], writes=["sgn0"])
            S.dve(lambda h: h.tensor_scalar(out=sgn[:, 1:2], in0=sgn[:, 0:1], scalar1=1.0 / 16.0, scalar2=-1.0, op0=ALU.mult, op1=ALU.add),
                  reads=["sgn0"], writes=["sgn1"])
            S.pool(lambda h: h.iota(pos_i[:], pattern=[[1, SEQ]], base=0, channel_multiplier=0), writes=["pos_i"])
            S.dve(lambda h: h.tensor_copy(out=ta[:], in_=pos_i[:]), reads=["pos_i"], writes=["ta"])
            S.dve(lambda h: h.tensor_scalar(out=ang[:], in0=ta[:], scalar1=invf[:, 0:1], scalar2=None, op0=ALU.mult),
                  reads=["ta", "invf"], writes=["ang"])
            TWO_PI = 2.0 * math.pi
            C1 = 6.28125
            C2 = TWO_PI - C1
            MAGIC = 12582912.0
            LIM = 3.1415925
            S.dve(lambda h: h.tensor_scalar(out=ta[:], in0=ang[:], scalar1=1.0 / TWO_PI, scalar2=None, op0=ALU.mult),
                  reads=["ang", "ta"], writes=["ta"])
            S.dve(lambda h: h.tensor_scalar(out=tb[:], in0=ta[:], scalar1=MAGIC, scalar2=MAGIC, op0=ALU.add, op1=ALU.subtract),
                  reads=["ta"], writes=["tb"])
            S.dve(lambda h: h.scalar_tensor_tensor(out=ta[:], in0=tb[:], scalar=-C1, in1=ang[:], op0=ALU.mult, op1=ALU.add),
                  reads=["tb", "ang", "ta"], writes=["ta"])
            S.dve(lambda h: h.scalar_tensor_tensor(out=tc[:], in0=tb[:], scalar=-C2, in1=ta[:], op0=ALU.mult, op1=ALU.add),
                  reads=["tb", "ta"], writes=["tc"])
            S.dve(lambda h: h.tensor_scalar(out=ta[:], in0=tc[:], scalar1=LIM, scalar2=-LIM, op0=ALU.min, op1=ALU.max),
                  reads=["tc", "ta"], writes=["ta"])
            S.act(lambda h: h.activation(out=tb[:], in_=ta[:], func=AF.Sin), reads=["ta", "tb"], writes=["tb"])
            S.dve(lambda h: h.tensor_scalar(out=self.sinT[:], in0=tb[:], scalar1=sgn[:, 1:2], scalar2=None, op0=ALU.mult),
                  reads=["tb", "sgn1"], writes=["sinT"])
            S.dve(lambda h: h.tensor_scalar(out=ang[:], in0=tc[:], scalar1=math.pi / 2, scalar2=None, op0=ALU.add),
                  reads=["tc", "ang"], writes=["ang"])
            S.dve(lambda h: h.tensor_scalar(out=ta[:], in0=ang[:], scalar1=math.pi, scalar2=None, op0=ALU.is_gt),
                  reads=["ang", "ta"], writes=["ta"])
            S.dve(lambda h: h.scalar_tensor_tensor(out=tc[:], in0=ta[:], scalar=-TWO_PI, in1=ang[:], op0=ALU.mult, op1=ALU.add),
                  reads=["ta", "ang", "tc"], writes=["tc"])
            S.dve(lambda h: h.tensor_scalar(out=ta[:], in0=tc[:], scalar1=LIM, scalar2=-LIM, op0=ALU.min, op1=ALU.max),
                  reads=["tc", "ta"], writes=["ta"])
            S.act(lambda h: h.activation(out=self.cosT[:], in_=ta[:], func=AF.Sin), reads=["ta"], writes=["cosT"])
            for tt in range(16):
                sl = tt % 3
                S.dma("sp", "xs%d" % sl, lambda h, tt=tt, sl=sl: h.dma_start(out=xs[sl][:], in_=self.x[tt * 128:(tt + 1) * 128, :]),
                      writes=[("xs", sl)])
                T = tt // 4
                for half in range(2):
                    bi = 4 + (2 * tt + half) % 4
                    bank = ps[bi]
                    for jj in range(4):
                        c = half * 4 + jj
                        S.pe(lambda h, bank=bank, jj=jj, c=c, sl=sl: h.transpose(out=bank[:, jj * 128:(jj + 1) * 128],
                                                                               in_=xs[sl][:, c * 128:(c + 1) * 128], identity=identf[:]),
                             reads=[("xs", sl), "identf"], writes=[("ps", bi)])
                    dst = self.h[:, half * 4:half * 4 + 4, tt * 128:(tt + 1) * 128]
                    src = bank[:, :].rearrange("p (a b) -> p a b", b=128)
                    wr = [("h", half * 4 + jj, T) for jj in range(4)]
                    if (2 * tt + half) % 2 == 0:
                        S.act(lambda h, dst=dst, src=src: h.activation(out=dst, in_=src, func=AF.Copy), reads=[("ps", bi)], writes=wr)
                    else:
                        S.dve(lambda h, dst=dst, src=src: h.tensor_copy(out=dst, in_=src), reads=[("ps", bi)], writes=wr)
            S.barrier()
            S.flush()

    def rmsnorm(self, grow, out_fn=None):
        S, ps = self.S, self.ps
        for T in range(NT):
            bi = 6 + T % 2
            bank = ps[bi]
            tsl = slice(T * TT, (T + 1) * TT)
            for c in range(NCH):
                sq = self.nsq[:, c % 2, :]
                S.act(lambda h, sq=sq, c=c, tsl=tsl: h.activation(out=sq, in_=self.h[:, c, tsl], func=AF.Square),
                      reads=[("h", c, T)], writes=[("nsq", c % 2)])
                self.mm(bank[:, :], self.onesb[:], sq, c == 0, c == NCH - 1, [("nsq", c % 2), "onesb"], [("ps", bi)])
            sd = self.nstd[:, T % 2, :]
            S.act(lambda h, sd=sd, bank=bank: h.activation(out=sd, in_=bank[:, :], func=AF.Sqrt, scale=1.0 / D, bias=NORM_EPS),
                  reads=[("ps", bi)], writes=[("nstd", T % 2)])
            S.dve(lambda h, sd=sd, bank=bank: h.reciprocal(bank[:, :], sd), reads=[("nstd", T % 2)], writes=[("ps", bi)])
            for c in range(NCH):
                if out_fn is None:
                    dst = self.hn[:, c, tsl]
                    wr = [("hn", c, T)]
                else:
                    dst, wr = out_fn(c, T)
                S.dve(lambda h, dst=dst, c=c, tsl=tsl, bank=bank: h.scalar_tensor_tensor(
                    out=dst, in0=self.h[:, c, tsl], scalar=self.prow(grow + c), in1=bank[:, :], op0=ALU.mult, op1=ALU.mult),
                    reads=[("h", c, T), ("ps", bi), "ptab"], writes=wr)

    def load_w(self, slot_ap, src_ap, key, group):
        self.S.dma("pool", group, lambda h: h.dma_start(out=slot_ap, in_=src_ap), writes=[key])

    def phase_attention(self, j):
        nc, S, ps = self.nc, self.S, self.ps
        hn, h = self.hn, self.h
        with ExitStack() as es:
            qT = self.sb(es, [128, 2, SEQ], BF16, "qT")
            kT = self.sb(es, [128, 2, 2, SEQ], BF16, "kT")
            Vt = self.sb(es, [128, 16, 4, 128], BF16, "Vt")
            NW = 5
            wsl = [self.sb(es, [128, 2048], BF16, "wsl") for _ in range(NW)]
            biasT = self.sb(es, [128, 2, 1024], BF16, "biasT")
            kms = self.sb(es, [128, 4, 8], F32, "kms")
            kmT = self.sb(es, [128, 4, 8], BF16, "kmT")
            gsb = self.sb(es, [128, 2, 32], F32, "gsb")
            cmp_ = self.sb(es, [128, 2, 4 * 49], F32, "cmp")
            rank = self.sb(es, [128, 2, 32], F32, "rank")
            btok = self.sb(es, [128, 2, 256], BF16, "btok")
            pT = [self.sb(es, [128, TT], BF16, "pT") for _ in range(4)]
            rec = [self.sb(es, [128, TT], F32, "rec") for _ in range(2)]
            qs = [self.sb(es, [128, TT], BF16, "qs") for _ in range(2)]
            t1 = [self.sb(es, [128, TT], F32, "t1") for _ in range(2)]
            t2 = [self.sb(es, [128, TT], F32, "t2") for _ in range(2)]

            wcount = [0]

            def next_slot():
                i = wcount[0] % NW
                wcount[0] += 1
                return i

            S.pool(lambda h_: h_.memset(Vt[:, :, 0:4:2, 64:128], 1.0), writes=[("Vt1", 0)])
            S.pool(lambda h_: h_.memset(Vt[:, :, 1:4:2, 0:64], 1.0), writes=[("Vt1", 1)])
            S.pool(lambda h_: h_.memset(biasT[:], 0.0), writes=[("biasT", a, u) for a in range(2) for u in range(2)])
            S.pool(lambda h_: h_.memset(kT[64:128, :, 0, :], 0.0), writes=[("kz", 0)])
            S.pool(lambda h_: h_.memset(kT[0:64, :, 1, :], 0.0), writes=[("kz", 1)])

            wq_src = self.w_qkv[j].rearrange("(c p) f -> p c f", p=128)
            wo_src = self.w_o[j].rearrange("(c p) f -> p c f", p=128)

            def load_group(g):
                sl = {}
                for nm, off in (("q", 0), ("k", D), ("v", 2 * D)):
                    i = next_slot()
                    sl[nm] = i
                    self.load_w(wsl[i][:, :].rearrange("p (c f) -> p c f", f=256), wq_src[:, :, off + g * 256: off + (g + 1) * 256],
                                ("wsl", i), "aw%d" % i)
                i = next_slot()
                sl["o"] = i
                self.load_w(wsl[i][:, :].rearrange("p (c f) -> p c f", f=1024), wo_src[:, 2 * g:2 * g + 2, :], ("wsl", i), "aw%d" % i)
                return sl

            pending = load_group(0)
            rope_i = [0]
            sub = self.stop_after.split(":")[1] if (self.stop_after and ":" in self.stop_after) else None
            for g in range(4):
                if sub is not None and g > 0:
                    break
                sl = pending
                wq = wsl[sl["q"]][:, :].rearrange("p (c f) -> p c f", f=256)
                wk = wsl[sl["k"]][:, :].rearrange("p (c f) -> p c f", f=256)
                wv = wsl[sl["v"]][:, :].rearrange("p (c f) -> p c f", f=256)
                wo = wsl[sl["o"]][:, :].rearrange("p (c f) -> p c f", f=1024)
                for which in ("q", "k"):
                    wsrc = wq if which == "q" else wk
                    wkey = ("wsl", sl[which])
                    dstT = qT if which == "q" else kT
                    scale = DH ** -0.5 if which == "q" else 1.0
                    for cc in range(2):
                        for T in range(NT):
                            ri = rope_i[0]
                            rope_i[0] += 1
                            ba, bb = ri % 2, 2 + ri % 2
                            A, B = ps[ba], ps[bb]
                            tsl = slice(T * TT, (T + 1) * TT)
                            for c in range(NCH):
                                self.mm(A[:, :], wsrc[:, c, cc * 128:(cc + 1) * 128], hn[:, c, tsl], c == 0, c == NCH - 1,
                                        [wkey, ("hn", c, T)], [("ps", ba)])
                            q_s, t1_, t2_ = qs[ri % 2], t1[ri % 2], t2[ri % 2]
                            S.act(lambda h_, q_s=q_s, A=A, scale=scale: h_.activation(out=q_s[:], in_=A[:, :], func=AF.Copy, scale=scale),
                                  reads=[("ps", ba)], writes=[("qs", ri % 2)])
                            self.mm(B[:, :], self.prot[:], q_s[:], True, True, [("qs", ri % 2), "prot"], [("ps", bb)])
                            S.dve(lambda h_, t1_=t1_, A=A, scale=scale, tsl=tsl: h_.scalar_tensor_tensor(
                                out=t1_[:], in0=A[:, :], scalar=scale, in1=self.cosT[:, tsl], op0=ALU.mult, op1=ALU.mult),
                                reads=[("ps", ba), "cosT"], writes=[("t1", ri % 2)])
                            S.dve(lambda h_, t2_=t2_, B=B, tsl=tsl: h_.tensor_tensor(out=t2_[:], in0=B[:, :], in1=self.sinT[:, tsl], op=ALU.mult),
                                  reads=[("ps", bb), "sinT"], writes=[("t2", ri % 2)])
                            if which == "q":
                                dst = qT[:, cc, tsl]
                                wr = [("q", cc, T, 0), ("q", cc, T, 1)]
                                S.pool(lambda h_, dst=dst, t1_=t1_, t2_=t2_: h_.tensor_tensor(out=dst, in0=t1_[:], in1=t2_[:], op=ALU.add),
                                       reads=[("t1", ri % 2), ("t2", ri % 2)], writes=wr)
                            else:
                                for hp in range(2):
                                    prt = slice(hp * 64, (hp + 1) * 64)
                                    S.pool(lambda h_, prt=prt, hp=hp, cc=cc, tsl=tsl, t1_=t1_, t2_=t2_: h_.tensor_tensor(
                                        out=kT[prt, cc, hp, tsl], in0=t1_[prt, :], in1=t2_[prt, :], op=ALU.add),
                                        reads=[("t1", ri % 2), ("t2", ri % 2)], writes=[("k", cc, T, hp)])
                if sub == "qk":
                    break
                for tp in range(8):
                    bi = 4 + tp % 2
                    bank = ps[bi]
                    for u in range(2):
                        tt = 2 * tp + u
                        T = tt // 4
                        for c in range(NCH):
                            self.mm(bank[:, u * 256:(u + 1) * 256], hn[:, c, tt * 128:(tt + 1) * 128], wv[:, c, :], c == 0, c == NCH - 1,
                                    [("wsl", sl["v"]), ("hn", c, T)], [("ps", bi)])
                    src = bank[:, :].rearrange("p (u a b e) -> p u a b e", u=2, a=2, b=2)
                    S.act(lambda h_, src=src, tp=tp: h_.activation(out=Vt[:, 2 * tp:2 * tp + 2, 0:4:2, 0:64], in_=src[:, :, :, 0, :], func=AF.Copy),
                          reads=[("ps", bi)], writes=[("Vt", 2 * tp, 0), ("Vt", 2 * tp + 1, 0)])
                    S.dve(lambda h_, src=src, tp=tp: h_.tensor_copy(out=Vt[:, 2 * tp:2 * tp + 2, 1:4:2, 64:128], in_=src[:, :, :, 1, :]),
                          reads=[("ps", bi)], writes=[("Vt", 2 * tp, 1), ("Vt", 2 * tp + 1, 1)])
                if sub == "v":
                    break
                if g + 1 < 4 and sub is None:
                    pending = load_group(g + 1)
                for ch in range(4):
                    cc, hp = ch // 2, ch % 2
                    S.dve(lambda h_, cc=cc, hp=hp, ch=ch: h_.tensor_reduce(out=kms[:, ch, :], in_=kT[:, cc, hp, :].rearrange("p (n k) -> p n k", k=256),
                                                                       axis=AX.X, op=ALU.add),
                          reads=[("k", cc, T, hp) for T in range(NT)] + [("kz", hp)], writes=[("kms", ch)])
                    S.dve(lambda h_, ch=ch: h_.tensor_scalar(out=kmT[:, ch, :], in0=kms[:, ch, :], scalar1=1.0 / 256.0, scalar2=None, op0=ALU.mult),
                          reads=[("kms", ch)], writes=[("kmT", ch)])
                import os
                GL = int(os.environ.get("GATE_LEVEL", "9"))
                for qt in range(8, 16):
                    if GL < 2:
                        break
                    qb = qt // 2
                    T = qt // 4
                    sI = qt % 2
                    gi = 6 + qt % 2
                    gbank = ps[gi]
                    for hl in range(4):
                        cc, hp = hl // 2, hl % 2
                        self.mm(gbank[:, hl * 8:hl * 8 + 8], qT[:, cc, qt * 128:(qt + 1) * 128],
                                kmT[:, hl, :], True, True,
                                [("q", cc, T, 0), ("q", cc, T, 1), ("kmT", hl)], [("ps", gi)])
                    S.act(lambda h_, sI=sI, gbank=gbank: h_.activation(out=gsb[:, sI, :], in_=gbank[:, 0:32], func=AF.Copy),
                          reads=[("ps", gi)], writes=[("gsb", sI)])
                    if GL < 3:
                        continue
                    g3 = gsb[:, sI, :].rearrange("p (a n) -> p a n", n=8)[:, :, 0:qb]
                    in0 = g3.unsqueeze(2).broadcast_to([128, 4, qb, qb])
                    in1 = g3.unsqueeze(3).broadcast_to([128, 4, qb, qb])
                    cm = cmp_[:, sI, 0:4 * qb * qb].rearrange("p (a n m) -> p a n m", a=4, n=qb)
                    S.dve(lambda h_, cm=cm, in0=in0, in1=in1: h_.tensor_tensor(out=cm, in0=in0, in1=in1, op=ALU.is_gt),
                          reads=[("gsb", sI)], writes=[("cmp", sI)])
                    rk = rank[:, sI, :].rearrange("p (a n) -> p a n", n=8)[:, :, 0:qb]
                    S.dve(lambda h_, rk=rk, cm=cm: h_.tensor_reduce(out=rk, in_=cm, axis=AX.X, op=ALU.add),
                          reads=[("cmp", sI)], writes=[("rank", sI)])
                    if GL < 4:
                        continue
                    S.pool(lambda h_, sI=sI: h_.memset(btok[:, sI, :], 0.0), writes=[("btok", sI)])
                    bo = btok[:, sI, :].rearrange("p (a b n) -> p a b n", a=2, b=2)[:, :, :, 0:qb]
                    rk4 = rank[:, sI, :].rearrange("p (a b n) -> p a b n", a=2, b=2)[:, :, :, 0:qb]
                    S.dve(lambda h_, bo=bo, rk4=rk4: h_.tensor_scalar(out=bo, in0=rk4, scalar1=2.5, scalar2=NEG, op0=ALU.is_gt, op1=ALU.mult),
                          reads=[("rank", sI)], writes=[("btok", sI)])
                    if GL < 5:
                        continue
                    for a in range(2):
                        bti = 4 + a
                        self.mm(ps[bti][:, (qt % 4) * 128:(qt % 4 + 1) * 128], btok[:, sI, a * 128:(a + 1) * 128], self.identb[:], True, True,
                                [("btok", sI), "identb"], [("ps", bti)])
                        if qt % 4 == 3 and GL != 5:
                            q0 = (qt - 3 - 8) * 128
                            S.act(lambda h_, a=a, q0=q0, bti=bti: h_.activation(out=biasT[:, a, q0:q0 + TT], in_=ps[bti][:, :], func=AF.Copy),
                                  reads=[("ps", bti)], writes=[("biasT", a, q0 // TT)])
                if sub == "gate":
                    break
                it = 0
                for cc in range(2):
                    for T in range(NT):
                        nk = 4 * T + 4
                        ob = [4 + 2 * (it % 2), 5 + 2 * (it % 2)]
                        it += 1
                        for kt in range(nk):
                            n = kt // 2
                            q_lo = max(0, kt - 4 * T) * 128
                            qsl = slice(q_lo, TT)
                            Tk = kt // 4
                            for hp in range(2):
                                hl = 2 * cc + hp
                                prt = slice(hp * 64, (hp + 1) * 64)
                                sbi = (kt % 2) * 2 + hp
                                sbk = ps[sbi]
                                need_bias = (T >= 2 and kt < 4 * T + 2)
                                need_mask = kt >= 4 * T
                                self.mm(sbk[:, qsl], kT[:, cc, hp, kt * 128:(kt + 1) * 128], qT[:, cc, T * TT + q_lo:(T + 1) * TT],
                                        True, not (need_bias or need_mask),
                                        [("k", cc, Tk, hp), ("kz", hp), ("q", cc, T, 0), ("q", cc, T, 1)], [("ps", sbi)])
                                if need_bias:
                                    self.mm(sbk[:, qsl], self.ind[hp][:, n * 128:(n + 1) * 128],
                                            biasT[:, cc, (T - 2) * TT + q_lo:(T - 1) * TT],
                                            False, not need_mask, ["ind", ("biasT", cc, T - 2)], [("ps", sbi)])
                                if need_mask:
                                    self.mm(sbk[:, q_lo:q_lo + 128], self.identb[:], self.maskT[:], False, True,
                                            ["identb", "maskT"], [("ps", sbi)])
                                pi = (kt * 2 + hp) % 4
                                S.act(lambda h_, pi=pi, sbk=sbk, qsl=qsl: h_.activation(out=pT[pi][:, qsl], in_=sbk[:, qsl], func=AF.Exp),
                                      reads=[("ps", sbi)], writes=[("pT", pi)])
                                self.mm(ps[ob[hp]][:, qsl], Vt[:, kt, hl, :], pT[pi][:, qsl], kt == 0, kt == nk - 1,
                                        [("pT", pi), ("Vt", kt, hp), ("Vt1", hp)], [("ps", ob[hp])])
                        for hp in range(2):
                            o = ps[ob[hp]]
                            num = slice(hp * 64, (hp + 1) * 64)
                            den = slice((1 - hp) * 64, (2 - hp) * 64)
                            rc = rec[hp]
                            S.dve(lambda h_, rc=rc, o=o, den=den: h_.reciprocal(rc[den, :], o[den, :]),
                                  reads=[("ps", ob[hp])], writes=[("rec", hp)])
                            S.dve(lambda h_, rc=rc, o=o, den=den, num=num, cc=cc, T=T: h_.tensor_tensor(
                                out=qT[num, cc, T * TT:(T + 1) * TT], in0=o[num, :], in1=rc[den, :], op=ALU.mult),
                                reads=[("ps", ob[hp]), ("rec", hp)], writes=[("q", cc, T, hp)])
                if sub == "core":
                    break
                for T in range(NT):
                    tsl = slice(T * TT, (T + 1) * TT)
                    for dc in range(NCH):
                        bi = (T * NCH + dc) % 4
                        for cc in range(2):
                            self.mm(ps[bi][:, :], wo[:, cc, dc * 128:(dc + 1) * 128], qT[:, cc, tsl], cc == 0, cc == 1,
                                    [("wsl", sl["o"]), ("q", cc, T, 0), ("q", cc, T, 1)], [("ps", bi)])
                        S.dve(lambda h_, bi=bi, dc=dc, tsl=tsl: h_.tensor_tensor(out=h[:, dc, tsl], in0=ps[bi][:, :], in1=h[:, dc, tsl], op=ALU.add),
                              reads=[("ps", bi), ("h", dc, T)], writes=[("h", dc, T)])
            S.barrier()
            S.flush()

    def phase_conformer(self, j):
        nc, S, ps = self.nc, self.S, self.ps
        hn, h = self.hn, self.h
        PADL = 32
        with ExitStack() as es:
            glu = self.sb(es, [128, NCH, PADL + SEQ], BF16, "glu")
            ybf = self.sb(es, [128, NCH, TT], BF16, "ybf")
            ysq = self.sb(es, [128, NCH, TT], BF16, "ysq")
            dg = self.sb(es, [128, CW, 128], BF16, "dg")
            NW = 3
            wsl = [self.sb(es, [128, NCH, 256], BF16, "cw") for _ in range(NW)]
            sgm = [self.sb(es, [128, TT], F32, "sgm") for _ in range(2)]
            tA = [self.sb(es, [128, TT], F32, "tA") for _ in range(2)]
            mean_t = self.sb(es, [128, TT], F32, "mean_t")
            m2_t = self.sb(es, [128, TT], F32, "m2_t")
            std_t = self.sb(es, [128, TT], F32, "std_t")
            wcount = [0]

            def next_slot():
                i = wcount[0] % NW
                wcount[0] += 1
                return i

            S.pool(lambda h_: h_.memset(glu[:, :, 0:PADL], 0.0), writes=[("glupad",)])
            w1 = self.w_pw1[j].rearrange("(c p) f -> p c f", p=128)
            w2 = self.w_pw2[j].rearrange("(c p) f -> p c f", p=128)

            def load_pw1(cb):
                ia = next_slot()
                self.load_w(wsl[ia][:], w1[:, :, cb * 256:(cb + 1) * 256], ("cw", ia), "cw%d" % ia)
                ig = next_slot()
                self.load_w(wsl[ig][:], w1[:, :, D + cb * 256:D + (cb + 1) * 256], ("cw", ig), "cw%d" % ig)
                return ia, ig

            it = 0
            for cb in range(4):
                ia, ig = load_pw1(cb)
                for ci in range(2):
                    cc = 2 * cb + ci
                    for T in range(NT):
                        ba, bg = (it % 2) * 2, (it % 2) * 2 + 1
                        it += 1
                        tsl = slice(T * TT, (T + 1) * TT)
                        for c in range(NCH):
                            self.mm(ps[ba][:, :], wsl[ia][:, c, ci * 128:(ci + 1) * 128], hn[:, c, tsl], c == 0, c == NCH - 1,
                                    [("cw", ia), ("hn", c, T)], [("ps", ba)])
                        for c in range(NCH):
                            self.mm(ps[bg][:, :], wsl[ig][:, c, ci * 128:(ci + 1) * 128], hn[:, c, tsl], c == 0, c == NCH - 1,
                                    [("cw", ig), ("hn", c, T)], [("ps", bg)])
                        sg_ = sgm[it % 2]
                        S.act(lambda h_, sg_=sg_, bg=bg, cc=cc: h_.activation(out=sg_[:], in_=ps[bg][:, :], func=AF.Sigmoid,
                                                                               bias=self.prow(R_BPW1 + j * 16 + 8 + cc)),
                              reads=[("ps", bg), "ptab"], writes=[("sgm", it % 2)])
                        S.dve(lambda h_, sg_=sg_, ba=ba, cc=cc, T=T: h_.scalar_tensor_tensor(
                            out=glu[:, cc, PADL + T * TT:PADL + (T + 1) * TT], in0=ps[ba][:, :], scalar=self.prow(R_BPW1 + j * 16 + cc),
                            in1=sg_[:], op0=ALU.add, op1=ALU.mult),
                            reads=[("ps", ba), ("sgm", it % 2), "ptab"], writes=[("glu", cc, T)])
            pw2_slots = {}

            def load_pw2(db):
                i = next_slot()
                self.load_w(wsl[i][:], w2[:, :, db * 256:(db + 1) * 256], ("cw", i), "cw%d" % i)
                pw2_slots[db] = i

            load_pw2(0)
            load_pw2(1)
            dcount = 0
            for T in range(NT):
                for cc in range(NCH):
                    yb = cc % 2
                    for tap in range(CW):
                        if dcount % 2 == 0:
                            S.dve(lambda h_, tap=tap, cc=cc: h_.tensor_scalar(out=dg[:, tap, :], in0=self.identb[:],
                                                                              scalar1=self.prow(R_WDW + (j * CW + tap) * 8 + cc), scalar2=None, op0=ALU.mult),
                                  reads=["identb", "ptab"], writes=[("dg", tap)])
                        else:
                            S.pool(lambda h_, tap=tap, cc=cc: h_.tensor_scalar(out=dg[:, tap, :], in0=self.identb[:],
                                                                               scalar1=self.prow(R_WDW + (j * CW + tap) * 8 + cc), scalar2=1.0,
                                                                               op0=ALU.mult, op1=ALU.mult),
                                   reads=["identb", "ptab"], writes=[("dg", tap)])
                        dcount += 1
                    for tap in range(CW):
                        o0 = PADL + T * TT - (CW - 1) + tap
                        rd = [("dg", tap), ("glu", cc, T)]
                        if T > 0:
                            rd.append(("glu", cc, T - 1))
                        else:
                            rd.append(("glupad",))
                        self.mm(ps[yb][:, :], dg[:, tap, :], glu[:, cc, o0:o0 + TT], tap == 0, tap == CW - 1, rd, [("ps", yb)])
                    S.act(lambda h_, cc=cc, yb=yb: h_.activation(out=ybf[:, cc, :], in_=ps[yb][:, :], func=AF.Identity,
                                                                  bias=self.prow(R_BDW + j * 8 + cc)),
                          reads=[("ps", yb), "ptab"], writes=[("ybf", cc)])
                    S.act(lambda h_, cc=cc: h_.activation(out=ysq[:, cc, :], in_=ybf[:, cc, :], func=AF.Square),
                          reads=[("ybf", cc)], writes=[("ysq", cc)])
                bm, bq = 2 + (T % 2) * 2, 3 + (T % 2) * 2
                for cc in range(NCH):
                    self.mm(ps[bm][:, :], self.onesb[:], ybf[:, cc, :], cc == 0, cc == NCH - 1, ["onesb", ("ybf", cc)], [("ps", bm)])
                for cc in range(NCH):
                    self.mm(ps[bq][:, :], self.onesb[:], ysq[:, cc, :], cc == 0, cc == NCH - 1, ["onesb", ("ysq", cc)], [("ps", bq)])
                S.act(lambda h_, bm=bm: h_.activation(out=mean_t[:], in_=ps[bm][:, :], func=AF.Copy, scale=1.0 / D),
                      reads=[("ps", bm)], writes=["mean_t"])
                S.dve(lambda h_: h_.tensor_tensor(out=m2_t[:], in0=mean_t[:], in1=mean_t[:], op=ALU.mult), reads=["mean_t"], writes=["m2_t"])
                S.dve(lambda h_, bq=bq: h_.scalar_tensor_tensor(out=m2_t[:], in0=ps[bq][:, :], scalar=1.0 / D, in1=m2_t[:],
                                                                 op0=ALU.mult, op1=ALU.subtract),
                      reads=[("ps", bq), "m2_t"], writes=["m2_t"])
                S.act(lambda h_: h_.activation(out=std_t[:], in_=m2_t[:], func=AF.Sqrt, bias=LN_EPS), reads=["m2_t"], writes=["std_t"])
                S.dve(lambda h_, bm=bm: h_.reciprocal(ps[bm][:, :], std_t[:]), reads=["std_t"], writes=[("ps", bm)])
                S.dve(lambda h_, bm=bm, bq=bq: h_.tensor_tensor(out=ps[bq][:, :], in0=ps[bm][:, :], in1=mean_t[:], op=ALU.mult),
                      reads=[("ps", bm), "mean_t"], writes=[("ps", bq)])
                for cc in range(NCH):
                    ta_ = tA[cc % 2]
                    S.dve(lambda h_, ta_=ta_, cc=cc, bm=bm: h_.tensor_tensor(out=ta_[:], in0=ps[bm][:, :], in1=ybf[:, cc, :], op=ALU.mult),
                          reads=[("ps", bm), ("ybf", cc)], writes=[("tA", cc % 2)])
                    S.dve(lambda h_, ta_=ta_, bq=bq: h_.tensor_tensor(out=ta_[:], in0=ta_[:], in1=ps[bq][:, :], op=ALU.subtract),
                          reads=[("ps", bq), ("tA", cc % 2)], writes=[("tA", cc % 2)])
                    S.act(lambda h_, ta_=ta_, cc=cc, T=T: h_.activation(out=hn[:, cc, T * TT:(T + 1) * TT], in_=ta_[:], func=AF.Silu,
                                                                         scale=self.prow(R_LNG + j * 8 + cc), bias=self.prow(R_LNB + j * 8 + cc)),
                          reads=[("tA", cc % 2), "ptab"], writes=[("hn", cc, T)])
            it = 0
            for db in range(4):
                if db + 2 < 4:
                    load_pw2(db + 2)
                i = pw2_slots[db]
                for di in range(2):
                    dc = 2 * db + di
                    for T in range(NT):
                        bi = 6 + it % 2
                        it += 1
                        tsl = slice(T * TT, (T + 1) * TT)
                        for cc in range(NCH):
                            self.mm(ps[bi][:, :], wsl[i][:, cc, di * 128:(di + 1) * 128], hn[:, cc, tsl], cc == 0, cc == NCH - 1,
                                    [("cw", i), ("hn", cc, T)], [("ps", bi)])
                        S.dve(lambda h_, bi=bi, dc=dc, tsl=tsl: h_.scalar_tensor_tensor(
                            out=h[:, dc, tsl], in0=ps[bi][:, :], scalar=self.prow(R_BPW2 + j * 8 + dc), in1=h[:, dc, tsl], op0=ALU.add, op1=ALU.add),
                            reads=[("ps", bi), ("h", dc, T), "ptab"], writes=[("h", dc, T)])
            S.barrier()
            S.flush()

    def phase_ffn(self, i):
        nc, S, ps = self.nc, self.S, self.ps
        hn, h = self.hn, self.h
        groups = [[0, 1, 2], [3, 4, 5], [6, 7, 8], [9, 10]]
        with ExitStack() as es:
            act = self.sb(es, [128, 6, SEQ], BF16, "act")
            NWU = 3
            wup = [self.sb(es, [128, NCH, 512], BF16, "wup") for _ in range(NWU)]
            wdn = [self.sb(es, [128, 6, D], BF16, "wdn") for _ in range(2)]
            Ag = [self.sb(es, [128, TT], F32, "Ag") for _ in range(2)]
            Av = [self.sb(es, [128, TT], F32, "Av") for _ in range(2)]
            sg = [self.sb(es, [128, TT], F32, "sg") for _ in range(2)]
            halo = self.sb(es, [128, 2, 2, 2], F32, "halo")
            wu_src = self.w_up[i].rearrange("(c p) f -> p c f", p=128)
            wd_src = self.w_down[i].rearrange("(c p) f -> p c f", p=128)
            ucount = [0]
            blocks = [b for g in groups for b in g]
            up_slot = {}

            def load_up(b):
                s = ucount[0] % NWU
                ucount[0] += 1
                up_slot[b] = s
                self.load_w(wup[s][:, :, 0:256], wu_src[:, :, b * 256:(b + 1) * 256], ("wup", s), "wu%d" % s)
                self.load_w(wup[s][:, :, 256:512], wu_src[:, :, DFF + b * 256:DFF + (b + 1) * 256], ("wup", s), "wu%d" % s)

            def load_dn(gi):
                import os
                if os.environ.get("FFN_NODN"):
                    return
                g = groups[gi]
                np_ = 2 * len(g)
                j0 = 2 * g[0]
                self.load_w(wdn[gi % 2][:, 0:np_, :], wd_src[:, j0:j0 + np_, :], ("wdn", gi % 2), "wd%d" % (gi % 2))

            load_up(blocks[0])
            load_up(blocks[1])
            load_dn(0)
            nxt = 2
            it = 0
            for gi, g in enumerate(groups):
                if gi + 1 < len(groups):
                    load_dn(gi + 1)
                for bl, b in enumerate(g):
                    if nxt < len(blocks):
                        load_up(blocks[nxt])
                        nxt += 1
                    s = up_slot[b]
                    for pi in range(2):
                        jp = 2 * b + pi
                        jl = 2 * bl + pi
                        rows = {}
                        for kind, fc in (("g", jp), ("v", NFP + jp)):
                            rows[kind] = [R_FWDW + (i * 3 + tap) * 44 + fc for tap in range(3)] + [R_FBDW + i * 44 + fc]
                        for T in range(NT):
                            par = it % 2
                            it += 1
                            tsl = slice(T * TT, (T + 1) * TT)
                            bg_, bv_ = par * 2, par * 2 + 1
                            import os
                            for c in range(NCH if not os.environ.get("FFN_NOMM") else 0):
                                self.mm(ps[bg_][:, :], wup[s][:, c, pi * 128:(pi + 1) * 128], hn[:, c, tsl], c == 0, c == NCH - 1,
                                        [("wup", s), ("hn", c, T)], [("ps", bg_)])
                            for c in range(NCH if not os.environ.get("FFN_NOMM") else 0):
                                self.mm(ps[bv_][:, :], wup[s][:, c, 256 + pi * 128:256 + (pi + 1) * 128], hn[:, c, tsl], c == 0, c == NCH - 1,
                                        [("wup", s), ("hn", c, T)], [("ps", bv_)])
                            A = {"g": Ag[par], "v": Av[par]}
                            U = {"g": ps[bg_], "v": ps[bv_]}
                            UB = {"g": bg_, "v": bv_}
                            AK = {"g": ("Ag", par), "v": ("Av", par)}
                            KI = {"g": 0, "v": 1}
                            hp_prev = (T - 1) % 2
                            hp_cur = T % 2
                            import os
                            FL = int(os.environ.get("FFN_LEVEL", "9"))
                            for kind in ("g", "v"):
                                if FL < 2:
                                    break
                                r = rows[kind]
                                S.act(lambda h_, A_=A[kind], U_=U[kind], r=r: h_.activation(out=A_[:], in_=U_[:, :], func=AF.Identity,
                                                                                          scale=self.prow(r[2]), bias=self.prow(r[3])),
                                      reads=[("ps", UB[kind]), "ptab"], writes=[AK[kind]])
                                if T < NT - 1 and FL >= 3:
                                    S.act(lambda h_, U_=U[kind], kind=kind, hp_cur=hp_cur: h_.activation(
                                        out=halo[:, hp_cur, KI[kind], :], in_=U_[:, TT - 2:TT], func=AF.Copy),
                                        reads=[("ps", UB[kind])], writes=[("halo", hp_cur, kind)])
                            for kind in ("g", "v"):
                                if FL < 4:
                                    break
                                r = rows[kind]
                                S.dve(lambda h_, A_=A[kind], U_=U[kind], r=r: h_.scalar_tensor_tensor(
                                    out=A_[:, 1:TT], in0=U_[:, 0:TT - 1], scalar=self.prow(r[1]), in1=A_[:, 1:TT], op0=ALU.mult, op1=ALU.add),
                                    reads=[("ps", UB[kind]), AK[kind], "ptab"], writes=[AK[kind]])
                            for kind in ("g", "v"):
                                if FL < 4:
                                    break
                                r = rows[kind]
                                S.dve(lambda h_, A_=A[kind], U_=U[kind], r=r: h_.scalar_tensor_tensor(
                                    out=A_[:, 2:TT], in0=U_[:, 0:TT - 2], scalar=self.prow(r[0]), in1=A_[:, 2:TT], op0=ALU.mult, op1=ALU.add),
                                    reads=[("ps", UB[kind]), AK[kind], "ptab"], writes=[AK[kind]])
                            if T > 0 and FL >= 5:
                                for kind in ("g", "v"):
                                    r = rows[kind]
                                    hl_ = halo[:, hp_prev, KI[kind], :]
                                    S.dve(lambda h_, A_=A[kind], hl_=hl_, r=r: h_.scalar_tensor_tensor(
                                        out=A_[:, 0:1], in0=hl_[:, 1:2], scalar=self.prow(r[1]), in1=A_[:, 0:1], op0=ALU.mult, op1=ALU.add),
                                        reads=[("halo", hp_prev, kind), AK[kind], "ptab"], writes=[AK[kind]])
                                for kind in ("g", "v"):
                                    r = rows[kind]
                                    hl_ = halo[:, hp_prev, KI[kind], :]
                                    S.dve(lambda h_, A_=A[kind], hl_=hl_, r=r: h_.scalar_tensor_tensor(
                                        out=A_[:, 0:2], in0=hl_[:, 0:2], scalar=self.prow(r[0]), in1=A_[:, 0:2], op0=ALU.mult, op1=ALU.add),
                                        reads=[("halo", hp_prev, kind), AK[kind], "ptab"], writes=[AK[kind]])
                            if FL < 6:
                                continue
                            sg_ = sg[par]
                            S.act(lambda h_, sg_=sg_, A_=A["g"]: h_.activation(out=sg_[:], in_=A_[:], func=AF.Silu),
                                  reads=[AK["g"]], writes=[("sg", par)])
                            S.pool(lambda h_, sg_=sg_, A_=A["v"], jl=jl, tsl=tsl: h_.tensor_tensor(out=act[:, jl, tsl], in0=sg_[:], in1=A_[:], op=ALU.mult),
                                   reads=[("sg", par), AK["v"]], writes=[("act", jl, T)])
                np_ = 2 * len(g)
                wd = wdn[gi % 2]
                dn = 0
                for T in range(NT if FL >= 7 else 0):
                    tsl = slice(T * TT, (T + 1) * TT)
                    for dc in range(NCH):
                        bi = 4 + dn % 2
                        dn += 1
                        for jl in range(np_):
                            self.mm(ps[bi][:, :], wd[:, jl, dc * 128:(dc + 1) * 128], act[:, jl, tsl], jl == 0, jl == np_ - 1,
                                    [("wdn", gi % 2), ("act", jl, T)], [("ps", bi)])
                        S.dve(lambda h_, bi=bi, dc=dc, tsl=tsl: h_.tensor_tensor(out=h[:, dc, tsl], in0=ps[bi][:, :], in1=h[:, dc, tsl], op=ALU.add),
                              reads=[("ps", bi), ("h", dc, T)], writes=[("h", dc, T)])
            S.barrier()
            S.flush()

    def phase_final(self, raw=False):
        nc, S, ps = self.nc, self.S, self.ps
        with ExitStack() as es:
            ofm = self.sb(es, [128, NCH, TT], F32, "ofm")
            ost = [self.sb(es, [128, D], F32, "ost") for _ in range(3)]
            oc = 0
            for T in range(NT):
                tsl = slice(T * TT, (T + 1) * TT)
                if raw:
                    for c in range(NCH):
                        eng = S.act if c % 2 == 0 else S.dve
                        if c % 2 == 0:
                            S.act(lambda h_, c=c, tsl=tsl: h_.activation(out=ofm[:, c, :], in_=self.h[:, c, tsl], func=AF.Copy),
                                  reads=[("h", c, T)], writes=[("ofm", c)])
                        else:
                            S.dve(lambda h_, c=c, tsl=tsl: h_.tensor_copy(out=ofm[:, c, :], in_=self.h[:, c, tsl]),
                                  reads=[("h", c, T)], writes=[("ofm", c)])
                else:
                    self.rmsnorm_tile(T, R_FIN, ofm)
                for ts in range(4):
                    tt = T * 4 + ts
                    sl = oc % 3
                    oc += 1
                    for half in range(2):
                        bi = (2 * tt + half) % 4
                        bank = ps[bi]
                        for jj in range(4):
                            c = half * 4 + jj
                            S.pe(lambda h_, bank=bank, jj=jj, c=c, ts=ts: h_.transpose(out=bank[:, jj * 128:(jj + 1) * 128],
                                                                                         in_=ofm[:, c, ts * 128:(ts + 1) * 128], identity=self.identf[:]),
                                 reads=[("ofm", c), "identf"], writes=[("ps", bi)])
                        dst = ost[sl][:, half * 512:(half + 1) * 512]
                        if half == 0:
                            S.act(lambda h_, dst=dst, bank=bank: h_.activation(out=dst, in_=bank[:, :], func=AF.Copy),
                                  reads=[("ps", bi)], writes=[("ost", sl, half)])
                        else:
                            S.dve(lambda h_, dst=dst, bank=bank: h_.tensor_copy(out=dst, in_=bank[:, :]),
                                  reads=[("ps", bi)], writes=[("ost", sl, half)])
                    S.dma("sp", "ost%d" % sl, lambda h_, sl=sl, tt=tt: h_.dma_start(out=self.out[tt * 128:(tt + 1) * 128, :], in_=ost[sl][:]),
                          reads=[("ost", sl, 0), ("ost", sl, 1)])
            S.barrier()
            S.flush()

    def rmsnorm_tile(self, T, grow, ofm):
        S, ps = self.S, self.ps
        bi = 6 + T % 2
        bank = ps[bi]
        tsl = slice(T * TT, (T + 1) * TT)
        for c in range(NCH):
            sq = self.nsq[:, c % 2, :]
            S.act(lambda h, sq=sq, c=c: h.activation(out=sq, in_=self.h[:, c, tsl], func=AF.Square),
                  reads=[("h", c, T)], writes=[("nsq", c % 2)])
            self.mm(bank[:, :], self.onesb[:], sq, c == 0, c == NCH - 1, [("nsq", c % 2), "onesb"], [("ps", bi)])
        sd = self.nstd[:, T % 2, :]
        S.act(lambda h: h.activation(out=sd, in_=bank[:, :], func=AF.Sqrt, scale=1.0 / D, bias=NORM_EPS),
              reads=[("ps", bi)], writes=[("nstd", T % 2)])
        S.dve(lambda h: h.reciprocal(bank[:, :], sd), reads=[("nstd", T % 2)], writes=[("ps", bi)])
        for c in range(NCH):
            S.dve(lambda h, c=c: h.scalar_tensor_tensor(out=ofm[:, c, :], in0=self.h[:, c, tsl], scalar=self.prow(grow + c), in1=bank[:, :],
                                                        op0=ALU.mult, op1=ALU.mult),
                  reads=[("h", c, T), ("ps", bi), "ptab"], writes=[("ofm", c)])


def _pack_ptab(inp):
    f = lambda a: np.ascontiguousarray(np.asarray(a, dtype=np.float32)).reshape(-1, 128)
    parts = [
        f(inp["norm_mix_g"]), f(inp["norm_ffn_g"]), f(inp["final_norm_g"]), f(inp["conv_b_pw1"]),
        f(inp["conv_w_dw"]), f(inp["conv_b_dw"]), f(inp["conv_ln_g"]), f(inp["conv_ln_b"]), f(inp["conv_b_pw2"]),
        f(inp["ffn_w_dw"]), f(inp["ffn_b_dw"]),
    ]
    tab = np.concatenate(parts, axis=0)
    assert tab.shape[0] == 1368
    pad = np.zeros((R_TOT - tab.shape[0], 128), np.float32)
    return np.ascontiguousarray(np.concatenate([tab, pad], axis=0))


_NC_CACHE = {}


def _run(inputs, stop_after=None, trace=False):
    x = np.ascontiguousarray(np.asarray(inputs["x"], dtype=np.float32))
    B = x.shape[0]
    key = stop_after
    if key not in _NC_CACHE:
        _NC_CACHE[key] = Builder(stop_after).build()
    nc = _NC_CACHE[key]
    ptab = _pack_ptab(inputs)
    c = lambda k: np.ascontiguousarray(np.asarray(inputs[k], dtype=np.float32))
    shared = {
        "ptab_in": ptab, "w_qkv": c("attn_w_qkv"), "w_o": c("attn_w_o"), "w_pw1": c("conv_w_pw1"), "w_pw2": c("conv_w_pw2"),
        "w_up": c("ffn_w_up"), "w_down": c("ffn_w_down"),
    }
    in_maps = [dict(shared, x=x[b]) for b in range(B)]
    res = run_bass_kernel_spmd(nc, in_maps, core_ids=list(range(B)), trace=trace)
    out = np.stack([np.asarray(r["out"]) for r in res.results], axis=0).astype(np.float32)
    return out, res


def kernel(**inputs):
    out, _ = _run(inputs)
    return out
```

```python
import math
import numpy as np
from contextlib import ExitStack
import concourse.bass as bass
import concourse.mybir as mybir
from concourse.bass_utils import run_bass_kernel_spmd

F32 = mybir.dt.float32
BF16 = mybir.dt.bfloat16
I32 = mybir.dt.int32
ALU = mybir.AluOpType
AF = mybir.ActivationFunctionType
AX = mybir.AxisListType

D = 1024
SEQ = 2048
NCH = 8
NT = 4
TT = 512
H = 16
DH = 64
DFF = 2816
NFP = 22
DEPTH = 4
NEG = -30000.0
NORM_EPS = 1e-6
LN_EPS = 1e-5
CW = 31

R_MIX = 0
R_FFN = 32
R_FIN = 64
R_BPW1 = 72
R_WDW = 104
R_BDW = 600
R_LNG = 616
R_LNB = 632
R_BPW2 = 648
R_FWDW = 664
R_FBDW = 1192
R_TOT = 1408


class _Op:
    __slots__ = ("eng", "fn", "deps", "needs_inc", "semval", "grp", "gen", "pre")

    def __init__(self, eng, fn):
        self.eng = eng
        self.fn = fn
        self.deps = []
        self.needs_inc = False
        self.semval = None
        self.grp = None
        self.gen = 0
        self.pre = None


class _Grp:
    __slots__ = ("sem", "gens", "closed")

    def __init__(self, sem):
        self.sem = sem
        self.gens = [0]
        self.closed = False


class Sched:
    ENGS = ("pe", "act", "dve", "pool", "sp")

    def __init__(self, nc, es):
        self.nc = nc
        self.es = es
        self.q = {e: [] for e in self.ENGS}
        self.lastw = {}
        self.readers = {}
        self.esem = {e: es.enter_context(nc.semaphore("sem_" + e)) for e in ("pe", "act", "dve", "pool")}
        self.ecnt = {e: 0 for e in ("pe", "act", "dve", "pool")}
        self.seen = {e: {} for e in self.ENGS}
        self.groups = {}
        self.lastreal = {e: None for e in self.ENGS}
        self.nops = 0

    def _group(self, name):
        g = self.groups.get(name)
        if g is None:
            g = _Grp(self.es.enter_context(self.nc.semaphore("dg_" + name)))
            self.groups[name] = g
        return g

    def add(self, eng, fn, reads=(), writes=(), dma=None):
        op = _Op(eng, fn)
        deps = {}
        for k in reads:
            w = self.lastw.get(k)
            if w is not None:
                deps[id(w)] = w
            if isinstance(k, tuple) and k[0] == "ps":
                rd = self.readers.get(k)
                if rd:
                    for rk_, r in rd.items():
                        if rk_ != eng:
                            deps[id(r)] = r
        for k in writes:
            w = self.lastw.get(k)
            if w is not None:
                deps[id(w)] = w
            rd = self.readers.get(k)
            if rd:
                for r in rd.values():
                    deps[id(r)] = r
        if dma is not None:
            g = self._group(dma)
            op.grp = g
            if g.closed:
                op.pre = [(g, len(g.gens) - 1)]
                g.gens.append(g.gens[-1])
                g.closed = False
            g.gens[-1] += 16
            op.gen = len(g.gens) - 1
        for d in deps.values():
            if eng == "pe" and d.eng == "pe" and d.grp is None:
                continue
            if op.grp is not None and d.grp is op.grp and d.gen == op.gen:
                continue
            op.deps.append(d)
            d.needs_inc = True
            if d.grp is not None:
                d.grp.closed = True
        rk = eng if dma is None else ("dma", id(op))
        for k in reads:
            self.readers.setdefault(k, {})[rk] = op
        for k in writes:
            self.lastw[k] = op
            self.readers[k] = {}
        self.q[eng].append(op)
        if dma is None:
            self.lastreal[eng] = op
        self.nops += 1
        return op

    def pe(self, fn, reads=(), writes=()):
        return self.add("pe", fn, reads, writes)

    def act(self, fn, reads=(), writes=()):
        return self.add("act", fn, reads, writes)

    def dve(self, fn, reads=(), writes=()):
        return self.add("dve", fn, reads, writes)

    def pool(self, fn, reads=(), writes=()):
        return self.add("pool", fn, reads, writes)

    def dma(self, queue, group, fn, reads=(), writes=()):
        return self.add(queue, fn, reads, writes, dma=group)

    def barrier(self):
        lasts = [op for op in self.lastreal.values() if op is not None and op.grp is None]
        gl = [(g, len(g.gens) - 1) for g in self.groups.values() if g.gens[-1] > 0]
        for g, _ in gl:
            g.closed = True
        for e in self.ENGS:
            op = _Op(e, None)
            for d in lasts:
                if not (e == "pe" and d.eng == "pe"):
                    op.deps.append(d)
                    d.needs_inc = True
            op.pre = list(gl)
            self.q[e].append(op)
        self.lastw = {}
        self.readers = {}

    def _assign(self):
        for e in ("pe", "act", "dve", "pool"):
            c = self.ecnt[e]
            for op in self.q[e]:
                if op.grp is None and op.fn is not None and op.needs_inc and op.semval is None:
                    c += 1
                    op.semval = c
            self.ecnt[e] = c

    def _emit_engine(self, e, h):
        seen = self.seen[e]

        def wait(sem, val):
            key = id(sem)
            if seen.get(key, 0) < val:
                h.wait_ge(sem, val)
                seen[key] = val

        for op in self.q[e]:
            if op.pre is not None:
                for g, gi in op.pre:
                    wait(g.sem, g.gens[gi])
            for d in op.deps:
                if d.grp is not None:
                    wait(d.grp.sem, d.grp.gens[d.gen])
                else:
                    wait(self.esem[d.eng], d.semval)
            if op.fn is not None:
                ins = op.fn(h)
                if op.grp is not None:
                    ins.then_inc(op.grp.sem, 16)
                elif op.needs_inc:
                    ins.then_inc(self.esem[e], 1)
        self.q[e] = []
        self.lastreal[e] = None

    def flush(self):
        self._assign()
        with self.nc.Block() as block:
            @block.tensor
            def _(h):
                self._emit_engine("pe", h)

            @block.scalar
            def _(h):
                self._emit_engine("act", h)

            @block.vector
            def _(h):
                self._emit_engine("dve", h)

            @block.gpsimd
            def _(h):
                self._emit_engine("pool", h)

            @block.sync
            def _(h):
                self._emit_engine("sp", h)


class Builder:
    def __init__(self, stop_after=None):
        self.stop_after = stop_after
        self.nc = bass.Bass("TRN2", target_bir_lowering=False)
        nc = self.nc
        dt = nc.dram_tensor
        self.x = dt("x", [SEQ, D], F32, kind="ExternalInput").ap()
        self.ptab_in = dt("ptab_in", [R_TOT, 128], F32, kind="ExternalInput").ap()
        self.w_qkv = dt("w_qkv", [2, D, 3 * D], F32, kind="ExternalInput").ap()
        self.w_o = dt("w_o", [2, D, D], F32, kind="ExternalInput").ap()
        self.w_pw1 = dt("w_pw1", [2, D, 2 * D], F32, kind="ExternalInput").ap()
        self.w_pw2 = dt("w_pw2", [2, D, D], F32, kind="ExternalInput").ap()
        self.w_up = dt("w_up", [DEPTH, D, 2 * DFF], F32, kind="ExternalInput").ap()
        self.w_down = dt("w_down", [DEPTH, DFF, D], F32, kind="ExternalInput").ap()
        self.out = dt("out", [SEQ, D], F32, kind="ExternalOutput").ap()
        self._uid = 0

    def sb(self, es, shape, dtype, name=None):
        self._uid += 1
        return es.enter_context(self.nc.sbuf_tensor("%s_%d" % (name or "t", self._uid), shape, dtype))

    def mm(self, out, lhsT, rhs, start, stop, reads, writes):
        self.S.pe(lambda h: h.matmul(out, lhsT=lhsT, rhs=rhs, start=start, stop=stop), reads, writes)

    def prow(self, r):
        return self.ptab[:, r:r + 1]

    def build(self):
        nc = self.nc
        with ExitStack() as es:
            self.S = S = Sched(nc, es)
            self.ps = [es.enter_context(nc.psum_tensor("ps%d" % i, [128, TT], F32)) for i in range(8)]
            self.h = self.sb(es, [128, NCH, SEQ], F32, "h")
            self.hn = self.sb(es, [128, NCH, SEQ], BF16, "hn")
            self.cosT = self.sb(es, [128, SEQ], BF16, "cosT")
            self.sinT = self.sb(es, [128, SEQ], BF16, "sinT")
            self.ptab = self.sb(es, [128, R_TOT], F32, "ptab")
            self.identf = self.sb(es, [128, 128], F32, "identf")
            self.identb = self.sb(es, [128, 128], BF16, "identb")
            self.onesb = self.sb(es, [128, 128], BF16, "onesb")
            self.maskT = self.sb(es, [128, 128], BF16, "maskT")
            self.prot = self.sb(es, [128, 128], BF16, "prot")
            self.ind = [self.sb(es, [128, 8 * 128], BF16, "ind") for _ in range(2)]
            self.nsq = self.sb(es, [128, 2, TT], BF16, "nsq")
            self.nstd = self.sb(es, [128, 2, TT], F32, "nstd")

            self.phase_setup()
            stop = self.stop_after
            done = stop == "load"
            for i in range(DEPTH):
                if done:
                    break
                j = i // 2
                self.rmsnorm(R_MIX + i * 8)
                if i % 2 == 0:
                    self.phase_attention(j)
                else:
                    self.phase_conformer(j)
                if stop is not None and stop.split(":")[0] == "mix%d" % i:
                    done = True
                    break
                self.rmsnorm(R_FFN + i * 8)
                fsub = stop.split(":")[1] if (stop and ":" in stop and stop.startswith("ffn")) else None
                if fsub != "norm":
                    self.phase_ffn(i)
                if stop is not None and stop.split(":")[0] == "ffn%d" % i:
                    done = True
                    break
            self.phase_final(raw=(stop is not None))
        return nc

    def phase_setup(self):
        nc, S = self.nc, self.S
        ps = self.ps
        with ExitStack() as es:
            xs = [self.sb(es, [128, D], F32, "xs") for _ in range(3)]
            pst = [self.sb(es, [128, 128], F32, "pst") for _ in range(2)]
            pi_i = self.sb(es, [128, 1], I32, "pi_i")
            pm_i = self.sb(es, [128, 2], I32, "pm_i")
            sgn = self.sb(es, [128, 2], F32, "sgn")
            invrow = self.sb(es, [1, 128], F32, "invrow")
            invf = self.sb(es, [128, 1], F32, "invf")
            pos_i = self.sb(es, [128, SEQ], I32, "pos_i")
            ang = self.sb(es, [128, SEQ], F32, "ang")
            ta = self.sb(es, [128, SEQ], F32, "ta")
            tb = self.sb(es, [128, SEQ], F32, "tb")
            tc = self.sb(es, [128, SEQ], F32, "tc")

            identf, identb, onesb, maskT, prot, ind = self.identf, self.identb, self.onesb, self.maskT, self.prot, self.ind
            S.pool(lambda h: h.memset(identf[:], 1.0), writes=["identf"])
            S.pool(lambda h: h.affine_select(out=identf[:], in_=identf[:], pattern=[[-1, 128]], compare_op=ALU.is_equal,
                                             fill=0.0, base=0, channel_multiplier=1), reads=["identf"], writes=["identf"])
            S.dve(lambda h: h.tensor_copy(out=identb[:], in_=identf[:]), reads=["identf"], writes=["identb"])
            S.pool(lambda h: h.memset(onesb[:], 1.0), writes=["onesb"])
            S.pool(lambda h: h.memset(maskT[:], 0.0), writes=["maskT"])
            S.pool(lambda h: h.affine_select(out=maskT[:], in_=maskT[:], pattern=[[1, 128]], compare_op=ALU.is_ge,
                                             fill=NEG, base=0, channel_multiplier=-1), reads=["maskT"], writes=["maskT"])
            for (dst, src) in ((0, 32), (32, 0), (64, 96), (96, 64)):
                S.dve(lambda h, dst=dst, src=src: h.tensor_copy(out=prot[:, dst:dst + 32], in_=identb[:, src:src + 32]),
                      reads=["identb"], writes=["prot"])
            for jj in range(2):
                S.pool(lambda h, jj=jj: h.memset(ind[jj][:], 0.0), writes=["ind"])
                S.pool(lambda h, jj=jj: h.memset(ind[jj][64 * jj:64 * jj + 64, :], 1.0), reads=["ind"], writes=["ind"])
                S.pool(lambda h, jj=jj: h.affine_select(out=ind[jj][64 * jj:64 * jj + 64, :], in_=ind[jj][64 * jj:64 * jj + 64, :],
                                                        pattern=[[1, 8], [0, 128]], compare_op=ALU.is_equal, fill=0.0,
                                                        base=0, channel_multiplier=-1), reads=["ind"], writes=["ind"])
            for r in range(R_TOT // 128):
                st = pst[r % 2]
                S.dma("sp", "pst%d" % (r % 2), lambda h, r=r, st=st: h.dma_start(out=st[:], in_=self.ptab_in[r * 128:(r + 1) * 128, :]),
                      writes=[("pst", r % 2)])
                bank = ps[r % 2]
                S.pe(lambda h, st=st, bank=bank: h.transpose(out=bank[:, 0:128], in_=st[:], identity=identf[:]),
                     reads=[("pst", r % 2), "identf"], writes=[("ps", r % 2)])
                S.act(lambda h, r=r, bank=bank: h.activation(out=self.ptab[:, r * 128:(r + 1) * 128], in_=bank[:, 0:128], func=AF.Copy),
                      reads=[("ps", r % 2)], writes=["ptab"])
            inv = (np.float32(1.0) / np.power(np.float32(10000.0), np.arange(0, DH, 2, dtype=np.float32) / np.float32(DH))).astype(np.float32)
            irv = invrow[:].rearrange("o (a b) -> o a b", b=32)
            for i in range(32):
                S.dve(lambda h, i=i: h.memset(irv[:, :, i:i + 1], float(inv[i])), writes=["invrow"])
            S.pe(lambda h: h.transpose(out=ps[2][:, 0:1], in_=invrow[:], identity=identf[0:1, 0:1]),
                 reads=["invrow", "identf"], writes=[("ps", 2)])
            S.act(lambda h: h.activation(out=invf[:], in_=ps[2][:, 0:1], func=AF.Copy), reads=[("ps", 2)], writes=["invf"])
            S.pool(lambda h: h.iota(pi_i[:], pattern=[[0, 1]], base=0, channel_multiplier=1), writes=["pi_i"])
            S.dve(lambda h: h.tensor_scalar(out=pm_i[:, 0:1], in0=pi_i[:], scalar1=32, scalar2=None, op0=ALU.bitwise_and),
                  reads=["pi_i"], writes=["pm_i"])
            S.dve(lambda h: h.tensor_copy(out=sgn[:, 0:1], in_=pm_i[:, 0:1]), reads=["pm_i"], writes=["sgn0"])
            S.dve(lambda h: h.tensor_scalar(out=sgn[:, 1:2], in0=sgn[:, 0:1], scalar1=1.0 / 16.0, scalar2=-1.0, op0=ALU.mult, op1=ALU.add),
                  reads=["sgn0"], writes=["sgn1"])
            S.pool(lambda h: h.iota(pos_i[:], pattern=[[1, SEQ]], base=0, channel_multiplier=0), writes=["pos_i"])
            S.dve(lambda h: h.tensor_copy(out=ta[:], in_=pos_i[:]), reads=["pos_i"], writes=["ta"])
            S.dve(lambda h: h.tensor_scalar(out=ang[:], in0=ta[:], scalar1=invf[:, 0:1], scalar2=None, op0=ALU.mult),
                  reads=["ta", "invf"], writes=["ang"])
            TWO_PI = 2.0 * math.pi
            C1 = 6.28125
            C2 = TWO_PI - C1
            MAGIC = 12582912.0
            LIM = 3.1415925
            S.dve(lambda h: h.tensor_scalar(out=ta[:], in0=ang[:], scalar1=1.0 / TWO_PI, scalar2=None, op0=ALU.mult),
                  reads=["ang", "ta"], writes=["ta"])
            S.dve(lambda h: h.tensor_scalar(out=tb[:], in0=ta[:], scalar1=MAGIC, scalar2=MAGIC, op0=ALU.add, op1=ALU.subtract),
                  reads=["ta"], writes=["tb"])
            S.dve(lambda h: h.scalar_tensor_tensor(out=ta[:], in0=tb[:], scalar=-C1, in1=ang[:], op0=ALU.mult, op1=ALU.add),
                  reads=["tb", "ang", "ta"], writes=["ta"])
            S.dve(lambda h: h.scalar_tensor_tensor(out=tc[:], in0=tb[:], scalar=-C2, in1=ta[:], op0=ALU.mult, op1=ALU.add),
                  reads=["tb", "ta"], writes=["tc"])
            S.dve(lambda h: h.tensor_scalar(out=ta[:], in0=tc[:], scalar1=LIM, scalar2=-LIM, op0=ALU.min, op1=ALU.max),
                  reads=["tc", "ta"], writes=["ta"])
            S.act(lambda h: h.activation(out=tb[:], in_=ta[:], func=AF.Sin), reads=["ta", "tb"], writes=["tb"])
            S.dve(lambda h: h.tensor_scalar(out=self.sinT[:], in0=tb[:], scalar1=sgn[:, 1:2], scalar2=None, op0=ALU.mult),
                  reads=["tb", "sgn1"], writes=["sinT"])
            S.dve(lambda h: h.tensor_scalar(out=ang[:], in0=tc[:], scalar1=math.pi / 2, scalar2=None, op0=ALU.add),
                  reads=["tc", "ang"], writes=["ang"])
            S.dve(lambda h: h.tensor_scalar(out=ta[:], in0=ang[:], scalar1=math.pi, scalar2=None, op0=ALU.is_gt),
                  reads=["ang", "ta"], writes=["ta"])
            S.dve(lambda h: h.scalar_tensor_tensor(out=tc[:], in0=ta[:], scalar=-TWO_PI, in1=ang[:], op0=ALU.mult, op1=ALU.add),
                  reads=["ta", "ang", "tc"], writes=["tc"])
            S.dve(lambda h: h.tensor_scalar(out=ta[:], in0=tc[:], scalar1=LIM, scalar2=-LIM, op0=ALU.min, op1=ALU.max),
                  reads=["tc", "ta"], writes=["ta"])
            S.act(lambda h: h.activation(out=self.cosT[:], in_=ta[:], func=AF.Sin), reads=["ta"], writes=["cosT"])
            for tt in range(16):
                sl = tt % 3
                S.dma("sp", "xs%d" % sl, lambda h, tt=tt, sl=sl: h.dma_start(out=xs[sl][:], in_=self.x[tt * 128:(tt + 1) * 128, :]),
                      writes=[("xs", sl)])
                T = tt // 4
                for half in range(2):
                    bi = 4 + (2 * tt + half) % 4
                    bank = ps[bi]
                    for jj in range(4):
                        c = half * 4 + jj
                        S.pe(lambda h, bank=bank, jj=jj, c=c, sl=sl: h.transpose(out=bank[:, jj * 128:(jj + 1) * 128],
                                                                               in_=xs[sl][:, c * 128:(c + 1) * 128], identity=identf[:]),
                             reads=[("xs", sl), "identf"], writes=[("ps", bi)])
                    dst = self.h[:, half * 4:half * 4 + 4, tt * 128:(tt + 1) * 128]
                    src = bank[:, :].rearrange("p (a b) -> p a b", b=128)
                    wr = [("h", half * 4 + jj, T) for jj in range(4)]
                    if (2 * tt + half) % 2 == 0:
                        S.act(lambda h, dst=dst, src=src: h.activation(out=dst, in_=src, func=AF.Copy), reads=[("ps", bi)], writes=wr)
                    else:
                        S.dve(lambda h, dst=dst, src=src: h.tensor_copy(out=dst, in_=src), reads=[("ps", bi)], writes=wr)
            S.barrier()
            S.flush()

    def rmsnorm(self, grow, out_fn=None):
        S, ps = self.S, self.ps
        for T in range(NT):
            bi = 6 + T % 2
            bank = ps[bi]
            tsl = slice(T * TT, (T + 1) * TT)
            for c in range(NCH):
                sq = self.nsq[:, c % 2, :]
                S.act(lambda h, sq=sq, c=c, tsl=tsl: h.activation(out=sq, in_=self.h[:, c, tsl], func=AF.Square),
                      reads=[("h", c, T)], writes=[("nsq", c % 2)])
                self.mm(bank[:, :], self.onesb[:], sq, c == 0, c == NCH - 1, [("nsq", c % 2), "onesb"], [("ps", bi)])
            sd = self.nstd[:, T % 2, :]
            S.act(lambda h, sd=sd, bank=bank: h.activation(out=sd, in_=bank[:, :], func=AF.Sqrt, scale=1.0 / D, bias=NORM_EPS),
                  reads=[("ps", bi)], writes=[("nstd", T % 2)])
            S.dve(lambda h, sd=sd, bank=bank: h.reciprocal(bank[:, :], sd), reads=[("nstd", T % 2)], writes=[("ps", bi)])
            for c in range(NCH):
                if out_fn is None:
                    dst = self.hn[:, c, tsl]
                    wr = [("hn", c, T)]
                else:
                    dst, wr = out_fn(c, T)
                S.dve(lambda h, dst=dst, c=c, tsl=tsl, bank=bank: h.scalar_tensor_tensor(
                    out=dst, in0=self.h[:, c, tsl], scalar=self.prow(grow + c), in1=bank[:, :], op0=ALU.mult, op1=ALU.mult),
                    reads=[("h", c, T), ("ps", bi), "ptab"], writes=wr)

    def load_w(self, slot_ap, src_ap, key, group):
        self.S.dma("pool", group, lambda h: h.dma_start(out=slot_ap, in_=src_ap), writes=[key])

    def phase_attention(self, j):
        nc, S, ps = self.nc, self.S, self.ps
        hn, h = self.hn, self.h
        with ExitStack() as es:
            qT = self.sb(es, [128, 2, SEQ], BF16, "qT")
            kT = self.sb(es, [128, 2, 2, SEQ], BF16, "kT")
            Vt = self.sb(es, [128, 16, 4, 128], BF16, "Vt")
            NW = 5
            wsl = [self.sb(es, [128, 2048], BF16, "wsl") for _ in range(NW)]
            biasT = self.sb(es, [128, 2, 1024], BF16, "biasT")
            kms = self.sb(es, [128, 4, 8], F32, "kms")
            kmT = self.sb(es, [128, 4, 8], BF16, "kmT")
            gsb = self.sb(es, [128, 2, 32], F32, "gsb")
            cmp_ = self.sb(es, [128, 2, 4 * 49], F32, "cmp")
            rank = self.sb(es, [128, 2, 32], F32, "rank")
            btok = self.sb(es, [128, 2, 256], BF16, "btok")
            pT = [self.sb(es, [128, TT], BF16, "pT") for _ in range(4)]
            rec = [self.sb(es, [128, TT], F32, "rec") for _ in range(2)]
            qs = [self.sb(es, [128, TT], BF16, "qs") for _ in range(2)]
            t1 = [self.sb(es, [128, TT], F32, "t1") for _ in range(2)]
            t2 = [self.sb(es, [128, TT], F32, "t2") for _ in range(2)]

            wcount = [0]

            def next_slot():
                i = wcount[0] % NW
                wcount[0] += 1
                return i

            S.pool(lambda h_: h_.memset(Vt[:, :, 0:4:2, 64:128], 1.0), writes=[("Vt1", 0)])
            S.pool(lambda h_: h_.memset(Vt[:, :, 1:4:2, 0:64], 1.0), writes=[("Vt1", 1)])
            S.pool(lambda h_: h_.memset(biasT[:], 0.0), writes=[("biasT", a, u) for a in range(2) for u in range(2)])
            S.pool(lambda h_: h_.memset(kT[64:128, :, 0, :], 0.0), writes=[("kz", 0)])
            S.pool(lambda h_: h_.memset(kT[0:64, :, 1, :], 0.0), writes=[("kz", 1)])

            wq_src = self.w_qkv[j].rearrange("(c p) f -> p c f", p=128)
            wo_src = self.w_o[j].rearrange("(c p) f -> p c f", p=128)

            def load_group(g):
                sl = {}
                for nm, off in (("q", 0), ("k", D), ("v", 2 * D)):
                    i = next_slot()
                    sl[nm] = i
                    self.load_w(wsl[i][:, :].rearrange("p (c f) -> p c f", f=256), wq_src[:, :, off + g * 256: off + (g + 1) * 256],
                                ("wsl", i), "aw%d" % i)
                i = next_slot()
                sl["o"] = i
                self.load_w(wsl[i][:, :].rearrange("p (c f) -> p c f", f=1024), wo_src[:, 2 * g:2 * g + 2, :], ("wsl", i), "aw%d" % i)
                return sl

            pending = load_group(0)
            rope_i = [0]
            sub = self.stop_after.split(":")[1] if (self.stop_after and ":" in self.stop_after) else None
            for g in range(4):
                if sub is not None and g > 0:
                    break
                sl = pending
                wq = wsl[sl["q"]][:, :].rearrange("p (c f) -> p c f", f=256)
                wk = wsl[sl["k"]][:, :].rearrange("p (c f) -> p c f", f=256)
                wv = wsl[sl["v"]][:, :].rearrange("p (c f) -> p c f", f=256)
                wo = wsl[sl["o"]][:, :].rearrange("p (c f) -> p c f", f=1024)
                units = [(which, cc, T) for which in ("q", "k") for cc in range(2) for T in range(NT)]
                pend = []

                def rope_a(which, cc, T):
                    wsrc = wq if which == "q" else wk
                    wkey = ("wsl", sl[which])
                    scale = DH ** -0.5 if which == "q" else 1.0
                    ri = rope_i[0]
                    rope_i[0] += 1
                    ba = ri % 2
                    A = ps[ba]
                    tsl = slice(T * TT, (T + 1) * TT)
                    for c in range(NCH):
                        self.mm(A[:, :], wsrc[:, c, cc * 128:(cc + 1) * 128], hn[:, c, tsl], c == 0, c == NCH - 1,
                                [wkey, ("hn", c, T)], [("ps", ba)])
                    q_s = qs[ri % 2]
                    S.act(lambda h_, q_s=q_s, A=A, scale=scale: h_.activation(out=q_s[:], in_=A[:, :], func=AF.Copy, scale=scale),
                          reads=[("ps", ba)], writes=[("qs", ri % 2)])
                    return (which, cc, T, ri, scale)

                def rope_b(which, cc, T, ri, scale):
                    ba, bb = ri % 2, 2 + ri % 2
                    A, B = ps[ba], ps[bb]
                    tsl = slice(T * TT, (T + 1) * TT)
                    q_s, t1_, t2_ = qs[ri % 2], t1[ri % 2], t2[ri % 2]
                    self.mm(B[:, :], self.prot[:], q_s[:], True, True, [("qs", ri % 2), "prot"], [("ps", bb)])
                    S.dve(lambda h_, t1_=t1_, A=A, scale=scale, tsl=tsl: h_.scalar_tensor_tensor(
                        out=t1_[:], in0=A[:, :], scalar=scale, in1=self.cosT[:, tsl], op0=ALU.mult, op1=ALU.mult),
                        reads=[("ps", ba), "cosT"], writes=[("t1", ri % 2)])
                    S.dve(lambda h_, t2_=t2_, B=B, tsl=tsl: h_.tensor_tensor(out=t2_[:], in0=B[:, :], in1=self.sinT[:, tsl], op=ALU.mult),
                          reads=[("ps", bb), "sinT"], writes=[("t2", ri % 2)])
                    if which == "q":
                        dst = qT[:, cc, tsl]
                        wr = [("q", cc, T, 0), ("q", cc, T, 1)]
                        S.pool(lambda h_, dst=dst, t1_=t1_, t2_=t2_: h_.tensor_tensor(out=dst, in0=t1_[:], in1=t2_[:], op=ALU.add),
                               reads=[("t1", ri % 2), ("t2", ri % 2)], writes=wr)
                    else:
                        for hp in range(2):
                            prt = slice(hp * 64, (hp + 1) * 64)
                            S.pool(lambda h_, prt=prt, hp=hp, cc=cc, tsl=tsl, t1_=t1_, t2_=t2_: h_.tensor_tensor(
                                out=kT[prt, cc, hp, tsl], in0=t1_[prt, :], in1=t2_[prt, :], op=ALU.add),
                                reads=[("t1", ri % 2), ("t2", ri % 2)], writes=[("k", cc, T, hp)])

                for u_ in units:
                    st_ = rope_a(*u_)
                    if pend:
                        rope_b(*pend.pop(0))
                    pend.append(st_)
                while pend:
                    rope_b(*pend.pop(0))
                if sub == "qk":
                    break
                for tp in range(8):
                    bi = 4 + tp % 2
                    bank = ps[bi]
                    for u in range(2):
                        tt = 2 * tp + u
                        T = tt // 4
                        for c in range(NCH):
                            self.mm(bank[:, u * 256:(u + 1) * 256], hn[:, c, tt * 128:(tt + 1) * 128], wv[:, c, :], c == 0, c == NCH - 1,
                                    [("wsl", sl["v"]), ("hn", c, T)], [("ps", bi)])
                    src = bank[:, :].rearrange("p (u a b e) -> p u a b e", u=2, a=2, b=2)
                    S.act(lambda h_, src=src, tp=tp: h_.activation(out=Vt[:, 2 * tp:2 * tp + 2, 0:4:2, 0:64], in_=src[:, :, :, 0, :], func=AF.Copy),
                          reads=[("ps", bi)], writes=[("Vt", 2 * tp, 0), ("Vt", 2 * tp + 1, 0)])
                    S.dve(lambda h_, src=src, tp=tp: h_.tensor_copy(out=Vt[:, 2 * tp:2 * tp + 2, 1:4:2, 64:128], in_=src[:, :, :, 1, :]),
                          reads=[("ps", bi)], writes=[("Vt", 2 * tp, 1), ("Vt", 2 * tp + 1, 1)])
                if sub == "v":
                    break
                if g + 1 < 4 and sub is None:
                    pending = load_group(g + 1)
                for ch in range(4):
                    cc, hp = ch // 2, ch % 2
                    S.dve(lambda h_, cc=cc, hp=hp, ch=ch: h_.tensor_reduce(out=kms[:, ch, :], in_=kT[:, cc, hp, :].rearrange("p (n k) -> p n k", k=256),
                                                                       axis=AX.X, op=ALU.add),
                          reads=[("k", cc, T, hp) for T in range(NT)] + [("kz", hp)], writes=[("kms", ch)])
                    S.dve(lambda h_, ch=ch: h_.tensor_scalar(out=kmT[:, ch, :], in0=kms[:, ch, :], scalar1=1.0 / 256.0, scalar2=None, op0=ALU.mult),
                          reads=[("kms", ch)], writes=[("kmT", ch)])
                import os
                GL = int(os.environ.get("GATE_LEVEL", "9"))
                for qt in range(8, 16):
                    if GL < 2:
                        break
                    qb = qt // 2
                    T = qt // 4
                    sI = qt % 2
                    gi = 6 + qt % 2
                    gbank = ps[gi]
                    for hl in range(4):
                        cc, hp = hl // 2, hl % 2
                        self.mm(gbank[:, hl * 8:hl * 8 + 8], qT[:, cc, qt * 128:(qt + 1) * 128],
                                kmT[:, hl, :], True, True,
                                [("q", cc, T, 0), ("q", cc, T, 1), ("kmT", hl)], [("ps", gi)])
                    S.act(lambda h_, sI=sI, gbank=gbank: h_.activation(out=gsb[:, sI, :], in_=gbank[:, 0:32], func=AF.Copy),
                          reads=[("ps", gi)], writes=[("gsb", sI)])
                    if GL < 3:
                        continue
                    g3 = gsb[:, sI, :].rearrange("p (a n) -> p a n", n=8)[:, :, 0:qb]
                    in0 = g3.unsqueeze(2).broadcast_to([128, 4, qb, qb])
                    in1 = g3.unsqueeze(3).broadcast_to([128, 4, qb, qb])
                    cm = cmp_[:, sI, 0:4 * qb * qb].rearrange("p (a n m) -> p a n m", a=4, n=qb)
                    S.dve(lambda h_, cm=cm, in0=in0, in1=in1: h_.tensor_tensor(out=cm, in0=in0, in1=in1, op=ALU.is_gt),
                          reads=[("gsb", sI)], writes=[("cmp", sI)])
                    rk = rank[:, sI, :].rearrange("p (a n) -> p a n", n=8)[:, :, 0:qb]
                    S.dve(lambda h_, rk=rk, cm=cm: h_.tensor_reduce(out=rk, in_=cm, axis=AX.X, op=ALU.add),
                          reads=[("cmp", sI)], writes=[("rank", sI)])
                    if GL < 4:
                        continue
                    S.pool(lambda h_, sI=sI: h_.memset(btok[:, sI, :], 0.0), writes=[("btok", sI)])
                    bo = btok[:, sI, :].rearrange("p (a b n) -> p a b n", a=2, b=2)[:, :, :, 0:qb]
                    rk4 = rank[:, sI, :].rearrange("p (a b n) -> p a b n", a=2, b=2)[:, :, :, 0:qb]
                    S.dve(lambda h_, bo=bo, rk4=rk4: h_.tensor_scalar(out=bo, in0=rk4, scalar1=2.5, scalar2=NEG, op0=ALU.is_gt, op1=ALU.mult),
                          reads=[("rank", sI)], writes=[("btok", sI)])
                    if GL < 5:
                        continue
                    for a in range(2):
                        bti = 4 + a
                        self.mm(ps[bti][:, (qt % 4) * 128:(qt % 4 + 1) * 128], btok[:, sI, a * 128:(a + 1) * 128], self.identb[:], True, True,
                                [("btok", sI), "identb"], [("ps", bti)])
                        if qt % 4 == 3 and GL != 5:
                            q0 = (qt - 3 - 8) * 128
                            S.act(lambda h_, a=a, q0=q0, bti=bti: h_.activation(out=biasT[:, a, q0:q0 + TT], in_=ps[bti][:, :], func=AF.Copy),
                                  reads=[("ps", bti)], writes=[("biasT", a, q0 // TT)])
                if sub == "gate":
                    break
                iters = []
                itc = 0
                for cc in range(2):
                    for T in range(NT):
                        nk = 4 * T + 4
                        ob = [4 + 2 * (itc % 2), 5 + 2 * (itc % 2)]
                        itc += 1
                        for kt in range(nk):
                            for hp in range(2):
                                iters.append((cc, T, kt, hp, nk, ob[hp]))
                LAG = 3

                def stage_a(n_, cc, T, kt, hp, nk, obk):
                    nb = kt // 2
                    q_lo = max(0, kt - 4 * T) * 128
                    qsl = slice(q_lo, TT)
                    Tk = kt // 4
                    sbi = n_ % 4
                    sbk = ps[sbi]
                    need_bias = (T >= 2 and kt < 4 * T + 2)
                    need_mask = kt >= 4 * T
                    self.mm(sbk[:, qsl], kT[:, cc, hp, kt * 128:(kt + 1) * 128], qT[:, cc, T * TT + q_lo:(T + 1) * TT],
                            True, not (need_bias or need_mask),
                            [("k", cc, Tk, hp), ("kz", hp), ("q", cc, T, 0), ("q", cc, T, 1)], [("ps", sbi)])
                    if need_bias:
                        self.mm(sbk[:, qsl], self.ind[hp][:, nb * 128:(nb + 1) * 128],
                                biasT[:, cc, (T - 2) * TT + q_lo:(T - 1) * TT],
                                False, not need_mask, ["ind", ("biasT", cc, T - 2)], [("ps", sbi)])
                    if need_mask:
                        self.mm(sbk[:, q_lo:q_lo + 128], self.identb[:], self.maskT[:], False, True,
                                ["identb", "maskT"], [("ps", sbi)])

                def stage_bc(n_, cc, T, kt, hp, nk, obk):
                    hl = 2 * cc + hp
                    q_lo = max(0, kt - 4 * T) * 128
                    qsl = slice(q_lo, TT)
                    sbi = n_ % 4
                    sbk = ps[sbi]
                    pi = n_ % 4
                    S.act(lambda h_, pi=pi, sbk=sbk, qsl=qsl: h_.activation(out=pT[pi][:, qsl], in_=sbk[:, qsl], func=AF.Exp),
                          reads=[("ps", sbi)], writes=[("pT", pi)])
                    self.mm(ps[obk][:, qsl], Vt[:, kt, hl, :], pT[pi][:, qsl], kt == 0, kt == nk - 1,
                            [("pT", pi), ("Vt", kt, hp), ("Vt1", hp)], [("ps", obk)])
                    if kt == nk - 1:
                        o = ps[obk]
                        num = slice(hp * 64, (hp + 1) * 64)
                        den = slice((1 - hp) * 64, (2 - hp) * 64)
                        rc = rec[hp]
                        S.dve(lambda h_, rc=rc, o=o, den=den: h_.reciprocal(rc[den, :], o[den, :]),
                              reads=[("ps", obk)], writes=[("rec", hp)])
                        S.dve(lambda h_, rc=rc, o=o, den=den, num=num, cc=cc, T=T: h_.tensor_tensor(
                            out=qT[num, cc, T * TT:(T + 1) * TT], in0=o[num, :], in1=rc[den, :], op=ALU.mult),
                            reads=[("ps", obk), ("rec", hp)], writes=[("q", cc, T, hp)])

                NI = len(iters)
                for n_ in range(NI + LAG):
                    if n_ < NI:
                        stage_a(n_, *iters[n_])
                    m_ = n_ - LAG
                    if m_ >= 0:
                        stage_bc(m_, *iters[m_])
                if sub == "core":
                    break
                for T in range(NT):
                    tsl = slice(T * TT, (T + 1) * TT)
                    for dc in range(NCH):
                        bi = (T * NCH + dc) % 4
                        for cc in range(2):
                            self.mm(ps[bi][:, :], wo[:, cc, dc * 128:(dc + 1) * 128], qT[:, cc, tsl], cc == 0, cc == 1,
                                    [("wsl", sl["o"]), ("q", cc, T, 0), ("q", cc, T, 1)], [("ps", bi)])
                        S.dve(lambda h_, bi=bi, dc=dc, tsl=tsl: h_.tensor_tensor(out=h[:, dc, tsl], in0=ps[bi][:, :], in1=h[:, dc, tsl], op=ALU.add),
                              reads=[("ps", bi), ("h", dc, T)], writes=[("h", dc, T)])
            S.barrier()
            S.flush()

    def phase_conformer(self, j):
        nc, S, ps = self.nc, self.S, self.ps
        hn, h = self.hn, self.h
        PADL = 32
        with ExitStack() as es:
            glu = self.sb(es, [128, NCH, PADL + SEQ], BF16, "glu")
            ybf = self.sb(es, [128, NCH, TT], BF16, "ybf")
            ysq = self.sb(es, [128, NCH, TT], BF16, "ysq")
            dg = self.sb(es, [128, CW, 128], BF16, "dg")
            NW = 3
            wsl = [self.sb(es, [128, NCH, 256], BF16, "cw") for _ in range(NW)]
            sgm = [self.sb(es, [128, TT], F32, "sgm") for _ in range(2)]
            tA = [self.sb(es, [128, TT], F32, "tA") for _ in range(2)]
            mean_t = self.sb(es, [128, TT], F32, "mean_t")
            m2_t = self.sb(es, [128, TT], F32, "m2_t")
            std_t = self.sb(es, [128, TT], F32, "std_t")
            wcount = [0]

            def next_slot():
                i = wcount[0] % NW
                wcount[0] += 1
                return i

            S.pool(lambda h_: h_.memset(glu[:, :, 0:PADL], 0.0), writes=[("glupad",)])
            w1 = self.w_pw1[j].rearrange("(c p) f -> p c f", p=128)
            w2 = self.w_pw2[j].rearrange("(c p) f -> p c f", p=128)

            def load_pw1(cb):
                ia = next_slot()
                self.load_w(wsl[ia][:], w1[:, :, cb * 256:(cb + 1) * 256], ("cw", ia), "cw%d" % ia)
                ig = next_slot()
                self.load_w(wsl[ig][:], w1[:, :, D + cb * 256:D + (cb + 1) * 256], ("cw", ig), "cw%d" % ig)
                return ia, ig

            it = 0
            for cb in range(4):
                ia, ig = load_pw1(cb)
                for ci in range(2):
                    cc = 2 * cb + ci
                    for T in range(NT):
                        ba, bg = (it % 2) * 2, (it % 2) * 2 + 1
                        it += 1
                        tsl = slice(T * TT, (T + 1) * TT)
                        for c in range(NCH):
                            self.mm(ps[ba][:, :], wsl[ia][:, c, ci * 128:(ci + 1) * 128], hn[:, c, tsl], c == 0, c == NCH - 1,
                                    [("cw", ia), ("hn", c, T)], [("ps", ba)])
                        for c in range(NCH):
                            self.mm(ps[bg][:, :], wsl[ig][:, c, ci * 128:(ci + 1) * 128], hn[:, c, tsl], c == 0, c == NCH - 1,
                                    [("cw", ig), ("hn", c, T)], [("ps", bg)])
                        sg_ = sgm[it % 2]
                        S.act(lambda h_, sg_=sg_, bg=bg, cc=cc: h_.activation(out=sg_[:], in_=ps[bg][:, :], func=AF.Sigmoid,
                                                                               bias=self.prow(R_BPW1 + j * 16 + 8 + cc)),
                              reads=[("ps", bg), "ptab"], writes=[("sgm", it % 2)])
                        S.dve(lambda h_, sg_=sg_, ba=ba, cc=cc, T=T: h_.scalar_tensor_tensor(
                            out=glu[:, cc, PADL + T * TT:PADL + (T + 1) * TT], in0=ps[ba][:, :], scalar=self.prow(R_BPW1 + j * 16 + cc),
                            in1=sg_[:], op0=ALU.add, op1=ALU.mult),
                            reads=[("ps", ba), ("sgm", it % 2), "ptab"], writes=[("glu", cc, T)])
            pw2_slots = {}

            def load_pw2(db):
                i = next_slot()
                self.load_w(wsl[i][:], w2[:, :, db * 256:(db + 1) * 256], ("cw", i), "cw%d" % i)
                pw2_slots[db] = i

            load_pw2(0)
            load_pw2(1)
            dcount = 0
            for T in range(NT):
                for cc in range(NCH):
                    yb = cc % 2
                    for tap in range(CW):
                        if dcount % 2 == 0:
                            S.dve(lambda h_, tap=tap, cc=cc: h_.tensor_scalar(out=dg[:, tap, :], in0=self.identb[:],
                                                                              scalar1=self.prow(R_WDW + (j * CW + tap) * 8 + cc), scalar2=None, op0=ALU.mult),
                                  reads=["identb", "ptab"], writes=[("dg", tap)])
                        else:
                            S.pool(lambda h_, tap=tap, cc=cc: h_.tensor_scalar(out=dg[:, tap, :], in0=self.identb[:],
                                                                               scalar1=self.prow(R_WDW + (j * CW + tap) * 8 + cc), scalar2=1.0,
                                                                               op0=ALU.mult, op1=ALU.mult),
                                   reads=["identb", "ptab"], writes=[("dg", tap)])
                        dcount += 1
                    for tap in range(CW):
                        o0 = PADL + T * TT - (CW - 1) + tap
                        rd = [("dg", tap), ("glu", cc, T)]
                        if T > 0:
                            rd.append(("glu", cc, T - 1))
                        else:
                            rd.append(("glupad",))
                        self.mm(ps[yb][:, :], dg[:, tap, :], glu[:, cc, o0:o0 + TT], tap == 0, tap == CW - 1, rd, [("ps", yb)])
                    S.act(lambda h_, cc=cc, yb=yb: h_.activation(out=ybf[:, cc, :], in_=ps[yb][:, :], func=AF.Identity,
                                                                  bias=self.prow(R_BDW + j * 8 + cc)),
                          reads=[("ps", yb), "ptab"], writes=[("ybf", cc)])
                    S.act(lambda h_, cc=cc: h_.activation(out=ysq[:, cc, :], in_=ybf[:, cc, :], func=AF.Square),
                          reads=[("ybf", cc)], writes=[("ysq", cc)])
                bm, bq = 2 + (T % 2) * 2, 3 + (T % 2) * 2
                for cc in range(NCH):
                    self.mm(ps[bm][:, :], self.onesb[:], ybf[:, cc, :], cc == 0, cc == NCH - 1, ["onesb", ("ybf", cc)], [("ps", bm)])
                for cc in range(NCH):
                    self.mm(ps[bq][:, :], self.onesb[:], ysq[:, cc, :], cc == 0, cc == NCH - 1, ["onesb", ("ysq", cc)], [("ps", bq)])
                S.act(lambda h_, bm=bm: h_.activation(out=mean_t[:], in_=ps[bm][:, :], func=AF.Copy, scale=1.0 / D),
                      reads=[("ps", bm)], writes=["mean_t"])
                S.dve(lambda h_: h_.tensor_tensor(out=m2_t[:], in0=mean_t[:], in1=mean_t[:], op=ALU.mult), reads=["mean_t"], writes=["m2_t"])
                S.dve(lambda h_, bq=bq: h_.scalar_tensor_tensor(out=m2_t[:], in0=ps[bq][:, :], scalar=1.0 / D, in1=m2_t[:],
                                                                 op0=ALU.mult, op1=ALU.subtract),
                      reads=[("ps", bq), "m2_t"], writes=["m2_t"])
                S.act(lambda h_: h_.activation(out=std_t[:], in_=m2_t[:], func=AF.Sqrt, bias=LN_EPS), reads=["m2_t"], writes=["std_t"])
                S.dve(lambda h_, bm=bm: h_.reciprocal(ps[bm][:, :], std_t[:]), reads=["std_t"], writes=[("ps", bm)])
                S.dve(lambda h_, bm=bm, bq=bq: h_.tensor_tensor(out=ps[bq][:, :], in0=ps[bm][:, :], in1=mean_t[:], op=ALU.mult),
                      reads=[("ps", bm), "mean_t"], writes=[("ps", bq)])
                for cc in range(NCH):
                    ta_ = tA[cc % 2]
                    S.dve(lambda h_, ta_=ta_, cc=cc, bm=bm: h_.tensor_tensor(out=ta_[:], in0=ps[bm][:, :], in1=ybf[:, cc, :], op=ALU.mult),
                          reads=[("ps", bm), ("ybf", cc)], writes=[("tA", cc % 2)])
                    S.dve(lambda h_, ta_=ta_, bq=bq: h_.tensor_tensor(out=ta_[:], in0=ta_[:], in1=ps[bq][:, :], op=ALU.subtract),
                          reads=[("ps", bq), ("tA", cc % 2)], writes=[("tA", cc % 2)])
                    S.act(lambda h_, ta_=ta_, cc=cc, T=T: h_.activation(out=hn[:, cc, T * TT:(T + 1) * TT], in_=ta_[:], func=AF.Silu,
                                                                         scale=self.prow(R_LNG + j * 8 + cc), bias=self.prow(R_LNB + j * 8 + cc)),
                          reads=[("tA", cc % 2), "ptab"], writes=[("hn", cc, T)])
            it = 0
            for db in range(4):
                if db + 2 < 4:
                    load_pw2(db + 2)
                i = pw2_slots[db]
                for di in range(2):
                    dc = 2 * db + di
                    for T in range(NT):
                        bi = 6 + it % 2
                        it += 1
                        tsl = slice(T * TT, (T + 1) * TT)
                        for cc in range(NCH):
                            self.mm(ps[bi][:, :], wsl[i][:, cc, di * 128:(di + 1) * 128], hn[:, cc, tsl], cc == 0, cc == NCH - 1,
                                    [("cw", i), ("hn", cc, T)], [("ps", bi)])
                        S.dve(lambda h_, bi=bi, dc=dc, tsl=tsl: h_.scalar_tensor_tensor(
                            out=h[:, dc, tsl], in0=ps[bi][:, :], scalar=self.prow(R_BPW2 + j * 8 + dc), in1=h[:, dc, tsl], op0=ALU.add, op1=ALU.add),
                            reads=[("ps", bi), ("h", dc, T), "ptab"], writes=[("h", dc, T)])
            S.barrier()
            S.flush()

    def phase_ffn(self, i):
        nc, S, ps = self.nc, self.S, self.ps
        hn, h = self.hn, self.h
        groups = [[0, 1, 2], [3, 4, 5], [6, 7, 8], [9, 10]]
        with ExitStack() as es:
            act = self.sb(es, [128, 6, SEQ], BF16, "act")
            NWU = 3
            wup = [self.sb(es, [128, NCH, 512], BF16, "wup") for _ in range(NWU)]
            wdn = [self.sb(es, [128, 6, D], BF16, "wdn") for _ in range(2)]
            Ag = [self.sb(es, [128, TT], F32, "Ag") for _ in range(2)]
            Av = [self.sb(es, [128, TT], F32, "Av") for _ in range(2)]
            sg = [self.sb(es, [128, TT], F32, "sg") for _ in range(2)]
            halo = self.sb(es, [128, 2, 2, 2], F32, "halo")
            wu_src = self.w_up[i].rearrange("(c p) f -> p c f", p=128)
            wd_src = self.w_down[i].rearrange("(c p) f -> p c f", p=128)
            ucount = [0]
            blocks = [b for g in groups for b in g]
            up_slot = {}

            def load_up(b):
                s = ucount[0] % NWU
                ucount[0] += 1
                up_slot[b] = s
                self.load_w(wup[s][:, :, 0:256], wu_src[:, :, b * 256:(b + 1) * 256], ("wup", s), "wu%d" % s)
                self.load_w(wup[s][:, :, 256:512], wu_src[:, :, DFF + b * 256:DFF + (b + 1) * 256], ("wup", s), "wu%d" % s)

            def load_dn(gi):
                import os
                if os.environ.get("FFN_NODN"):
                    return
                g = groups[gi]
                np_ = 2 * len(g)
                j0 = 2 * g[0]
                self.load_w(wdn[gi % 2][:, 0:np_, :], wd_src[:, j0:j0 + np_, :], ("wdn", gi % 2), "wd%d" % (gi % 2))

            load_up(blocks[0])
            load_up(blocks[1])
            load_dn(0)
            nxt = 2
            it = 0
            for gi, g in enumerate(groups):
                if gi + 1 < len(groups):
                    load_dn(gi + 1)
                for bl, b in enumerate(g):
                    if nxt < len(blocks):
                        load_up(blocks[nxt])
                        nxt += 1
                    s = up_slot[b]
                    for pi in range(2):
                        jp = 2 * b + pi
                        jl = 2 * bl + pi
                        rows = {}
                        for kind, fc in (("g", jp), ("v", NFP + jp)):
                            rows[kind] = [R_FWDW + (i * 3 + tap) * 44 + fc for tap in range(3)] + [R_FBDW + i * 44 + fc]
                        for T in range(NT):
                            par = it % 2
                            it += 1
                            tsl = slice(T * TT, (T + 1) * TT)
                            bg_, bv_ = par * 2, par * 2 + 1
                            import os
                            for c in range(NCH if not os.environ.get("FFN_NOMM") else 0):
                                self.mm(ps[bg_][:, :], wup[s][:, c, pi * 128:(pi + 1) * 128], hn[:, c, tsl], c == 0, c == NCH - 1,
                                        [("wup", s), ("hn", c, T)], [("ps", bg_)])
                            for c in range(NCH if not os.environ.get("FFN_NOMM") else 0):
                                self.mm(ps[bv_][:, :], wup[s][:, c, 256 + pi * 128:256 + (pi + 1) * 128], hn[:, c, tsl], c == 0, c == NCH - 1,
                                        [("wup", s), ("hn", c, T)], [("ps", bv_)])
                            A = {"g": Ag[par], "v": Av[par]}
                            U = {"g": ps[bg_], "v": ps[bv_]}
                            UB = {"g": bg_, "v": bv_}
                            AK = {"g": ("Ag", par), "v": ("Av", par)}
                            KI = {"g": 0, "v": 1}
                            hp_prev = (T - 1) % 2
                            hp_cur = T % 2
                            import os
                            FL = int(os.environ.get("FFN_LEVEL", "9"))
                            for kind in ("g", "v"):
                                if FL < 2:
                                    break
                                r = rows[kind]
                                S.act(lambda h_, A_=A[kind], U_=U[kind], r=r: h_.activation(out=A_[:], in_=U_[:, :], func=AF.Identity,
                                                                                          scale=self.prow(r[2]), bias=self.prow(r[3])),
                                      reads=[("ps", UB[kind]), "ptab"], writes=[AK[kind]])
                                if T < NT - 1 and FL >= 3:
                                    S.act(lambda h_, U_=U[kind], kind=kind, hp_cur=hp_cur: h_.activation(
                                        out=halo[:, hp_cur, KI[kind], :], in_=U_[:, TT - 2:TT], func=AF.Copy),
                                        reads=[("ps", UB[kind])], writes=[("halo", hp_cur, kind)])
                            for kind in ("g", "v"):
                                if FL < 4:
                                    break
                                r = rows[kind]
                                S.dve(lambda h_, A_=A[kind], U_=U[kind], r=r: h_.scalar_tensor_tensor(
                                    out=A_[:, 1:TT], in0=U_[:, 0:TT - 1], scalar=self.prow(r[1]), in1=A_[:, 1:TT], op0=ALU.mult, op1=ALU.add),
                                    reads=[("ps", UB[kind]), AK[kind], "ptab"], writes=[AK[kind]])
                            for kind in ("g", "v"):
                                if FL < 4:
                                    break
                                r = rows[kind]
                                S.dve(lambda h_, A_=A[kind], U_=U[kind], r=r: h_.scalar_tensor_tensor(
                                    out=A_[:, 2:TT], in0=U_[:, 0:TT - 2], scalar=self.prow(r[0]), in1=A_[:, 2:TT], op0=ALU.mult, op1=ALU.add),
                                    reads=[("ps", UB[kind]), AK[kind], "ptab"], writes=[AK[kind]])
                            if T > 0 and FL >= 5:
                                for kind in ("g", "v"):
                                    r = rows[kind]
                                    hl_ = halo[:, hp_prev, KI[kind], :]
                                    S.dve(lambda h_, A_=A[kind], hl_=hl_, r=r: h_.scalar_tensor_tensor(
                                        out=A_[:, 0:1], in0=hl_[:, 1:2], scalar=self.prow(r[1]), in1=A_[:, 0:1], op0=ALU.mult, op1=ALU.add),
                                        reads=[("halo", hp_prev, kind), AK[kind], "ptab"], writes=[AK[kind]])
                                for kind in ("g", "v"):
                                    r = rows[kind]
                                    hl_ = halo[:, hp_prev, KI[kind], :]
                                    S.dve(lambda h_, A_=A[kind], hl_=hl_, r=r: h_.scalar_tensor_tensor(
                                        out=A_[:, 0:2], in0=hl_[:, 0:2], scalar=self.prow(r[0]), in1=A_[:, 0:2], op0=ALU.mult, op1=ALU.add),
                                        reads=[("halo", hp_prev, kind), AK[kind], "ptab"], writes=[AK[kind]])
                            if FL < 6:
                                continue
                            sg_ = sg[par]
                            S.act(lambda h_, sg_=sg_, A_=A["g"]: h_.activation(out=sg_[:], in_=A_[:], func=AF.Silu),
                                  reads=[AK["g"]], writes=[("sg", par)])
                            S.pool(lambda h_, sg_=sg_, A_=A["v"], jl=jl, tsl=tsl: h_.tensor_tensor(out=act[:, jl, tsl], in0=sg_[:], in1=A_[:], op=ALU.mult),
                                   reads=[("sg", par), AK["v"]], writes=[("act", jl, T)])
                np_ = 2 * len(g)
                wd = wdn[gi % 2]
                dn = 0
                for T in range(NT if FL >= 7 else 0):
                    tsl = slice(T * TT, (T + 1) * TT)
                    for dc in range(NCH):
                        bi = 4 + dn % 2
                        dn += 1
                        for jl in range(np_):
                            self.mm(ps[bi][:, :], wd[:, jl, dc * 128:(dc + 1) * 128], act[:, jl, tsl], jl == 0, jl == np_ - 1,
                                    [("wdn", gi % 2), ("act", jl, T)], [("ps", bi)])
                        S.dve(lambda h_, bi=bi, dc=dc, tsl=tsl: h_.tensor_tensor(out=h[:, dc, tsl], in0=ps[bi][:, :], in1=h[:, dc, tsl], op=ALU.add),
                              reads=[("ps", bi), ("h", dc, T)], writes=[("h", dc, T)])
            S.barrier()
            S.flush()

    def phase_final(self, raw=False):
        nc, S, ps = self.nc, self.S, self.ps
        with ExitStack() as es:
            ofm = self.sb(es, [128, NCH, TT], F32, "ofm")
            ost = [self.sb(es, [128, D], F32, "ost") for _ in range(3)]
            oc = 0
            for T in range(NT):
                tsl = slice(T * TT, (T + 1) * TT)
                if raw:
                    for c in range(NCH):
                        eng = S.act if c % 2 == 0 else S.dve
                        if c % 2 == 0:
                            S.act(lambda h_, c=c, tsl=tsl: h_.activation(out=ofm[:, c, :], in_=self.h[:, c, tsl], func=AF.Copy),
                                  reads=[("h", c, T)], writes=[("ofm", c)])
                        else:
                            S.dve(lambda h_, c=c, tsl=tsl: h_.tensor_copy(out=ofm[:, c, :], in_=self.h[:, c, tsl]),
                                  reads=[("h", c, T)], writes=[("ofm", c)])
                else:
                    self.rmsnorm_tile(T, R_FIN, ofm)
                for ts in range(4):
                    tt = T * 4 + ts
                    sl = oc % 3
                    oc += 1
                    for half in range(2):
                        bi = (2 * tt + half) % 4
                        bank = ps[bi]
                        for jj in range(4):
                            c = half * 4 + jj
                            S.pe(lambda h_, bank=bank, jj=jj, c=c, ts=ts: h_.transpose(out=bank[:, jj * 128:(jj + 1) * 128],
                                                                                         in_=ofm[:, c, ts * 128:(ts + 1) * 128], identity=self.identf[:]),
                                 reads=[("ofm", c), "identf"], writes=[("ps", bi)])
                        dst = ost[sl][:, half * 512:(half + 1) * 512]
                        if half == 0:
                            S.act(lambda h_, dst=dst, bank=bank: h_.activation(out=dst, in_=bank[:, :], func=AF.Copy),
                                  reads=[("ps", bi)], writes=[("ost", sl, half)])
                        else:
                            S.dve(lambda h_, dst=dst, bank=bank: h_.tensor_copy(out=dst, in_=bank[:, :]),
                                  reads=[("ps", bi)], writes=[("ost", sl, half)])
                    S.dma("sp", "ost%d" % sl, lambda h_, sl=sl, tt=tt: h_.dma_start(out=self.out[tt * 128:(tt + 1) * 128, :], in_=ost[sl][:]),
                          reads=[("ost", sl, 0), ("ost", sl, 1)])
            S.barrier()
            S.flush()

    def rmsnorm_tile(self, T, grow, ofm):
        S, ps = self.S, self.ps
        bi = 6 + T % 2
        bank = ps[bi]
        tsl = slice(T * TT, (T + 1) * TT)
        for c in range(NCH):
            sq = self.nsq[:, c % 2, :]
            S.act(lambda h, sq=sq, c=c: h.activation(out=sq, in_=self.h[:, c, tsl], func=AF.Square),
                  reads=[("h", c, T)], writes=[("nsq", c % 2)])
            self.mm(bank[:, :], self.onesb[:], sq, c == 0, c == NCH - 1, [("nsq", c % 2), "onesb"], [("ps", bi)])
        sd = self.nstd[:, T % 2, :]
        S.act(lambda h: h.activation(out=sd, in_=bank[:, :], func=AF.Sqrt, scale=1.0 / D, bias=NORM_EPS),
              reads=[("ps", bi)], writes=[("nstd", T % 2)])
        S.dve(lambda h: h.reciprocal(bank[:, :], sd), reads=[("nstd", T % 2)], writes=[("ps", bi)])
        for c in range(NCH):
            S.dve(lambda h, c=c: h.scalar_tensor_tensor(out=ofm[:, c, :], in0=self.h[:, c, tsl], scalar=self.prow(grow + c), in1=bank[:, :],
                                                        op0=ALU.mult, op1=ALU.mult),
                  reads=[("h", c, T), ("ps", bi), "ptab"], writes=[("ofm", c)])


def _pack_ptab(inp):
    f = lambda a: np.ascontiguousarray(np.asarray(a, dtype=np.float32)).reshape(-1, 128)
    parts = [
        f(inp["norm_mix_g"]), f(inp["norm_ffn_g"]), f(inp["final_norm_g"]), f(inp["conv_b_pw1"]),
        f(inp["conv_w_dw"]), f(inp["conv_b_dw"]), f(inp["conv_ln_g"]), f(inp["conv_ln_b"]), f(inp["conv_b_pw2"]),
        f(inp["ffn_w_dw"]), f(inp["ffn_b_dw"]),
    ]
    tab = np.concatenate(parts, axis=0)
    assert tab.shape[0] == 1368
    pad = np.zeros((R_TOT - tab.shape[0], 128), np.float32)
    return np.ascontiguousarray(np.concatenate([tab, pad], axis=0))


_NC_CACHE = {}


def _run(inputs, stop_after=None, trace=False):
    x = np.ascontiguousarray(np.asarray(inputs["x"], dtype=np.float32))
    B = x.shape[0]
    key = stop_after
    if key not in _NC_CACHE:
        _NC_CACHE[key] = Builder(stop_after).build()
    nc = _NC_CACHE[key]
    ptab = _pack_ptab(inputs)
    c = lambda k: np.ascontiguousarray(np.asarray(inputs[k], dtype=np.float32))
    shared = {
        "ptab_in": ptab, "w_qkv": c("attn_w_qkv"), "w_o": c("attn_w_o"), "w_pw1": c("conv_w_pw1"), "w_pw2": c("conv_w_pw2"),
        "w_up": c("ffn_w_up"), "w_down": c("ffn_w_down"),
    }
    in_maps = [dict(shared, x=x[b]) for b in range(B)]
    res = run_bass_kernel_spmd(nc, in_maps, core_ids=list(range(B)), trace=trace)
    out = np.stack([np.asarray(r["out"]) for r in res.results], axis=0).astype(np.float32)
    return out, res


def kernel(**inputs):
    out, _ = _run(inputs)
    return out
```

```python
import math
import numpy as np
from contextlib import ExitStack
import concourse.bass as bass
import concourse.mybir as mybir
from concourse.bass_utils import run_bass_kernel_spmd

F32 = mybir.dt.float32
BF16 = mybir.dt.bfloat16
I32 = mybir.dt.int32
ALU = mybir.AluOpType
AF = mybir.ActivationFunctionType
AX = mybir.AxisListType

D = 1024
SEQ = 2048
NCH = 8
NT = 4
TT = 512
H = 16
DH = 64
DFF = 2816
NFP = 22
DEPTH = 4
NEG = -30000.0
NORM_EPS = 1e-6
LN_EPS = 1e-5
CW = 31

R_MIX = 0
R_FFN = 32
R_FIN = 64
R_BPW1 = 72
R_WDW = 104
R_BDW = 600
R_LNG = 616
R_LNB = 632
R_BPW2 = 648
R_FWDW = 664
R_FBDW = 1192
R_TOT = 1408


class _Op:
    __slots__ = ("eng", "fn", "deps", "needs_inc", "semval", "grp", "gen", "pre")

    def __init__(self, eng, fn):
        self.eng = eng
        self.fn = fn
        self.deps = []
        self.needs_inc = False
        self.semval = None
        self.grp = None
        self.gen = 0
        self.pre = None


class _Grp:
    __slots__ = ("sem", "gens", "closed")

    def __init__(self, sem):
        self.sem = sem
        self.gens = [0]
        self.closed = False


class Sched:
    ENGS = ("pe", "act", "dve", "pool", "sp")

    def __init__(self, nc, es):
        self.nc = nc
        self.es = es
        self.q = {e: [] for e in self.ENGS}
        self.lastw = {}
        self.readers = {}
        self.esem = {e: es.enter_context(nc.semaphore("sem_" + e)) for e in ("pe", "act", "dve", "pool")}
        self.ecnt = {e: 0 for e in ("pe", "act", "dve", "pool")}
        self.seen = {e: {} for e in self.ENGS}
        self.groups = {}
        self.lastreal = {e: None for e in self.ENGS}
        self.nops = 0

    def _group(self, name):
        g = self.groups.get(name)
        if g is None:
            g = _Grp(self.es.enter_context(self.nc.semaphore("dg_" + name)))
            self.groups[name] = g
        return g

    def add(self, eng, fn, reads=(), writes=(), dma=None):
        op = _Op(eng, fn)
        deps = {}
        for k in reads:
            w = self.lastw.get(k)
            if w is not None:
                deps[id(w)] = w
            if isinstance(k, tuple) and k[0] == "ps":
                rd = self.readers.get(k)
                if rd:
                    for rk_, r in rd.items():
                        if rk_ != eng:
                            deps[id(r)] = r
        for k in writes:
            w = self.lastw.get(k)
            if w is not None:
                deps[id(w)] = w
            rd = self.readers.get(k)
            if rd:
                for r in rd.values():
                    deps[id(r)] = r
        if dma is not None:
            g = self._group(dma)
            op.grp = g
            if g.closed:
                op.pre = [(g, len(g.gens) - 1)]
                g.gens.append(g.gens[-1])
                g.closed = False
            g.gens[-1] += 16
            op.gen = len(g.gens) - 1
        for d in deps.values():
            if eng == "pe" and d.eng == "pe" and d.grp is None:
                continue
            if op.grp is not None and d.grp is op.grp:
                continue
            op.deps.append(d)
            d.needs_inc = True
            if d.grp is not None:
                d.grp.closed = True
        rk = eng if dma is None else ("dma", id(op))
        for k in reads:
            self.readers.setdefault(k, {})[rk] = op
        for k in writes:
            self.lastw[k] = op
            self.readers[k] = {}
        self.q[eng].append(op)
        if dma is None:
            self.lastreal[eng] = op
        self.nops += 1
        return op

    def pe(self, fn, reads=(), writes=()):
        return self.add("pe", fn, reads, writes)

    def act(self, fn, reads=(), writes=()):
        return self.add("act", fn, reads, writes)

    def dve(self, fn, reads=(), writes=()):
        return self.add("dve", fn, reads, writes)

    def pool(self, fn, reads=(), writes=()):
        return self.add("pool", fn, reads, writes)

    def dma(self, queue, group, fn, reads=(), writes=()):
        return self.add(queue, fn, reads, writes, dma=group)

    def barrier(self):
        lasts = [op for op in self.lastreal.values() if op is not None and op.grp is None]
        gl = [(g, len(g.gens) - 1) for g in self.groups.values() if g.gens[-1] > 0]
        for g, _ in gl:
            g.closed = True
        for e in self.ENGS:
            op = _Op(e, None)
            for d in lasts:
                if not (e == "pe" and d.eng == "pe"):
                    op.deps.append(d)
                    d.needs_inc = True
            op.pre = list(gl)
            self.q[e].append(op)
        self.lastw = {}
        self.readers = {}

    def _assign(self):
        for e in ("pe", "act", "dve", "pool"):
            c = self.ecnt[e]
            for op in self.q[e]:
                if op.grp is None and op.fn is not None and op.needs_inc and op.semval is None:
                    c += 1
                    op.semval = c
            self.ecnt[e] = c

    def _emit_engine(self, e, h):
        seen = self.seen[e]

        def wait(sem, val):
            key = id(sem)
            if seen.get(key, 0) < val:
                h.wait_ge(sem, val)
                seen[key] = val

        for op in self.q[e]:
            if op.pre is not None:
                for g, gi in op.pre:
                    wait(g.sem, g.gens[gi])
            for d in op.deps:
                if d.grp is not None:
                    wait(d.grp.sem, d.grp.gens[d.gen])
                else:
                    wait(self.esem[d.eng], d.semval)
            if op.fn is not None:
                ins = op.fn(h)
                if op.grp is not None:
                    ins.then_inc(op.grp.sem, 16)
                elif op.needs_inc:
                    ins.then_inc(self.esem[e], 1)
        self.q[e] = []
        self.lastreal[e] = None

    def flush(self):
        self._assign()
        with self.nc.Block() as block:
            @block.tensor
            def _(h):
                self._emit_engine("pe", h)

            @block.scalar
            def _(h):
                self._emit_engine("act", h)

            @block.vector
            def _(h):
                self._emit_engine("dve", h)

            @block.gpsimd
            def _(h):
                self._emit_engine("pool", h)

            @block.sync
            def _(h):
                self._emit_engine("sp", h)


class Builder:
    def __init__(self, stop_after=None):
        self.stop_after = stop_after
        self.nc = bass.Bass("TRN2", target_bir_lowering=False)
        nc = self.nc
        dt = nc.dram_tensor
        self.x = dt("x", [SEQ, D], F32, kind="ExternalInput").ap()
        self.ptab_in = dt("ptab_in", [R_TOT, 128], F32, kind="ExternalInput").ap()
        self.w_qkv = dt("w_qkv", [2, D, 3 * D], F32, kind="ExternalInput").ap()
        self.w_o = dt("w_o", [2, D, D], F32, kind="ExternalInput").ap()
        self.w_pw1 = dt("w_pw1", [2, D, 2 * D], F32, kind="ExternalInput").ap()
        self.w_pw2 = dt("w_pw2", [2, D, D], F32, kind="ExternalInput").ap()
        self.w_up = dt("w_up", [DEPTH, D, 2 * DFF], F32, kind="ExternalInput").ap()
        self.w_down = dt("w_down", [DEPTH, DFF, D], F32, kind="ExternalInput").ap()
        self.out = dt("out", [SEQ, D], F32, kind="ExternalOutput").ap()
        self._uid = 0

    def sb(self, es, shape, dtype, name=None):
        self._uid += 1
        return es.enter_context(self.nc.sbuf_tensor("%s_%d" % (name or "t", self._uid), shape, dtype))

    def mm(self, out, lhsT, rhs, start, stop, reads, writes):
        self.S.pe(lambda h: h.matmul(out, lhsT=lhsT, rhs=rhs, start=start, stop=stop), reads, writes)

    def prow(self, r):
        return self.ptab[:, r:r + 1]

    def build(self):
        nc = self.nc
        with ExitStack() as es:
            self.S = S = Sched(nc, es)
            self.ps = [es.enter_context(nc.psum_tensor("ps%d" % i, [128, TT], F32)) for i in range(8)]
            self.h = self.sb(es, [128, NCH, SEQ], F32, "h")
            self.hn = self.sb(es, [128, NCH, SEQ], BF16, "hn")
            self.cosT = self.sb(es, [128, SEQ], BF16, "cosT")
            self.sinT = self.sb(es, [128, SEQ], BF16, "sinT")
            self.ptab = self.sb(es, [128, R_TOT], F32, "ptab")
            self.identf = self.sb(es, [128, 128], F32, "identf")
            self.identb = self.sb(es, [128, 128], BF16, "identb")
            self.onesb = self.sb(es, [128, 128], BF16, "onesb")
            self.maskT = self.sb(es, [128, 128], BF16, "maskT")
            self.prot = self.sb(es, [128, 128], BF16, "prot")
            self.ind = [self.sb(es, [128, 8 * 128], BF16, "ind") for _ in range(2)]
            self.nsq = self.sb(es, [128, 2, TT], BF16, "nsq")
            self.nstd = self.sb(es, [128, 2, TT], F32, "nstd")

            self.phase_setup()
            stop = self.stop_after
            done = stop == "load"
            for i in range(DEPTH):
                if done:
                    break
                j = i // 2
                self.rmsnorm(R_MIX + i * 8)
                if i % 2 == 0:
                    self.phase_attention(j)
                else:
                    self.phase_conformer(j)
                if stop is not None and stop.split(":")[0] == "mix%d" % i:
                    done = True
                    break
                self.rmsnorm(R_FFN + i * 8)
                fsub = stop.split(":")[1] if (stop and ":" in stop and stop.startswith("ffn")) else None
                if fsub != "norm":
                    self.phase_ffn(i)
                if stop is not None and stop.split(":")[0] == "ffn%d" % i:
                    done = True
                    break
            self.phase_final(raw=(stop is not None))
        return nc

    def phase_setup(self):
        nc, S = self.nc, self.S
        ps = self.ps
        with ExitStack() as es:
            xs = [self.sb(es, [128, D], F32, "xs") for _ in range(3)]
            pst = [self.sb(es, [128, 128], F32, "pst") for _ in range(2)]
            pi_i = self.sb(es, [128, 1], I32, "pi_i")
            pm_i = self.sb(es, [128, 2], I32, "pm_i")
            sgn = self.sb(es, [128, 2], F32, "sgn")
            invrow = self.sb(es, [1, 128], F32, "invrow")
            invf = self.sb(es, [128, 1], F32, "invf")
            pos_i = self.sb(es, [128, SEQ], I32, "pos_i")
            ang = self.sb(es, [128, SEQ], F32, "ang")
            ta = self.sb(es, [128, SEQ], F32, "ta")
            tb = self.sb(es, [128, SEQ], F32, "tb")
            tc = self.sb(es, [128, SEQ], F32, "tc")

            identf, identb, onesb, maskT, prot, ind = self.identf, self.identb, self.onesb, self.maskT, self.prot, self.ind
            S.pool(lambda h: h.memset(identf[:], 1.0), writes=["identf"])
            S.pool(lambda h: h.affine_select(out=identf[:], in_=identf[:], pattern=[[-1, 128]], compare_op=ALU.is_equal,
                                             fill=0.0, base=0, channel_multiplier=1), reads=["identf"], writes=["identf"])
            S.dve(lambda h: h.tensor_copy(out=identb[:], in_=identf[:]), reads=["identf"], writes=["identb"])
            S.pool(lambda h: h.memset(onesb[:], 1.0), writes=["onesb"])
            S.pool(lambda h: h.memset(maskT[:], 0.0), writes=["maskT"])
            S.pool(lambda h: h.affine_select(out=maskT[:], in_=maskT[:], pattern=[[1, 128]], compare_op=ALU.is_ge,
                                             fill=NEG, base=0, channel_multiplier=-1), reads=["maskT"], writes=["maskT"])
            for (dst, src) in ((0, 32), (32, 0), (64, 96), (96, 64)):
                S.dve(lambda h, dst=dst, src=src: h.tensor_copy(out=prot[:, dst:dst + 32], in_=identb[:, src:src + 32]),
                      reads=["identb"], writes=["prot"])
            for jj in range(2):
                S.pool(lambda h, jj=jj: h.memset(ind[jj][:], 0.0), writes=["ind"])
                S.pool(lambda h, jj=jj: h.memset(ind[jj][64 * jj:64 * jj + 64, :], 1.0), reads=["ind"], writes=["ind"])
                S.pool(lambda h, jj=jj: h.affine_select(out=ind[jj][64 * jj:64 * jj + 64, :], in_=ind[jj][64 * jj:64 * jj + 64, :],
                                                        pattern=[[1, 8], [0, 128]], compare_op=ALU.is_equal, fill=0.0,
                                                        base=0, channel_multiplier=-1), reads=["ind"], writes=["ind"])
            for r in range(R_TOT // 128):
                st = pst[r % 2]
                S.dma("sp", "pst%d" % (r % 2), lambda h, r=r, st=st: h.dma_start(out=st[:], in_=self.ptab_in[r * 128:(r + 1) * 128, :]),
                      writes=[("pst", r % 2)])
                bank = ps[r % 2]
                S.pe(lambda h, st=st, bank=bank: h.transpose(out=bank[:, 0:128], in_=st[:], identity=identf[:]),
                     reads=[("pst", r % 2), "identf"], writes=[("ps", r % 2)])
                S.act(lambda h, r=r, bank=bank: h.activation(out=self.ptab[:, r * 128:(r + 1) * 128], in_=bank[:, 0:128], func=AF.Copy),
                      reads=[("ps", r % 2)], writes=["ptab"])
            inv = (np.float32(1.0) / np.power(np.float32(10000.0), np.arange(0, DH, 2, dtype=np.float32) / np.float32(DH))).astype(np.float32)
            irv = invrow[:].rearrange("o (a b) -> o a b", b=32)
            for i in range(32):
                S.dve(lambda h, i=i: h.memset(irv[:, :, i:i + 1], float(inv[i])), writes=["invrow"])
            S.pe(lambda h: h.transpose(out=ps[2][:, 0:1], in_=invrow[:], identity=identf[0:1, 0:1]),
                 reads=["invrow", "identf"], writes=[("ps", 2)])
            S.act(lambda h: h.activation(out=invf[:], in_=ps[2][:, 0:1], func=AF.Copy), reads=[("ps", 2)], writes=["invf"])
            S.pool(lambda h: h.iota(pi_i[:], pattern=[[0, 1]], base=0, channel_multiplier=1), writes=["pi_i"])
            S.dve(lambda h: h.tensor_scalar(out=pm_i[:, 0:1], in0=pi_i[:], scalar1=32, scalar2=None, op0=ALU.bitwise_and),
                  reads=["pi_i"], writes=["pm_i"])
            S.dve(lambda h: h.tensor_copy(out=sgn[:, 0:1], in_=pm_i[:, 0:1]), reads=["pm_i"], writes=["sgn0"])
            S.dve(lambda h: h.tensor_scalar(out=sgn[:, 1:2], in0=sgn[:, 0:1], scalar1=1.0 / 16.0, scalar2=-1.0, op0=ALU.mult, op1=ALU.add),
                  reads=["sgn0"], writes=["sgn1"])
            S.pool(lambda h: h.iota(pos_i[:], pattern=[[1, SEQ]], base=0, channel_multiplier=0), writes=["pos_i"])
            S.dve(lambda h: h.tensor_copy(out=ta[:], in_=pos_i[:]), reads=["pos_i"], writes=["ta"])
            S.dve(lambda h: h.tensor_scalar(out=ang[:], in0=ta[:], scalar1=invf[:, 0:1], scalar2=None, op0=ALU.mult),
                  reads=["ta", "invf"], writes=["ang"])
            TWO_PI = 2.0 * math.pi
            C1 = 6.28125
            C2 = TWO_PI - C1
            MAGIC = 12582912.0
            LIM = 3.1415925
            S.dve(lambda h: h.tensor_scalar(out=ta[:], in0=ang[:], scalar1=1.0 / TWO_PI, scalar2=None, op0=ALU.mult),
                  reads=["ang", "ta"], writes=["ta"])
            S.dve(lambda h: h.tensor_scalar(out=tb[:], in0=ta[:], scalar1=MAGIC, scalar2=MAGIC, op0=ALU.add, op1=ALU.subtract),
                  reads=["ta"], writes=["tb"])
            S.dve(lambda h: h.scalar_tensor_tensor(out=ta[:], in0=tb[:], scalar=-C1, in1=ang[:], op0=ALU.mult, op1=ALU.add),
                  reads=["tb", "ang", "ta"], writes=["ta"])
            S.dve(lambda h: h.scalar_tensor_tensor(out=tc[:], in0=tb[:], scalar=-C2, in1=ta[:], op0=ALU.mult, op1=ALU.add),
                  reads=["tb", "ta"], writes=["tc"])
            S.dve(lambda h: h.tensor_scalar(out=ta[:], in0=tc[:], scalar1=LIM, scalar2=-LIM, op0=ALU.min, op1=ALU.max),
                  reads=["tc", "ta"], writes=["ta"])
            S.act(lambda h: h.activation(out=tb[:], in_=ta[:], func=AF.Sin), reads=["ta", "tb"], writes=["tb"])
            S.dve(lambda h: h.tensor_scalar(out=self.sinT[:], in0=tb[:], scalar1=sgn[:, 1:2], scalar2=None, op0=ALU.mult),
                  reads=["tb", "sgn1"], writes=["sinT"])
            S.dve(lambda h: h.tensor_scalar(out=ang[:], in0=tc[:], scalar1=math.pi / 2, scalar2=None, op0=ALU.add),
                  reads=["tc", "ang"], writes=["ang"])
            S.dve(lambda h: h.tensor_scalar(out=ta[:], in0=ang[:], scalar1=math.pi, scalar2=None, op0=ALU.is_gt),
                  reads=["ang", "ta"], writes=["ta"])
            S.dve(lambda h: h.scalar_tensor_tensor(out=tc[:], in0=ta[:], scalar=-TWO_PI, in1=ang[:], op0=ALU.mult, op1=ALU.add),
                  reads=["ta", "ang", "tc"], writes=["tc"])
            S.dve(lambda h: h.tensor_scalar(out=ta[:], in0=tc[:], scalar1=LIM, scalar2=-LIM, op0=ALU.min, op1=ALU.max),
                  reads=["tc", "ta"], writes=["ta"])
            S.act(lambda h: h.activation(out=self.cosT[:], in_=ta[:], func=AF.Sin), reads=["ta"], writes=["cosT"])
            for tt in range(16):
                sl = tt % 3
                S.dma("sp", "xs%d" % sl, lambda h, tt=tt, sl=sl: h.dma_start(out=xs[sl][:], in_=self.x[tt * 128:(tt + 1) * 128, :]),
                      writes=[("xs", sl)])
                T = tt // 4
                for half in range(2):
                    bi = 4 + (2 * tt + half) % 4
                    bank = ps[bi]
                    for jj in range(4):
                        c = half * 4 + jj
                        S.pe(lambda h, bank=bank, jj=jj, c=c, sl=sl: h.transpose(out=bank[:, jj * 128:(jj + 1) * 128],
                                                                               in_=xs[sl][:, c * 128:(c + 1) * 128], identity=identf[:]),
                             reads=[("xs", sl), "identf"], writes=[("ps", bi)])
                    dst = self.h[:, half * 4:half * 4 + 4, tt * 128:(tt + 1) * 128]
                    src = bank[:, :].rearrange("p (a b) -> p a b", b=128)
                    wr = [("h", half * 4 + jj, T) for jj in range(4)]
                    if (2 * tt + half) % 2 == 0:
                        S.act(lambda h, dst=dst, src=src: h.activation(out=dst, in_=src, func=AF.Copy), reads=[("ps", bi)], writes=wr)
                    else:
                        S.dve(lambda h, dst=dst, src=src: h.tensor_copy(out=dst, in_=src), reads=[("ps", bi)], writes=wr)
            S.barrier()
            S.flush()

    def rmsnorm(self, grow, out_fn=None):
        S, ps = self.S, self.ps
        for T in range(NT):
            bi = 6 + T % 2
            bank = ps[bi]
            tsl = slice(T * TT, (T + 1) * TT)
            for c in range(NCH):
                sq = self.nsq[:, c % 2, :]
                S.act(lambda h, sq=sq, c=c, tsl=tsl: h.activation(out=sq, in_=self.h[:, c, tsl], func=AF.Square),
                      reads=[("h", c, T)], writes=[("nsq", c % 2)])
                self.mm(bank[:, :], self.onesb[:], sq, c == 0, c == NCH - 1, [("nsq", c % 2), "onesb"], [("ps", bi)])
            sd = self.nstd[:, T % 2, :]
            S.act(lambda h, sd=sd, bank=bank: h.activation(out=sd, in_=bank[:, :], func=AF.Sqrt, scale=1.0 / D, bias=NORM_EPS),
                  reads=[("ps", bi)], writes=[("nstd", T % 2)])
            S.dve(lambda h, sd=sd, bank=bank: h.reciprocal(bank[:, :], sd), reads=[("nstd", T % 2)], writes=[("ps", bi)])
            for c in range(NCH):
                if out_fn is None:
                    dst = self.hn[:, c, tsl]
                    wr = [("hn", c, T)]
                else:
                    dst, wr = out_fn(c, T)
                S.dve(lambda h, dst=dst, c=c, tsl=tsl, bank=bank: h.scalar_tensor_tensor(
                    out=dst, in0=self.h[:, c, tsl], scalar=self.prow(grow + c), in1=bank[:, :], op0=ALU.mult, op1=ALU.mult),
                    reads=[("h", c, T), ("ps", bi), "ptab"], writes=wr)

    def load_w(self, slot_ap, src_ap, key, group):
        self.S.dma("pool", group, lambda h: h.dma_start(out=slot_ap, in_=src_ap), writes=[key])

    def phase_attention(self, j):
        nc, S, ps = self.nc, self.S, self.ps
        hn, h = self.hn, self.h
        with ExitStack() as es:
            qT = self.sb(es, [128, 2, SEQ], BF16, "qT")
            kT = self.sb(es, [128, 2, 2, SEQ], BF16, "kT")
            Vt = self.sb(es, [128, 16, 4, 128], BF16, "Vt")
            NW = 5
            wsl = [self.sb(es, [128, 2048], BF16, "wsl") for _ in range(NW)]
            biasT = self.sb(es, [128, 2, 1024], BF16, "biasT")
            kms = self.sb(es, [128, 4, 8], F32, "kms")
            kmT = self.sb(es, [128, 4, 8], BF16, "kmT")
            gsb = self.sb(es, [128, 256], F32, "gsb")
            cmp_ = self.sb(es, [128, 2, 4 * 49], BF16, "cmp")
            rank = self.sb(es, [128, 2, 32], F32, "rank")
            btok = self.sb(es, [128, 4, 256], BF16, "btok")
            pT = [self.sb(es, [128, TT], BF16, "pT") for _ in range(4)]
            rec = [self.sb(es, [128, TT], F32, "rec") for _ in range(2)]
            qs = [self.sb(es, [128, TT], BF16, "qs") for _ in range(2)]
            t1 = [self.sb(es, [128, TT], F32, "t1") for _ in range(2)]
            t2 = [self.sb(es, [128, TT], F32, "t2") for _ in range(2)]

            wcount = [0]

            def next_slot():
                i = wcount[0] % NW
                wcount[0] += 1
                return i

            S.pool(lambda h_: h_.memset(Vt[:, :, 0:4:2, 64:128], 1.0), writes=[("Vt1", 0)])
            S.pool(lambda h_: h_.memset(Vt[:, :, 1:4:2, 0:64], 1.0), writes=[("Vt1", 1)])
            S.pool(lambda h_: h_.memset(biasT[:], 0.0), writes=[("biasT", a, u) for a in range(2) for u in range(2)])
            S.pool(lambda h_: h_.memset(kT[64:128, :, 0, :], 0.0), writes=[("kz", 0)])
            S.pool(lambda h_: h_.memset(kT[0:64, :, 1, :], 0.0), writes=[("kz", 1)])

            wq_src = self.w_qkv[j].rearrange("(c p) f -> p c f", p=128)
            wo_src = self.w_o[j].rearrange("(c p) f -> p c f", p=128)

            def load_group(g):
                sl = {}
                for nm, off in (("q", 0), ("k", D), ("v", 2 * D)):
                    i = next_slot()
                    sl[nm] = i
                    self.load_w(wsl[i][:, :].rearrange("p (c f) -> p c f", f=256), wq_src[:, :, off + g * 256: off + (g + 1) * 256],
                                ("wsl", i), "aw%d" % i)
                i = next_slot()
                sl["o"] = i
                self.load_w(wsl[i][:, :].rearrange("p (c f) -> p c f", f=1024), wo_src[:, 2 * g:2 * g + 2, :], ("wsl", i), "aw%d" % i)
                return sl

            pending = load_group(0)
            rope_i = [0]
            sub = self.stop_after.split(":")[1] if (self.stop_after and ":" in self.stop_after) else None
            for g in range(4):
                if sub is not None and g > 0:
                    break
                sl = pending
                wq = wsl[sl["q"]][:, :].rearrange("p (c f) -> p c f", f=256)
                wk = wsl[sl["k"]][:, :].rearrange("p (c f) -> p c f", f=256)
                wv = wsl[sl["v"]][:, :].rearrange("p (c f) -> p c f", f=256)
                wo = wsl[sl["o"]][:, :].rearrange("p (c f) -> p c f", f=1024)
                units = [(which, cc, T) for which in ("q", "k") for cc in range(2) for T in range(NT)]
                pend = []

                def rope_a(which, cc, T):
                    wsrc = wq if which == "q" else wk
                    wkey = ("wsl", sl[which])
                    scale = DH ** -0.5 if which == "q" else 1.0
                    ri = rope_i[0]
                    rope_i[0] += 1
                    ba = ri % 2
                    A = ps[ba]
                    tsl = slice(T * TT, (T + 1) * TT)
                    for c in range(NCH):
                        self.mm(A[:, :], wsrc[:, c, cc * 128:(cc + 1) * 128], hn[:, c, tsl], c == 0, c == NCH - 1,
                                [wkey, ("hn", c, T)], [("ps", ba)])
                    q_s = qs[ri % 2]
                    S.act(lambda h_, q_s=q_s, A=A, scale=scale: h_.activation(out=q_s[:], in_=A[:, :], func=AF.Copy, scale=scale),
                          reads=[("ps", ba)], writes=[("qs", ri % 2)])
                    return (which, cc, T, ri, scale)

                def rope_b(which, cc, T, ri, scale):
                    ba, bb = ri % 2, 2 + ri % 2
                    A, B = ps[ba], ps[bb]
                    tsl = slice(T * TT, (T + 1) * TT)
                    q_s, t1_, t2_ = qs[ri % 2], t1[ri % 2], t2[ri % 2]
                    self.mm(B[:, :], self.prot[:], q_s[:], True, True, [("qs", ri % 2), "prot"], [("ps", bb)])
                    S.dve(lambda h_, t1_=t1_, A=A, scale=scale, tsl=tsl: h_.scalar_tensor_tensor(
                        out=t1_[:], in0=A[:, :], scalar=scale, in1=self.cosT[:, tsl], op0=ALU.mult, op1=ALU.mult),
                        reads=[("ps", ba), "cosT"], writes=[("t1", ri % 2)])
                    S.dve(lambda h_, t2_=t2_, B=B, tsl=tsl: h_.tensor_tensor(out=t2_[:], in0=B[:, :], in1=self.sinT[:, tsl], op=ALU.mult),
                          reads=[("ps", bb), "sinT"], writes=[("t2", ri % 2)])
                    if which == "q":
                        dst = qT[:, cc, tsl]
                        wr = [("q", cc, T, 0), ("q", cc, T, 1)]
                        S.pool(lambda h_, dst=dst, t1_=t1_, t2_=t2_: h_.tensor_tensor(out=dst, in0=t1_[:], in1=t2_[:], op=ALU.add),
                               reads=[("t1", ri % 2), ("t2", ri % 2)], writes=wr)
                    else:
                        for hp in range(2):
                            prt = slice(hp * 64, (hp + 1) * 64)
                            S.pool(lambda h_, prt=prt, hp=hp, cc=cc, tsl=tsl, t1_=t1_, t2_=t2_: h_.tensor_tensor(
                                out=kT[prt, cc, hp, tsl], in0=t1_[prt, :], in1=t2_[prt, :], op=ALU.add),
                                reads=[("t1", ri % 2), ("t2", ri % 2)], writes=[("k", cc, T, hp)])

                for u_ in units:
                    st_ = rope_a(*u_)
                    if pend:
                        rope_b(*pend.pop(0))
                    pend.append(st_)
                while pend:
                    rope_b(*pend.pop(0))
                if sub == "qk":
                    break
                if sub == "v":
                    break
                for ch in range(4):
                    cc, hp = ch // 2, ch % 2
                    S.dve(lambda h_, cc=cc, hp=hp, ch=ch: h_.tensor_reduce(out=kms[:, ch, :], in_=kT[:, cc, hp, :].rearrange("p (n k) -> p n k", k=256),
                                                                       axis=AX.X, op=ALU.add),
                          reads=[("k", cc, T, hp) for T in range(NT)] + [("kz", hp)], writes=[("kms", ch)])
                    S.dve(lambda h_, ch=ch: h_.tensor_scalar(out=kmT[:, ch, :], in0=kms[:, ch, :], scalar1=1.0 / 256.0, scalar2=None, op0=ALU.mult),
                          reads=[("kms", ch)], writes=[("kmT", ch)])
                gbank = ps[6]
                for qt in range(8, 16):
                    T = qt // 4
                    for hl in range(4):
                        cc, hp = hl // 2, hl % 2
                        col = (qt - 8) * 32 + hl * 8
                        self.mm(gbank[:, col:col + 8], qT[:, cc, qt * 128:(qt + 1) * 128],
                                kmT[:, hl, :], True, True,
                                [("q", cc, T, 0), ("q", cc, T, 1), ("kmT", hl)], [("ps", 6)])
                S.act(lambda h_: h_.activation(out=gsb[:, :], in_=gbank[:, 0:256], func=AF.Copy),
                      reads=[("ps", 6)], writes=["gsb"])

                def gate_chain(qt):
                    qb = qt // 2
                    sI = qt % 4
                    c2 = qt % 2
                    g3 = gsb[:, (qt - 8) * 32:(qt - 7) * 32].rearrange("p (a n) -> p a n", n=8)[:, :, 0:qb]
                    in0 = g3.unsqueeze(2).broadcast_to([128, 4, qb, qb])
                    in1 = g3.unsqueeze(3).broadcast_to([128, 4, qb, qb])
                    cm = cmp_[:, c2, 0:4 * qb * qb].rearrange("p (a n m) -> p a n m", a=4, n=qb)
                    S.dve(lambda h_, cm=cm, in0=in0, in1=in1: h_.tensor_tensor(out=cm, in0=in0, in1=in1, op=ALU.is_gt),
                          reads=["gsb"], writes=[("cmp", c2)])
                    rk = rank[:, c2, :].rearrange("p (a n) -> p a n", n=8)[:, :, 0:qb]
                    S.dve(lambda h_, rk=rk, cm=cm: h_.tensor_reduce(out=rk, in_=cm, axis=AX.X, op=ALU.add),
                          reads=[("cmp", c2)], writes=[("rank", c2)])
                    S.pool(lambda h_, sI=sI: h_.memset(btok[:, sI, :], 0.0), writes=[("btok", sI)])
                    bo = btok[:, sI, :].rearrange("p (a b n) -> p a b n", a=2, b=2)[:, :, :, 0:qb]
                    rk4 = rank[:, c2, :].rearrange("p (a b n) -> p a b n", a=2, b=2)[:, :, :, 0:qb]
                    S.dve(lambda h_, bo=bo, rk4=rk4: h_.tensor_scalar(out=bo, in0=rk4, scalar1=2.5, scalar2=NEG, op0=ALU.is_gt, op1=ALU.mult),
                          reads=[("rank", c2)], writes=[("btok", sI)])

                def gate_transposes(q4):
                    for a in range(2):
                        bti = 2 + a
                        for qq in range(4):
                            qt = 8 + 4 * q4 + qq
                            sI = qt % 4
                            self.mm(ps[bti][:, qq * 128:(qq + 1) * 128], btok[:, sI, a * 128:(a + 1) * 128], self.identb[:], True, True,
                                    [("btok", sI), "identb"], [("ps", bti)])
                        S.act(lambda h_, a=a, q4=q4, bti=bti: h_.activation(out=biasT[:, a, q4 * TT:(q4 + 1) * TT], in_=ps[bti][:, :], func=AF.Copy),
                              reads=[("ps", bti)], writes=[("biasT", a, q4)])
                def v_proj(tp_lo, tp_hi):
                  for tp in range(tp_lo, tp_hi):
                      bi = 4 + tp % 2
                      bank = ps[bi]
                      for u in range(2):
                          tt = 2 * tp + u
                          T = tt // 4
                          for c in range(NCH):
                              self.mm(bank[:, u * 256:(u + 1) * 256], hn[:, c, tt * 128:(tt + 1) * 128], wv[:, c, :], c == 0, c == NCH - 1,
                                      [("wsl", sl["v"]), ("hn", c, T)], [("ps", bi)])
                      src = bank[:, :].rearrange("p (u a b e) -> p u a b e", u=2, a=2, b=2)
                      S.act(lambda h_, src=src, tp=tp: h_.activation(out=Vt[:, 2 * tp:2 * tp + 2, 0:4:2, 0:64], in_=src[:, :, :, 0, :], func=AF.Copy),
                            reads=[("ps", bi)], writes=[("Vt", 2 * tp, 0), ("Vt", 2 * tp + 1, 0)])
                      S.dve(lambda h_, src=src, tp=tp: h_.tensor_copy(out=Vt[:, 2 * tp:2 * tp + 2, 1:4:2, 64:128], in_=src[:, :, :, 1, :]),
                            reads=[("ps", bi)], writes=[("Vt", 2 * tp, 1), ("Vt", 2 * tp + 1, 1)])
                for qt in range(8, 12):
                    gate_chain(qt)
                v_proj(0, 4)
                gate_transposes(0)
                for qt in range(12, 16):
                    gate_chain(qt)
                v_proj(4, 8)
                gate_transposes(1)
                if g + 1 < 4 and sub is None:
                    pending = load_group(g + 1)
                if sub == "gate":
                    break
                iters = []
                itc = 0
                for cc in range(2):
                    for T in range(NT):
                        nk = 4 * T + 4
                        ob = [4 + 2 * (itc % 2), 5 + 2 * (itc % 2)]
                        itc += 1
                        for kt in range(nk):
                            for hp in range(2):
                                iters.append((cc, T, kt, hp, nk, ob[hp]))
                LAG = 3

                def stage_a(n_, cc, T, kt, hp, nk, obk):
                    nb = kt // 2
                    q_lo = max(0, kt - 4 * T) * 128
                    qsl = slice(q_lo, TT)
                    Tk = kt // 4
                    sbi = n_ % 4
                    sbk = ps[sbi]
                    need_bias = (T >= 2 and kt < 4 * T + 2)
                    need_mask = kt >= 4 * T
                    self.mm(sbk[:, qsl], kT[:, cc, hp, kt * 128:(kt + 1) * 128], qT[:, cc, T * TT + q_lo:(T + 1) * TT],
                            True, not (need_bias or need_mask),
                            [("k", cc, Tk, hp), ("kz", hp), ("q", cc, T, 0), ("q", cc, T, 1)], [("ps", sbi)])
                    if need_bias:
                        self.mm(sbk[:, qsl], self.ind[hp][:, nb * 128:(nb + 1) * 128],
                                biasT[:, cc, (T - 2) * TT + q_lo:(T - 1) * TT],
                                False, not need_mask, ["ind", ("biasT", cc, T - 2)], [("ps", sbi)])
                    if need_mask:
                        self.mm(sbk[:, q_lo:q_lo + 128], self.identb[:], self.maskT[:], False, True,
                                ["identb", "maskT"], [("ps", sbi)])

                def stage_bc(n_, cc, T, kt, hp, nk, obk):
                    hl = 2 * cc + hp
                    q_lo = max(0, kt - 4 * T) * 128
                    qsl = slice(q_lo, TT)
                    sbi = n_ % 4
                    sbk = ps[sbi]
                    pi = n_ % 4
                    S.act(lambda h_, pi=pi, sbk=sbk, qsl=qsl: h_.activation(out=pT[pi][:, qsl], in_=sbk[:, qsl], func=AF.Exp),
                          reads=[("ps", sbi)], writes=[("pT", pi)])
                    self.mm(ps[obk][:, qsl], Vt[:, kt, hl, :], pT[pi][:, qsl], kt == 0, kt == nk - 1,
                            [("pT", pi), ("Vt", kt, hp), ("Vt1", hp)], [("ps", obk)])
                    if kt == nk - 1:
                        o = ps[obk]
                        num = slice(hp * 64, (hp + 1) * 64)
                        den = slice((1 - hp) * 64, (2 - hp) * 64)
                        rc = rec[hp]
                        S.dve(lambda h_, rc=rc, o=o, den=den: h_.reciprocal(rc[den, :], o[den, :]),
                              reads=[("ps", obk)], writes=[("rec", hp)])
                        S.dve(lambda h_, rc=rc, o=o, den=den, num=num, cc=cc, T=T: h_.tensor_tensor(
                            out=qT[num, cc, T * TT:(T + 1) * TT], in0=o[num, :], in1=rc[den, :], op=ALU.mult),
                            reads=[("ps", obk), ("rec", hp)], writes=[("q", cc, T, hp)])

                NI = len(iters)
                for n_ in range(NI + LAG):
                    if n_ < NI:
                        stage_a(n_, *iters[n_])
                    m_ = n_ - LAG
                    if m_ >= 0:
                        stage_bc(m_, *iters[m_])
                if sub == "core":
                    break
                for T in range(NT):
                    tsl = slice(T * TT, (T + 1) * TT)
                    for dc in range(NCH):
                        bi = (T * NCH + dc) % 4
                        for cc in range(2):
                            self.mm(ps[bi][:, :], wo[:, cc, dc * 128:(dc + 1) * 128], qT[:, cc, tsl], cc == 0, cc == 1,
                                    [("wsl", sl["o"]), ("q", cc, T, 0), ("q", cc, T, 1)], [("ps", bi)])
                        S.dve(lambda h_, bi=bi, dc=dc, tsl=tsl: h_.tensor_tensor(out=h[:, dc, tsl], in0=ps[bi][:, :], in1=h[:, dc, tsl], op=ALU.add),
                              reads=[("ps", bi), ("h", dc, T)], writes=[("h", dc, T)])
            S.barrier()
            S.flush()

    def phase_conformer(self, j):
        nc, S, ps = self.nc, self.S, self.ps
        hn, h = self.hn, self.h
        PADL = 32
        with ExitStack() as es:
            glu = self.sb(es, [128, NCH, PADL + SEQ], BF16, "glu")
            ybf = self.sb(es, [128, NCH, TT], BF16, "ybf")
            ysq = self.sb(es, [128, NCH, TT], BF16, "ysq")
            dg = self.sb(es, [128, CW, 128], BF16, "dg")
            NW = 3
            wsl = [self.sb(es, [128, NCH, 256], BF16, "cw") for _ in range(NW)]
            sgm = [self.sb(es, [128, TT], F32, "sgm") for _ in range(2)]
            tA = [self.sb(es, [128, TT], F32, "tA") for _ in range(2)]
            mean_t = self.sb(es, [128, TT], F32, "mean_t")
            m2_t = self.sb(es, [128, TT], F32, "m2_t")
            std_t = self.sb(es, [128, TT], F32, "std_t")
            wcount = [0]

            def next_slot():
                i = wcount[0] % NW
                wcount[0] += 1
                return i

            S.pool(lambda h_: h_.memset(glu[:, :, 0:PADL], 0.0), writes=[("glupad",)])
            w1 = self.w_pw1[j].rearrange("(c p) f -> p c f", p=128)
            w2 = self.w_pw2[j].rearrange("(c p) f -> p c f", p=128)

            def load_pw1(cb):
                ia = next_slot()
                self.load_w(wsl[ia][:], w1[:, :, cb * 256:(cb + 1) * 256], ("cw", ia), "cw%d" % ia)
                ig = next_slot()
                self.load_w(wsl[ig][:], w1[:, :, D + cb * 256:D + (cb + 1) * 256], ("cw", ig), "cw%d" % ig)
                return ia, ig

            it = 0
            for cb in range(4):
                ia, ig = load_pw1(cb)
                for ci in range(2):
                    cc = 2 * cb + ci
                    for T in range(NT):
                        ba, bg = (it % 2) * 2, (it % 2) * 2 + 1
                        it += 1
                        tsl = slice(T * TT, (T + 1) * TT)
                        for c in range(NCH):
                            self.mm(ps[ba][:, :], wsl[ia][:, c, ci * 128:(ci + 1) * 128], hn[:, c, tsl], c == 0, c == NCH - 1,
                                    [("cw", ia), ("hn", c, T)], [("ps", ba)])
                        for c in range(NCH):
                            self.mm(ps[bg][:, :], wsl[ig][:, c, ci * 128:(ci + 1) * 128], hn[:, c, tsl], c == 0, c == NCH - 1,
                                    [("cw", ig), ("hn", c, T)], [("ps", bg)])
                        sg_ = sgm[it % 2]
                        S.act(lambda h_, sg_=sg_, bg=bg, cc=cc: h_.activation(out=sg_[:], in_=ps[bg][:, :], func=AF.Sigmoid,
                                                                               bias=self.prow(R_BPW1 + j * 16 + 8 + cc)),
                              reads=[("ps", bg), "ptab"], writes=[("sgm", it % 2)])
                        S.dve(lambda h_, sg_=sg_, ba=ba, cc=cc, T=T: h_.scalar_tensor_tensor(
                            out=glu[:, cc, PADL + T * TT:PADL + (T + 1) * TT], in0=ps[ba][:, :], scalar=self.prow(R_BPW1 + j * 16 + cc),
                            in1=sg_[:], op0=ALU.add, op1=ALU.mult),
                            reads=[("ps", ba), ("sgm", it % 2), "ptab"], writes=[("glu", cc, T)])
            pw2_slots = {}

            def load_pw2(db):
                i = next_slot()
                self.load_w(wsl[i][:], w2[:, :, db * 256:(db + 1) * 256], ("cw", i), "cw%d" % i)
                pw2_slots[db] = i

            load_pw2(0)
            load_pw2(1)
            dcount = 0
            for T in range(NT):
                for cc in range(NCH):
                    yb = cc % 2
                    for tap in range(CW):
                        if dcount % 2 == 0:
                            S.dve(lambda h_, tap=tap, cc=cc: h_.tensor_scalar(out=dg[:, tap, :], in0=self.identb[:],
                                                                              scalar1=self.prow(R_WDW + (j * CW + tap) * 8 + cc), scalar2=None, op0=ALU.mult),
                                  reads=["identb", "ptab"], writes=[("dg", tap)])
                        else:
                            S.pool(lambda h_, tap=tap, cc=cc: h_.tensor_scalar(out=dg[:, tap, :], in0=self.identb[:],
                                                                               scalar1=self.prow(R_WDW + (j * CW + tap) * 8 + cc), scalar2=1.0,
                                                                               op0=ALU.mult, op1=ALU.mult),
                                   reads=["identb", "ptab"], writes=[("dg", tap)])
                        dcount += 1
                    for tap in range(CW):
                        o0 = PADL + T * TT - (CW - 1) + tap
                        rd = [("dg", tap), ("glu", cc, T)]
                        if T > 0:
                            rd.append(("glu", cc, T - 1))
                        else:
                            rd.append(("glupad",))
                        self.mm(ps[yb][:, :], dg[:, tap, :], glu[:, cc, o0:o0 + TT], tap == 0, tap == CW - 1, rd, [("ps", yb)])
                    S.act(lambda h_, cc=cc, yb=yb: h_.activation(out=ybf[:, cc, :], in_=ps[yb][:, :], func=AF.Identity,
                                                                  bias=self.prow(R_BDW + j * 8 + cc)),
                          reads=[("ps", yb), "ptab"], writes=[("ybf", cc)])
                    S.act(lambda h_, cc=cc: h_.activation(out=ysq[:, cc, :], in_=ybf[:, cc, :], func=AF.Square),
                          reads=[("ybf", cc)], writes=[("ysq", cc)])
                bm, bq = 2 + (T % 2) * 2, 3 + (T % 2) * 2
                for cc in range(NCH):
                    self.mm(ps[bm][:, :], self.onesb[:], ybf[:, cc, :], cc == 0, cc == NCH - 1, ["onesb", ("ybf", cc)], [("ps", bm)])
                for cc in range(NCH):
                    self.mm(ps[bq][:, :], self.onesb[:], ysq[:, cc, :], cc == 0, cc == NCH - 1, ["onesb", ("ysq", cc)], [("ps", bq)])
                S.act(lambda h_, bm=bm: h_.activation(out=mean_t[:], in_=ps[bm][:, :], func=AF.Copy, scale=1.0 / D),
                      reads=[("ps", bm)], writes=["mean_t"])
                S.dve(lambda h_: h_.tensor_tensor(out=m2_t[:], in0=mean_t[:], in1=mean_t[:], op=ALU.mult), reads=["mean_t"], writes=["m2_t"])
                S.dve(lambda h_, bq=bq: h_.scalar_tensor_tensor(out=m2_t[:], in0=ps[bq][:, :], scalar=1.0 / D, in1=m2_t[:],
                                                                 op0=ALU.mult, op1=ALU.subtract),
                      reads=[("ps", bq), "m2_t"], writes=["m2_t"])
                S.act(lambda h_: h_.activation(out=std_t[:], in_=m2_t[:], func=AF.Sqrt, bias=LN_EPS), reads=["m2_t"], writes=["std_t"])
                S.dve(lambda h_, bm=bm: h_.reciprocal(ps[bm][:, :], std_t[:]), reads=["std_t"], writes=[("ps", bm)])
                S.dve(lambda h_, bm=bm, bq=bq: h_.tensor_tensor(out=ps[bq][:, :], in0=ps[bm][:, :], in1=mean_t[:], op=ALU.mult),
                      reads=[("ps", bm), "mean_t"], writes=[("ps", bq)])
                for cc in range(NCH):
                    ta_ = tA[cc % 2]
                    S.dve(lambda h_, ta_=ta_, cc=cc, bm=bm: h_.tensor_tensor(out=ta_[:], in0=ps[bm][:, :], in1=ybf[:, cc, :], op=ALU.mult),
                          reads=[("ps", bm), ("ybf", cc)], writes=[("tA", cc % 2)])
                    S.dve(lambda h_, ta_=ta_, bq=bq: h_.tensor_tensor(out=ta_[:], in0=ta_[:], in1=ps[bq][:, :], op=ALU.subtract),
                          reads=[("ps", bq), ("tA", cc % 2)], writes=[("tA", cc % 2)])
                    S.act(lambda h_, ta_=ta_, cc=cc, T=T: h_.activation(out=hn[:, cc, T * TT:(T + 1) * TT], in_=ta_[:], func=AF.Silu,
                                                                         scale=self.prow(R_LNG + j * 8 + cc), bias=self.prow(R_LNB + j * 8 + cc)),
                          reads=[("tA", cc % 2), "ptab"], writes=[("hn", cc, T)])
            it = 0
            for db in range(4):
                if db + 2 < 4:
                    load_pw2(db + 2)
                i = pw2_slots[db]
                for di in range(2):
                    dc = 2 * db + di
                    for T in range(NT):
                        bi = 6 + it % 2
                        it += 1
                        tsl = slice(T * TT, (T + 1) * TT)
                        for cc in range(NCH):
                            self.mm(ps[bi][:, :], wsl[i][:, cc, di * 128:(di + 1) * 128], hn[:, cc, tsl], cc == 0, cc == NCH - 1,
                                    [("cw", i), ("hn", cc, T)], [("ps", bi)])
                        S.dve(lambda h_, bi=bi, dc=dc, tsl=tsl: h_.scalar_tensor_tensor(
                            out=h[:, dc, tsl], in0=ps[bi][:, :], scalar=self.prow(R_BPW2 + j * 8 + dc), in1=h[:, dc, tsl], op0=ALU.add, op1=ALU.add),
                            reads=[("ps", bi), ("h", dc, T), "ptab"], writes=[("h", dc, T)])
            S.barrier()
            S.flush()

    def phase_ffn(self, i):
        nc, S, ps = self.nc, self.S, self.ps
        hn, h = self.hn, self.h
        groups = [[0, 1, 2], [3, 4, 5], [6, 7, 8], [9, 10]]
        with ExitStack() as es:
            act = self.sb(es, [128, 6, SEQ], BF16, "act")
            NWU = 3
            wup = [self.sb(es, [128, NCH, 512], BF16, "wup") for _ in range(NWU)]
            wdn = [self.sb(es, [128, 6, D], BF16, "wdn") for _ in range(2)]
            Ag = [self.sb(es, [128, TT], F32, "Ag") for _ in range(2)]
            Av = [self.sb(es, [128, TT], F32, "Av") for _ in range(2)]
            sg = [self.sb(es, [128, TT], F32, "sg") for _ in range(2)]
            halo = self.sb(es, [128, 2, 2, 2], F32, "halo")
            wu_src = self.w_up[i].rearrange("(c p) f -> p c f", p=128)
            wd_src = self.w_down[i].rearrange("(c p) f -> p c f", p=128)
            ucount = [0]
            blocks = [b for g in groups for b in g]
            up_slot = {}

            def load_up(b):
                s = ucount[0] % NWU
                ucount[0] += 1
                up_slot[b] = s
                self.load_w(wup[s][:, :, 0:256], wu_src[:, :, b * 256:(b + 1) * 256], ("wup", s), "wu%d" % s)
                self.load_w(wup[s][:, :, 256:512], wu_src[:, :, DFF + b * 256:DFF + (b + 1) * 256], ("wup", s), "wu%d" % s)

            def load_dn(gi):
                import os
                if os.environ.get("FFN_NODN"):
                    return
                g = groups[gi]
                np_ = 2 * len(g)
                j0 = 2 * g[0]
                self.load_w(wdn[gi % 2][:, 0:np_, :], wd_src[:, j0:j0 + np_, :], ("wdn", gi % 2), "wd%d" % (gi % 2))

            pend3 = []
            load_up(blocks[0])
            load_up(blocks[1])
            load_dn(0)
            nxt = 2
            it = 0
            for gi, g in enumerate(groups):
                for bl, b in enumerate(g):
                    s = up_slot[b]
                    for pi in range(2):
                        jp = 2 * b + pi
                        jl = 2 * bl + pi
                        rows = {}
                        for kind, fc in (("g", jp), ("v", NFP + jp)):
                            rows[kind] = [R_FWDW + (i * 3 + tap) * 44 + fc for tap in range(3)] + [R_FBDW + i * 44 + fc]
                        for T in range(NT):
                            par = it % 2
                            it += 1
                            tsl = slice(T * TT, (T + 1) * TT)
                            s3 = (it - 1) % 3
                            bg_, bv_ = s3 * 2, s3 * 2 + 1
                            import os
                            for c in range(NCH if not os.environ.get("FFN_NOMM") else 0):
                                self.mm(ps[bg_][:, :], wup[s][:, c, pi * 128:(pi + 1) * 128], hn[:, c, tsl], c == 0, c == NCH - 1,
                                        [("wup", s), ("hn", c, T)], [("ps", bg_)])
                            for c in range(NCH if not os.environ.get("FFN_NOMM") else 0):
                                self.mm(ps[bv_][:, :], wup[s][:, c, 256 + pi * 128:256 + (pi + 1) * 128], hn[:, c, tsl], c == 0, c == NCH - 1,
                                        [("wup", s), ("hn", c, T)], [("ps", bv_)])
                            A = {"g": Ag[par], "v": Av[par]}
                            U = {"g": ps[bg_], "v": ps[bv_]}
                            UB = {"g": bg_, "v": bv_}
                            AK = {"g": ("Ag", par), "v": ("Av", par)}
                            KI = {"g": 0, "v": 1}
                            hp_prev = (T - 1) % 2
                            hp_cur = T % 2
                            import os
                            FL = int(os.environ.get("FFN_LEVEL", "9"))
                            for kind in ("g", "v"):
                                if FL < 2:
                                    break
                                r = rows[kind]
                                S.act(lambda h_, A_=A[kind], U_=U[kind], r=r: h_.activation(out=A_[:], in_=U_[:, :], func=AF.Identity,
                                                                                          scale=self.prow(r[2]), bias=self.prow(r[3])),
                                      reads=[("ps", UB[kind]), "ptab"], writes=[AK[kind]])
                                if T < NT - 1 and FL >= 3:
                                    S.act(lambda h_, U_=U[kind], kind=kind, hp_cur=hp_cur: h_.activation(
                                        out=halo[:, hp_cur, KI[kind], :], in_=U_[:, TT - 2:TT], func=AF.Copy),
                                        reads=[("ps", UB[kind])], writes=[("halo", hp_cur, kind)])
                            for kind in ("g", "v"):
                                if FL < 4:
                                    break
                                r = rows[kind]
                                S.dve(lambda h_, A_=A[kind], U_=U[kind], r=r: h_.scalar_tensor_tensor(
                                    out=A_[:, 1:TT], in0=U_[:, 0:TT - 1], scalar=self.prow(r[1]), in1=A_[:, 1:TT], op0=ALU.mult, op1=ALU.add),
                                    reads=[("ps", UB[kind]), AK[kind], "ptab"], writes=[AK[kind]])
                            for kind in ("g", "v"):
                                if FL < 4:
                                    break
                                r = rows[kind]
                                S.dve(lambda h_, A_=A[kind], U_=U[kind], r=r: h_.scalar_tensor_tensor(
                                    out=A_[:, 2:TT], in0=U_[:, 0:TT - 2], scalar=self.prow(r[0]), in1=A_[:, 2:TT], op0=ALU.mult, op1=ALU.add),
                                    reads=[("ps", UB[kind]), AK[kind], "ptab"], writes=[AK[kind]])
                            if T > 0 and FL >= 5:
                                for kind in ("g", "v"):
                                    r = rows[kind]
                                    hl_ = halo[:, hp_prev, KI[kind], :]
                                    S.dve(lambda h_, A_=A[kind], hl_=hl_, r=r: h_.scalar_tensor_tensor(
                                        out=A_[:, 0:1], in0=hl_[:, 1:2], scalar=self.prow(r[1]), in1=A_[:, 0:1], op0=ALU.mult, op1=ALU.add),
                                        reads=[("halo", hp_prev, kind), AK[kind], "ptab"], writes=[AK[kind]])
                                for kind in ("g", "v"):
                                    r = rows[kind]
                                    hl_ = halo[:, hp_prev, KI[kind], :]
                                    S.dve(lambda h_, A_=A[kind], hl_=hl_, r=r: h_.scalar_tensor_tensor(
                                        out=A_[:, 0:2], in0=hl_[:, 0:2], scalar=self.prow(r[0]), in1=A_[:, 0:2], op0=ALU.mult, op1=ALU.add),
                                        reads=[("halo", hp_prev, kind), AK[kind], "ptab"], writes=[AK[kind]])
                            if FL < 6:
                                continue
                            def stage3(par=par, Ag_=A["g"], Av_=A["v"], jl=jl, tsl=tsl, T=T):
                                sg_ = sg[par]
                                S.act(lambda h_: h_.activation(out=sg_[:], in_=Ag_[:], func=AF.Silu),
                                      reads=[("Ag", par)], writes=[("sg", par)])
                                S.pool(lambda h_: h_.tensor_tensor(out=act[:, jl, tsl], in0=sg_[:], in1=Av_[:], op=ALU.mult),
                                       reads=[("sg", par), ("Av", par)], writes=[("act", jl, T)])
                            if pend3:
                                pend3.pop(0)()
                            pend3.append(stage3)
                            if pi == 0 and T == 2:
                                if nxt < len(blocks):
                                    load_up(blocks[nxt])
                                    nxt += 1
                                if bl == 0 and gi + 1 < len(groups):
                                    load_dn(gi + 1)
                while pend3:
                    pend3.pop(0)()
                np_ = 2 * len(g)
                wd = wdn[gi % 2]
                dn = 0
                for T in range(NT if FL >= 7 else 0):
                    tsl = slice(T * TT, (T + 1) * TT)
                    for dc in range(NCH):
                        bi = 6 + dn % 2
                        dn += 1
                        for jl in range(np_):
                            self.mm(ps[bi][:, :], wd[:, jl, dc * 128:(dc + 1) * 128], act[:, jl, tsl], jl == 0, jl == np_ - 1,
                                    [("wdn", gi % 2), ("act", jl, T)], [("ps", bi)])
                        S.dve(lambda h_, bi=bi, dc=dc, tsl=tsl: h_.tensor_tensor(out=h[:, dc, tsl], in0=ps[bi][:, :], in1=h[:, dc, tsl], op=ALU.add),
                              reads=[("ps", bi), ("h", dc, T)], writes=[("h", dc, T)])
            S.barrier()
            S.flush()

    def phase_final(self, raw=False):
        nc, S, ps = self.nc, self.S, self.ps
        with ExitStack() as es:
            ofm = self.sb(es, [128, NCH, TT], F32, "ofm")
            ost = [self.sb(es, [128, D], F32, "ost") for _ in range(3)]
            oc = 0
            for T in range(NT):
                tsl = slice(T * TT, (T + 1) * TT)
                if raw:
                    for c in range(NCH):
                        eng = S.act if c % 2 == 0 else S.dve
                        if c % 2 == 0:
                            S.act(lambda h_, c=c, tsl=tsl: h_.activation(out=ofm[:, c, :], in_=self.h[:, c, tsl], func=AF.Copy),
                                  reads=[("h", c, T)], writes=[("ofm", c)])
                        else:
                            S.dve(lambda h_, c=c, tsl=tsl: h_.tensor_copy(out=ofm[:, c, :], in_=self.h[:, c, tsl]),
                                  reads=[("h", c, T)], writes=[("ofm", c)])
                else:
                    self.rmsnorm_tile(T, R_FIN, ofm)
                for ts in range(4):
                    tt = T * 4 + ts
                    sl = oc % 3
                    oc += 1
                    for half in range(2):
                        bi = (2 * tt + half) % 4
                        bank = ps[bi]
                        for jj in range(4):
                            c = half * 4 + jj
                            S.pe(lambda h_, bank=bank, jj=jj, c=c, ts=ts: h_.transpose(out=bank[:, jj * 128:(jj + 1) * 128],
                                                                                         in_=ofm[:, c, ts * 128:(ts + 1) * 128], identity=self.identf[:]),
                                 reads=[("ofm", c), "identf"], writes=[("ps", bi)])
                        dst = ost[sl][:, half * 512:(half + 1) * 512]
                        if half == 0:
                            S.act(lambda h_, dst=dst, bank=bank: h_.activation(out=dst, in_=bank[:, :], func=AF.Copy),
                                  reads=[("ps", bi)], writes=[("ost", sl, half)])
                        else:
                            S.dve(lambda h_, dst=dst, bank=bank: h_.tensor_copy(out=dst, in_=bank[:, :]),
                                  reads=[("ps", bi)], writes=[("ost", sl, half)])
                    S.dma("sp", "ost%d" % sl, lambda h_, sl=sl, tt=tt: h_.dma_start(out=self.out[tt * 128:(tt + 1) * 128, :], in_=ost[sl][:]),
                          reads=[("ost", sl, 0), ("ost", sl, 1)])
            S.barrier()
            S.flush()

    def rmsnorm_tile(self, T, grow, ofm):
        S, ps = self.S, self.ps
        bi = 6 + T % 2
        bank = ps[bi]
        tsl = slice(T * TT, (T + 1) * TT)
        for c in range(NCH):
            sq = self.nsq[:, c % 2, :]
            S.act(lambda h, sq=sq, c=c: h.activation(out=sq, in_=self.h[:, c, tsl], func=AF.Square),
                  reads=[("h", c, T)], writes=[("nsq", c % 2)])
            self.mm(bank[:, :], self.onesb[:], sq, c == 0, c == NCH - 1, [("nsq", c % 2), "onesb"], [("ps", bi)])
        sd = self.nstd[:, T % 2, :]
        S.act(lambda h: h.activation(out=sd, in_=bank[:, :], func=AF.Sqrt, scale=1.0 / D, bias=NORM_EPS),
              reads=[("ps", bi)], writes=[("nstd", T % 2)])
        S.dve(lambda h: h.reciprocal(bank[:, :], sd), reads=[("nstd", T % 2)], writes=[("ps", bi)])
        for c in range(NCH):
            S.dve(lambda h, c=c: h.scalar_tensor_tensor(out=ofm[:, c, :], in0=self.h[:, c, tsl], scalar=self.prow(grow + c), in1=bank[:, :],
                                                        op0=ALU.mult, op1=ALU.mult),
                  reads=[("h", c, T), ("ps", bi), "ptab"], writes=[("ofm", c)])


def _pack_ptab(inp):
    f = lambda a: np.ascontiguousarray(np.asarray(a, dtype=np.float32)).reshape(-1, 128)
    parts = [
        f(inp["norm_mix_g"]), f(inp["norm_ffn_g"]), f(inp["final_norm_g"]), f(inp["conv_b_pw1"]),
        f(inp["conv_w_dw"]), f(inp["conv_b_dw"]), f(inp["conv_ln_g"]), f(inp["conv_ln_b"]), f(inp["conv_b_pw2"]),
        f(inp["ffn_w_dw"]), f(inp["ffn_b_dw"]),
    ]
    tab = np.concatenate(parts, axis=0)
    assert tab.shape[0] == 1368
    pad = np.zeros((R_TOT - tab.shape[0], 128), np.float32)
    return np.ascontiguousarray(np.concatenate([tab, pad], axis=0))


_NC_CACHE = {}


def _run(inputs, stop_after=None, trace=False):
    x = np.ascontiguousarray(np.asarray(inputs["x"], dtype=np.float32))
    B = x.shape[0]
    key = stop_after
    if key not in _NC_CACHE:
        _NC_CACHE[key] = Builder(stop_after).build()
    nc = _NC_CACHE[key]
    ptab = _pack_ptab(inputs)
    c = lambda k: np.ascontiguousarray(np.asarray(inputs[k], dtype=np.float32))
    shared = {
        "ptab_in": ptab, "w_qkv": c("attn_w_qkv"), "w_o": c("attn_w_o"), "w_pw1": c("conv_w_pw1"), "w_pw2": c("conv_w_pw2"),
        "w_up": c("ffn_w_up"), "w_down": c("ffn_w_down"),
    }
    in_maps = [dict(shared, x=x[b]) for b in range(B)]
    res = run_bass_kernel_spmd(nc, in_maps, core_ids=list(range(B)), trace=trace)
    out = np.stack([np.asarray(r["out"]) for r in res.results], axis=0).astype(np.float32)
    return out, res


def kernel(**inputs):
    out, _ = _run(inputs)
    return out
```

```python
import math
import numpy as np
from contextlib import ExitStack
import concourse.bass as bass
import concourse.mybir as mybir
from concourse.bass_utils import run_bass_kernel_spmd

F32 = mybir.dt.float32
BF16 = mybir.dt.bfloat16
I32 = mybir.dt.int32
ALU = mybir.AluOpType
AF = mybir.ActivationFunctionType
AX = mybir.AxisListType

D = 1024
SEQ = 2048
NCH = 8
NT = 4
TT = 512
H = 16
DH = 64
DFF = 2816
NFP = 22
DEPTH = 4
NEG = -30000.0
NORM_EPS = 1e-6
LN_EPS = 1e-5
CW = 31

R_MIX = 0
R_FFN = 32
R_FIN = 64
R_BPW1 = 72
R_WDW = 104
R_BDW = 600
R_LNG = 616
R_LNB = 632
R_BPW2 = 648
R_FWDW = 664
R_FBDW = 1192
R_TOT = 1408


class _Op:
    __slots__ = ("eng", "fn", "deps", "needs_inc", "semval", "grp", "gen", "pre")

    def __init__(self, eng, fn):
        self.eng = eng
        self.fn = fn
        self.deps = []
        self.needs_inc = False
        self.semval = None
        self.grp = None
        self.gen = 0
        self.pre = None


class _Grp:
    __slots__ = ("sem", "gens", "closed")

    def __init__(self, sem):
        self.sem = sem
        self.gens = [0]
        self.closed = False


class Sched:
    ENGS = ("pe", "act", "dve", "pool", "sp")

    def __init__(self, nc, es):
        self.nc = nc
        self.es = es
        self.q = {e: [] for e in self.ENGS}
        self.lastw = {}
        self.readers = {}
        self.esem = {e: es.enter_context(nc.semaphore("sem_" + e)) for e in ("pe", "act", "dve", "pool")}
        self.ecnt = {e: 0 for e in ("pe", "act", "dve", "pool")}
        self.seen = {e: {} for e in self.ENGS}
        self.groups = {}
        self.lastreal = {e: None for e in self.ENGS}
        self.nops = 0

    def _group(self, name):
        g = self.groups.get(name)
        if g is None:
            g = _Grp(self.es.enter_context(self.nc.semaphore("dg_" + name)))
            self.groups[name] = g
        return g

    def add(self, eng, fn, reads=(), writes=(), dma=None):
        op = _Op(eng, fn)
        deps = {}
        for k in reads:
            w = self.lastw.get(k)
            if w is not None:
                deps[id(w)] = w
            if isinstance(k, tuple) and k[0] == "ps":
                rd = self.readers.get(k)
                if rd:
                    for rk_, r in rd.items():
                        if rk_ != eng:
                            deps[id(r)] = r
        for k in writes:
            w = self.lastw.get(k)
            if w is not None:
                deps[id(w)] = w
            rd = self.readers.get(k)
            if rd:
                for r in rd.values():
                    deps[id(r)] = r
        if dma is not None:
            g = self._group(dma)
            op.grp = g
            if g.closed:
                op.pre = [(g, len(g.gens) - 1)]
                g.gens.append(g.gens[-1])
                g.closed = False
            g.gens[-1] += 16
            op.gen = len(g.gens) - 1
        for d in deps.values():
            if eng == "pe" and d.eng == "pe" and d.grp is None:
                continue
            if op.grp is not None and d.grp is op.grp:
                continue
            op.deps.append(d)
            d.needs_inc = True
            if d.grp is not None:
                d.grp.closed = True
        rk = eng if dma is None else ("dma", id(op))
        for k in reads:
            self.readers.setdefault(k, {})[rk] = op
        for k in writes:
            self.lastw[k] = op
            self.readers[k] = {}
        self.q[eng].append(op)
        if dma is None:
            self.lastreal[eng] = op
        self.nops += 1
        return op

    def pe(self, fn, reads=(), writes=()):
        return self.add("pe", fn, reads, writes)

    def act(self, fn, reads=(), writes=()):
        return self.add("act", fn, reads, writes)

    def dve(self, fn, reads=(), writes=()):
        return self.add("dve", fn, reads, writes)

    def pool(self, fn, reads=(), writes=()):
        return self.add("pool", fn, reads, writes)

    def dma(self, queue, group, fn, reads=(), writes=()):
        return self.add(queue, fn, reads, writes, dma=group)

    def barrier(self):
        lasts = [op for op in self.lastreal.values() if op is not None and op.grp is None]
        gl = [(g, len(g.gens) - 1) for g in self.groups.values() if g.gens[-1] > 0]
        for g, _ in gl:
            g.closed = True
        for e in self.ENGS:
            op = _Op(e, None)
            for d in lasts:
                if not (e == "pe" and d.eng == "pe"):
                    op.deps.append(d)
                    d.needs_inc = True
            op.pre = list(gl)
            self.q[e].append(op)
        self.lastw = {}
        self.readers = {}

    def _assign(self):
        for e in ("pe", "act", "dve", "pool"):
            c = self.ecnt[e]
            for op in self.q[e]:
                if op.grp is None and op.fn is not None and op.needs_inc and op.semval is None:
                    c += 1
                    op.semval = c
            self.ecnt[e] = c

    def _emit_engine(self, e, h):
        seen = self.seen[e]

        def wait(sem, val):
            key = id(sem)
            if seen.get(key, 0) < val:
                h.wait_ge(sem, val)
                seen[key] = val

        for op in self.q[e]:
            if op.pre is not None:
                for g, gi in op.pre:
                    wait(g.sem, g.gens[gi])
            for d in op.deps:
                if d.grp is not None:
                    wait(d.grp.sem, d.grp.gens[d.gen])
                else:
                    wait(self.esem[d.eng], d.semval)
            if op.fn is not None:
                ins = op.fn(h)
                if op.grp is not None:
                    ins.then_inc(op.grp.sem, 16)
                elif op.needs_inc:
                    ins.then_inc(self.esem[e], 1)
        self.q[e] = []
        self.lastreal[e] = None

    def flush(self):
        self._assign()
        with self.nc.Block() as block:
            @block.tensor
            def _(h):
                self._emit_engine("pe", h)

            @block.scalar
            def _(h):
                self._emit_engine("act", h)

            @block.vector
            def _(h):
                self._emit_engine("dve", h)

            @block.gpsimd
            def _(h):
                self._emit_engine("pool", h)

            @block.sync
            def _(h):
                self._emit_engine("sp", h)


class Builder:
    def __init__(self, stop_after=None):
        self.stop_after = stop_after
        self.nc = bass.Bass("TRN2", target_bir_lowering=False)
        nc = self.nc
        dt = nc.dram_tensor
        self.x = dt("x", [SEQ, D], F32, kind="ExternalInput").ap()
        self.ptab_in = dt("ptab_in", [R_TOT, 128], F32, kind="ExternalInput").ap()
        self.w_qkv = dt("w_qkv", [2, D, 3 * D], F32, kind="ExternalInput").ap()
        self.w_o = dt("w_o", [2, D, D], F32, kind="ExternalInput").ap()
        self.w_pw1 = dt("w_pw1", [2, D, 2 * D], F32, kind="ExternalInput").ap()
        self.w_pw2 = dt("w_pw2", [2, D, D], F32, kind="ExternalInput").ap()
        self.w_up = dt("w_up", [DEPTH, D, 2 * DFF], F32, kind="ExternalInput").ap()
        self.w_down = dt("w_down", [DEPTH, DFF, D], F32, kind="ExternalInput").ap()
        self.out = dt("out", [SEQ, D], F32, kind="ExternalOutput").ap()
        self._uid = 0

    def sb(self, es, shape, dtype, name=None):
        self._uid += 1
        return es.enter_context(self.nc.sbuf_tensor("%s_%d" % (name or "t", self._uid), shape, dtype))

    def mm(self, out, lhsT, rhs, start, stop, reads, writes):
        self.S.pe(lambda h: h.matmul(out, lhsT=lhsT, rhs=rhs, start=start, stop=stop), reads, writes)

    def prow(self, r):
        return self.ptab[:, r:r + 1]

    def build(self):
        nc = self.nc
        with ExitStack() as es:
            self.S = S = Sched(nc, es)
            self.ps = [es.enter_context(nc.psum_tensor("ps%d" % i, [128, TT], F32)) for i in range(8)]
            self.h = self.sb(es, [128, NCH, SEQ], F32, "h")
            self.hn = self.sb(es, [128, NCH, SEQ], BF16, "hn")
            self.cosT = self.sb(es, [128, SEQ], BF16, "cosT")
            self.sinT = self.sb(es, [128, SEQ], BF16, "sinT")
            self.ptab = self.sb(es, [128, R_TOT], F32, "ptab")
            self.identf = self.sb(es, [128, 128], F32, "identf")
            self.identb = self.sb(es, [128, 128], BF16, "identb")
            self.onesb = self.sb(es, [128, 128], BF16, "onesb")
            self.maskT = self.sb(es, [128, 128], BF16, "maskT")
            self.prot = self.sb(es, [128, 128], BF16, "prot")
            self.ind = [self.sb(es, [128, 8 * 128], BF16, "ind") for _ in range(2)]
            self.nsq = self.sb(es, [128, 2, TT], BF16, "nsq")
            self.nstd = self.sb(es, [128, 2, TT], F32, "nstd")

            self.phase_setup()
            stop = self.stop_after
            done = stop == "load"
            for i in range(DEPTH):
                if done:
                    break
                j = i // 2
                self.rmsnorm(R_MIX + i * 8)
                if i % 2 == 0:
                    self.phase_attention(j)
                else:
                    self.phase_conformer(j)
                if stop is not None and stop.split(":")[0] == "mix%d" % i:
                    done = True
                    break
                self.rmsnorm(R_FFN + i * 8)
                fsub = stop.split(":")[1] if (stop and ":" in stop and stop.startswith("ffn")) else None
                if fsub != "norm":
                    self.phase_ffn(i)
                if stop is not None and stop.split(":")[0] == "ffn%d" % i:
                    done = True
                    break
            self.phase_final(raw=(stop is not None))
        return nc

    def phase_setup(self):
        nc, S = self.nc, self.S
        ps = self.ps
        with ExitStack() as es:
            xs = [self.sb(es, [128, D], F32, "xs") for _ in range(3)]
            pst = [self.sb(es, [128, 128], F32, "pst") for _ in range(2)]
            pi_i = self.sb(es, [128, 1], I32, "pi_i")
            pm_i = self.sb(es, [128, 2], I32, "pm_i")
            sgn = self.sb(es, [128, 2], F32, "sgn")
            invrow = self.sb(es, [1, 128], F32, "invrow")
            invf = self.sb(es, [128, 1], F32, "invf")
            pos_i = self.sb(es, [128, SEQ], I32, "pos_i")
            ang = self.sb(es, [128, SEQ], F32, "ang")
            ta = self.sb(es, [128, SEQ], F32, "ta")
            tb = self.sb(es, [128, SEQ], F32, "tb")
            tc = self.sb(es, [128, SEQ], F32, "tc")

            identf, identb, onesb, maskT, prot, ind = self.identf, self.identb, self.onesb, self.maskT, self.prot, self.ind
            S.pool(lambda h: h.memset(identf[:], 1.0), writes=["identf"])
            S.pool(lambda h: h.affine_select(out=identf[:], in_=identf[:], pattern=[[-1, 128]], compare_op=ALU.is_equal,
                                             fill=0.0, base=0, channel_multiplier=1), reads=["identf"], writes=["identf"])
            S.dve(lambda h: h.tensor_copy(out=identb[:], in_=identf[:]), reads=["identf"], writes=["identb"])
            S.pool(lambda h: h.memset(onesb[:], 1.0), writes=["onesb"])
            S.pool(lambda h: h.memset(maskT[:], 0.0), writes=["maskT"])
            S.pool(lambda h: h.affine_select(out=maskT[:], in_=maskT[:], pattern=[[1, 128]], compare_op=ALU.is_ge,
                                             fill=NEG, base=0, channel_multiplier=-1), reads=["maskT"], writes=["maskT"])
            for (dst, src) in ((0, 32), (32, 0), (64, 96), (96, 64)):
                S.dve(lambda h, dst=dst, src=src: h.tensor_copy(out=prot[:, dst:dst + 32], in_=identb[:, src:src + 32]),
                      reads=["identb"], writes=["prot"])
            for jj in range(2):
                S.pool(lambda h, jj=jj: h.memset(ind[jj][:], 0.0), writes=["ind"])
                S.pool(lambda h, jj=jj: h.memset(ind[jj][64 * jj:64 * jj + 64, :], 1.0), reads=["ind"], writes=["ind"])
                S.pool(lambda h, jj=jj: h.affine_select(out=ind[jj][64 * jj:64 * jj + 64, :], in_=ind[jj][64 * jj:64 * jj + 64, :],
                                                        pattern=[[1, 8], [0, 128]], compare_op=ALU.is_equal, fill=0.0,
                                                        base=0, channel_multiplier=-1), reads=["ind"], writes=["ind"])
            for r in range(R_TOT // 128):
                st = pst[r % 2]
                S.dma("sp", "pst%d" % (r % 2), lambda h, r=r, st=st: h.dma_start(out=st[:], in_=self.ptab_in[r * 128:(r + 1) * 128, :]),
                      writes=[("pst", r % 2)])
                bank = ps[r % 2]
                S.pe(lambda h, st=st, bank=bank: h.transpose(out=bank[:, 0:128], in_=st[:], identity=identf[:]),
                     reads=[("pst", r % 2), "identf"], writes=[("ps", r % 2)])
                S.act(lambda h, r=r, bank=bank: h.activation(out=self.ptab[:, r * 128:(r + 1) * 128], in_=bank[:, 0:128], func=AF.Copy),
                      reads=[("ps", r % 2)], writes=["ptab"])
            inv = (np.float32(1.0) / np.power(np.float32(10000.0), np.arange(0, DH, 2, dtype=np.float32) / np.float32(DH))).astype(np.float32)
            irv = invrow[:].rearrange("o (a b) -> o a b", b=32)
            for i in range(32):
                S.dve(lambda h, i=i: h.memset(irv[:, :, i:i + 1], float(inv[i])), writes=["invrow"])
            S.pe(lambda h: h.transpose(out=ps[2][:, 0:1], in_=invrow[:], identity=identf[0:1, 0:1]),
                 reads=["invrow", "identf"], writes=[("ps", 2)])
            S.act(lambda h: h.activation(out=invf[:], in_=ps[2][:, 0:1], func=AF.Copy), reads=[("ps", 2)], writes=["invf"])
            S.pool(lambda h: h.iota(pi_i[:], pattern=[[0, 1]], base=0, channel_multiplier=1), writes=["pi_i"])
            S.dve(lambda h: h.tensor_scalar(out=pm_i[:, 0:1], in0=pi_i[:], scalar1=32, scalar2=None, op0=ALU.bitwise_and),
                  reads=["pi_i"], writes=["pm_i"])
            S.dve(lambda h: h.tensor_copy(out=sgn[:, 0:1], in_=pm_i[:, 0:1]), reads=["pm_i"], writes=["sgn0"])
            S.dve(lambda h: h.tensor_scalar(out=sgn[:, 1:2], in0=sgn[:, 0:1], scalar1=1.0 / 16.0, scalar2=-1.0, op0=ALU.mult, op1=ALU.add),
                  reads=["sgn0"], writes=["sgn1"])
            S.pool(lambda h: h.iota(pos_i[:], pattern=[[1, SEQ]], base=0, channel_multiplier=0), writes=["pos_i"])
            S.dve(lambda h: h.tensor_copy(out=ta[:], in_=pos_i[:]), reads=["pos_i"], writes=["ta"])
            S.dve(lambda h: h.tensor_scalar(out=ang[:], in0=ta[:], scalar1=invf[:, 0:1], scalar2=None, op0=ALU.mult),
                  reads=["ta", "invf"], writes=["ang"])
            TWO_PI = 2.0 * math.pi
            C1 = 6.28125
            C2 = TWO_PI - C1
            MAGIC = 12582912.0
            LIM = 3.1415925
            S.dve(lambda h: h.tensor_scalar(out=ta[:], in0=ang[:], scalar1=1.0 / TWO_PI, scalar2=None, op0=ALU.mult),
                  reads=["ang", "ta"], writes=["ta"])
            S.dve(lambda h: h.tensor_scalar(out=tb[:], in0=ta[:], scalar1=MAGIC, scalar2=MAGIC, op0=ALU.add, op1=ALU.subtract),
                  reads=["ta"], writes=["tb"])
            S.dve(lambda h: h.scalar_tensor_tensor(out=ta[:], in0=tb[:], scalar=-C1, in1=ang[:], op0=ALU.mult, op1=ALU.add),
                  reads=["tb", "ang", "ta"], writes=["ta"])
            S.dve(lambda h: h.scalar_tensor_tensor(out=tc[:], in0=tb[:], scalar=-C2, in1=ta[:], op0=ALU.mult, op1=ALU.add),
                  reads=["tb", "ta"], writes=["tc"])
            S.dve(lambda h: h.tensor_scalar(out=ta[:], in0=tc[:], scalar1=LIM, scalar2=-LIM, op0=ALU.min, op1=ALU.max),
                  reads=["tc", "ta"], writes=["ta"])
            S.act(lambda h: h.activation(out=tb[:], in_=ta[:], func=AF.Sin), reads=["ta", "tb"], writes=["tb"])
            S.dve(lambda h: h.tensor_scalar(out=self.sinT[:], in0=tb[:], scalar1=sgn[:, 1:2], scalar2=None, op0=ALU.mult),
                  reads=["tb", "sgn1"], writes=["sinT"])
            S.dve(lambda h: h.tensor_scalar(out=ang[:], in0=tc[:], scalar1=math.pi / 2, scalar2=None, op0=ALU.add),
                  reads=["tc", "ang"], writes=["ang"])
            S.dve(lambda h: h.tensor_scalar(out=ta[:], in0=ang[:], scalar1=math.pi, scalar2=None, op0=ALU.is_gt),
                  reads=["ang", "ta"], writes=["ta"])
            S.dve(lambda h: h.scalar_tensor_tensor(out=tc[:], in0=ta[:], scalar=-TWO_PI, in1=ang[:], op0=ALU.mult, op1=ALU.add),
                  reads=["ta", "ang", "tc"], writes=["tc"])
            S.dve(lambda h: h.tensor_scalar(out=ta[:], in0=tc[:], scalar1=LIM, scalar2=-LIM, op0=ALU.min, op1=ALU.max),
                  reads=["tc", "ta"], writes=["ta"])
            S.act(lambda h: h.activation(out=self.cosT[:], in_=ta[:], func=AF.Sin), reads=["ta"], writes=["cosT"])
            for tt in range(16):
                sl = tt % 3
                S.dma("sp", "xs%d" % sl, lambda h, tt=tt, sl=sl: h.dma_start(out=xs[sl][:], in_=self.x[tt * 128:(tt + 1) * 128, :]),
                      writes=[("xs", sl)])
                T = tt // 4
                for half in range(2):
                    bi = 4 + (2 * tt + half) % 4
                    bank = ps[bi]
                    for jj in range(4):
                        c = half * 4 + jj
                        S.pe(lambda h, bank=bank, jj=jj, c=c, sl=sl: h.transpose(out=bank[:, jj * 128:(jj + 1) * 128],
                                                                               in_=xs[sl][:, c * 128:(c + 1) * 128], identity=identf[:]),
                             reads=[("xs", sl), "identf"], writes=[("ps", bi)])
                    dst = self.h[:, half * 4:half * 4 + 4, tt * 128:(tt + 1) * 128]
                    src = bank[:, :].rearrange("p (a b) -> p a b", b=128)
                    wr = [("h", half * 4 + jj, T) for jj in range(4)]
                    if (2 * tt + half) % 2 == 0:
                        S.act(lambda h, dst=dst, src=src: h.activation(out=dst, in_=src, func=AF.Copy), reads=[("ps", bi)], writes=wr)
                    else:
                        S.dve(lambda h, dst=dst, src=src: h.tensor_copy(out=dst, in_=src), reads=[("ps", bi)], writes=wr)
            S.barrier()
            S.flush()

    def rmsnorm(self, grow, out_fn=None):
        S, ps = self.S, self.ps
        for T in range(NT):
            bi = 6 + T % 2
            bank = ps[bi]
            tsl = slice(T * TT, (T + 1) * TT)
            for c in range(NCH):
                sq = self.nsq[:, c % 2, :]
                S.act(lambda h, sq=sq, c=c, tsl=tsl: h.activation(out=sq, in_=self.h[:, c, tsl], func=AF.Square),
                      reads=[("h", c, T)], writes=[("nsq", c % 2)])
                self.mm(bank[:, :], self.onesb[:], sq, c == 0, c == NCH - 1, [("nsq", c % 2), "onesb"], [("ps", bi)])
            sd = self.nstd[:, T % 2, :]
            S.act(lambda h, sd=sd, bank=bank: h.activation(out=sd, in_=bank[:, :], func=AF.Sqrt, scale=1.0 / D, bias=NORM_EPS),
                  reads=[("ps", bi)], writes=[("nstd", T % 2)])
            S.dve(lambda h, sd=sd, bank=bank: h.reciprocal(bank[:, :], sd), reads=[("nstd", T % 2)], writes=[("ps", bi)])
            for c in range(NCH):
                if out_fn is None:
                    dst = self.hn[:, c, tsl]
                    wr = [("hn", c, T)]
                else:
                    dst, wr = out_fn(c, T)
                S.dve(lambda h, dst=dst, c=c, tsl=tsl, bank=bank: h.scalar_tensor_tensor(
                    out=dst, in0=self.h[:, c, tsl], scalar=self.prow(grow + c), in1=bank[:, :], op0=ALU.mult, op1=ALU.mult),
                    reads=[("h", c, T), ("ps", bi), "ptab"], writes=wr)

    def load_w(self, slot_ap, src_ap, key, group):
        self.S.dma("pool", group, lambda h: h.dma_start(out=slot_ap, in_=src_ap), writes=[key])

    def phase_attention(self, j):
        nc, S, ps = self.nc, self.S, self.ps
        hn, h = self.hn, self.h
        with ExitStack() as es:
            qT = self.sb(es, [128, 2, SEQ], BF16, "qT")
            kT = self.sb(es, [128, 2, 2, SEQ], BF16, "kT")
            Vt = self.sb(es, [128, 16, 4, 128], BF16, "Vt")
            NW = 5
            wsl = [self.sb(es, [128, 2048], BF16, "wsl") for _ in range(NW)]
            biasT = self.sb(es, [128, 2, 1024], BF16, "biasT")
            kms = self.sb(es, [128, 4, 8], F32, "kms")
            kmT = self.sb(es, [128, 4, 8], BF16, "kmT")
            gsb = self.sb(es, [128, 256], F32, "gsb")
            cmp_ = self.sb(es, [128, 2, 4 * 49], BF16, "cmp")
            rank = self.sb(es, [128, 2, 32], F32, "rank")
            btok = self.sb(es, [128, 4, 256], BF16, "btok")
            pT = [self.sb(es, [128, TT], BF16, "pT") for _ in range(4)]
            rec = [self.sb(es, [128, TT], F32, "rec") for _ in range(2)]
            qs = [self.sb(es, [128, TT], BF16, "qs") for _ in range(2)]
            t1 = [self.sb(es, [128, TT], F32, "t1") for _ in range(2)]
            t2 = [self.sb(es, [128, TT], F32, "t2") for _ in range(2)]

            wcount = [0]

            def next_slot():
                i = wcount[0] % NW
                wcount[0] += 1
                return i

            S.pool(lambda h_: h_.memset(Vt[:, :, 0:4:2, 64:128], 1.0), writes=[("Vt1", 0)])
            S.pool(lambda h_: h_.memset(Vt[:, :, 1:4:2, 0:64], 1.0), writes=[("Vt1", 1)])
            S.pool(lambda h_: h_.memset(biasT[:], 0.0), writes=[("biasT", a, u) for a in range(2) for u in range(2)])
            S.pool(lambda h_: h_.memset(kT[64:128, :, 0, :], 0.0), writes=[("kz", 0)])
            S.pool(lambda h_: h_.memset(kT[0:64, :, 1, :], 0.0), writes=[("kz", 1)])

            wq_src = self.w_qkv[j].rearrange("(c p) f -> p c f", p=128)
            wo_src = self.w_o[j].rearrange("(c p) f -> p c f", p=128)

            def load_group(g):
                sl = {}
                for nm, off in (("q", 0), ("k", D), ("v", 2 * D)):
                    i = next_slot()
                    sl[nm] = i
                    self.load_w(wsl[i][:, :].rearrange("p (c f) -> p c f", f=256), wq_src[:, :, off + g * 256: off + (g + 1) * 256],
                                ("wsl", i), "aw%d" % i)
                i = next_slot()
                sl["o"] = i
                self.load_w(wsl[i][:, :].rearrange("p (c f) -> p c f", f=1024), wo_src[:, 2 * g:2 * g + 2, :], ("wsl", i), "aw%d" % i)
                return sl

            pending = load_group(0)
            rope_i = [0]
            sub = self.stop_after.split(":")[1] if (self.stop_after and ":" in self.stop_after) else None
            for g in range(4):
                if sub is not None and g > 0:
                    break
                sl = pending
                wq = wsl[sl["q"]][:, :].rearrange("p (c f) -> p c f", f=256)
                wk = wsl[sl["k"]][:, :].rearrange("p (c f) -> p c f", f=256)
                wv = wsl[sl["v"]][:, :].rearrange("p (c f) -> p c f", f=256)
                wo = wsl[sl["o"]][:, :].rearrange("p (c f) -> p c f", f=1024)
                units = [(which, cc, T) for which in ("q", "k") for cc in range(2) for T in range(NT)]
                pend = []

                def rope_a(which, cc, T):
                    wsrc = wq if which == "q" else wk
                    wkey = ("wsl", sl[which])
                    scale = DH ** -0.5 if which == "q" else 1.0
                    ri = rope_i[0]
                    rope_i[0] += 1
                    ba = ri % 2
                    A = ps[ba]
                    tsl = slice(T * TT, (T + 1) * TT)
                    for c in range(NCH):
                        self.mm(A[:, :], wsrc[:, c, cc * 128:(cc + 1) * 128], hn[:, c, tsl], c == 0, c == NCH - 1,
                                [wkey, ("hn", c, T)], [("ps", ba)])
                    q_s = qs[ri % 2]
                    S.act(lambda h_, q_s=q_s, A=A, scale=scale: h_.activation(out=q_s[:], in_=A[:, :], func=AF.Copy, scale=scale),
                          reads=[("ps", ba)], writes=[("qs", ri % 2)])
                    return (which, cc, T, ri, scale)

                def rope_b(which, cc, T, ri, scale):
                    ba, bb = ri % 2, 2 + ri % 2
                    A, B = ps[ba], ps[bb]
                    tsl = slice(T * TT, (T + 1) * TT)
                    q_s, t1_, t2_ = qs[ri % 2], t1[ri % 2], t2[ri % 2]
                    self.mm(B[:, :], self.prot[:], q_s[:], True, True, [("qs", ri % 2), "prot"], [("ps", bb)])
                    S.dve(lambda h_, t1_=t1_, A=A, scale=scale, tsl=tsl: h_.scalar_tensor_tensor(
                        out=t1_[:], in0=A[:, :], scalar=scale, in1=self.cosT[:, tsl], op0=ALU.mult, op1=ALU.mult),
                        reads=[("ps", ba), "cosT"], writes=[("t1", ri % 2)])
                    S.dve(lambda h_, t2_=t2_, B=B, tsl=tsl: h_.tensor_tensor(out=t2_[:], in0=B[:, :], in1=self.sinT[:, tsl], op=ALU.mult),
                          reads=[("ps", bb), "sinT"], writes=[("t2", ri % 2)])
                    if which == "q":
                        dst = qT[:, cc, tsl]
                        wr = [("q", cc, T, 0), ("q", cc, T, 1)]
                        S.pool(lambda h_, dst=dst, t1_=t1_, t2_=t2_: h_.tensor_tensor(out=dst, in0=t1_[:], in1=t2_[:], op=ALU.add),
                               reads=[("t1", ri % 2), ("t2", ri % 2)], writes=wr)
                    else:
                        tk_ = qs[ri % 2]
                        S.pool(lambda h_, tk_=tk_, t1_=t1_, t2_=t2_: h_.tensor_tensor(out=tk_[:], in0=t1_[:], in1=t2_[:], op=ALU.add),
                               reads=[("t1", ri % 2), ("t2", ri % 2), ("qs", ri % 2)], writes=[("qs", ri % 2)])
                        for hp in range(2):
                            prt = slice(hp * 64, (hp + 1) * 64)
                            S.act(lambda h_, prt=prt, hp=hp, cc=cc, tsl=tsl, tk_=tk_: h_.activation(
                                out=kT[prt, cc, hp, tsl], in_=tk_[prt, :], func=AF.Copy),
                                reads=[("qs", ri % 2)], writes=[("k", cc, T, hp)])

                for u_ in units:
                    st_ = rope_a(*u_)
                    if pend:
                        rope_b(*pend.pop(0))
                    pend.append(st_)
                while pend:
                    rope_b(*pend.pop(0))
                if sub == "qk":
                    break
                if sub == "v":
                    break
                for ch in range(4):
                    cc, hp = ch // 2, ch % 2
                    S.dve(lambda h_, cc=cc, hp=hp, ch=ch: h_.tensor_reduce(out=kms[:, ch, :], in_=kT[:, cc, hp, :].rearrange("p (n k) -> p n k", k=256),
                                                                       axis=AX.X, op=ALU.add),
                          reads=[("k", cc, T, hp) for T in range(NT)] + [("kz", hp)], writes=[("kms", ch)])
                    S.dve(lambda h_, ch=ch: h_.tensor_scalar(out=kmT[:, ch, :], in0=kms[:, ch, :], scalar1=1.0 / 256.0, scalar2=None, op0=ALU.mult),
                          reads=[("kms", ch)], writes=[("kmT", ch)])
                def gate_matmuls():
                    gbank = ps[6]
                    for qt in range(8, 16):
                        T = qt // 4
                        for hl in range(4):
                            cc, hp = hl // 2, hl % 2
                            col = (qt - 8) * 32 + hl * 8
                            self.mm(gbank[:, col:col + 8], qT[:, cc, qt * 128:(qt + 1) * 128],
                                    kmT[:, hl, :], True, True,
                                    [("q", cc, T, 0), ("q", cc, T, 1), ("kmT", hl)], [("ps", 6)])
                    S.act(lambda h_: h_.activation(out=gsb[:, :], in_=gbank[:, 0:256], func=AF.Copy),
                          reads=[("ps", 6)], writes=["gsb"])

                def gate_chain(qt):
                    qb = qt // 2
                    sI = qt % 4
                    c2 = qt % 2
                    g3 = gsb[:, (qt - 8) * 32:(qt - 7) * 32].rearrange("p (a n) -> p a n", n=8)[:, :, 0:qb]
                    in0 = g3.unsqueeze(2).broadcast_to([128, 4, qb, qb])
                    in1 = g3.unsqueeze(3).broadcast_to([128, 4, qb, qb])
                    cm = cmp_[:, c2, 0:4 * qb * qb].rearrange("p (a n m) -> p a n m", a=4, n=qb)
                    S.dve(lambda h_, cm=cm, in0=in0, in1=in1: h_.tensor_tensor(out=cm, in0=in0, in1=in1, op=ALU.is_gt),
                          reads=["gsb"], writes=[("cmp", c2)])
                    rk = rank[:, c2, :].rearrange("p (a n) -> p a n", n=8)[:, :, 0:qb]
                    S.dve(lambda h_, rk=rk, cm=cm: h_.tensor_reduce(out=rk, in_=cm, axis=AX.X, op=ALU.add),
                          reads=[("cmp", c2)], writes=[("rank", c2)])
                    S.pool(lambda h_, sI=sI: h_.memset(btok[:, sI, :], 0.0), writes=[("btok", sI)])
                    bo = btok[:, sI, :].rearrange("p (a b n) -> p a b n", a=2, b=2)[:, :, :, 0:qb]
                    rk4 = rank[:, c2, :].rearrange("p (a b n) -> p a b n", a=2, b=2)[:, :, :, 0:qb]
                    S.dve(lambda h_, bo=bo, rk4=rk4: h_.tensor_scalar(out=bo, in0=rk4, scalar1=2.5, scalar2=NEG, op0=ALU.is_gt, op1=ALU.mult),
                          reads=[("rank", c2)], writes=[("btok", sI)])

                def gate_transposes(q4):
                    for a in range(2):
                        bti = 2 + a
                        for qq in range(4):
                            qt = 8 + 4 * q4 + qq
                            sI = qt % 4
                            self.mm(ps[bti][:, qq * 128:(qq + 1) * 128], btok[:, sI, a * 128:(a + 1) * 128], self.identb[:], True, True,
                                    [("btok", sI), "identb"], [("ps", bti)])
                        S.act(lambda h_, a=a, q4=q4, bti=bti: h_.activation(out=biasT[:, a, q4 * TT:(q4 + 1) * TT], in_=ps[bti][:, :], func=AF.Copy),
                              reads=[("ps", bti)], writes=[("biasT", a, q4)])
                def v_proj(tp_lo, tp_hi):
                  for tp in range(tp_lo, tp_hi):
                      bi = 4 + tp % 2
                      bank = ps[bi]
                      for u in range(2):
                          tt = 2 * tp + u
                          T = tt // 4
                          for c in range(NCH):
                              self.mm(bank[:, u * 256:(u + 1) * 256], hn[:, c, tt * 128:(tt + 1) * 128], wv[:, c, :], c == 0, c == NCH - 1,
                                      [("wsl", sl["v"]), ("hn", c, T)], [("ps", bi)])
                      src = bank[:, :].rearrange("p (u a b e) -> p u a b e", u=2, a=2, b=2)
                      S.act(lambda h_, src=src, tp=tp: h_.activation(out=Vt[:, 2 * tp:2 * tp + 2, 0:4:2, 0:64], in_=src[:, :, :, 0, :], func=AF.Copy),
                            reads=[("ps", bi)], writes=[("Vt", 2 * tp, 0), ("Vt", 2 * tp + 1, 0)])
                      S.dve(lambda h_, src=src, tp=tp: h_.tensor_copy(out=Vt[:, 2 * tp:2 * tp + 2, 1:4:2, 64:128], in_=src[:, :, :, 1, :]),
                            reads=[("ps", bi)], writes=[("Vt", 2 * tp, 1), ("Vt", 2 * tp + 1, 1)])
                v_proj(0, 3)
                gate_matmuls()
                for qt in range(8, 12):
                    gate_chain(qt)
                v_proj(3, 6)
                gate_transposes(0)
                for qt in range(12, 16):
                    gate_chain(qt)
                v_proj(6, 8)
                gate_transposes(1)
                if g + 1 < 4 and sub is None:
                    pending = load_group(g + 1)
                if sub == "gate":
                    break
                iters = []
                itc = 0
                for cc in range(2):
                    for T in range(NT):
                        nk = 4 * T + 4
                        ob = [4 + 2 * (itc % 2), 5 + 2 * (itc % 2)]
                        itc += 1
                        for kt in range(nk):
                            for hp in range(2):
                                iters.append((cc, T, kt, hp, nk, ob[hp]))
                LAG = 3

                def stage_a(n_, cc, T, kt, hp, nk, obk):
                    nb = kt // 2
                    q_lo = max(0, kt - 4 * T) * 128
                    qsl = slice(q_lo, TT)
                    Tk = kt // 4
                    sbi = n_ % 4
                    sbk = ps[sbi]
                    need_bias = (T >= 2 and kt < 4 * T + 2)
                    need_mask = kt >= 4 * T
                    self.mm(sbk[:, qsl], kT[:, cc, hp, kt * 128:(kt + 1) * 128], qT[:, cc, T * TT + q_lo:(T + 1) * TT],
                            True, not (need_bias or need_mask),
                            [("k", cc, Tk, hp), ("kz", hp), ("q", cc, T, 0), ("q", cc, T, 1)], [("ps", sbi)])
                    if need_bias:
                        self.mm(sbk[:, qsl], self.ind[hp][:, nb * 128:(nb + 1) * 128],
                                biasT[:, cc, (T - 2) * TT + q_lo:(T - 1) * TT],
                                False, not need_mask, ["ind", ("biasT", cc, T - 2)], [("ps", sbi)])
                    if need_mask:
                        self.mm(sbk[:, q_lo:q_lo + 128], self.identb[:], self.maskT[:], False, True,
                                ["identb", "maskT"], [("ps", sbi)])

                def stage_bc(n_, cc, T, kt, hp, nk, obk):
                    hl = 2 * cc + hp
                    q_lo = max(0, kt - 4 * T) * 128
                    qsl = slice(q_lo, TT)
                    sbi = n_ % 4
                    sbk = ps[sbi]
                    pi = n_ % 4
                    S.act(lambda h_, pi=pi, sbk=sbk, qsl=qsl: h_.activation(out=pT[pi][:, qsl], in_=sbk[:, qsl], func=AF.Exp),
                          reads=[("ps", sbi)], writes=[("pT", pi)])
                    self.mm(ps[obk][:, qsl], Vt[:, kt, hl, :], pT[pi][:, qsl], kt == 0, kt == nk - 1,
                            [("pT", pi), ("Vt", kt, hp), ("Vt1", hp)], [("ps", obk)])
                    if kt == nk - 1:
                        o = ps[obk]
                        num = slice(hp * 64, (hp + 1) * 64)
                        den = slice((1 - hp) * 64, (2 - hp) * 64)
                        rc = rec[hp]
                        S.dve(lambda h_, rc=rc, o=o, den=den: h_.reciprocal(rc[den, :], o[den, :]),
                              reads=[("ps", obk)], writes=[("rec", hp)])
                        S.dve(lambda h_, rc=rc, o=o, den=den, num=num, cc=cc, T=T: h_.tensor_tensor(
                            out=qT[num, cc, T * TT:(T + 1) * TT], in0=o[num, :], in1=rc[den, :], op=ALU.mult),
                            reads=[("ps", obk), ("rec", hp)], writes=[("q", cc, T, hp)])

                NI = len(iters)
                for n_ in range(NI + LAG):
                    if n_ < NI:
                        stage_a(n_, *iters[n_])
                    m_ = n_ - LAG
                    if m_ >= 0:
                        stage_bc(m_, *iters[m_])
                if sub == "core":
                    break
                for T in range(NT):
                    tsl = slice(T * TT, (T + 1) * TT)
                    for dc in range(NCH):
                        bi = (T * NCH + dc) % 4
                        for cc in range(2):
                            self.mm(ps[bi][:, :], wo[:, cc, dc * 128:(dc + 1) * 128], qT[:, cc, tsl], cc == 0, cc == 1,
                                    [("wsl", sl["o"]), ("q", cc, T, 0), ("q", cc, T, 1)], [("ps", bi)])
                        S.dve(lambda h_, bi=bi, dc=dc, tsl=tsl: h_.tensor_tensor(out=h[:, dc, tsl], in0=ps[bi][:, :], in1=h[:, dc, tsl], op=ALU.add),
                              reads=[("ps", bi), ("h", dc, T)], writes=[("h", dc, T)])
            S.barrier()
            S.flush()

    def phase_conformer(self, j):
        nc, S, ps = self.nc, self.S, self.ps
        hn, h = self.hn, self.h
        PADL = 32
        with ExitStack() as es:
            glu = self.sb(es, [128, NCH, PADL + SEQ], BF16, "glu")
            ybf = [self.sb(es, [128, NCH, TT], BF16, "ybf") for _ in range(2)]
            ysq = self.sb(es, [128, NCH, TT], BF16, "ysq")
            dg = self.sb(es, [128, CW, 128], BF16, "dg")
            NW = 3
            wsl = [self.sb(es, [128, NCH, 256], BF16, "cw") for _ in range(NW)]
            tA = [self.sb(es, [128, TT], F32, "tA") for _ in range(2)]
            sgm = tA
            mean_t = self.sb(es, [128, TT], F32, "mean_t")
            m2_t = self.sb(es, [128, TT], F32, "m2_t")
            wcount = [0]

            def next_slot():
                i = wcount[0] % NW
                wcount[0] += 1
                return i

            S.pool(lambda h_: h_.memset(glu[:, :, 0:PADL], 0.0), writes=[("glupad",)])
            w1 = self.w_pw1[j].rearrange("(c p) f -> p c f", p=128)
            w2 = self.w_pw2[j].rearrange("(c p) f -> p c f", p=128)

            def load_pw1(cb):
                ia = next_slot()
                self.load_w(wsl[ia][:], w1[:, :, cb * 256:(cb + 1) * 256], ("cw", ia), "cw%d" % ia)
                ig = next_slot()
                self.load_w(wsl[ig][:], w1[:, :, D + cb * 256:D + (cb + 1) * 256], ("cw", ig), "cw%d" % ig)
                return ia, ig

            it = 0
            for cb in range(4):
                ia, ig = load_pw1(cb)
                for ci in range(2):
                    cc = 2 * cb + ci
                    for T in range(NT):
                        ba, bg = (it % 2) * 2, (it % 2) * 2 + 1
                        it += 1
                        tsl = slice(T * TT, (T + 1) * TT)
                        for c in range(NCH):
                            self.mm(ps[ba][:, :], wsl[ia][:, c, ci * 128:(ci + 1) * 128], hn[:, c, tsl], c == 0, c == NCH - 1,
                                    [("cw", ia), ("hn", c, T)], [("ps", ba)])
                        for c in range(NCH):
                            self.mm(ps[bg][:, :], wsl[ig][:, c, ci * 128:(ci + 1) * 128], hn[:, c, tsl], c == 0, c == NCH - 1,
                                    [("cw", ig), ("hn", c, T)], [("ps", bg)])
                        sg_ = sgm[it % 2]
                        S.act(lambda h_, sg_=sg_, bg=bg, cc=cc: h_.activation(out=sg_[:], in_=ps[bg][:, :], func=AF.Sigmoid,
                                                                               bias=self.prow(R_BPW1 + j * 16 + 8 + cc)),
                              reads=[("ps", bg), "ptab"], writes=[("tA", it % 2)])
                        S.dve(lambda h_, sg_=sg_, ba=ba, cc=cc, T=T: h_.scalar_tensor_tensor(
                            out=glu[:, cc, PADL + T * TT:PADL + (T + 1) * TT], in0=ps[ba][:, :], scalar=self.prow(R_BPW1 + j * 16 + cc),
                            in1=sg_[:], op0=ALU.add, op1=ALU.mult),
                            reads=[("ps", ba), ("tA", it % 2), "ptab"], writes=[("glu", cc, T)])
            pw2_slots = {}

            def load_pw2(db):
                i = next_slot()
                self.load_w(wsl[i][:], w2[:, :, db * 256:(db + 1) * 256], ("cw", i), "cw%d" % i)
                pw2_slots[db] = i

            load_pw2(0)
            load_pw2(1)
            dcount = [0]

            def conv_unit(T, cc):
                yb = cc % 2
                yT = ybf[T % 2]
                for tap in range(CW):
                    if dcount[0] % 2 == 0:
                        S.dve(lambda h_, tap=tap, cc=cc: h_.tensor_scalar(out=dg[:, tap, :], in0=self.identb[:],
                                                                          scalar1=self.prow(R_WDW + (j * CW + tap) * 8 + cc), scalar2=None, op0=ALU.mult),
                              reads=["identb", "ptab"], writes=[("dg", tap)])
                    else:
                        S.pool(lambda h_, tap=tap, cc=cc: h_.tensor_scalar(out=dg[:, tap, :], in0=self.identb[:],
                                                                           scalar1=self.prow(R_WDW + (j * CW + tap) * 8 + cc), scalar2=1.0,
                                                                           op0=ALU.mult, op1=ALU.mult),
                               reads=["identb", "ptab"], writes=[("dg", tap)])
                    dcount[0] += 1
                for tap in range(CW):
                    o0 = PADL + T * TT - (CW - 1) + tap
                    rd = [("dg", tap), ("glu", cc, T)]
                    rd.append(("glu", cc, T - 1) if T > 0 else ("glupad",))
                    self.mm(ps[yb][:, :], dg[:, tap, :], glu[:, cc, o0:o0 + TT], tap == 0, tap == CW - 1, rd, [("ps", yb)])
                S.act(lambda h_: h_.activation(out=yT[:, cc, :], in_=ps[yb][:, :], func=AF.Identity,
                                               bias=self.prow(R_BDW + j * 8 + cc)),
                      reads=[("ps", yb), "ptab"], writes=[("ybf", T % 2, cc)])
                S.act(lambda h_: h_.activation(out=ysq[:, cc, :], in_=yT[:, cc, :], func=AF.Square),
                      reads=[("ybf", T % 2, cc)], writes=[("ysq", cc)])

            def ln_head(T):
                yT = ybf[T % 2]
                bm, bq = 2 + (T % 2) * 2, 3 + (T % 2) * 2
                for cc in range(NCH):
                    self.mm(ps[bm][:, :], self.onesb[:], yT[:, cc, :], cc == 0, cc == NCH - 1, ["onesb", ("ybf", T % 2, cc)], [("ps", bm)])
                for cc in range(NCH):
                    self.mm(ps[bq][:, :], self.onesb[:], ysq[:, cc, :], cc == 0, cc == NCH - 1, ["onesb", ("ysq", cc)], [("ps", bq)])
                S.act(lambda h_: h_.activation(out=mean_t[:], in_=ps[bm][:, :], func=AF.Copy, scale=1.0 / D),
                      reads=[("ps", bm)], writes=["mean_t"])
                S.dve(lambda h_: h_.tensor_tensor(out=m2_t[:], in0=mean_t[:], in1=mean_t[:], op=ALU.mult), reads=["mean_t"], writes=["m2_t"])
                S.dve(lambda h_: h_.scalar_tensor_tensor(out=m2_t[:], in0=ps[bq][:, :], scalar=1.0 / D, in1=m2_t[:],
                                                         op0=ALU.mult, op1=ALU.subtract),
                      reads=[("ps", bq), "m2_t"], writes=["m2_t"])
                S.act(lambda h_: h_.activation(out=m2_t[:], in_=m2_t[:], func=AF.Sqrt, bias=LN_EPS), reads=["m2_t"], writes=["m2_t"])
                S.dve(lambda h_: h_.reciprocal(ps[bm][:, :], m2_t[:]), reads=["m2_t"], writes=[("ps", bm)])
                S.dve(lambda h_: h_.tensor_tensor(out=ps[bq][:, :], in0=ps[bm][:, :], in1=mean_t[:], op=ALU.mult),
                      reads=[("ps", bm), "mean_t"], writes=[("ps", bq)])

            def ln_norm(T, cc):
                yT = ybf[T % 2]
                bm, bq = 2 + (T % 2) * 2, 3 + (T % 2) * 2
                ta_ = tA[cc % 2]
                S.dve(lambda h_: h_.tensor_tensor(out=ta_[:], in0=ps[bm][:, :], in1=yT[:, cc, :], op=ALU.mult),
                      reads=[("ps", bm), ("ybf", T % 2, cc)], writes=[("tA", cc % 2)])
                S.dve(lambda h_: h_.tensor_tensor(out=ta_[:], in0=ta_[:], in1=ps[bq][:, :], op=ALU.subtract),
                      reads=[("ps", bq), ("tA", cc % 2)], writes=[("tA", cc % 2)])
                S.act(lambda h_: h_.activation(out=hn[:, cc, T * TT:(T + 1) * TT], in_=ta_[:], func=AF.Silu,
                                               scale=self.prow(R_LNG + j * 8 + cc), bias=self.prow(R_LNB + j * 8 + cc)),
                      reads=[("tA", cc % 2), "ptab"], writes=[("hn", cc, T)])

            for T in range(NT):
                for cc in range(NCH):
                    conv_unit(T, cc)
                    if T > 0:
                        ln_norm(T - 1, cc)
                ln_head(T)
            for cc in range(NCH):
                ln_norm(NT - 1, cc)
            it = 0
            for db in range(4):
                if db + 2 < 4:
                    load_pw2(db + 2)
                i = pw2_slots[db]
                for T in range(NT):
                    for di in range(2):
                        dc = 2 * db + di
                        bi = 6 + it % 2
                        it += 1
                        tsl = slice(T * TT, (T + 1) * TT)
                        for cc in range(NCH):
                            self.mm(ps[bi][:, :], wsl[i][:, cc, di * 128:(di + 1) * 128], hn[:, cc, tsl], cc == 0, cc == NCH - 1,
                                    [("cw", i), ("hn", cc, T)], [("ps", bi)])
                        S.dve(lambda h_, bi=bi, dc=dc, tsl=tsl: h_.scalar_tensor_tensor(
                            out=h[:, dc, tsl], in0=ps[bi][:, :], scalar=self.prow(R_BPW2 + j * 8 + dc), in1=h[:, dc, tsl], op0=ALU.add, op1=ALU.add),
                            reads=[("ps", bi), ("h", dc, T), "ptab"], writes=[("h", dc, T)])
            S.barrier()
            S.flush()

    def phase_ffn(self, i):
        nc, S, ps = self.nc, self.S, self.ps
        hn, h = self.hn, self.h
        groups = [[0, 1, 2], [3, 4, 5], [6, 7, 8], [9, 10]]
        with ExitStack() as es:
            act = self.sb(es, [128, 6, SEQ], BF16, "act")
            NWU = 3
            wup = [self.sb(es, [128, NCH, 512], BF16, "wup") for _ in range(NWU)]
            wdn = [self.sb(es, [128, 6, D], BF16, "wdn") for _ in range(2)]
            Ag = [self.sb(es, [128, TT], F32, "Ag") for _ in range(2)]
            Av = [self.sb(es, [128, TT], F32, "Av") for _ in range(2)]
            sg = [self.sb(es, [128, TT], F32, "sg") for _ in range(2)]
            halo = self.sb(es, [128, 2, 2, 2], F32, "halo")
            wu_src = self.w_up[i].rearrange("(c p) f -> p c f", p=128)
            wd_src = self.w_down[i].rearrange("(c p) f -> p c f", p=128)
            ucount = [0]
            blocks = [b for g in groups for b in g]
            up_slot = {}

            def load_up(b):
                s = ucount[0] % NWU
                ucount[0] += 1
                up_slot[b] = s
                self.load_w(wup[s][:, :, 0:256], wu_src[:, :, b * 256:(b + 1) * 256], ("wup", s), "wu%d" % s)
                self.load_w(wup[s][:, :, 256:512], wu_src[:, :, DFF + b * 256:DFF + (b + 1) * 256], ("wup", s), "wu%d" % s)

            def load_dn(gi):
                import os
                if os.environ.get("FFN_NODN"):
                    return
                g = groups[gi]
                np_ = 2 * len(g)
                j0 = 2 * g[0]
                self.load_w(wdn[gi % 2][:, 0:np_, :], wd_src[:, j0:j0 + np_, :], ("wdn", gi % 2), "wd%d" % (gi % 2))

            pend3 = []
            load_up(blocks[0])
            load_up(blocks[1])
            load_dn(0)
            nxt = 2
            it = 0
            for gi, g in enumerate(groups):
                for bl, b in enumerate(g):
                    s = up_slot[b]
                    for pi in range(2):
                        jp = 2 * b + pi
                        jl = 2 * bl + pi
                        rows = {}
                        for kind, fc in (("g", jp), ("v", NFP + jp)):
                            rows[kind] = [R_FWDW + (i * 3 + tap) * 44 + fc for tap in range(3)] + [R_FBDW + i * 44 + fc]
                        for T in range(NT):
                            par = it % 2
                            it += 1
                            tsl = slice(T * TT, (T + 1) * TT)
                            s3 = (it - 1) % 3
                            bg_, bv_ = s3 * 2, s3 * 2 + 1
                            import os
                            for c in range(NCH if not os.environ.get("FFN_NOMM") else 0):
                                self.mm(ps[bg_][:, :], wup[s][:, c, pi * 128:(pi + 1) * 128], hn[:, c, tsl], c == 0, c == NCH - 1,
                                        [("wup", s), ("hn", c, T)], [("ps", bg_)])
                            for c in range(NCH if not os.environ.get("FFN_NOMM") else 0):
                                self.mm(ps[bv_][:, :], wup[s][:, c, 256 + pi * 128:256 + (pi + 1) * 128], hn[:, c, tsl], c == 0, c == NCH - 1,
                                        [("wup", s), ("hn", c, T)], [("ps", bv_)])
                            A = {"g": Ag[par], "v": Av[par]}
                            U = {"g": ps[bg_], "v": ps[bv_]}
                            UB = {"g": bg_, "v": bv_}
                            AK = {"g": ("Ag", par), "v": ("Av", par)}
                            KI = {"g": 0, "v": 1}
                            hp_prev = (T - 1) % 2
                            hp_cur = T % 2
                            import os
                            FL = int(os.environ.get("FFN_LEVEL", "9"))
                            for kind in ("g", "v"):
                                if FL < 2:
                                    break
                                r = rows[kind]
                                S.act(lambda h_, A_=A[kind], U_=U[kind], r=r: h_.activation(out=A_[:], in_=U_[:, :], func=AF.Identity,
                                                                                          scale=self.prow(r[2]), bias=self.prow(r[3])),
                                      reads=[("ps", UB[kind]), "ptab"], writes=[AK[kind]])
                                if T < NT - 1 and FL >= 3:
                                    S.act(lambda h_, U_=U[kind], kind=kind, hp_cur=hp_cur: h_.activation(
                                        out=halo[:, hp_cur, KI[kind], :], in_=U_[:, TT - 2:TT], func=AF.Copy),
                                        reads=[("ps", UB[kind])], writes=[("halo", hp_cur, kind)])
                            for kind in ("g", "v"):
                                if FL < 4:
                                    break
                                r = rows[kind]
                                S.dve(lambda h_, A_=A[kind], U_=U[kind], r=r: h_.scalar_tensor_tensor(
                                    out=A_[:, 1:TT], in0=U_[:, 0:TT - 1], scalar=self.prow(r[1]), in1=A_[:, 1:TT], op0=ALU.mult, op1=ALU.add),
                                    reads=[("ps", UB[kind]), AK[kind], "ptab"], writes=[AK[kind]])
                            for kind in ("g", "v"):
                                if FL < 4:
                                    break
                                r = rows[kind]
                                S.dve(lambda h_, A_=A[kind], U_=U[kind], r=r: h_.scalar_tensor_tensor(
                                    out=A_[:, 2:TT], in0=U_[:, 0:TT - 2], scalar=self.prow(r[0]), in1=A_[:, 2:TT], op0=ALU.mult, op1=ALU.add),
                                    reads=[("ps", UB[kind]), AK[kind], "ptab"], writes=[AK[kind]])
                            if T > 0 and FL >= 5:
                                for kind in ("g", "v"):
                                    r = rows[kind]
                                    hl_ = halo[:, hp_prev, KI[kind], :]
                                    S.dve(lambda h_, A_=A[kind], hl_=hl_, r=r: h_.scalar_tensor_tensor(
                                        out=A_[:, 0:1], in0=hl_[:, 1:2], scalar=self.prow(r[1]), in1=A_[:, 0:1], op0=ALU.mult, op1=ALU.add),
                                        reads=[("halo", hp_prev, kind), AK[kind], "ptab"], writes=[AK[kind]])
                                for kind in ("g", "v"):
                                    r = rows[kind]
                                    hl_ = halo[:, hp_prev, KI[kind], :]
                                    S.dve(lambda h_, A_=A[kind], hl_=hl_, r=r: h_.scalar_tensor_tensor(
                                        out=A_[:, 0:2], in0=hl_[:, 0:2], scalar=self.prow(r[0]), in1=A_[:, 0:2], op0=ALU.mult, op1=ALU.add),
                                        reads=[("halo", hp_prev, kind), AK[kind], "ptab"], writes=[AK[kind]])
                            if FL < 6:
                                continue
                            def stage3(par=par, Ag_=A["g"], Av_=A["v"], jl=jl, tsl=tsl, T=T):
                                sg_ = sg[par]
                                S.act(lambda h_: h_.activation(out=sg_[:], in_=Ag_[:], func=AF.Silu),
                                      reads=[("Ag", par)], writes=[("sg", par)])
                                S.pool(lambda h_: h_.tensor_tensor(out=act[:, jl, tsl], in0=sg_[:], in1=Av_[:], op=ALU.mult),
                                       reads=[("sg", par), ("Av", par)], writes=[("act", jl, T)])
                            if pend3:
                                pend3.pop(0)()
                            pend3.append(stage3)
                            if pi == 0 and T == 2:
                                if nxt < len(blocks):
                                    load_up(blocks[nxt])
                                    nxt += 1
                                if bl == 0 and gi + 1 < len(groups):
                                    load_dn(gi + 1)
                while pend3:
                    pend3.pop(0)()
                np_ = 2 * len(g)
                wd = wdn[gi % 2]
                dn = 0
                for T in range(NT if FL >= 7 else 0):
                    tsl = slice(T * TT, (T + 1) * TT)
                    for dc in range(NCH):
                        bi = 6 + dn % 2
                        dn += 1
                        for jl in range(np_):
                            self.mm(ps[bi][:, :], wd[:, jl, dc * 128:(dc + 1) * 128], act[:, jl, tsl], jl == 0, jl == np_ - 1,
                                    [("wdn", gi % 2), ("act", jl, T)], [("ps", bi)])
                        S.dve(lambda h_, bi=bi, dc=dc, tsl=tsl: h_.tensor_tensor(out=h[:, dc, tsl], in0=ps[bi][:, :], in1=h[:, dc, tsl], op=ALU.add),
                              reads=[("ps", bi), ("h", dc, T)], writes=[("h", dc, T)])
            S.barrier()
            S.flush()

    def phase_final(self, raw=False):
        nc, S, ps = self.nc, self.S, self.ps
        with ExitStack() as es:
            ofm = self.sb(es, [128, NCH, TT], F32, "ofm")
            ost = [self.sb(es, [128, D], F32, "ost") for _ in range(3)]
            oc = 0
            for T in range(NT):
                tsl = slice(T * TT, (T + 1) * TT)
                if raw:
                    for c in range(NCH):
                        eng = S.act if c % 2 == 0 else S.dve
                        if c % 2 == 0:
                            S.act(lambda h_, c=c, tsl=tsl: h_.activation(out=ofm[:, c, :], in_=self.h[:, c, tsl], func=AF.Copy),
                                  reads=[("h", c, T)], writes=[("ofm", c)])
                        else:
                            S.dve(lambda h_, c=c, tsl=tsl: h_.tensor_copy(out=ofm[:, c, :], in_=self.h[:, c, tsl]),
                                  reads=[("h", c, T)], writes=[("ofm", c)])
                else:
                    self.rmsnorm_tile(T, R_FIN, ofm)
                for ts in range(4):
                    tt = T * 4 + ts
                    sl = oc % 3
                    oc += 1
                    for half in range(2):
                        bi = (2 * tt + half) % 4
                        bank = ps[bi]
                        for jj in range(4):
                            c = half * 4 + jj
                            S.pe(lambda h_, bank=bank, jj=jj, c=c, ts=ts: h_.transpose(out=bank[:, jj * 128:(jj + 1) * 128],
                                                                                         in_=ofm[:, c, ts * 128:(ts + 1) * 128], identity=self.identf[:]),
                                 reads=[("ofm", c), "identf"], writes=[("ps", bi)])
                        dst = ost[sl][:, half * 512:(half + 1) * 512]
                        if half == 0:
                            S.act(lambda h_, dst=dst, bank=bank: h_.activation(out=dst, in_=bank[:, :], func=AF.Copy),
                                  reads=[("ps", bi)], writes=[("ost", sl, half)])
                        else:
                            S.dve(lambda h_, dst=dst, bank=bank: h_.tensor_copy(out=dst, in_=bank[:, :]),
                                  reads=[("ps", bi)], writes=[("ost", sl, half)])
                    S.dma("sp", "ost%d" % sl, lambda h_, sl=sl, tt=tt: h_.dma_start(out=self.out[tt * 128:(tt + 1) * 128, :], in_=ost[sl][:]),
                          reads=[("ost", sl, 0), ("ost", sl, 1)])
            S.barrier()
            S.flush()

    def rmsnorm_tile(self, T, grow, ofm):
        S, ps = self.S, self.ps
        bi = 6 + T % 2
        bank = ps[bi]
        tsl = slice(T * TT, (T + 1) * TT)
        for c in range(NCH):
            sq = self.nsq[:, c % 2, :]
            S.act(lambda h, sq=sq, c=c: h.activation(out=sq, in_=self.h[:, c, tsl], func=AF.Square),
                  reads=[("h", c, T)], writes=[("nsq", c % 2)])
            self.mm(bank[:, :], self.onesb[:], sq, c == 0, c == NCH - 1, [("nsq", c % 2), "onesb"], [("ps", bi)])
        sd = self.nstd[:, T % 2, :]
        S.act(lambda h: h.activation(out=sd, in_=bank[:, :], func=AF.Sqrt, scale=1.0 / D, bias=NORM_EPS),
              reads=[("ps", bi)], writes=[("nstd", T % 2)])
        S.dve(lambda h: h.reciprocal(bank[:, :], sd), reads=[("nstd", T % 2)], writes=[("ps", bi)])
        for c in range(NCH):
            S.dve(lambda h, c=c: h.scalar_tensor_tensor(out=ofm[:, c, :], in0=self.h[:, c, tsl], scalar=self.prow(grow + c), in1=bank[:, :],
                                                        op0=ALU.mult, op1=ALU.mult),
                  reads=[("h", c, T), ("ps", bi), "ptab"], writes=[("ofm", c)])


def _pack_ptab(inp):
    f = lambda a: np.ascontiguousarray(np.asarray(a, dtype=np.float32)).reshape(-1, 128)
    parts = [
        f(inp["norm_mix_g"]), f(inp["norm_ffn_g"]), f(inp["final_norm_g"]), f(inp["conv_b_pw1"]),
        f(inp["conv_w_dw"]), f(inp["conv_b_dw"]), f(inp["conv_ln_g"]), f(inp["conv_ln_b"]), f(inp["conv_b_pw2"]),
        f(inp["ffn_w_dw"]), f(inp["ffn_b_dw"]),
    ]
    tab = np.concatenate(parts, axis=0)
    assert tab.shape[0] == 1368
    pad = np.zeros((R_TOT - tab.shape[0], 128), np.float32)
    return np.ascontiguousarray(np.concatenate([tab, pad], axis=0))


_NC_CACHE = {}


def _run(inputs, stop_after=None, trace=False):
    x = np.ascontiguousarray(np.asarray(inputs["x"], dtype=np.float32))
    B = x.shape[0]
    key = stop_after
    if key not in _NC_CACHE:
        _NC_CACHE[key] = Builder(stop_after).build()
    nc = _NC_CACHE[key]
    ptab = _pack_ptab(inputs)
    c = lambda k: np.ascontiguousarray(np.asarray(inputs[k], dtype=np.float32))
    shared = {
        "ptab_in": ptab, "w_qkv": c("attn_w_qkv"), "w_o": c("attn_w_o"), "w_pw1": c("conv_w_pw1"), "w_pw2": c("conv_w_pw2"),
        "w_up": c("ffn_w_up"), "w_down": c("ffn_w_down"),
    }
    in_maps = [dict(shared, x=x[b]) for b in range(B)]
    res = run_bass_kernel_spmd(nc, in_maps, core_ids=list(range(B)), trace=trace)
    out = np.stack([np.asarray(r["out"]) for r in res.results], axis=0).astype(np.float32)
    return out, res


def kernel(**inputs):
    out, _ = _run(inputs)
    return out
```

```python
import math
import numpy as np
from contextlib import ExitStack
import concourse.bass as bass
import concourse.mybir as mybir
from concourse.bass_utils import run_bass_kernel_spmd

F32 = mybir.dt.float32
BF16 = mybir.dt.bfloat16
I32 = mybir.dt.int32
ALU = mybir.AluOpType
AF = mybir.ActivationFunctionType
AX = mybir.AxisListType

D = 1024
SEQ = 2048
NCH = 8
NT = 4
TT = 512
H = 16
DH = 64
DFF = 2816
NFP = 22
DEPTH = 4
NEG = -30000.0
NORM_EPS = 1e-6
LN_EPS = 1e-5
CW = 31

R_MIX = 0
R_FFN = 32
R_FIN = 64
R_BPW1 = 72
R_WDW = 104
R_BDW = 600
R_LNG = 616
R_LNB = 632
R_BPW2 = 648
R_FWDW = 664
R_FBDW = 1192
R_TOT = 1408


class _Op:
    __slots__ = ("eng", "fn", "deps", "needs_inc", "semval", "grp", "gen", "pre")

    def __init__(self, eng, fn):
        self.eng = eng
        self.fn = fn
        self.deps = []
        self.needs_inc = False
        self.semval = None
        self.grp = None
        self.gen = 0
        self.pre = None


class _Grp:
    __slots__ = ("sem", "gens", "closed")

    def __init__(self, sem):
        self.sem = sem
        self.gens = [0]
        self.closed = False


class Sched:
    ENGS = ("pe", "act", "dve", "pool", "sp")

    def __init__(self, nc, es):
        self.nc = nc
        self.es = es
        self.q = {e: [] for e in self.ENGS}
        self.lastw = {}
        self.readers = {}
        self.esem = {e: es.enter_context(nc.semaphore("sem_" + e)) for e in ("pe", "act", "dve", "pool")}
        self.ecnt = {e: 0 for e in ("pe", "act", "dve", "pool")}
        self.seen = {e: {} for e in self.ENGS}
        self.groups = {}
        self.lastreal = {e: None for e in self.ENGS}
        self.nops = 0

    def _group(self, name):
        g = self.groups.get(name)
        if g is None:
            g = _Grp(self.es.enter_context(self.nc.semaphore("dg_" + name)))
            self.groups[name] = g
        return g

    def add(self, eng, fn, reads=(), writes=(), dma=None):
        op = _Op(eng, fn)
        deps = {}
        for k in reads:
            w = self.lastw.get(k)
            if w is not None:
                deps[id(w)] = w
            if isinstance(k, tuple) and k[0] == "ps":
                rd = self.readers.get(k)
                if rd:
                    for rk_, r in rd.items():
                        if rk_ != eng:
                            deps[id(r)] = r
        for k in writes:
            w = self.lastw.get(k)
            if w is not None:
                deps[id(w)] = w
            rd = self.readers.get(k)
            if rd:
                for r in rd.values():
                    deps[id(r)] = r
        if dma is not None:
            g = self._group(dma)
            op.grp = g
            if g.closed:
                op.pre = [(g, len(g.gens) - 1)]
                g.gens.append(g.gens[-1])
                g.closed = False
            g.gens[-1] += 16
            op.gen = len(g.gens) - 1
        for d in deps.values():
            if eng == "pe" and d.eng == "pe" and d.grp is None:
                continue
            if op.grp is not None and d.grp is op.grp:
                continue
            op.deps.append(d)
            d.needs_inc = True
            if d.grp is not None:
                d.grp.closed = True
        rk = eng if dma is None else ("dma", id(op))
        for k in reads:
            self.readers.setdefault(k, {})[rk] = op
        for k in writes:
            self.lastw[k] = op
            self.readers[k] = {}
        self.q[eng].append(op)
        if dma is None:
            self.lastreal[eng] = op
        self.nops += 1
        return op

    def pe(self, fn, reads=(), writes=()):
        return self.add("pe", fn, reads, writes)

    def act(self, fn, reads=(), writes=()):
        return self.add("act", fn, reads, writes)

    def dve(self, fn, reads=(), writes=()):
        return self.add("dve", fn, reads, writes)

    def pool(self, fn, reads=(), writes=()):
        return self.add("pool", fn, reads, writes)

    def dma(self, queue, group, fn, reads=(), writes=()):
        return self.add(queue, fn, reads, writes, dma=group)

    def barrier(self):
        lasts = [op for op in self.lastreal.values() if op is not None and op.grp is None]
        gl = [(g, len(g.gens) - 1) for g in self.groups.values() if g.gens[-1] > 0]
        for g, _ in gl:
            g.closed = True
        for e in self.ENGS:
            op = _Op(e, None)
            for d in lasts:
                if not (e == "pe" and d.eng == "pe"):
                    op.deps.append(d)
                    d.needs_inc = True
            op.pre = list(gl)
            self.q[e].append(op)
        self.lastw = {}
        self.readers = {}

    def _assign(self):
        for e in ("pe", "act", "dve", "pool"):
            c = self.ecnt[e]
            for op in self.q[e]:
                if op.grp is None and op.fn is not None and op.needs_inc and op.semval is None:
                    c += 1
                    op.semval = c
            self.ecnt[e] = c

    def _emit_engine(self, e, h):
        seen = self.seen[e]

        def wait(sem, val):
            key = id(sem)
            if seen.get(key, 0) < val:
                h.wait_ge(sem, val)
                seen[key] = val

        for op in self.q[e]:
            if op.pre is not None:
                for g, gi in op.pre:
                    wait(g.sem, g.gens[gi])
            for d in op.deps:
                if d.grp is not None:
                    wait(d.grp.sem, d.grp.gens[d.gen])
                else:
                    wait(self.esem[d.eng], d.semval)
            if op.fn is not None:
                ins = op.fn(h)
                if op.grp is not None:
                    ins.then_inc(op.grp.sem, 16)
                elif op.needs_inc:
                    ins.then_inc(self.esem[e], 1)
        self.q[e] = []
        self.lastreal[e] = None

    def flush(self):
        self._assign()
        with self.nc.Block() as block:
            @block.tensor
            def _(h):
                self._emit_engine("pe", h)

            @block.scalar
            def _(h):
                self._emit_engine("act", h)

            @block.vector
            def _(h):
                self._emit_engine("dve", h)

            @block.gpsimd
            def _(h):
                self._emit_engine("pool", h)

            @block.sync
            def _(h):
                self._emit_engine("sp", h)


class Builder:
    def __init__(self, stop_after=None):
        self.stop_after = stop_after
        self.nc = bass.Bass("TRN2", target_bir_lowering=False)
        nc = self.nc
        dt = nc.dram_tensor
        self.x = dt("x", [SEQ, D], F32, kind="ExternalInput").ap()
        self.ptab_in = dt("ptab_in", [R_TOT, 128], F32, kind="ExternalInput").ap()
        self.w_qkv = dt("w_qkv", [2, D, 3 * D], F32, kind="ExternalInput").ap()
        self.w_o = dt("w_o", [2, D, D], F32, kind="ExternalInput").ap()
        self.w_pw1 = dt("w_pw1", [2, D, 2 * D], F32, kind="ExternalInput").ap()
        self.w_pw2 = dt("w_pw2", [2, D, D], F32, kind="ExternalInput").ap()
        self.w_up = dt("w_up", [DEPTH, D, 2 * DFF], F32, kind="ExternalInput").ap()
        self.w_down = dt("w_down", [DEPTH, DFF, D], F32, kind="ExternalInput").ap()
        self.out = dt("out", [SEQ, D], F32, kind="ExternalOutput").ap()
        self._uid = 0

    def sb(self, es, shape, dtype, name=None):
        self._uid += 1
        return es.enter_context(self.nc.sbuf_tensor("%s_%d" % (name or "t", self._uid), shape, dtype))

    def mm(self, out, lhsT, rhs, start, stop, reads, writes):
        self.S.pe(lambda h: h.matmul(out, lhsT=lhsT, rhs=rhs, start=start, stop=stop), reads, writes)

    def prow(self, r):
        return self.ptab[:, r:r + 1]

    def build(self):
        nc = self.nc
        with ExitStack() as es:
            self.S = S = Sched(nc, es)
            self.ps = [es.enter_context(nc.psum_tensor("ps%d" % i, [128, TT], F32)) for i in range(8)]
            self.h = self.sb(es, [128, NCH, SEQ], F32, "h")
            self.hn = self.sb(es, [128, NCH, SEQ], BF16, "hn")
            self.cosT = self.sb(es, [128, SEQ], BF16, "cosT")
            self.sinT = self.sb(es, [128, SEQ], BF16, "sinT")
            self.ptab = self.sb(es, [128, R_TOT], F32, "ptab")
            self.identf = self.sb(es, [128, 128], F32, "identf")
            self.identb = self.sb(es, [128, 128], BF16, "identb")
            self.onesb = self.sb(es, [128, 128], BF16, "onesb")
            self.maskT = self.sb(es, [128, 128], BF16, "maskT")
            self.prot = self.sb(es, [128, 128], BF16, "prot")
            self.ind = [self.sb(es, [128, 8 * 128], BF16, "ind") for _ in range(2)]
            self.nsq = self.sb(es, [128, 2, TT], BF16, "nsq")
            self.nstd = self.sb(es, [128, 2, TT], F32, "nstd")

            self.phase_setup()
            stop = self.stop_after
            done = stop == "load"
            for i in range(DEPTH):
                if done:
                    break
                j = i // 2
                self.rmsnorm(R_MIX + i * 8)
                if i % 2 == 0:
                    self.phase_attention(j)
                else:
                    self.phase_conformer(j)
                if stop is not None and stop.split(":")[0] == "mix%d" % i:
                    done = True
                    break
                self.rmsnorm(R_FFN + i * 8)
                fsub = stop.split(":")[1] if (stop and ":" in stop and stop.startswith("ffn")) else None
                if fsub != "norm":
                    self.phase_ffn(i)
                if stop is not None and stop.split(":")[0] == "ffn%d" % i:
                    done = True
                    break
            self.phase_final(raw=(stop is not None))
        return nc

    def phase_setup(self):
        nc, S = self.nc, self.S
        ps = self.ps
        with ExitStack() as es:
            xs = [self.sb(es, [128, D], F32, "xs") for _ in range(3)]
            pst = [self.sb(es, [128, 128], F32, "pst") for _ in range(2)]
            pi_i = self.sb(es, [128, 1], I32, "pi_i")
            pm_i = self.sb(es, [128, 2], I32, "pm_i")
            sgn = self.sb(es, [128, 2], F32, "sgn")
            invrow = self.sb(es, [1, 128], F32, "invrow")
            invf = self.sb(es, [128, 1], F32, "invf")
            pos_i = self.sb(es, [128, SEQ], I32, "pos_i")
            ang = self.sb(es, [128, SEQ], F32, "ang")
            ta = self.sb(es, [128, SEQ], F32, "ta")
            tb = self.sb(es, [128, SEQ], F32, "tb")
            tc = self.sb(es, [128, SEQ], F32, "tc")

            identf, identb, onesb, maskT, prot, ind = self.identf, self.identb, self.onesb, self.maskT, self.prot, self.ind
            S.pool(lambda h: h.memset(identf[:], 1.0), writes=["identf"])
            S.pool(lambda h: h.affine_select(out=identf[:], in_=identf[:], pattern=[[-1, 128]], compare_op=ALU.is_equal,
                                             fill=0.0, base=0, channel_multiplier=1), reads=["identf"], writes=["identf"])
            S.dve(lambda h: h.tensor_copy(out=identb[:], in_=identf[:]), reads=["identf"], writes=["identb"])
            S.pool(lambda h: h.memset(onesb[:], 1.0), writes=["onesb"])
            S.pool(lambda h: h.memset(maskT[:], 0.0), writes=["maskT"])
            S.pool(lambda h: h.affine_select(out=maskT[:], in_=maskT[:], pattern=[[1, 128]], compare_op=ALU.is_ge,
                                             fill=NEG, base=0, channel_multiplier=-1), reads=["maskT"], writes=["maskT"])
            for (dst, src) in ((0, 32), (32, 0), (64, 96), (96, 64)):
                S.dve(lambda h, dst=dst, src=src: h.tensor_copy(out=prot[:, dst:dst + 32], in_=identb[:, src:src + 32]),
                      reads=["identb"], writes=["prot"])
            for jj in range(2):
                S.pool(lambda h, jj=jj: h.memset(ind[jj][:], 0.0), writes=["ind"])
                S.pool(lambda h, jj=jj: h.memset(ind[jj][64 * jj:64 * jj + 64, :], 1.0), reads=["ind"], writes=["ind"])
                S.pool(lambda h, jj=jj: h.affine_select(out=ind[jj][64 * jj:64 * jj + 64, :], in_=ind[jj][64 * jj:64 * jj + 64, :],
                                                        pattern=[[1, 8], [0, 128]], compare_op=ALU.is_equal, fill=0.0,
                                                        base=0, channel_multiplier=-1), reads=["ind"], writes=["ind"])
            for r in range(R_TOT // 128):
                st = pst[r % 2]
                S.dma("sp", "pst%d" % (r % 2), lambda h, r=r, st=st: h.dma_start(out=st[:], in_=self.ptab_in[r * 128:(r + 1) * 128, :]),
                      writes=[("pst", r % 2)])
                bank = ps[r % 2]
                S.pe(lambda h, st=st, bank=bank: h.transpose(out=bank[:, 0:128], in_=st[:], identity=identf[:]),
                     reads=[("pst", r % 2), "identf"], writes=[("ps", r % 2)])
                S.act(lambda h, r=r, bank=bank: h.activation(out=self.ptab[:, r * 128:(r + 1) * 128], in_=bank[:, 0:128], func=AF.Copy),
                      reads=[("ps", r % 2)], writes=["ptab"])
            inv = (np.float32(1.0) / np.power(np.float32(10000.0), np.arange(0, DH, 2, dtype=np.float32) / np.float32(DH))).astype(np.float32)
            irv = invrow[:].rearrange("o (a b) -> o a b", b=32)
            for i in range(32):
                S.dve(lambda h, i=i: h.memset(irv[:, :, i:i + 1], float(inv[i])), writes=["invrow"])
            S.pe(lambda h: h.transpose(out=ps[2][:, 0:1], in_=invrow[:], identity=identf[0:1, 0:1]),
                 reads=["invrow", "identf"], writes=[("ps", 2)])
            S.act(lambda h: h.activation(out=invf[:], in_=ps[2][:, 0:1], func=AF.Copy), reads=[("ps", 2)], writes=["invf"])
            S.pool(lambda h: h.iota(pi_i[:], pattern=[[0, 1]], base=0, channel_multiplier=1), writes=["pi_i"])
            S.dve(lambda h: h.tensor_scalar(out=pm_i[:, 0:1], in0=pi_i[:], scalar1=32, scalar2=None, op0=ALU.bitwise_and),
                  reads=["pi_i"], writes=["pm_i"])
            S.dve(lambda h: h.tensor_copy(out=sgn[:, 0:1], in_=pm_i[:, 0:1]), reads=["pm_i"], writes=["sgn0"])
            S.dve(lambda h: h.tensor_scalar(out=sgn[:, 1:2], in0=sgn[:, 0:1], scalar1=1.0 / 16.0, scalar2=-1.0, op0=ALU.mult, op1=ALU.add),
                  reads=["sgn0"], writes=["sgn1"])
            S.pool(lambda h: h.iota(pos_i[:], pattern=[[1, SEQ]], base=0, channel_multiplier=0), writes=["pos_i"])
            S.dve(lambda h: h.tensor_copy(out=ta[:], in_=pos_i[:]), reads=["pos_i"], writes=["ta"])
            S.dve(lambda h: h.tensor_scalar(out=ang[:], in0=ta[:], scalar1=invf[:, 0:1], scalar2=None, op0=ALU.mult),
                  reads=["ta", "invf"], writes=["ang"])
            TWO_PI = 2.0 * math.pi
            C1 = 6.28125
            C2 = TWO_PI - C1
            MAGIC = 12582912.0
            LIM = 3.1415925
            S.dve(lambda h: h.tensor_scalar(out=ta[:], in0=ang[:], scalar1=1.0 / TWO_PI, scalar2=None, op0=ALU.mult),
                  reads=["ang", "ta"], writes=["ta"])
            S.dve(lambda h: h.tensor_scalar(out=tb[:], in0=ta[:], scalar1=MAGIC, scalar2=MAGIC, op0=ALU.add, op1=ALU.subtract),
                  reads=["ta"], writes=["tb"])
            S.dve(lambda h: h.scalar_tensor_tensor(out=ta[:], in0=tb[:], scalar=-C1, in1=ang[:], op0=ALU.mult, op1=ALU.add),
                  reads=["tb", "ang", "ta"], writes=["ta"])
            S.dve(lambda h: h.scalar_tensor_tensor(out=tc[:], in0=tb[:], scalar=-C2, in1=ta[:], op0=ALU.mult, op1=ALU.add),
                  reads=["tb", "ta"], writes=["tc"])
            S.dve(lambda h: h.tensor_scalar(out=ta[:], in0=tc[:], scalar1=LIM, scalar2=-LIM, op0=ALU.min, op1=ALU.max),
                  reads=["tc", "ta"], writes=["ta"])
            S.act(lambda h: h.activation(out=tb[:], in_=ta[:], func=AF.Sin), reads=["ta", "tb"], writes=["tb"])
            S.dve(lambda h: h.tensor_scalar(out=self.sinT[:], in0=tb[:], scalar1=sgn[:, 1:2], scalar2=None, op0=ALU.mult),
                  reads=["tb", "sgn1"], writes=["sinT"])
            S.dve(lambda h: h.tensor_scalar(out=ang[:], in0=tc[:], scalar1=math.pi / 2, scalar2=None, op0=ALU.add),
                  reads=["tc", "ang"], writes=["ang"])
            S.dve(lambda h: h.tensor_scalar(out=ta[:], in0=ang[:], scalar1=math.pi, scalar2=None, op0=ALU.is_gt),
                  reads=["ang", "ta"], writes=["ta"])
            S.dve(lambda h: h.scalar_tensor_tensor(out=tc[:], in0=ta[:], scalar=-TWO_PI, in1=ang[:], op0=ALU.mult, op1=ALU.add),
                  reads=["ta", "ang", "tc"], writes=["tc"])
            S.dve(lambda h: h.tensor_scalar(out=ta[:], in0=tc[:], scalar1=LIM, scalar2=-LIM, op0=ALU.min, op1=ALU.max),
                  reads=["tc", "ta"], writes=["ta"])
            S.act(lambda h: h.activation(out=self.cosT[:], in_=ta[:], func=AF.Sin), reads=["ta"], writes=["cosT"])
            for tt in range(16):
                sl = tt % 3
                S.dma("sp", "xs%d" % sl, lambda h, tt=tt, sl=sl: h.dma_start(out=xs[sl][:], in_=self.x[tt * 128:(tt + 1) * 128, :]),
                      writes=[("xs", sl)])
                T = tt // 4
                for half in range(2):
                    bi = 4 + (2 * tt + half) % 4
                    bank = ps[bi]
                    for jj in range(4):
                        c = half * 4 + jj
                        S.pe(lambda h, bank=bank, jj=jj, c=c, sl=sl: h.transpose(out=bank[:, jj * 128:(jj + 1) * 128],
                                                                               in_=xs[sl][:, c * 128:(c + 1) * 128], identity=identf[:]),
                             reads=[("xs", sl), "identf"], writes=[("ps", bi)])
                    dst = self.h[:, half * 4:half * 4 + 4, tt * 128:(tt + 1) * 128]
                    src = bank[:, :].rearrange("p (a b) -> p a b", b=128)
                    wr = [("h", half * 4 + jj, T) for jj in range(4)]
                    if (2 * tt + half) % 2 == 0:
                        S.act(lambda h, dst=dst, src=src: h.activation(out=dst, in_=src, func=AF.Copy), reads=[("ps", bi)], writes=wr)
                    else:
                        S.dve(lambda h, dst=dst, src=src: h.tensor_copy(out=dst, in_=src), reads=[("ps", bi)], writes=wr)
            S.barrier()
            S.flush()

    def rmsnorm(self, grow, out_fn=None):
        S, ps = self.S, self.ps
        for T in range(NT):
            bi = 6 + T % 2
            bank = ps[bi]
            tsl = slice(T * TT, (T + 1) * TT)
            for c in range(NCH):
                sq = self.nsq[:, c % 2, :]
                S.act(lambda h, sq=sq, c=c, tsl=tsl: h.activation(out=sq, in_=self.h[:, c, tsl], func=AF.Square),
                      reads=[("h", c, T)], writes=[("nsq", c % 2)])
                self.mm(bank[:, :], self.onesb[:], sq, c == 0, c == NCH - 1, [("nsq", c % 2), "onesb"], [("ps", bi)])
            sd = self.nstd[:, T % 2, :]
            S.act(lambda h, sd=sd, bank=bank: h.activation(out=sd, in_=bank[:, :], func=AF.Sqrt, scale=1.0 / D, bias=NORM_EPS),
                  reads=[("ps", bi)], writes=[("nstd", T % 2)])
            S.dve(lambda h, sd=sd, bank=bank: h.reciprocal(bank[:, :], sd), reads=[("nstd", T % 2)], writes=[("ps", bi)])
            for c in range(NCH):
                if out_fn is None:
                    dst = self.hn[:, c, tsl]
                    wr = [("hn", c, T)]
                else:
                    dst, wr = out_fn(c, T)
                S.dve(lambda h, dst=dst, c=c, tsl=tsl, bank=bank: h.scalar_tensor_tensor(
                    out=dst, in0=self.h[:, c, tsl], scalar=self.prow(grow + c), in1=bank[:, :], op0=ALU.mult, op1=ALU.mult),
                    reads=[("h", c, T), ("ps", bi), "ptab"], writes=wr)

    def load_w(self, slot_ap, src_ap, key, group):
        self.S.dma("pool", group, lambda h: h.dma_start(out=slot_ap, in_=src_ap), writes=[key])

    def phase_attention(self, j):
        nc, S, ps = self.nc, self.S, self.ps
        hn, h = self.hn, self.h
        with ExitStack() as es:
            qT = self.sb(es, [128, 2, SEQ], BF16, "qT")
            kT = self.sb(es, [128, 2, 2, SEQ], BF16, "kT")
            Vt = self.sb(es, [128, 16, 4, 128], BF16, "Vt")
            NW = 5
            wsl = [self.sb(es, [128, 2048], BF16, "wsl") for _ in range(NW)]
            biasT = self.sb(es, [128, 2, 1024], BF16, "biasT")
            kms = self.sb(es, [128, 4, 8], F32, "kms")
            kmT = self.sb(es, [128, 4, 8], BF16, "kmT")
            gsb = self.sb(es, [128, 256], F32, "gsb")
            cmp_ = self.sb(es, [128, 2, 4 * 49], BF16, "cmp")
            rank = self.sb(es, [128, 2, 32], F32, "rank")
            btok = self.sb(es, [128, 4, 256], BF16, "btok")
            pT = [self.sb(es, [128, TT], BF16, "pT") for _ in range(4)]
            rec = [self.sb(es, [128, TT], F32, "rec") for _ in range(2)]
            qs = [self.sb(es, [128, TT], BF16, "qs") for _ in range(2)]
            t1 = [self.sb(es, [128, TT], F32, "t1") for _ in range(2)]
            t2 = [self.sb(es, [128, TT], F32, "t2") for _ in range(2)]

            wcount = [0]

            def next_slot():
                i = wcount[0] % NW
                wcount[0] += 1
                return i

            S.pool(lambda h_: h_.memset(Vt[:, :, 0:4:2, 64:128], 1.0), writes=[("Vt1", 0)])
            S.pool(lambda h_: h_.memset(Vt[:, :, 1:4:2, 0:64], 1.0), writes=[("Vt1", 1)])
            S.pool(lambda h_: h_.memset(biasT[:], 0.0), writes=[("biasT", a, u) for a in range(2) for u in range(2)])
            S.pool(lambda h_: h_.memset(kT[64:128, :, 0, :], 0.0), writes=[("kz", 0)])
            S.pool(lambda h_: h_.memset(kT[0:64, :, 1, :], 0.0), writes=[("kz", 1)])

            wq_src = self.w_qkv[j].rearrange("(c p) f -> p c f", p=128)
            wo_src = self.w_o[j].rearrange("(c p) f -> p c f", p=128)

            def load_group(g):
                sl = {}
                for nm, off in (("q", 0), ("k", D), ("v", 2 * D)):
                    i = next_slot()
                    sl[nm] = i
                    self.load_w(wsl[i][:, :].rearrange("p (c f) -> p c f", f=256), wq_src[:, :, off + g * 256: off + (g + 1) * 256],
                                ("wsl", i), "aw%d" % i)
                i = next_slot()
                sl["o"] = i
                self.load_w(wsl[i][:, :].rearrange("p (c f) -> p c f", f=1024), wo_src[:, 2 * g:2 * g + 2, :], ("wsl", i), "aw%d" % i)
                return sl

            pending = load_group(0)
            rope_i = [0]
            sub = self.stop_after.split(":")[1] if (self.stop_after and ":" in self.stop_after) else None
            for g in range(4):
                if sub is not None and g > 0:
                    break
                sl = pending
                wq = wsl[sl["q"]][:, :].rearrange("p (c f) -> p c f", f=256)
                wk = wsl[sl["k"]][:, :].rearrange("p (c f) -> p c f", f=256)
                wv = wsl[sl["v"]][:, :].rearrange("p (c f) -> p c f", f=256)
                wo = wsl[sl["o"]][:, :].rearrange("p (c f) -> p c f", f=1024)
                units = [(which, cc, T) for which in ("q", "k") for cc in range(2) for T in range(NT)]
                pend = []

                def rope_a(which, cc, T):
                    wsrc = wq if which == "q" else wk
                    wkey = ("wsl", sl[which])
                    scale = DH ** -0.5 if which == "q" else 1.0
                    ri = rope_i[0]
                    rope_i[0] += 1
                    ba = ri % 2
                    A = ps[ba]
                    tsl = slice(T * TT, (T + 1) * TT)
                    for c in range(NCH):
                        self.mm(A[:, :], wsrc[:, c, cc * 128:(cc + 1) * 128], hn[:, c, tsl], c == 0, c == NCH - 1,
                                [wkey, ("hn", c, T)], [("ps", ba)])
                    q_s = qs[ri % 2]
                    S.act(lambda h_, q_s=q_s, A=A, scale=scale: h_.activation(out=q_s[:], in_=A[:, :], func=AF.Copy, scale=scale),
                          reads=[("ps", ba)], writes=[("qs", ri % 2)])
                    return (which, cc, T, ri, scale)

                def rope_b(which, cc, T, ri, scale):
                    ba, bb = ri % 2, 2 + ri % 2
                    A, B = ps[ba], ps[bb]
                    tsl = slice(T * TT, (T + 1) * TT)
                    q_s, t1_, t2_ = qs[ri % 2], t1[ri % 2], t2[ri % 2]
                    self.mm(B[:, :], self.prot[:], q_s[:], True, True, [("qs", ri % 2), "prot"], [("ps", bb)])
                    S.dve(lambda h_, t1_=t1_, A=A, scale=scale, tsl=tsl: h_.scalar_tensor_tensor(
                        out=t1_[:], in0=A[:, :], scalar=scale, in1=self.cosT[:, tsl], op0=ALU.mult, op1=ALU.mult),
                        reads=[("ps", ba), "cosT"], writes=[("t1", ri % 2)])
                    S.dve(lambda h_, t2_=t2_, B=B, tsl=tsl: h_.tensor_tensor(out=t2_[:], in0=B[:, :], in1=self.sinT[:, tsl], op=ALU.mult),
                          reads=[("ps", bb), "sinT"], writes=[("t2", ri % 2)])
                    if which == "q":
                        dst = qT[:, cc, tsl]
                        wr = [("q", cc, T, 0), ("q", cc, T, 1)]
                        S.pool(lambda h_, dst=dst, t1_=t1_, t2_=t2_: h_.tensor_tensor(out=dst, in0=t1_[:], in1=t2_[:], op=ALU.add),
                               reads=[("t1", ri % 2), ("t2", ri % 2)], writes=wr)
                    else:
                        tk_ = qs[ri % 2]
                        S.pool(lambda h_, tk_=tk_, t1_=t1_, t2_=t2_: h_.tensor_tensor(out=tk_[:], in0=t1_[:], in1=t2_[:], op=ALU.add),
                               reads=[("t1", ri % 2), ("t2", ri % 2), ("qs", ri % 2)], writes=[("qs", ri % 2)])
                        for hp in range(2):
                            prt = slice(hp * 64, (hp + 1) * 64)
                            S.act(lambda h_, prt=prt, hp=hp, cc=cc, tsl=tsl, tk_=tk_: h_.activation(
                                out=kT[prt, cc, hp, tsl], in_=tk_[prt, :], func=AF.Copy),
                                reads=[("qs", ri % 2)], writes=[("k", cc, T, hp)])

                for u_ in units:
                    st_ = rope_a(*u_)
                    if pend:
                        rope_b(*pend.pop(0))
                    pend.append(st_)
                while pend:
                    rope_b(*pend.pop(0))
                if sub == "qk":
                    break
                if sub == "v":
                    break
                for ch in range(4):
                    cc, hp = ch // 2, ch % 2
                    S.dve(lambda h_, cc=cc, hp=hp, ch=ch: h_.tensor_reduce(out=kms[:, ch, :], in_=kT[:, cc, hp, :].rearrange("p (n k) -> p n k", k=256),
                                                                       axis=AX.X, op=ALU.add),
                          reads=[("k", cc, T, hp) for T in range(NT)] + [("kz", hp)], writes=[("kms", ch)])
                    S.dve(lambda h_, ch=ch: h_.tensor_scalar(out=kmT[:, ch, :], in0=kms[:, ch, :], scalar1=1.0 / 256.0, scalar2=None, op0=ALU.mult),
                          reads=[("kms", ch)], writes=[("kmT", ch)])
                def gate_matmuls():
                    gbank = ps[6]
                    for qt in range(8, 16):
                        T = qt // 4
                        for hl in range(4):
                            cc, hp = hl // 2, hl % 2
                            col = (qt - 8) * 32 + hl * 8
                            self.mm(gbank[:, col:col + 8], qT[:, cc, qt * 128:(qt + 1) * 128],
                                    kmT[:, hl, :], True, True,
                                    [("q", cc, T, 0), ("q", cc, T, 1), ("kmT", hl)], [("ps", 6)])
                    S.act(lambda h_: h_.activation(out=gsb[:, :], in_=gbank[:, 0:256], func=AF.Copy),
                          reads=[("ps", 6)], writes=["gsb"])

                def gate_chain(qt):
                    qb = qt // 2
                    sI = qt % 4
                    c2 = qt % 2
                    g3 = gsb[:, (qt - 8) * 32:(qt - 7) * 32].rearrange("p (a n) -> p a n", n=8)[:, :, 0:qb]
                    in0 = g3.unsqueeze(2).broadcast_to([128, 4, qb, qb])
                    in1 = g3.unsqueeze(3).broadcast_to([128, 4, qb, qb])
                    cm = cmp_[:, c2, 0:4 * qb * qb].rearrange("p (a n m) -> p a n m", a=4, n=qb)
                    S.dve(lambda h_, cm=cm, in0=in0, in1=in1: h_.tensor_tensor(out=cm, in0=in0, in1=in1, op=ALU.is_gt),
                          reads=["gsb"], writes=[("cmp", c2)])
                    rk = rank[:, c2, :].rearrange("p (a n) -> p a n", n=8)[:, :, 0:qb]
                    S.dve(lambda h_, rk=rk, cm=cm: h_.tensor_reduce(out=rk, in_=cm, axis=AX.X, op=ALU.add),
                          reads=[("cmp", c2)], writes=[("rank", c2)])
                    S.pool(lambda h_, sI=sI: h_.memset(btok[:, sI, :], 0.0), writes=[("btok", sI)])
                    bo = btok[:, sI, :].rearrange("p (a b n) -> p a b n", a=2, b=2)[:, :, :, 0:qb]
                    rk4 = rank[:, c2, :].rearrange("p (a b n) -> p a b n", a=2, b=2)[:, :, :, 0:qb]
                    S.dve(lambda h_, bo=bo, rk4=rk4: h_.tensor_scalar(out=bo, in0=rk4, scalar1=2.5, scalar2=NEG, op0=ALU.is_gt, op1=ALU.mult),
                          reads=[("rank", c2)], writes=[("btok", sI)])

                def gate_transposes(q4):
                    for a in range(2):
                        bti = 2 + a
                        for qq in range(4):
                            qt = 8 + 4 * q4 + qq
                            sI = qt % 4
                            self.mm(ps[bti][:, qq * 128:(qq + 1) * 128], btok[:, sI, a * 128:(a + 1) * 128], self.identb[:], True, True,
                                    [("btok", sI), "identb"], [("ps", bti)])
                        S.act(lambda h_, a=a, q4=q4, bti=bti: h_.activation(out=biasT[:, a, q4 * TT:(q4 + 1) * TT], in_=ps[bti][:, :], func=AF.Copy),
                              reads=[("ps", bti)], writes=[("biasT", a, q4)])
                def v_proj(tp_lo, tp_hi):
                  for tp in range(tp_lo, tp_hi):
                      bi = 4 + tp % 2
                      bank = ps[bi]
                      for u in range(2):
                          tt = 2 * tp + u
                          T = tt // 4
                          for c in range(NCH):
                              self.mm(bank[:, u * 256:(u + 1) * 256], hn[:, c, tt * 128:(tt + 1) * 128], wv[:, c, :], c == 0, c == NCH - 1,
                                      [("wsl", sl["v"]), ("hn", c, T)], [("ps", bi)])
                      src = bank[:, :].rearrange("p (u a b e) -> p u a b e", u=2, a=2, b=2)
                      S.act(lambda h_, src=src, tp=tp: h_.activation(out=Vt[:, 2 * tp:2 * tp + 2, 0:4:2, 0:64], in_=src[:, :, :, 0, :], func=AF.Copy),
                            reads=[("ps", bi)], writes=[("Vt", 2 * tp, 0), ("Vt", 2 * tp + 1, 0)])
                      S.dve(lambda h_, src=src, tp=tp: h_.tensor_copy(out=Vt[:, 2 * tp:2 * tp + 2, 1:4:2, 64:128], in_=src[:, :, :, 1, :]),
                            reads=[("ps", bi)], writes=[("Vt", 2 * tp, 1), ("Vt", 2 * tp + 1, 1)])
                v_proj(0, 3)
                gate_matmuls()
                for qt in range(8, 12):
                    gate_chain(qt)
                v_proj(3, 6)
                gate_transposes(0)
                for qt in range(12, 16):
                    gate_chain(qt)
                v_proj(6, 8)
                gate_transposes(1)
                if g + 1 < 4 and sub is None:
                    pending = load_group(g + 1)
                if sub == "gate":
                    break
                iters = []
                itc = 0
                for cc in range(2):
                    for T in range(NT):
                        nk = 4 * T + 4
                        ob = [4 + 2 * (itc % 2), 5 + 2 * (itc % 2)]
                        itc += 1
                        for kt in range(nk):
                            for hp in range(2):
                                iters.append((cc, T, kt, hp, nk, ob[hp]))
                LAG = 3

                def stage_a(n_, cc, T, kt, hp, nk, obk):
                    nb = kt // 2
                    q_lo = max(0, kt - 4 * T) * 128
                    qsl = slice(q_lo, TT)
                    Tk = kt // 4
                    sbi = n_ % 4
                    sbk = ps[sbi]
                    need_bias = (T >= 2 and kt < 4 * T + 2)
                    need_mask = kt >= 4 * T
                    self.mm(sbk[:, qsl], kT[:, cc, hp, kt * 128:(kt + 1) * 128], qT[:, cc, T * TT + q_lo:(T + 1) * TT],
                            True, not (need_bias or need_mask),
                            [("k", cc, Tk, hp), ("kz", hp), ("q", cc, T, 0), ("q", cc, T, 1)], [("ps", sbi)])
                    if need_bias:
                        self.mm(sbk[:, qsl], self.ind[hp][:, nb * 128:(nb + 1) * 128],
                                biasT[:, cc, (T - 2) * TT + q_lo:(T - 1) * TT],
                                False, not need_mask, ["ind", ("biasT", cc, T - 2)], [("ps", sbi)])
                    if need_mask:
                        self.mm(sbk[:, q_lo:q_lo + 128], self.identb[:], self.maskT[:], False, True,
                                ["identb", "maskT"], [("ps", sbi)])

                def stage_bc(n_, cc, T, kt, hp, nk, obk):
                    hl = 2 * cc + hp
                    q_lo = max(0, kt - 4 * T) * 128
                    qsl = slice(q_lo, TT)
                    sbi = n_ % 4
                    sbk = ps[sbi]
                    pi = n_ % 4
                    S.act(lambda h_, pi=pi, sbk=sbk, qsl=qsl: h_.activation(out=pT[pi][:, qsl], in_=sbk[:, qsl], func=AF.Exp),
                          reads=[("ps", sbi)], writes=[("pT", pi)])
                    self.mm(ps[obk][:, qsl], Vt[:, kt, hl, :], pT[pi][:, qsl], kt == 0, kt == nk - 1,
                            [("pT", pi), ("Vt", kt, hp), ("Vt1", hp)], [("ps", obk)])
                    if kt == nk - 1:
                        o = ps[obk]
                        num = slice(hp * 64, (hp + 1) * 64)
                        den = slice((1 - hp) * 64, (2 - hp) * 64)
                        rc = rec[hp]
                        S.dve(lambda h_, rc=rc, o=o, den=den: h_.reciprocal(rc[den, :], o[den, :]),
                              reads=[("ps", obk)], writes=[("rec", hp)])
                        S.dve(lambda h_, rc=rc, o=o, den=den, num=num, cc=cc, T=T: h_.tensor_tensor(
                            out=qT[num, cc, T * TT:(T + 1) * TT], in0=o[num, :], in1=rc[den, :], op=ALU.mult),
                            reads=[("ps", obk), ("rec", hp)], writes=[("q", cc, T, hp)])

                NI = len(iters)
                for n_ in range(NI + LAG):
                    if n_ < NI:
                        stage_a(n_, *iters[n_])
                    m_ = n_ - LAG
                    if m_ >= 0:
                        stage_bc(m_, *iters[m_])
                if sub == "core":
                    break
                for T in range(NT):
                    tsl = slice(T * TT, (T + 1) * TT)
                    for dc in range(NCH):
                        bi = (T * NCH + dc) % 4
                        for cc in range(2):
                            self.mm(ps[bi][:, :], wo[:, cc, dc * 128:(dc + 1) * 128], qT[:, cc, tsl], cc == 0, cc == 1,
                                    [("wsl", sl["o"]), ("q", cc, T, 0), ("q", cc, T, 1)], [("ps", bi)])
                        S.dve(lambda h_, bi=bi, dc=dc, tsl=tsl: h_.tensor_tensor(out=h[:, dc, tsl], in0=ps[bi][:, :], in1=h[:, dc, tsl], op=ALU.add),
                              reads=[("ps", bi), ("h", dc, T)], writes=[("h", dc, T)])
            S.barrier()
            S.flush()

    def phase_conformer(self, j):
        nc, S, ps = self.nc, self.S, self.ps
        hn, h = self.hn, self.h
        PADL = 32
        with ExitStack() as es:
            glu = self.sb(es, [128, NCH, PADL + SEQ], BF16, "glu")
            ybf = [self.sb(es, [128, NCH, TT], BF16, "ybf") for _ in range(2)]
            ysq = self.sb(es, [128, NCH, TT], BF16, "ysq")
            dg = self.sb(es, [128, CW, 128], BF16, "dg")
            NW = 3
            wsl = [self.sb(es, [128, NCH, 256], BF16, "cw") for _ in range(NW)]
            tA = [self.sb(es, [128, TT], F32, "tA") for _ in range(2)]
            sgm = tA
            mean_t = self.sb(es, [128, TT], F32, "mean_t")
            m2_t = self.sb(es, [128, TT], F32, "m2_t")
            wcount = [0]

            def next_slot():
                i = wcount[0] % NW
                wcount[0] += 1
                return i

            S.pool(lambda h_: h_.memset(glu[:, :, 0:PADL], 0.0), writes=[("glupad",)])
            w1 = self.w_pw1[j].rearrange("(c p) f -> p c f", p=128)
            w2 = self.w_pw2[j].rearrange("(c p) f -> p c f", p=128)

            def load_pw1(cb):
                ia = next_slot()
                self.load_w(wsl[ia][:], w1[:, :, cb * 256:(cb + 1) * 256], ("cw", ia), "cw%d" % ia)
                ig = next_slot()
                self.load_w(wsl[ig][:], w1[:, :, D + cb * 256:D + (cb + 1) * 256], ("cw", ig), "cw%d" % ig)
                return ia, ig

            it = 0
            for cb in range(4):
                ia, ig = load_pw1(cb)
                for ci in range(2):
                    cc = 2 * cb + ci
                    for T in range(NT):
                        ba, bg = (it % 2) * 2, (it % 2) * 2 + 1
                        it += 1
                        tsl = slice(T * TT, (T + 1) * TT)
                        for c in range(NCH):
                            self.mm(ps[ba][:, :], wsl[ia][:, c, ci * 128:(ci + 1) * 128], hn[:, c, tsl], c == 0, c == NCH - 1,
                                    [("cw", ia), ("hn", c, T)], [("ps", ba)])
                        for c in range(NCH):
                            self.mm(ps[bg][:, :], wsl[ig][:, c, ci * 128:(ci + 1) * 128], hn[:, c, tsl], c == 0, c == NCH - 1,
                                    [("cw", ig), ("hn", c, T)], [("ps", bg)])
                        sg_ = sgm[it % 2]
                        S.act(lambda h_, sg_=sg_, bg=bg, cc=cc: h_.activation(out=sg_[:], in_=ps[bg][:, :], func=AF.Sigmoid,
                                                                               bias=self.prow(R_BPW1 + j * 16 + 8 + cc)),
                              reads=[("ps", bg), "ptab"], writes=[("tA", it % 2)])
                        S.dve(lambda h_, sg_=sg_, ba=ba, cc=cc, T=T: h_.scalar_tensor_tensor(
                            out=glu[:, cc, PADL + T * TT:PADL + (T + 1) * TT], in0=ps[ba][:, :], scalar=self.prow(R_BPW1 + j * 16 + cc),
                            in1=sg_[:], op0=ALU.add, op1=ALU.mult),
                            reads=[("ps", ba), ("tA", it % 2), "ptab"], writes=[("glu", cc, T)])
            pw2_slots = {}

            def load_pw2(db):
                i = next_slot()
                self.load_w(wsl[i][:], w2[:, :, db * 256:(db + 1) * 256], ("cw", i), "cw%d" % i)
                pw2_slots[db] = i

            load_pw2(0)
            load_pw2(1)
            dcount = [0]

            def conv_unit(T, cc):
                yb = cc % 2
                yT = ybf[T % 2]
                for tap in range(CW):
                    if dcount[0] % 2 == 0:
                        S.dve(lambda h_, tap=tap, cc=cc: h_.tensor_scalar(out=dg[:, tap, :], in0=self.identb[:],
                                                                          scalar1=self.prow(R_WDW + (j * CW + tap) * 8 + cc), scalar2=None, op0=ALU.mult),
                              reads=["identb", "ptab"], writes=[("dg", tap)])
                    else:
                        S.pool(lambda h_, tap=tap, cc=cc: h_.tensor_scalar(out=dg[:, tap, :], in0=self.identb[:],
                                                                           scalar1=self.prow(R_WDW + (j * CW + tap) * 8 + cc), scalar2=1.0,
                                                                           op0=ALU.mult, op1=ALU.mult),
                               reads=["identb", "ptab"], writes=[("dg", tap)])
                    dcount[0] += 1
                for tap in range(CW):
                    o0 = PADL + T * TT - (CW - 1) + tap
                    rd = [("dg", tap), ("glu", cc, T)]
                    rd.append(("glu", cc, T - 1) if T > 0 else ("glupad",))
                    self.mm(ps[yb][:, :], dg[:, tap, :], glu[:, cc, o0:o0 + TT], tap == 0, tap == CW - 1, rd, [("ps", yb)])

            def conv_evac(T, cc):
                yb = cc % 2
                yT = ybf[T % 2]
                S.act(lambda h_: h_.activation(out=yT[:, cc, :], in_=ps[yb][:, :], func=AF.Identity,
                                               bias=self.prow(R_BDW + j * 8 + cc)),
                      reads=[("ps", yb), "ptab"], writes=[("ybf", T % 2, cc)])
                S.act(lambda h_: h_.activation(out=ysq[:, cc, :], in_=yT[:, cc, :], func=AF.Square),
                      reads=[("ybf", T % 2, cc)], writes=[("ysq", cc)])

            def ln_head(T):
                yT = ybf[T % 2]
                bm, bq = 2 + (T % 2) * 2, 3 + (T % 2) * 2
                for cc in range(NCH):
                    self.mm(ps[bm][:, :], self.onesb[:], yT[:, cc, :], cc == 0, cc == NCH - 1, ["onesb", ("ybf", T % 2, cc)], [("ps", bm)])
                for cc in range(NCH):
                    self.mm(ps[bq][:, :], self.onesb[:], ysq[:, cc, :], cc == 0, cc == NCH - 1, ["onesb", ("ysq", cc)], [("ps", bq)])
                S.act(lambda h_: h_.activation(out=mean_t[:], in_=ps[bm][:, :], func=AF.Copy, scale=1.0 / D),
                      reads=[("ps", bm)], writes=["mean_t"])
                S.dve(lambda h_: h_.tensor_tensor(out=m2_t[:], in0=mean_t[:], in1=mean_t[:], op=ALU.mult), reads=["mean_t"], writes=["m2_t"])
                S.dve(lambda h_: h_.scalar_tensor_tensor(out=m2_t[:], in0=ps[bq][:, :], scalar=1.0 / D, in1=m2_t[:],
                                                         op0=ALU.mult, op1=ALU.subtract),
                      reads=[("ps", bq), "m2_t"], writes=["m2_t"])
                S.act(lambda h_: h_.activation(out=m2_t[:], in_=m2_t[:], func=AF.Sqrt, bias=LN_EPS), reads=["m2_t"], writes=["m2_t"])
                S.dve(lambda h_: h_.reciprocal(ps[bm][:, :], m2_t[:]), reads=["m2_t"], writes=[("ps", bm)])
                S.dve(lambda h_: h_.tensor_tensor(out=ps[bq][:, :], in0=ps[bm][:, :], in1=mean_t[:], op=ALU.mult),
                      reads=[("ps", bm), "mean_t"], writes=[("ps", bq)])

            def ln_norm(T, cc):
                yT = ybf[T % 2]
                bm, bq = 2 + (T % 2) * 2, 3 + (T % 2) * 2
                ta_ = tA[cc % 2]
                S.dve(lambda h_: h_.tensor_tensor(out=ta_[:], in0=ps[bm][:, :], in1=yT[:, cc, :], op=ALU.mult),
                      reads=[("ps", bm), ("ybf", T % 2, cc)], writes=[("tA", cc % 2)])
                S.dve(lambda h_: h_.tensor_tensor(out=ta_[:], in0=ta_[:], in1=ps[bq][:, :], op=ALU.subtract),
                      reads=[("ps", bq), ("tA", cc % 2)], writes=[("tA", cc % 2)])
                S.act(lambda h_: h_.activation(out=hn[:, cc, T * TT:(T + 1) * TT], in_=ta_[:], func=AF.Silu,
                                               scale=self.prow(R_LNG + j * 8 + cc), bias=self.prow(R_LNB + j * 8 + cc)),
                      reads=[("tA", cc % 2), "ptab"], writes=[("hn", cc, T)])

            for T in range(NT):
                for cc in range(NCH):
                    conv_unit(T, cc)
                    if T > 0 and cc == 0:
                        ln_head(T - 1)
                    conv_evac(T, cc)
                    if T > 0:
                        ln_norm(T - 1, cc)
            ln_head(NT - 1)
            for cc in range(NCH):
                ln_norm(NT - 1, cc)
            it = 0
            for db in range(4):
                if db + 2 < 4:
                    load_pw2(db + 2)
                i = pw2_slots[db]
                for T in range(NT):
                    for di in range(2):
                        dc = 2 * db + di
                        bi = 6 + it % 2
                        it += 1
                        tsl = slice(T * TT, (T + 1) * TT)
                        for cc in range(NCH):
                            self.mm(ps[bi][:, :], wsl[i][:, cc, di * 128:(di + 1) * 128], hn[:, cc, tsl], cc == 0, cc == NCH - 1,
                                    [("cw", i), ("hn", cc, T)], [("ps", bi)])
                        S.dve(lambda h_, bi=bi, dc=dc, tsl=tsl: h_.scalar_tensor_tensor(
                            out=h[:, dc, tsl], in0=ps[bi][:, :], scalar=self.prow(R_BPW2 + j * 8 + dc), in1=h[:, dc, tsl], op0=ALU.add, op1=ALU.add),
                            reads=[("ps", bi), ("h", dc, T), "ptab"], writes=[("h", dc, T)])
            S.barrier()
            S.flush()

    def phase_ffn(self, i):
        nc, S, ps = self.nc, self.S, self.ps
        hn, h = self.hn, self.h
        groups = [[0, 1, 2], [3, 4, 5], [6, 7, 8], [9, 10]]
        with ExitStack() as es:
            act = self.sb(es, [128, 6, SEQ], BF16, "act")
            NWU = 3
            wup = [self.sb(es, [128, NCH, 512], BF16, "wup") for _ in range(NWU)]
            wdn = [self.sb(es, [128, 6, D], BF16, "wdn") for _ in range(2)]
            Ag = [self.sb(es, [128, TT], F32, "Ag") for _ in range(2)]
            Av = [self.sb(es, [128, TT], F32, "Av") for _ in range(2)]
            sg = [self.sb(es, [128, TT], F32, "sg") for _ in range(2)]
            halo = self.sb(es, [128, 2, 2, 2], F32, "halo")
            wu_src = self.w_up[i].rearrange("(c p) f -> p c f", p=128)
            wd_src = self.w_down[i].rearrange("(c p) f -> p c f", p=128)
            ucount = [0]
            blocks = [b for g in groups for b in g]
            up_slot = {}

            def load_up(b):
                s = ucount[0] % NWU
                ucount[0] += 1
                up_slot[b] = s
                self.load_w(wup[s][:, :, 0:256], wu_src[:, :, b * 256:(b + 1) * 256], ("wup", s), "wu%d" % s)
                self.load_w(wup[s][:, :, 256:512], wu_src[:, :, DFF + b * 256:DFF + (b + 1) * 256], ("wup", s), "wu%d" % s)

            def load_dn(gi):
                import os
                if os.environ.get("FFN_NODN"):
                    return
                g = groups[gi]
                np_ = 2 * len(g)
                j0 = 2 * g[0]
                self.load_w(wdn[gi % 2][:, 0:np_, :], wd_src[:, j0:j0 + np_, :], ("wdn", gi % 2), "wd%d" % (gi % 2))

            pend3 = []
            load_up(blocks[0])
            load_up(blocks[1])
            load_dn(0)
            nxt = 2
            it = 0
            for gi, g in enumerate(groups):
                for bl, b in enumerate(g):
                    s = up_slot[b]
                    for pi in range(2):
                        jp = 2 * b + pi
                        jl = 2 * bl + pi
                        rows = {}
                        for kind, fc in (("g", jp), ("v", NFP + jp)):
                            rows[kind] = [R_FWDW + (i * 3 + tap) * 44 + fc for tap in range(3)] + [R_FBDW + i * 44 + fc]
                        for T in range(NT):
                            par = it % 2
                            it += 1
                            tsl = slice(T * TT, (T + 1) * TT)
                            s3 = (it - 1) % 3
                            bg_, bv_ = s3 * 2, s3 * 2 + 1
                            import os
                            for c in range(NCH if not os.environ.get("FFN_NOMM") else 0):
                                self.mm(ps[bg_][:, :], wup[s][:, c, pi * 128:(pi + 1) * 128], hn[:, c, tsl], c == 0, c == NCH - 1,
                                        [("wup", s), ("hn", c, T)], [("ps", bg_)])
                            for c in range(NCH if not os.environ.get("FFN_NOMM") else 0):
                                self.mm(ps[bv_][:, :], wup[s][:, c, 256 + pi * 128:256 + (pi + 1) * 128], hn[:, c, tsl], c == 0, c == NCH - 1,
                                        [("wup", s), ("hn", c, T)], [("ps", bv_)])
                            A = {"g": Ag[par], "v": Av[par]}
                            U = {"g": ps[bg_], "v": ps[bv_]}
                            UB = {"g": bg_, "v": bv_}
                            AK = {"g": ("Ag", par), "v": ("Av", par)}
                            KI = {"g": 0, "v": 1}
                            hp_prev = (T - 1) % 2
                            hp_cur = T % 2
                            import os
                            FL = int(os.environ.get("FFN_LEVEL", "9"))
                            for kind in ("g", "v"):
                                if FL < 2:
                                    break
                                r = rows[kind]
                                S.act(lambda h_, A_=A[kind], U_=U[kind], r=r: h_.activation(out=A_[:], in_=U_[:, :], func=AF.Identity,
                                                                                          scale=self.prow(r[2]), bias=self.prow(r[3])),
                                      reads=[("ps", UB[kind]), "ptab"], writes=[AK[kind]])
                                if T < NT - 1 and FL >= 3:
                                    S.act(lambda h_, U_=U[kind], kind=kind, hp_cur=hp_cur: h_.activation(
                                        out=halo[:, hp_cur, KI[kind], :], in_=U_[:, TT - 2:TT], func=AF.Copy),
                                        reads=[("ps", UB[kind])], writes=[("halo", hp_cur, kind)])
                            for kind in ("g", "v"):
                                if FL < 4:
                                    break
                                r = rows[kind]
                                S.dve(lambda h_, A_=A[kind], U_=U[kind], r=r: h_.scalar_tensor_tensor(
                                    out=A_[:, 1:TT], in0=U_[:, 0:TT - 1], scalar=self.prow(r[1]), in1=A_[:, 1:TT], op0=ALU.mult, op1=ALU.add),
                                    reads=[("ps", UB[kind]), AK[kind], "ptab"], writes=[AK[kind]])
                            for kind in ("g", "v"):
                                if FL < 4:
                                    break
                                r = rows[kind]
                                S.dve(lambda h_, A_=A[kind], U_=U[kind], r=r: h_.scalar_tensor_tensor(
                                    out=A_[:, 2:TT], in0=U_[:, 0:TT - 2], scalar=self.prow(r[0]), in1=A_[:, 2:TT], op0=ALU.mult, op1=ALU.add),
                                    reads=[("ps", UB[kind]), AK[kind], "ptab"], writes=[AK[kind]])
                            if T > 0 and FL >= 5:
                                for kind in ("g", "v"):
                                    r = rows[kind]
                                    hl_ = halo[:, hp_prev, KI[kind], :]
                                    S.dve(lambda h_, A_=A[kind], hl_=hl_, r=r: h_.scalar_tensor_tensor(
                                        out=A_[:, 0:1], in0=hl_[:, 1:2], scalar=self.prow(r[1]), in1=A_[:, 0:1], op0=ALU.mult, op1=ALU.add),
                                        reads=[("halo", hp_prev, kind), AK[kind], "ptab"], writes=[AK[kind]])
                                for kind in ("g", "v"):
                                    r = rows[kind]
                                    hl_ = halo[:, hp_prev, KI[kind], :]
                                    S.dve(lambda h_, A_=A[kind], hl_=hl_, r=r: h_.scalar_tensor_tensor(
                                        out=A_[:, 0:2], in0=hl_[:, 0:2], scalar=self.prow(r[0]), in1=A_[:, 0:2], op0=ALU.mult, op1=ALU.add),
                                        reads=[("halo", hp_prev, kind), AK[kind], "ptab"], writes=[AK[kind]])
                            if FL < 6:
                                continue
                            def stage3(par=par, Ag_=A["g"], Av_=A["v"], jl=jl, tsl=tsl, T=T):
                                sg_ = sg[par]
                                S.act(lambda h_: h_.activation(out=sg_[:], in_=Ag_[:], func=AF.Silu),
                                      reads=[("Ag", par)], writes=[("sg", par)])
                                S.pool(lambda h_: h_.tensor_tensor(out=act[:, jl, tsl], in0=sg_[:], in1=Av_[:], op=ALU.mult),
                                       reads=[("sg", par), ("Av", par)], writes=[("act", jl, T)])
                            if pend3:
                                pend3.pop(0)()
                            pend3.append(stage3)
                            if pi == 0 and T == 2:
                                if nxt < len(blocks):
                                    load_up(blocks[nxt])
                                    nxt += 1
                                if bl == 0 and gi + 1 < len(groups):
                                    load_dn(gi + 1)
                while pend3:
                    pend3.pop(0)()
                np_ = 2 * len(g)
                wd = wdn[gi % 2]
                dn = 0
                for T in range(NT if FL >= 7 else 0):
                    tsl = slice(T * TT, (T + 1) * TT)
                    for dc in range(NCH):
                        s_next = it % 3
                        bi = (6, 7, 2 * s_next, 2 * s_next + 1)[dn % 4]
                        dn += 1
                        for jl in range(np_):
                            self.mm(ps[bi][:, :], wd[:, jl, dc * 128:(dc + 1) * 128], act[:, jl, tsl], jl == 0, jl == np_ - 1,
                                    [("wdn", gi % 2), ("act", jl, T)], [("ps", bi)])
                        S.dve(lambda h_, bi=bi, dc=dc, tsl=tsl: h_.tensor_tensor(out=h[:, dc, tsl], in0=ps[bi][:, :], in1=h[:, dc, tsl], op=ALU.add),
                              reads=[("ps", bi), ("h", dc, T)], writes=[("h", dc, T)])
            S.barrier()
            S.flush()

    def phase_final(self, raw=False):
        nc, S, ps = self.nc, self.S, self.ps
        with ExitStack() as es:
            ofm = self.sb(es, [128, NCH, TT], F32, "ofm")
            ost = [self.sb(es, [128, D], F32, "ost") for _ in range(3)]
            oc = 0
            for T in range(NT):
                tsl = slice(T * TT, (T + 1) * TT)
                if raw:
                    for c in range(NCH):
                        eng = S.act if c % 2 == 0 else S.dve
                        if c % 2 == 0:
                            S.act(lambda h_, c=c, tsl=tsl: h_.activation(out=ofm[:, c, :], in_=self.h[:, c, tsl], func=AF.Copy),
                                  reads=[("h", c, T)], writes=[("ofm", c)])
                        else:
                            S.dve(lambda h_, c=c, tsl=tsl: h_.tensor_copy(out=ofm[:, c, :], in_=self.h[:, c, tsl]),
                                  reads=[("h", c, T)], writes=[("ofm", c)])
                else:
                    self.rmsnorm_tile(T, R_FIN, ofm)
                for ts in range(4):
                    tt = T * 4 + ts
                    sl = oc % 3
                    oc += 1
                    for half in range(2):
                        bi = (2 * tt + half) % 4
                        bank = ps[bi]
                        for jj in range(4):
                            c = half * 4 + jj
                            S.pe(lambda h_, bank=bank, jj=jj, c=c, ts=ts: h_.transpose(out=bank[:, jj * 128:(jj + 1) * 128],
                                                                                         in_=ofm[:, c, ts * 128:(ts + 1) * 128], identity=self.identf[:]),
                                 reads=[("ofm", c), "identf"], writes=[("ps", bi)])
                        dst = ost[sl][:, half * 512:(half + 1) * 512]
                        if half == 0:
                            S.act(lambda h_, dst=dst, bank=bank: h_.activation(out=dst, in_=bank[:, :], func=AF.Copy),
                                  reads=[("ps", bi)], writes=[("ost", sl, half)])
                        else:
                            S.dve(lambda h_, dst=dst, bank=bank: h_.tensor_copy(out=dst, in_=bank[:, :]),
                                  reads=[("ps", bi)], writes=[("ost", sl, half)])
                    S.dma("sp", "ost%d" % sl, lambda h_, sl=sl, tt=tt: h_.dma_start(out=self.out[tt * 128:(tt + 1) * 128, :], in_=ost[sl][:]),
                          reads=[("ost", sl, 0), ("ost", sl, 1)])
            S.barrier()
            S.flush()

    def rmsnorm_tile(self, T, grow, ofm):
        S, ps = self.S, self.ps
        bi = 6 + T % 2
        bank = ps[bi]
        tsl = slice(T * TT, (T + 1) * TT)
        for c in range(NCH):
            sq = self.nsq[:, c % 2, :]
            S.act(lambda h, sq=sq, c=c: h.activation(out=sq, in_=self.h[:, c, tsl], func=AF.Square),
                  reads=[("h", c, T)], writes=[("nsq", c % 2)])
            self.mm(bank[:, :], self.onesb[:], sq, c == 0, c == NCH - 1, [("nsq", c % 2), "onesb"], [("ps", bi)])
        sd = self.nstd[:, T % 2, :]
        S.act(lambda h: h.activation(out=sd, in_=bank[:, :], func=AF.Sqrt, scale=1.0 / D, bias=NORM_EPS),
              reads=[("ps", bi)], writes=[("nstd", T % 2)])
        S.dve(lambda h: h.reciprocal(bank[:, :], sd), reads=[("nstd", T % 2)], writes=[("ps", bi)])
        for c in range(NCH):
            S.dve(lambda h, c=c: h.scalar_tensor_tensor(out=ofm[:, c, :], in0=self.h[:, c, tsl], scalar=self.prow(grow + c), in1=bank[:, :],
                                                        op0=ALU.mult, op1=ALU.mult),
                  reads=[("h", c, T), ("ps", bi), "ptab"], writes=[("ofm", c)])


def _pack_ptab(inp):
    f = lambda a: np.ascontiguousarray(np.asarray(a, dtype=np.float32)).reshape(-1, 128)
    parts = [
        f(inp["norm_mix_g"]), f(inp["norm_ffn_g"]), f(inp["final_norm_g"]), f(inp["conv_b_pw1"]),
        f(inp["conv_w_dw"]), f(inp["conv_b_dw"]), f(inp["conv_ln_g"]), f(inp["conv_ln_b"]), f(inp["conv_b_pw2"]),
        f(inp["ffn_w_dw"]), f(inp["ffn_b_dw"]),
    ]
    tab = np.concatenate(parts, axis=0)
    assert tab.shape[0] == 1368
    pad = np.zeros((R_TOT - tab.shape[0], 128), np.float32)
    return np.ascontiguousarray(np.concatenate([tab, pad], axis=0))


_NC_CACHE = {}


def _run(inputs, stop_after=None, trace=False):
    x = np.ascontiguousarray(np.asarray(inputs["x"], dtype=np.float32))
    B = x.shape[0]
    key = stop_after
    if key not in _NC_CACHE:
        _NC_CACHE[key] = Builder(stop_after).build()
    nc = _NC_CACHE[key]
    ptab = _pack_ptab(inputs)
    c = lambda k: np.ascontiguousarray(np.asarray(inputs[k], dtype=np.float32))
    shared = {
        "ptab_in": ptab, "w_qkv": c("attn_w_qkv"), "w_o": c("attn_w_o"), "w_pw1": c("conv_w_pw1"), "w_pw2": c("conv_w_pw2"),
        "w_up": c("ffn_w_up"), "w_down": c("ffn_w_down"),
    }
    in_maps = [dict(shared, x=x[b]) for b in range(B)]
    res = run_bass_kernel_spmd(nc, in_maps, core_ids=list(range(B)), trace=trace)
    out = np.stack([np.asarray(r["out"]) for r in res.results], axis=0).astype(np.float32)
    return out, res


def kernel(**inputs):
    out, _ = _run(inputs)
    return out
```

```python
import math
import numpy as np
from contextlib import ExitStack
import concourse.bass as bass
import concourse.mybir as mybir
from concourse.bass_utils import run_bass_kernel_spmd

F32 = mybir.dt.float32
BF16 = mybir.dt.bfloat16
I32 = mybir.dt.int32
ALU = mybir.AluOpType
AF = mybir.ActivationFunctionType
AX = mybir.AxisListType

D = 1024
SEQ = 2048
NCH = 8
NT = 4
TT = 512
H = 16
DH = 64
DFF = 2816
NFP = 22
DEPTH = 4
NEG = -30000.0
NORM_EPS = 1e-6
LN_EPS = 1e-5
CW = 31

R_MIX = 0
R_FFN = 32
R_FIN = 64
R_BPW1 = 72
R_WDW = 104
R_BDW = 600
R_LNG = 616
R_LNB = 632
R_BPW2 = 648
R_FWDW = 664
R_FBDW = 1192
R_TOT = 1408


class _Op:
    __slots__ = ("eng", "fn", "deps", "needs_inc", "semval", "grp", "gen", "pre")

    def __init__(self, eng, fn):
        self.eng = eng
        self.fn = fn
        self.deps = []
        self.needs_inc = False
        self.semval = None
        self.grp = None
        self.gen = 0
        self.pre = None


class _Grp:
    __slots__ = ("sem", "gens", "closed")

    def __init__(self, sem):
        self.sem = sem
        self.gens = [0]
        self.closed = False


class Sched:
    ENGS = ("pe", "act", "dve", "pool", "sp")

    def __init__(self, nc, es):
        self.nc = nc
        self.es = es
        self.q = {e: [] for e in self.ENGS}
        self.lastw = {}
        self.readers = {}
        self.esem = {e: es.enter_context(nc.semaphore("sem_" + e)) for e in ("pe", "act", "dve", "pool")}
        self.ecnt = {e: 0 for e in ("pe", "act", "dve", "pool")}
        self.seen = {e: {} for e in self.ENGS}
        self.groups = {}
        self.lastreal = {e: None for e in self.ENGS}
        self.nops = 0

    def _group(self, name):
        g = self.groups.get(name)
        if g is None:
            g = _Grp(self.es.enter_context(self.nc.semaphore("dg_" + name)))
            self.groups[name] = g
        return g

    def add(self, eng, fn, reads=(), writes=(), dma=None):
        op = _Op(eng, fn)
        deps = {}
        for k in reads:
            w = self.lastw.get(k)
            if w is not None:
                deps[id(w)] = w
            if isinstance(k, tuple) and k[0] == "ps":
                rd = self.readers.get(k)
                if rd:
                    for rk_, r in rd.items():
                        if rk_ != eng:
                            deps[id(r)] = r
        for k in writes:
            w = self.lastw.get(k)
            if w is not None:
                deps[id(w)] = w
            rd = self.readers.get(k)
            if rd:
                for r in rd.values():
                    deps[id(r)] = r
        if dma is not None:
            g = self._group(dma)
            op.grp = g
            if g.closed:
                op.pre = [(g, len(g.gens) - 1)]
                g.gens.append(g.gens[-1])
                g.closed = False
            g.gens[-1] += 16
            op.gen = len(g.gens) - 1
        for d in deps.values():
            if eng == "pe" and d.eng == "pe" and d.grp is None:
                continue
            if op.grp is not None and d.grp is op.grp:
                continue
            op.deps.append(d)
            d.needs_inc = True
            if d.grp is not None:
                d.grp.closed = True
        rk = eng if dma is None else ("dma", id(op))
        for k in reads:
            self.readers.setdefault(k, {})[rk] = op
        for k in writes:
            self.lastw[k] = op
            self.readers[k] = {}
        self.q[eng].append(op)
        if dma is None:
            self.lastreal[eng] = op
        self.nops += 1
        return op

    def pe(self, fn, reads=(), writes=()):
        return self.add("pe", fn, reads, writes)

    def act(self, fn, reads=(), writes=()):
        return self.add("act", fn, reads, writes)

    def dve(self, fn, reads=(), writes=()):
        return self.add("dve", fn, reads, writes)

    def pool(self, fn, reads=(), writes=()):
        return self.add("pool", fn, reads, writes)

    def dma(self, queue, group, fn, reads=(), writes=()):
        return self.add(queue, fn, reads, writes, dma=group)

    def barrier(self):
        lasts = [op for op in self.lastreal.values() if op is not None and op.grp is None]
        gl = [(g, len(g.gens) - 1) for g in self.groups.values() if g.gens[-1] > 0]
        for g, _ in gl:
            g.closed = True
        for e in self.ENGS:
            op = _Op(e, None)
            for d in lasts:
                if not (e == "pe" and d.eng == "pe"):
                    op.deps.append(d)
                    d.needs_inc = True
            op.pre = list(gl)
            self.q[e].append(op)
        self.lastw = {}
        self.readers = {}

    def _assign(self):
        for e in ("pe", "act", "dve", "pool"):
            c = self.ecnt[e]
            for op in self.q[e]:
                if op.grp is None and op.fn is not None and op.needs_inc and op.semval is None:
                    c += 1
                    op.semval = c
            self.ecnt[e] = c

    def _emit_engine(self, e, h):
        seen = self.seen[e]

        def wait(sem, val):
            key = id(sem)
            if seen.get(key, 0) < val:
                h.wait_ge(sem, val)
                seen[key] = val

        for op in self.q[e]:
            if op.pre is not None:
                for g, gi in op.pre:
                    wait(g.sem, g.gens[gi])
            for d in op.deps:
                if d.grp is not None:
                    wait(d.grp.sem, d.grp.gens[d.gen])
                else:
                    wait(self.esem[d.eng], d.semval)
            if op.fn is not None:
                ins = op.fn(h)
                if op.grp is not None:
                    ins.then_inc(op.grp.sem, 16)
                elif op.needs_inc:
                    ins.then_inc(self.esem[e], 1)
        self.q[e] = []
        self.lastreal[e] = None

    def flush(self):
        self._assign()
        with self.nc.Block() as block:
            @block.tensor
            def _(h):
                self._emit_engine("pe", h)

            @block.scalar
            def _(h):
                self._emit_engine("act", h)

            @block.vector
            def _(h):
                self._emit_engine("dve", h)

            @block.gpsimd
            def _(h):
                self._emit_engine("pool", h)

            @block.sync
            def _(h):
                self._emit_engine("sp", h)


class Builder:
    def __init__(self, stop_after=None):
        self.stop_after = stop_after
        self.nc = bass.Bass("TRN2", target_bir_lowering=False)
        nc = self.nc
        dt = nc.dram_tensor
        self.x = dt("x", [SEQ, D], F32, kind="ExternalInput").ap()
        self.ptab_in = dt("ptab_in", [R_TOT, 128], F32, kind="ExternalInput").ap()
        self.w_qkv = dt("w_qkv", [2, D, 3 * D], F32, kind="ExternalInput").ap()
        self.w_o = dt("w_o", [2, D, D], F32, kind="ExternalInput").ap()
        self.w_pw1 = dt("w_pw1", [2, D, 2 * D], F32, kind="ExternalInput").ap()
        self.w_pw2 = dt("w_pw2", [2, D, D], F32, kind="ExternalInput").ap()
        self.w_up = dt("w_up", [DEPTH, D, 2 * DFF], F32, kind="ExternalInput").ap()
        self.w_down = dt("w_down", [DEPTH, DFF, D], F32, kind="ExternalInput").ap()
        self.out = dt("out", [SEQ, D], F32, kind="ExternalOutput").ap()
        self._uid = 0

    def sb(self, es, shape, dtype, name=None):
        self._uid += 1
        return es.enter_context(self.nc.sbuf_tensor("%s_%d" % (name or "t", self._uid), shape, dtype))

    def mm(self, out, lhsT, rhs, start, stop, reads, writes):
        self.S.pe(lambda h: h.matmul(out, lhsT=lhsT, rhs=rhs, start=start, stop=stop), reads, writes)

    def prow(self, r):
        return self.ptab[:, r:r + 1]

    def build(self):
        nc = self.nc
        with ExitStack() as es:
            self.S = S = Sched(nc, es)
            self.ps = [es.enter_context(nc.psum_tensor("ps%d" % i, [128, TT], F32)) for i in range(8)]
            self.h = self.sb(es, [128, NCH, SEQ], F32, "h")
            self.hn = self.sb(es, [128, NCH, SEQ], BF16, "hn")
            self.cosT = self.sb(es, [128, SEQ], BF16, "cosT")
            self.sinT = self.sb(es, [128, SEQ], BF16, "sinT")
            self.ptab = self.sb(es, [128, R_TOT], F32, "ptab")
            self.identf = self.sb(es, [128, 128], F32, "identf")
            self.identb = self.sb(es, [128, 128], BF16, "identb")
            self.onesb = self.sb(es, [128, 128], BF16, "onesb")
            self.maskT = self.sb(es, [128, 128], BF16, "maskT")
            self.prot = self.sb(es, [128, 128], BF16, "prot")
            self.ind = [self.sb(es, [128, 8 * 128], BF16, "ind") for _ in range(2)]
            self.nsq = self.sb(es, [128, 2, TT], BF16, "nsq")
            self.nstd = self.sb(es, [128, 2, TT], F32, "nstd")

            self.phase_setup()
            stop = self.stop_after
            done = stop == "load"
            for i in range(DEPTH):
                if done:
                    break
                j = i // 2
                self.rmsnorm(R_MIX + i * 8)
                if i % 2 == 0:
                    self.phase_attention(j)
                else:
                    self.phase_conformer(j)
                if stop is not None and stop.split(":")[0] == "mix%d" % i:
                    done = True
                    break
                self.rmsnorm(R_FFN + i * 8)
                fsub = stop.split(":")[1] if (stop and ":" in stop and stop.startswith("ffn")) else None
                if fsub != "norm":
                    self.phase_ffn(i)
                if stop is not None and stop.split(":")[0] == "ffn%d" % i:
                    done = True
                    break
            self.phase_final(raw=(stop is not None))
        return nc

    def phase_setup(self):
        nc, S = self.nc, self.S
        ps = self.ps
        with ExitStack() as es:
            xs = [self.sb(es, [128, D], F32, "xs") for _ in range(3)]
            pst = [self.sb(es, [128, 128], F32, "pst") for _ in range(2)]
            pi_i = self.sb(es, [128, 1], I32, "pi_i")
            pm_i = self.sb(es, [128, 2], I32, "pm_i")
            sgn = self.sb(es, [128, 2], F32, "sgn")
            invrow = self.sb(es, [1, 128], F32, "invrow")
            invf = self.sb(es, [128, 1], F32, "invf")
            pos_i = self.sb(es, [128, SEQ], I32, "pos_i")
            ang = self.sb(es, [128, SEQ], F32, "ang")
            ta = self.sb(es, [128, SEQ], F32, "ta")
            tb = self.sb(es, [128, SEQ], F32, "tb")
            tc = self.sb(es, [128, SEQ], F32, "tc")

            identf, identb, onesb, maskT, prot, ind = self.identf, self.identb, self.onesb, self.maskT, self.prot, self.ind
            S.pool(lambda h: h.memset(identf[:], 1.0), writes=["identf"])
            S.pool(lambda h: h.affine_select(out=identf[:], in_=identf[:], pattern=[[-1, 128]], compare_op=ALU.is_equal,
                                             fill=0.0, base=0, channel_multiplier=1), reads=["identf"], writes=["identf"])
            S.dve(lambda h: h.tensor_copy(out=identb[:], in_=identf[:]), reads=["identf"], writes=["identb"])
            S.pool(lambda h: h.memset(onesb[:], 1.0), writes=["onesb"])
            S.pool(lambda h: h.memset(maskT[:], 0.0), writes=["maskT"])
            S.pool(lambda h: h.affine_select(out=maskT[:], in_=maskT[:], pattern=[[1, 128]], compare_op=ALU.is_ge,
                                             fill=NEG, base=0, channel_multiplier=-1), reads=["maskT"], writes=["maskT"])
            for (dst, src) in ((0, 32), (32, 0), (64, 96), (96, 64)):
                S.dve(lambda h, dst=dst, src=src: h.tensor_copy(out=prot[:, dst:dst + 32], in_=identb[:, src:src + 32]),
                      reads=["identb"], writes=["prot"])
            for jj in range(2):
                S.pool(lambda h, jj=jj: h.memset(ind[jj][:], 0.0), writes=["ind"])
                S.pool(lambda h, jj=jj: h.memset(ind[jj][64 * jj:64 * jj + 64, :], 1.0), reads=["ind"], writes=["ind"])
                S.pool(lambda h, jj=jj: h.affine_select(out=ind[jj][64 * jj:64 * jj + 64, :], in_=ind[jj][64 * jj:64 * jj + 64, :],
                                                        pattern=[[1, 8], [0, 128]], compare_op=ALU.is_equal, fill=0.0,
                                                        base=0, channel_multiplier=-1), reads=["ind"], writes=["ind"])
            for r in range(R_TOT // 128):
                st = pst[r % 2]
                S.dma("sp", "pst%d" % (r % 2), lambda h, r=r, st=st: h.dma_start(out=st[:], in_=self.ptab_in[r * 128:(r + 1) * 128, :]),
                      writes=[("pst", r % 2)])
                bank = ps[r % 2]
                S.pe(lambda h, st=st, bank=bank: h.transpose(out=bank[:, 0:128], in_=st[:], identity=identf[:]),
                     reads=[("pst", r % 2), "identf"], writes=[("ps", r % 2)])
                S.act(lambda h, r=r, bank=bank: h.activation(out=self.ptab[:, r * 128:(r + 1) * 128], in_=bank[:, 0:128], func=AF.Copy),
                      reads=[("ps", r % 2)], writes=["ptab"])
            inv = (np.float32(1.0) / np.power(np.float32(10000.0), np.arange(0, DH, 2, dtype=np.float32) / np.float32(DH))).astype(np.float32)
            irv = invrow[:].rearrange("o (a b) -> o a b", b=32)
            for i in range(32):
                S.dve(lambda h, i=i: h.memset(irv[:, :, i:i + 1], float(inv[i])), writes=["invrow"])
            S.pe(lambda h: h.transpose(out=ps[2][:, 0:1], in_=invrow[:], identity=identf[0:1, 0:1]),
                 reads=["invrow", "identf"], writes=[("ps", 2)])
            S.act(lambda h: h.activation(out=invf[:], in_=ps[2][:, 0:1], func=AF.Copy), reads=[("ps", 2)], writes=["invf"])
            S.pool(lambda h: h.iota(pi_i[:], pattern=[[0, 1]], base=0, channel_multiplier=1), writes=["pi_i"])
            S.dve(lambda h: h.tensor_scalar(out=pm_i[:, 0:1], in0=pi_i[:], scalar1=32, scalar2=None, op0=ALU.bitwise_and),
                  reads=["pi_i"], writes=["pm_i"])
            S.dve(lambda h: h.tensor_copy(out=sgn[:, 0:1], in_=pm_i[:, 0:1]), reads=["pm_i"], writes=["sgn0"])
            S.dve(lambda h: h.tensor_scalar(out=sgn[:, 1:2], in0=sgn[:, 0:1], scalar1=1.0 / 16.0, scalar2=-1.0, op0=ALU.mult, op1=ALU.add),
                  reads=["sgn0"], writes=["sgn1"])
            S.pool(lambda h: h.iota(pos_i[:], pattern=[[1, SEQ]], base=0, channel_multiplier=0), writes=["pos_i"])
            S.dve(lambda h: h.tensor_copy(out=ta[:], in_=pos_i[:]), reads=["pos_i"], writes=["ta"])
            S.dve(lambda h: h.tensor_scalar(out=ang[:], in0=ta[:], scalar1=invf[:, 0:1], scalar2=None, op0=ALU.mult),
                  reads=["ta", "invf"], writes=["ang"])
            TWO_PI = 2.0 * math.pi
            C1 = 6.28125
            C2 = TWO_PI - C1
            MAGIC = 12582912.0
            LIM = 3.1415925
            S.dve(lambda h: h.tensor_scalar(out=ta[:], in0=ang[:], scalar1=1.0 / TWO_PI, scalar2=None, op0=ALU.mult),
                  reads=["ang", "ta"], writes=["ta"])
            S.dve(lambda h: h.tensor_scalar(out=tb[:], in0=ta[:], scalar1=MAGIC, scalar2=MAGIC, op0=ALU.add, op1=ALU.subtract),
                  reads=["ta"], writes=["tb"])
            S.dve(lambda h: h.scalar_tensor_tensor(out=ta[:], in0=tb[:], scalar=-C1, in1=ang[:], op0=ALU.mult, op1=ALU.add),
                  reads=["tb", "ang", "ta"], writes=["ta"])
            S.dve(lambda h: h.scalar_tensor_tensor(out=tc[:], in0=tb[:], scalar=-C2, in1=ta[:], op0=ALU.mult, op1=ALU.add),
                  reads=["tb", "ta"], writes=["tc"])
            S.dve(lambda h: h.tensor_scalar(out=ta[:], in0=tc[:], scalar1=LIM, scalar2=-LIM, op0=ALU.min, op1=ALU.max),
                  reads=["tc", "ta"], writes=["ta"])
            S.act(lambda h: h.activation(out=tb[:], in_=ta[:], func=AF.Sin), reads=["ta", "tb"], writes=["tb"])
            S.dve(lambda h: h.tensor_scalar(out=self.sinT[:], in0=tb[:], scalar1=sgn[:, 1:2], scalar2=None, op0=ALU.mult),
                  reads=["tb", "sgn1"], writes=["sinT"])
            S.dve(lambda h: h.tensor_scalar(out=ang[:], in0=tc[:], scalar1=math.pi / 2, scalar2=None, op0=ALU.add),
                  reads=["tc", "ang"], writes=["ang"])
            S.dve(lambda h: h.tensor_scalar(out=ta[:], in0=ang[:], scalar1=math.pi, scalar2=None, op0=ALU.is_gt),
                  reads=["ang", "ta"], writes=["ta"])
            S.dve(lambda h: h.scalar_tensor_tensor(out=tc[:], in0=ta[:], scalar=-TWO_PI, in1=ang[:], op0=ALU.mult, op1=ALU.add),
                  reads=["ta", "ang", "tc"], writes=["tc"])
            S.dve(lambda h: h.tensor_scalar(out=ta[:], in0=tc[:], scalar1=LIM, scalar2=-LIM, op0=ALU.min, op1=ALU.max),
                  reads=["tc", "ta"], writes=["ta"])
            S.act(lambda h: h.activation(out=self.cosT[:], in_=ta[:], func=AF.Sin), reads=["ta"], writes=["cosT"])
            for tt in range(16):
                sl = tt % 3
                S.dma("sp", "xs%d" % sl, lambda h, tt=tt, sl=sl: h.dma_start(out=xs[sl][:], in_=self.x[tt * 128:(tt + 1) * 128, :]),
                      writes=[("xs", sl)])
                T = tt // 4
                for half in range(2):
                    bi = 4 + (2 * tt + half) % 4
                    bank = ps[bi]
                    for jj in range(4):
                        c = half * 4 + jj
                        S.pe(lambda h, bank=bank, jj=jj, c=c, sl=sl: h.transpose(out=bank[:, jj * 128:(jj + 1) * 128],
                                                                               in_=xs[sl][:, c * 128:(c + 1) * 128], identity=identf[:]),
                             reads=[("xs", sl), "identf"], writes=[("ps", bi)])
                    dst = self.h[:, half * 4:half * 4 + 4, tt * 128:(tt + 1) * 128]
                    src = bank[:, :].rearrange("p (a b) -> p a b", b=128)
                    wr = [("h", half * 4 + jj, T) for jj in range(4)]
                    if (2 * tt + half) % 2 == 0:
                        S.act(lambda h, dst=dst, src=src: h.activation(out=dst, in_=src, func=AF.Copy), reads=[("ps", bi)], writes=wr)
                    else:
                        S.dve(lambda h, dst=dst, src=src: h.tensor_copy(out=dst, in_=src), reads=[("ps", bi)], writes=wr)
            S.barrier()
            S.flush()

    def rmsnorm(self, grow, out_fn=None):
        S, ps = self.S, self.ps
        for T in range(NT):
            bi = 6 + T % 2
            bank = ps[bi]
            tsl = slice(T * TT, (T + 1) * TT)
            for c in range(NCH):
                sq = self.nsq[:, c % 2, :]
                S.act(lambda h, sq=sq, c=c, tsl=tsl: h.activation(out=sq, in_=self.h[:, c, tsl], func=AF.Square),
                      reads=[("h", c, T)], writes=[("nsq", c % 2)])
                self.mm(bank[:, :], self.onesb[:], sq, c == 0, c == NCH - 1, [("nsq", c % 2), "onesb"], [("ps", bi)])
            sd = self.nstd[:, T % 2, :]
            S.act(lambda h, sd=sd, bank=bank: h.activation(out=sd, in_=bank[:, :], func=AF.Sqrt, scale=1.0 / D, bias=NORM_EPS),
                  reads=[("ps", bi)], writes=[("nstd", T % 2)])
            S.dve(lambda h, sd=sd, bank=bank: h.reciprocal(bank[:, :], sd), reads=[("nstd", T % 2)], writes=[("ps", bi)])
            for c in range(NCH):
                if out_fn is None:
                    dst = self.hn[:, c, tsl]
                    wr = [("hn", c, T)]
                else:
                    dst, wr = out_fn(c, T)
                S.dve(lambda h, dst=dst, c=c, tsl=tsl, bank=bank: h.scalar_tensor_tensor(
                    out=dst, in0=self.h[:, c, tsl], scalar=self.prow(grow + c), in1=bank[:, :], op0=ALU.mult, op1=ALU.mult),
                    reads=[("h", c, T), ("ps", bi), "ptab"], writes=wr)

    def load_w(self, slot_ap, src_ap, key, group):
        self.S.dma("pool", group, lambda h: h.dma_start(out=slot_ap, in_=src_ap), writes=[key])

    def phase_attention(self, j):
        nc, S, ps = self.nc, self.S, self.ps
        hn, h = self.hn, self.h
        with ExitStack() as es:
            qT = self.sb(es, [128, 2, SEQ], BF16, "qT")
            kT = self.sb(es, [128, 2, 2, SEQ], BF16, "kT")
            Vt = self.sb(es, [128, 16, 4, 128], BF16, "Vt")
            NW = 5
            wsl = [self.sb(es, [128, 2048], BF16, "wsl") for _ in range(NW)]
            biasT = self.sb(es, [128, 2, 1024], BF16, "biasT")
            kms = self.sb(es, [128, 4, 8], F32, "kms")
            kmT = self.sb(es, [128, 4, 8], BF16, "kmT")
            gsb = self.sb(es, [128, 256], F32, "gsb")
            cmp_ = self.sb(es, [128, 2, 4 * 49], BF16, "cmp")
            rank = self.sb(es, [128, 2, 32], F32, "rank")
            btok = self.sb(es, [128, 4, 256], BF16, "btok")
            pT = [self.sb(es, [128, TT], BF16, "pT") for _ in range(4)]
            rec = [self.sb(es, [128, TT], F32, "rec") for _ in range(2)]
            qs = [self.sb(es, [128, TT], BF16, "qs") for _ in range(2)]
            t1 = [self.sb(es, [128, TT], F32, "t1") for _ in range(2)]
            t2 = [self.sb(es, [128, TT], F32, "t2") for _ in range(2)]

            wcount = [0]

            def next_slot():
                i = wcount[0] % NW
                wcount[0] += 1
                return i

            S.pool(lambda h_: h_.memset(Vt[:, :, 0:4:2, 64:128], 1.0), writes=[("Vt1", 0)])
            S.pool(lambda h_: h_.memset(Vt[:, :, 1:4:2, 0:64], 1.0), writes=[("Vt1", 1)])
            S.pool(lambda h_: h_.memset(biasT[:], 0.0), writes=[("biasT", a, u) for a in range(2) for u in range(2)])
            S.pool(lambda h_: h_.memset(kT[64:128, :, 0, :], 0.0), writes=[("kz", 0)])
            S.pool(lambda h_: h_.memset(kT[0:64, :, 1, :], 0.0), writes=[("kz", 1)])

            wq_src = self.w_qkv[j].rearrange("(c p) f -> p c f", p=128)
            wo_src = self.w_o[j].rearrange("(c p) f -> p c f", p=128)

            def load_group(g):
                sl = {}
                for nm, off in (("q", 0), ("k", D), ("v", 2 * D)):
                    i = next_slot()
                    sl[nm] = i
                    self.load_w(wsl[i][:, :].rearrange("p (c f) -> p c f", f=256), wq_src[:, :, off + g * 256: off + (g + 1) * 256],
                                ("wsl", i), "aw%d" % i)
                i = next_slot()
                sl["o"] = i
                self.load_w(wsl[i][:, :].rearrange("p (c f) -> p c f", f=1024), wo_src[:, 2 * g:2 * g + 2, :], ("wsl", i), "aw%d" % i)
                return sl

            pending = load_group(0)
            rope_i = [0]
            sub = self.stop_after.split(":")[1] if (self.stop_after and ":" in self.stop_after) else None
            for g in range(4):
                if sub is not None and g > 0:
                    break
                sl = pending
                wq = wsl[sl["q"]][:, :].rearrange("p (c f) -> p c f", f=256)
                wk = wsl[sl["k"]][:, :].rearrange("p (c f) -> p c f", f=256)
                wv = wsl[sl["v"]][:, :].rearrange("p (c f) -> p c f", f=256)
                wo = wsl[sl["o"]][:, :].rearrange("p (c f) -> p c f", f=1024)
                units = [(which, cc, T) for which in ("q", "k") for cc in range(2) for T in range(NT)]
                pend = []

                def rope_a(which, cc, T):
                    wsrc = wq if which == "q" else wk
                    wkey = ("wsl", sl[which])
                    scale = DH ** -0.5 if which == "q" else 1.0
                    ri = rope_i[0]
                    rope_i[0] += 1
                    ba = ri % 2
                    A = ps[ba]
                    tsl = slice(T * TT, (T + 1) * TT)
                    for c in range(NCH):
                        self.mm(A[:, :], wsrc[:, c, cc * 128:(cc + 1) * 128], hn[:, c, tsl], c == 0, c == NCH - 1,
                                [wkey, ("hn", c, T)], [("ps", ba)])
                    q_s = qs[ri % 2]
                    S.act(lambda h_, q_s=q_s, A=A, scale=scale: h_.activation(out=q_s[:], in_=A[:, :], func=AF.Copy, scale=scale),
                          reads=[("ps", ba)], writes=[("qs", ri % 2)])
                    return (which, cc, T, ri, scale)

                def rope_b(which, cc, T, ri, scale):
                    ba, bb = ri % 2, 2 + ri % 2
                    A, B = ps[ba], ps[bb]
                    tsl = slice(T * TT, (T + 1) * TT)
                    q_s, t1_, t2_ = qs[ri % 2], t1[ri % 2], t2[ri % 2]
                    self.mm(B[:, :], self.prot[:], q_s[:], True, True, [("qs", ri % 2), "prot"], [("ps", bb)])
                    S.dve(lambda h_, t1_=t1_, A=A, scale=scale, tsl=tsl: h_.scalar_tensor_tensor(
                        out=t1_[:], in0=A[:, :], scalar=scale, in1=self.cosT[:, tsl], op0=ALU.mult, op1=ALU.mult),
                        reads=[("ps", ba), "cosT"], writes=[("t1", ri % 2)])
                    S.dve(lambda h_, t2_=t2_, B=B, tsl=tsl: h_.tensor_tensor(out=t2_[:], in0=B[:, :], in1=self.sinT[:, tsl], op=ALU.mult),
                          reads=[("ps", bb), "sinT"], writes=[("t2", ri % 2)])
                    if which == "q":
                        dst = qT[:, cc, tsl]
                        wr = [("q", cc, T, 0), ("q", cc, T, 1)]
                        S.pool(lambda h_, dst=dst, t1_=t1_, t2_=t2_: h_.tensor_tensor(out=dst, in0=t1_[:], in1=t2_[:], op=ALU.add),
                               reads=[("t1", ri % 2), ("t2", ri % 2)], writes=wr)
                    else:
                        tk_ = qs[ri % 2]
                        S.pool(lambda h_, tk_=tk_, t1_=t1_, t2_=t2_: h_.tensor_tensor(out=tk_[:], in0=t1_[:], in1=t2_[:], op=ALU.add),
                               reads=[("t1", ri % 2), ("t2", ri % 2), ("qs", ri % 2)], writes=[("qs", ri % 2)])
                        for hp in range(2):
                            prt = slice(hp * 64, (hp + 1) * 64)
                            S.act(lambda h_, prt=prt, hp=hp, cc=cc, tsl=tsl, tk_=tk_: h_.activation(
                                out=kT[prt, cc, hp, tsl], in_=tk_[prt, :], func=AF.Copy),
                                reads=[("qs", ri % 2)], writes=[("k", cc, T, hp)])

                def kmean(ch):
                    cc, hp = ch // 2, ch % 2
                    S.dve(lambda h_: h_.tensor_reduce(out=kms[:, ch, :], in_=kT[:, cc, hp, :].rearrange("p (n k) -> p n k", k=256),
                                                      axis=AX.X, op=ALU.add),
                          reads=[("k", cc, T, hp) for T in range(NT)] + [("kz", hp)], writes=[("kms", ch)])
                    S.dve(lambda h_: h_.tensor_scalar(out=kmT[:, ch, :], in0=kms[:, ch, :], scalar1=1.0 / 256.0, scalar2=None, op0=ALU.mult),
                          reads=[("kms", ch)], writes=[("kmT", ch)])

                for ui_, u_ in enumerate(units):
                    st_ = rope_a(*u_)
                    if pend:
                        rope_b(*pend.pop(0))
                    pend.append(st_)
                    if ui_ == 3 * NT and sub is None:
                        kmean(0)
                        kmean(1)
                while pend:
                    rope_b(*pend.pop(0))
                if sub == "qk":
                    break
                if sub == "v":
                    break
                def gate_matmuls():
                    gbank = ps[6]
                    for qt in range(8, 16):
                        T = qt // 4
                        for hl in range(4):
                            cc, hp = hl // 2, hl % 2
                            col = (qt - 8) * 32 + hl * 8
                            self.mm(gbank[:, col:col + 8], qT[:, cc, qt * 128:(qt + 1) * 128],
                                    kmT[:, hl, :], True, True,
                                    [("q", cc, T, 0), ("q", cc, T, 1), ("kmT", hl)], [("ps", 6)])
                    S.act(lambda h_: h_.activation(out=gsb[:, :], in_=gbank[:, 0:256], func=AF.Copy),
                          reads=[("ps", 6)], writes=["gsb"])

                def gate_chain(qt):
                    qb = qt // 2
                    sI = qt % 4
                    c2 = qt % 2
                    g3 = gsb[:, (qt - 8) * 32:(qt - 7) * 32].rearrange("p (a n) -> p a n", n=8)[:, :, 0:qb]
                    in0 = g3.unsqueeze(2).broadcast_to([128, 4, qb, qb])
                    in1 = g3.unsqueeze(3).broadcast_to([128, 4, qb, qb])
                    cm = cmp_[:, c2, 0:4 * qb * qb].rearrange("p (a n m) -> p a n m", a=4, n=qb)
                    S.dve(lambda h_, cm=cm, in0=in0, in1=in1: h_.tensor_tensor(out=cm, in0=in0, in1=in1, op=ALU.is_gt),
                          reads=["gsb"], writes=[("cmp", c2)])
                    rk = rank[:, c2, :].rearrange("p (a n) -> p a n", n=8)[:, :, 0:qb]
                    S.dve(lambda h_, rk=rk, cm=cm: h_.tensor_reduce(out=rk, in_=cm, axis=AX.X, op=ALU.add),
                          reads=[("cmp", c2)], writes=[("rank", c2)])
                    S.pool(lambda h_, sI=sI: h_.memset(btok[:, sI, :], 0.0), writes=[("btok", sI)])
                    bo = btok[:, sI, :].rearrange("p (a b n) -> p a b n", a=2, b=2)[:, :, :, 0:qb]
                    rk4 = rank[:, c2, :].rearrange("p (a b n) -> p a b n", a=2, b=2)[:, :, :, 0:qb]
                    S.dve(lambda h_, bo=bo, rk4=rk4: h_.tensor_scalar(out=bo, in0=rk4, scalar1=2.5, scalar2=NEG, op0=ALU.is_gt, op1=ALU.mult),
                          reads=[("rank", c2)], writes=[("btok", sI)])

                def gate_transposes(q4):
                    for a in range(2):
                        bti = 2 + a
                        for qq in range(4):
                            qt = 8 + 4 * q4 + qq
                            sI = qt % 4
                            self.mm(ps[bti][:, qq * 128:(qq + 1) * 128], btok[:, sI, a * 128:(a + 1) * 128], self.identb[:], True, True,
                                    [("btok", sI), "identb"], [("ps", bti)])
                        S.act(lambda h_, a=a, q4=q4, bti=bti: h_.activation(out=biasT[:, a, q4 * TT:(q4 + 1) * TT], in_=ps[bti][:, :], func=AF.Copy),
                              reads=[("ps", bti)], writes=[("biasT", a, q4)])
                def v_proj(tp_lo, tp_hi):
                  for tp in range(tp_lo, tp_hi):
                      bi = 4 + tp % 2
                      bank = ps[bi]
                      for u in range(2):
                          tt = 2 * tp + u
                          T = tt // 4
                          for c in range(NCH):
                              self.mm(bank[:, u * 256:(u + 1) * 256], hn[:, c, tt * 128:(tt + 1) * 128], wv[:, c, :], c == 0, c == NCH - 1,
                                      [("wsl", sl["v"]), ("hn", c, T)], [("ps", bi)])
                      src = bank[:, :].rearrange("p (u a b e) -> p u a b e", u=2, a=2, b=2)
                      S.act(lambda h_, src=src, tp=tp: h_.activation(out=Vt[:, 2 * tp:2 * tp + 2, 0:4:2, 0:64], in_=src[:, :, :, 0, :], func=AF.Copy),
                            reads=[("ps", bi)], writes=[("Vt", 2 * tp, 0), ("Vt", 2 * tp + 1, 0)])
                      S.dve(lambda h_, src=src, tp=tp: h_.tensor_copy(out=Vt[:, 2 * tp:2 * tp + 2, 1:4:2, 64:128], in_=src[:, :, :, 1, :]),
                            reads=[("ps", bi)], writes=[("Vt", 2 * tp, 1), ("Vt", 2 * tp + 1, 1)])
                v_proj(0, 1)
                kmean(2)
                v_proj(1, 2)
                kmean(3)
                v_proj(2, 4)
                gate_matmuls()
                for qt in range(8, 12):
                    gate_chain(qt)
                v_proj(4, 6)
                gate_transposes(0)
                for qt in range(12, 16):
                    gate_chain(qt)
                v_proj(6, 8)
                gate_transposes(1)
                if g + 1 < 4 and sub is None:
                    pending = load_group(g + 1)
                if sub == "gate":
                    break
                iters = []
                itc = 0
                for cc in range(2):
                    for T in range(NT):
                        nk = 4 * T + 4
                        ob = [4 + 2 * (itc % 2), 5 + 2 * (itc % 2)]
                        itc += 1
                        for kt in range(nk):
                            for hp in range(2):
                                iters.append((cc, T, kt, hp, nk, ob[hp]))
                LAG = 3

                def stage_a(n_, cc, T, kt, hp, nk, obk):
                    nb = kt // 2
                    q_lo = max(0, kt - 4 * T) * 128
                    qsl = slice(q_lo, TT)
                    Tk = kt // 4
                    sbi = n_ % 4
                    sbk = ps[sbi]
                    need_bias = (T >= 2 and kt < 4 * T + 2)
                    need_mask = kt >= 4 * T
                    self.mm(sbk[:, qsl], kT[:, cc, hp, kt * 128:(kt + 1) * 128], qT[:, cc, T * TT + q_lo:(T + 1) * TT],
                            True, not (need_bias or need_mask),
                            [("k", cc, Tk, hp), ("kz", hp), ("q", cc, T, 0), ("q", cc, T, 1)], [("ps", sbi)])
                    if need_bias:
                        self.mm(sbk[:, qsl], self.ind[hp][:, nb * 128:(nb + 1) * 128],
                                biasT[:, cc, (T - 2) * TT + q_lo:(T - 1) * TT],
                                False, not need_mask, ["ind", ("biasT", cc, T - 2)], [("ps", sbi)])
                    if need_mask:
                        self.mm(sbk[:, q_lo:q_lo + 128], self.identb[:], self.maskT[:], False, True,
                                ["identb", "maskT"], [("ps", sbi)])

                def stage_bc(n_, cc, T, kt, hp, nk, obk):
                    hl = 2 * cc + hp
                    q_lo = max(0, kt - 4 * T) * 128
                    qsl = slice(q_lo, TT)
                    sbi = n_ % 4
                    sbk = ps[sbi]
                    pi = n_ % 4
                    S.act(lambda h_, pi=pi, sbk=sbk, qsl=qsl: h_.activation(out=pT[pi][:, qsl], in_=sbk[:, qsl], func=AF.Exp),
                          reads=[("ps", sbi)], writes=[("pT", pi)])
                    self.mm(ps[obk][:, qsl], Vt[:, kt, hl, :], pT[pi][:, qsl], kt == 0, kt == nk - 1,
                            [("pT", pi), ("Vt", kt, hp), ("Vt1", hp)], [("ps", obk)])
                    if kt == nk - 1:
                        o = ps[obk]
                        num = slice(hp * 64, (hp + 1) * 64)
                        den = slice((1 - hp) * 64, (2 - hp) * 64)
                        rc = rec[hp]
                        S.dve(lambda h_, rc=rc, o=o, den=den: h_.reciprocal(rc[den, :], o[den, :]),
                              reads=[("ps", obk)], writes=[("rec", hp)])
                        S.dve(lambda h_, rc=rc, o=o, den=den, num=num, cc=cc, T=T: h_.tensor_tensor(
                            out=qT[num, cc, T * TT:(T + 1) * TT], in0=o[num, :], in1=rc[den, :], op=ALU.mult),
                            reads=[("ps", obk), ("rec", hp)], writes=[("q", cc, T, hp)])

                NI = len(iters)
                for n_ in range(NI + LAG):
                    if n_ < NI:
                        stage_a(n_, *iters[n_])
                    m_ = n_ - LAG
                    if m_ >= 0:
                        stage_bc(m_, *iters[m_])
                if sub == "core":
                    break
                for T in range(NT):
                    tsl = slice(T * TT, (T + 1) * TT)
                    for dc in range(NCH):
                        bi = 4 + (T * NCH + dc) % 4
                        for cc in range(2):
                            self.mm(ps[bi][:, :], wo[:, cc, dc * 128:(dc + 1) * 128], qT[:, cc, tsl], cc == 0, cc == 1,
                                    [("wsl", sl["o"]), ("q", cc, T, 0), ("q", cc, T, 1)], [("ps", bi)])
                        S.dve(lambda h_, bi=bi, dc=dc, tsl=tsl: h_.tensor_tensor(out=h[:, dc, tsl], in0=ps[bi][:, :], in1=h[:, dc, tsl], op=ALU.add),
                              reads=[("ps", bi), ("h", dc, T)], writes=[("h", dc, T)])
            S.barrier()
            S.flush()

    def phase_conformer(self, j):
        nc, S, ps = self.nc, self.S, self.ps
        hn, h = self.hn, self.h
        PADL = 32
        with ExitStack() as es:
            glu = self.sb(es, [128, NCH, PADL + SEQ], BF16, "glu")
            ybf = [self.sb(es, [128, NCH, TT], BF16, "ybf") for _ in range(2)]
            ysq = self.sb(es, [128, NCH, TT], BF16, "ysq")
            dg = self.sb(es, [128, CW, 128], BF16, "dg")
            NW = 3
            wsl = [self.sb(es, [128, NCH, 256], BF16, "cw") for _ in range(NW)]
            tA = [self.sb(es, [128, TT], F32, "tA") for _ in range(2)]
            sgm = tA
            mean_t = self.sb(es, [128, TT], F32, "mean_t")
            m2_t = self.sb(es, [128, TT], F32, "m2_t")
            wcount = [0]

            def next_slot():
                i = wcount[0] % NW
                wcount[0] += 1
                return i

            S.pool(lambda h_: h_.memset(glu[:, :, 0:PADL], 0.0), writes=[("glupad",)])
            w1 = self.w_pw1[j].rearrange("(c p) f -> p c f", p=128)
            w2 = self.w_pw2[j].rearrange("(c p) f -> p c f", p=128)

            def load_pw1(cb):
                ia = next_slot()
                self.load_w(wsl[ia][:], w1[:, :, cb * 256:(cb + 1) * 256], ("cw", ia), "cw%d" % ia)
                ig = next_slot()
                self.load_w(wsl[ig][:], w1[:, :, D + cb * 256:D + (cb + 1) * 256], ("cw", ig), "cw%d" % ig)
                return ia, ig

            it = 0
            for cb in range(4):
                ia, ig = load_pw1(cb)
                for ci in range(2):
                    cc = 2 * cb + ci
                    for T in range(NT):
                        ba, bg = (it % 2) * 2, (it % 2) * 2 + 1
                        it += 1
                        tsl = slice(T * TT, (T + 1) * TT)
                        for c in range(NCH):
                            self.mm(ps[ba][:, :], wsl[ia][:, c, ci * 128:(ci + 1) * 128], hn[:, c, tsl], c == 0, c == NCH - 1,
                                    [("cw", ia), ("hn", c, T)], [("ps", ba)])
                        for c in range(NCH):
                            self.mm(ps[bg][:, :], wsl[ig][:, c, ci * 128:(ci + 1) * 128], hn[:, c, tsl], c == 0, c == NCH - 1,
                                    [("cw", ig), ("hn", c, T)], [("ps", bg)])
                        sg_ = sgm[it % 2]
                        S.act(lambda h_, sg_=sg_, bg=bg, cc=cc: h_.activation(out=sg_[:], in_=ps[bg][:, :], func=AF.Sigmoid,
                                                                               bias=self.prow(R_BPW1 + j * 16 + 8 + cc)),
                              reads=[("ps", bg), "ptab"], writes=[("tA", it % 2)])
                        S.dve(lambda h_, sg_=sg_, ba=ba, cc=cc, T=T: h_.scalar_tensor_tensor(
                            out=glu[:, cc, PADL + T * TT:PADL + (T + 1) * TT], in0=ps[ba][:, :], scalar=self.prow(R_BPW1 + j * 16 + cc),
                            in1=sg_[:], op0=ALU.add, op1=ALU.mult),
                            reads=[("ps", ba), ("tA", it % 2), "ptab"], writes=[("glu", cc, T)])
            pw2_slots = {}

            def load_pw2(db):
                i = next_slot()
                self.load_w(wsl[i][:], w2[:, :, db * 256:(db + 1) * 256], ("cw", i), "cw%d" % i)
                pw2_slots[db] = i

            load_pw2(0)
            load_pw2(1)
            dcount = [0]

            YB = (0, 1, 6, 7)

            def conv_unit(T, cc):
                yb = YB[cc % 4]
                yT = ybf[T % 2]
                for tap in range(CW):
                    if dcount[0] % 2 == 0:
                        S.dve(lambda h_, tap=tap, cc=cc: h_.tensor_scalar(out=dg[:, tap, :], in0=self.identb[:],
                                                                          scalar1=self.prow(R_WDW + (j * CW + tap) * 8 + cc), scalar2=None, op0=ALU.mult),
                              reads=["identb", "ptab"], writes=[("dg", tap)])
                    else:
                        S.pool(lambda h_, tap=tap, cc=cc: h_.tensor_scalar(out=dg[:, tap, :], in0=self.identb[:],
                                                                           scalar1=self.prow(R_WDW + (j * CW + tap) * 8 + cc), scalar2=1.0,
                                                                           op0=ALU.mult, op1=ALU.mult),
                               reads=["identb", "ptab"], writes=[("dg", tap)])
                    dcount[0] += 1
                for tap in range(CW):
                    o0 = PADL + T * TT - (CW - 1) + tap
                    rd = [("dg", tap), ("glu", cc, T)]
                    rd.append(("glu", cc, T - 1) if T > 0 else ("glupad",))
                    self.mm(ps[yb][:, :], dg[:, tap, :], glu[:, cc, o0:o0 + TT], tap == 0, tap == CW - 1, rd, [("ps", yb)])

            def conv_evac(T, cc):
                yb = YB[cc % 4]
                yT = ybf[T % 2]
                S.act(lambda h_: h_.activation(out=yT[:, cc, :], in_=ps[yb][:, :], func=AF.Identity,
                                               bias=self.prow(R_BDW + j * 8 + cc)),
                      reads=[("ps", yb), "ptab"], writes=[("ybf", T % 2, cc)])
                S.act(lambda h_: h_.activation(out=ysq[:, cc, :], in_=yT[:, cc, :], func=AF.Square),
                      reads=[("ybf", T % 2, cc)], writes=[("ysq", cc)])

            def ln_head(T):
                yT = ybf[T % 2]
                bm, bq = 2 + (T % 2) * 2, 3 + (T % 2) * 2
                for cc in range(NCH):
                    self.mm(ps[bm][:, :], self.onesb[:], yT[:, cc, :], cc == 0, cc == NCH - 1, ["onesb", ("ybf", T % 2, cc)], [("ps", bm)])
                for cc in range(NCH):
                    self.mm(ps[bq][:, :], self.onesb[:], ysq[:, cc, :], cc == 0, cc == NCH - 1, ["onesb", ("ysq", cc)], [("ps", bq)])

            def ln_head_b(T):
                bm, bq = 2 + (T % 2) * 2, 3 + (T % 2) * 2
                S.act(lambda h_: h_.activation(out=mean_t[:], in_=ps[bm][:, :], func=AF.Copy, scale=1.0 / D),
                      reads=[("ps", bm)], writes=["mean_t"])
                S.dve(lambda h_: h_.tensor_tensor(out=m2_t[:], in0=mean_t[:], in1=mean_t[:], op=ALU.mult), reads=["mean_t"], writes=["m2_t"])
                S.dve(lambda h_: h_.scalar_tensor_tensor(out=m2_t[:], in0=ps[bq][:, :], scalar=1.0 / D, in1=m2_t[:],
                                                         op0=ALU.mult, op1=ALU.subtract),
                      reads=[("ps", bq), "m2_t"], writes=["m2_t"])
                S.act(lambda h_: h_.activation(out=m2_t[:], in_=m2_t[:], func=AF.Sqrt, bias=LN_EPS), reads=["m2_t"], writes=["m2_t"])
                S.dve(lambda h_: h_.reciprocal(ps[bm][:, :], m2_t[:]), reads=["m2_t"], writes=[("ps", bm)])
                S.dve(lambda h_: h_.tensor_tensor(out=ps[bq][:, :], in0=ps[bm][:, :], in1=mean_t[:], op=ALU.mult),
                      reads=[("ps", bm), "mean_t"], writes=[("ps", bq)])

            def ln_norm(T, cc):
                yT = ybf[T % 2]
                bm, bq = 2 + (T % 2) * 2, 3 + (T % 2) * 2
                ta_ = tA[cc % 2]
                S.dve(lambda h_: h_.tensor_tensor(out=ta_[:], in0=ps[bm][:, :], in1=yT[:, cc, :], op=ALU.mult),
                      reads=[("ps", bm), ("ybf", T % 2, cc)], writes=[("tA", cc % 2)])
                S.dve(lambda h_: h_.tensor_tensor(out=ta_[:], in0=ta_[:], in1=ps[bq][:, :], op=ALU.subtract),
                      reads=[("ps", bq), ("tA", cc % 2)], writes=[("tA", cc % 2)])
                S.act(lambda h_: h_.activation(out=hn[:, cc, T * TT:(T + 1) * TT], in_=ta_[:], func=AF.Silu,
                                               scale=self.prow(R_LNG + j * 8 + cc), bias=self.prow(R_LNB + j * 8 + cc)),
                      reads=[("tA", cc % 2), "ptab"], writes=[("hn", cc, T)])

            for T in range(NT):
                for cc in range(NCH):
                    conv_unit(T, cc)
                    if T > 0 and cc == 0:
                        ln_head(T - 1)
                    if T > 0 and cc == 1:
                        ln_head_b(T - 1)
                    conv_evac(T, cc)
                    if T > 0 and cc >= 1:
                        ln_norm(T - 1, cc - 1)
                if T > 0:
                    ln_norm(T - 1, NCH - 1)
            ln_head(NT - 1)
            ln_head_b(NT - 1)
            for cc in range(NCH):
                ln_norm(NT - 1, cc)
            it = 0
            for db in range(4):
                if db + 2 < 4:
                    load_pw2(db + 2)
                i = pw2_slots[db]
                for T in range(NT):
                    for di in range(2):
                        dc = 2 * db + di
                        bi = (6, 7, 0, 1, 2, 3, 4, 5)[it % 8]
                        it += 1
                        tsl = slice(T * TT, (T + 1) * TT)
                        for cc in range(NCH):
                            self.mm(ps[bi][:, :], wsl[i][:, cc, di * 128:(di + 1) * 128], hn[:, cc, tsl], cc == 0, cc == NCH - 1,
                                    [("cw", i), ("hn", cc, T)], [("ps", bi)])
                        S.dve(lambda h_, bi=bi, dc=dc, tsl=tsl: h_.scalar_tensor_tensor(
                            out=h[:, dc, tsl], in0=ps[bi][:, :], scalar=self.prow(R_BPW2 + j * 8 + dc), in1=h[:, dc, tsl], op0=ALU.add, op1=ALU.add),
                            reads=[("ps", bi), ("h", dc, T), "ptab"], writes=[("h", dc, T)])
            S.barrier()
            S.flush()

    def phase_ffn(self, i):
        nc, S, ps = self.nc, self.S, self.ps
        hn, h = self.hn, self.h
        groups = [[0, 1, 2], [3, 4, 5], [6, 7, 8], [9, 10]]
        with ExitStack() as es:
            act = self.sb(es, [128, 6, SEQ], BF16, "act")
            NWU = 3
            wup = [self.sb(es, [128, NCH, 512], BF16, "wup") for _ in range(NWU)]
            wdn = [self.sb(es, [128, 6, D], BF16, "wdn") for _ in range(2)]
            Ag = [self.sb(es, [128, TT], F32, "Ag") for _ in range(2)]
            Av = [self.sb(es, [128, TT], F32, "Av") for _ in range(2)]
            sg = [self.sb(es, [128, TT], F32, "sg") for _ in range(2)]
            halo = self.sb(es, [128, 2, 2, 2], F32, "halo")
            wu_src = self.w_up[i].rearrange("(c p) f -> p c f", p=128)
            wd_src = self.w_down[i].rearrange("(c p) f -> p c f", p=128)
            ucount = [0]
            blocks = [b for g in groups for b in g]
            up_slot = {}

            def load_up(b):
                s = ucount[0] % NWU
                ucount[0] += 1
                up_slot[b] = s
                self.load_w(wup[s][:, :, 0:256], wu_src[:, :, b * 256:(b + 1) * 256], ("wup", s), "wu%d" % s)
                self.load_w(wup[s][:, :, 256:512], wu_src[:, :, DFF + b * 256:DFF + (b + 1) * 256], ("wup", s), "wu%d" % s)

            def load_dn(gi):
                import os
                if os.environ.get("FFN_NODN"):
                    return
                g = groups[gi]
                np_ = 2 * len(g)
                j0 = 2 * g[0]
                self.load_w(wdn[gi % 2][:, 0:np_, :], wd_src[:, j0:j0 + np_, :], ("wdn", gi % 2), "wd%d" % (gi % 2))

            pend3 = []
            load_up(blocks[0])
            load_up(blocks[1])
            load_dn(0)
            nxt = 2
            it = 0
            for gi, g in enumerate(groups):
                for bl, b in enumerate(g):
                    s = up_slot[b]
                    for pi in range(2):
                        jp = 2 * b + pi
                        jl = 2 * bl + pi
                        rows = {}
                        for kind, fc in (("g", jp), ("v", NFP + jp)):
                            rows[kind] = [R_FWDW + (i * 3 + tap) * 44 + fc for tap in range(3)] + [R_FBDW + i * 44 + fc]
                        for T in range(NT):
                            par = it % 2
                            it += 1
                            tsl = slice(T * TT, (T + 1) * TT)
                            s3 = (it - 1) % 3
                            bg_, bv_ = s3 * 2, s3 * 2 + 1
                            import os
                            for c in range(NCH if not os.environ.get("FFN_NOMM") else 0):
                                self.mm(ps[bg_][:, :], wup[s][:, c, pi * 128:(pi + 1) * 128], hn[:, c, tsl], c == 0, c == NCH - 1,
                                        [("wup", s), ("hn", c, T)], [("ps", bg_)])
                            for c in range(NCH if not os.environ.get("FFN_NOMM") else 0):
                                self.mm(ps[bv_][:, :], wup[s][:, c, 256 + pi * 128:256 + (pi + 1) * 128], hn[:, c, tsl], c == 0, c == NCH - 1,
                                        [("wup", s), ("hn", c, T)], [("ps", bv_)])
                            A = {"g": Ag[par], "v": Av[par]}
                            U = {"g": ps[bg_], "v": ps[bv_]}
                            UB = {"g": bg_, "v": bv_}
                            AK = {"g": ("Ag", par), "v": ("Av", par)}
                            KI = {"g": 0, "v": 1}
                            hp_prev = (T - 1) % 2
                            hp_cur = T % 2
                            import os
                            FL = int(os.environ.get("FFN_LEVEL", "9"))
                            for kind in ("g", "v"):
                                if FL < 2:
                                    break
                                r = rows[kind]
                                S.act(lambda h_, A_=A[kind], U_=U[kind], r=r: h_.activation(out=A_[:], in_=U_[:, :], func=AF.Identity,
                                                                                          scale=self.prow(r[2]), bias=self.prow(r[3])),
                                      reads=[("ps", UB[kind]), "ptab"], writes=[AK[kind]])
                                if T < NT - 1 and FL >= 3:
                                    S.act(lambda h_, U_=U[kind], kind=kind, hp_cur=hp_cur: h_.activation(
                                        out=halo[:, hp_cur, KI[kind], :], in_=U_[:, TT - 2:TT], func=AF.Copy),
                                        reads=[("ps", UB[kind])], writes=[("halo", hp_cur, kind)])
                            for kind in ("g", "v"):
                                if FL < 4:
                                    break
                                r = rows[kind]
                                S.dve(lambda h_, A_=A[kind], U_=U[kind], r=r: h_.scalar_tensor_tensor(
                                    out=A_[:, 1:TT], in0=U_[:, 0:TT - 1], scalar=self.prow(r[1]), in1=A_[:, 1:TT], op0=ALU.mult, op1=ALU.add),
                                    reads=[("ps", UB[kind]), AK[kind], "ptab"], writes=[AK[kind]])
                            for kind in ("g", "v"):
                                if FL < 4:
                                    break
                                r = rows[kind]
                                S.dve(lambda h_, A_=A[kind], U_=U[kind], r=r: h_.scalar_tensor_tensor(
                                    out=A_[:, 2:TT], in0=U_[:, 0:TT - 2], scalar=self.prow(r[0]), in1=A_[:, 2:TT], op0=ALU.mult, op1=ALU.add),
                                    reads=[("ps", UB[kind]), AK[kind], "ptab"], writes=[AK[kind]])
                            if T > 0 and FL >= 5:
                                for kind in ("g", "v"):
                                    r = rows[kind]
                                    hl_ = halo[:, hp_prev, KI[kind], :]
                                    S.dve(lambda h_, A_=A[kind], hl_=hl_, r=r: h_.scalar_tensor_tensor(
                                        out=A_[:, 0:1], in0=hl_[:, 1:2], scalar=self.prow(r[1]), in1=A_[:, 0:1], op0=ALU.mult, op1=ALU.add),
                                        reads=[("halo", hp_prev, kind), AK[kind], "ptab"], writes=[AK[kind]])
                                for kind in ("g", "v"):
                                    r = rows[kind]
                                    hl_ = halo[:, hp_prev, KI[kind], :]
                                    S.dve(lambda h_, A_=A[kind], hl_=hl_, r=r: h_.scalar_tensor_tensor(
                                        out=A_[:, 0:2], in0=hl_[:, 0:2], scalar=self.prow(r[0]), in1=A_[:, 0:2], op0=ALU.mult, op1=ALU.add),
                                        reads=[("halo", hp_prev, kind), AK[kind], "ptab"], writes=[AK[kind]])
                            if FL < 6:
                                continue
                            def stage3(par=par, Ag_=A["g"], Av_=A["v"], jl=jl, tsl=tsl, T=T):
                                sg_ = sg[par]
                                S.act(lambda h_: h_.activation(out=sg_[:], in_=Ag_[:], func=AF.Silu),
                                      reads=[("Ag", par)], writes=[("sg", par)])
                                S.pool(lambda h_: h_.tensor_tensor(out=act[:, jl, tsl], in0=sg_[:], in1=Av_[:], op=ALU.mult),
                                       reads=[("sg", par), ("Av", par)], writes=[("act", jl, T)])
                            if pend3:
                                pend3.pop(0)()
                            pend3.append(stage3)
                            if pi == 0 and T == 2:
                                if nxt < len(blocks):
                                    load_up(blocks[nxt])
                                    nxt += 1
                                if bl == 0 and gi + 1 < len(groups):
                                    load_dn(gi + 1)
                while pend3:
                    pend3.pop(0)()
                np_ = 2 * len(g)
                wd = wdn[gi % 2]
                dn = 0
                for T in range(NT if FL >= 7 else 0):
                    tsl = slice(T * TT, (T + 1) * TT)
                    for dc in range(NCH):
                        s_next = it % 3
                        bi = (6, 7, 2 * s_next, 2 * s_next + 1)[dn % 4]
                        dn += 1
                        for jl in range(np_):
                            self.mm(ps[bi][:, :], wd[:, jl, dc * 128:(dc + 1) * 128], act[:, jl, tsl], jl == 0, jl == np_ - 1,
                                    [("wdn", gi % 2), ("act", jl, T)], [("ps", bi)])
                        S.dve(lambda h_, bi=bi, dc=dc, tsl=tsl: h_.tensor_tensor(out=h[:, dc, tsl], in0=ps[bi][:, :], in1=h[:, dc, tsl], op=ALU.add),
                              reads=[("ps", bi), ("h", dc, T)], writes=[("h", dc, T)])
            S.barrier()
            S.flush()

    def phase_final(self, raw=False):
        nc, S, ps = self.nc, self.S, self.ps
        with ExitStack() as es:
            ofm = self.sb(es, [128, NCH, TT], F32, "ofm")
            ost = [self.sb(es, [128, D], F32, "ost") for _ in range(3)]
            oc = 0
            for T in range(NT):
                tsl = slice(T * TT, (T + 1) * TT)
                if raw:
                    for c in range(NCH):
                        eng = S.act if c % 2 == 0 else S.dve
                        if c % 2 == 0:
                            S.act(lambda h_, c=c, tsl=tsl: h_.activation(out=ofm[:, c, :], in_=self.h[:, c, tsl], func=AF.Copy),
                                  reads=[("h", c, T)], writes=[("ofm", c)])
                        else:
                            S.dve(lambda h_, c=c, tsl=tsl: h_.tensor_copy(out=ofm[:, c, :], in_=self.h[:, c, tsl]),
                                  reads=[("h", c, T)], writes=[("ofm", c)])
                else:
                    self.rmsnorm_tile(T, R_FIN, ofm)
                for ts in range(4):
                    tt = T * 4 + ts
                    sl = oc % 3
                    oc += 1
                    for half in range(2):
                        bi = (2 * tt + half) % 4
                        bank = ps[bi]
                        for jj in range(4):
                            c = half * 4 + jj
                            S.pe(lambda h_, bank=bank, jj=jj, c=c, ts=ts: h_.transpose(out=bank[:, jj * 128:(jj + 1) * 128],
                                                                                         in_=ofm[:, c, ts * 128:(ts + 1) * 128], identity=self.identf[:]),
                                 reads=[("ofm", c), "identf"], writes=[("ps", bi)])
                        dst = ost[sl][:, half * 512:(half + 1) * 512]
                        if half == 0:
                            S.act(lambda h_, dst=dst, bank=bank: h_.activation(out=dst, in_=bank[:, :], func=AF.Copy),
                                  reads=[("ps", bi)], writes=[("ost", sl, half)])
                        else:
                            S.dve(lambda h_, dst=dst, bank=bank: h_.tensor_copy(out=dst, in_=bank[:, :]),
                                  reads=[("ps", bi)], writes=[("ost", sl, half)])
                    S.dma("sp", "ost%d" % sl, lambda h_, sl=sl, tt=tt: h_.dma_start(out=self.out[tt * 128:(tt + 1) * 128, :], in_=ost[sl][:]),
                          reads=[("ost", sl, 0), ("ost", sl, 1)])
            S.barrier()
            S.flush()

    def rmsnorm_tile(self, T, grow, ofm):
        S, ps = self.S, self.ps
        bi = 6 + T % 2
        bank = ps[bi]
        tsl = slice(T * TT, (T + 1) * TT)
        for c in range(NCH):
            sq = self.nsq[:, c % 2, :]
            S.act(lambda h, sq=sq, c=c: h.activation(out=sq, in_=self.h[:, c, tsl], func=AF.Square),
                  reads=[("h", c, T)], writes=[("nsq", c % 2)])
            self.mm(bank[:, :], self.onesb[:], sq, c == 0, c == NCH - 1, [("nsq", c % 2), "onesb"], [("ps", bi)])
        sd = self.nstd[:, T % 2, :]
        S.act(lambda h: h.activation(out=sd, in_=bank[:, :], func=AF.Sqrt, scale=1.0 / D, bias=NORM_EPS),
              reads=[("ps", bi)], writes=[("nstd", T % 2)])
        S.dve(lambda h: h.reciprocal(bank[:, :], sd), reads=[("nstd", T % 2)], writes=[("ps", bi)])
        for c in range(NCH):
            S.dve(lambda h, c=c: h.scalar_tensor_tensor(out=ofm[:, c, :], in0=self.h[:, c, tsl], scalar=self.prow(grow + c), in1=bank[:, :],
                                                        op0=ALU.mult, op1=ALU.mult),
                  reads=[("h", c, T), ("ps", bi), "ptab"], writes=[("ofm", c)])


def _pack_ptab(inp):
    f = lambda a: np.ascontiguousarray(np.asarray(a, dtype=np.float32)).reshape(-1, 128)
    parts = [
        f(inp["norm_mix_g"]), f(inp["norm_ffn_g"]), f(inp["final_norm_g"]), f(inp["conv_b_pw1"]),
        f(inp["conv_w_dw"]), f(inp["conv_b_dw"]), f(inp["conv_ln_g"]), f(inp["conv_ln_b"]), f(inp["conv_b_pw2"]),
        f(inp["ffn_w_dw"]), f(inp["ffn_b_dw"]),
    ]
    tab = np.concatenate(parts, axis=0)
    assert tab.shape[0] == 1368
    pad = np.zeros((R_TOT - tab.shape[0], 128), np.float32)
    return np.ascontiguousarray(np.concatenate([tab, pad], axis=0))


_NC_CACHE = {}


def _run(inputs, stop_after=None, trace=False):
    x = np.ascontiguousarray(np.asarray(inputs["x"], dtype=np.float32))
    B = x.shape[0]
    key = stop_after
    if key not in _NC_CACHE:
        _NC_CACHE[key] = Builder(stop_after).build()
    nc = _NC_CACHE[key]
    ptab = _pack_ptab(inputs)
    c = lambda k: np.ascontiguousarray(np.asarray(inputs[k], dtype=np.float32))
    shared = {
        "ptab_in": ptab, "w_qkv": c("attn_w_qkv"), "w_o": c("attn_w_o"), "w_pw1": c("conv_w_pw1"), "w_pw2": c("conv_w_pw2"),
        "w_up": c("ffn_w_up"), "w_down": c("ffn_w_down"),
    }
    in_maps = [dict(shared, x=x[b]) for b in range(B)]
    res = run_bass_kernel_spmd(nc, in_maps, core_ids=list(range(B)), trace=trace)
    out = np.stack([np.asarray(r["out"]) for r in res.results], axis=0).astype(np.float32)
    return out, res


def kernel(**inputs):
    out, _ = _run(inputs)
    return out
```

```python
import math
import numpy as np
from contextlib import ExitStack
import concourse.bass as bass
import concourse.mybir as mybir
from concourse.bass_utils import run_bass_kernel_spmd

F32 = mybir.dt.float32
BF16 = mybir.dt.bfloat16
I32 = mybir.dt.int32
ALU = mybir.AluOpType
AF = mybir.ActivationFunctionType
AX = mybir.AxisListType

D = 1024
SEQ = 2048
NCH = 8
NT = 4
TT = 512
H = 16
DH = 64
DFF = 2816
NFP = 22
DEPTH = 4
NEG = -30000.0
NORM_EPS = 1e-6
LN_EPS = 1e-5
CW = 31

R_MIX = 0
R_FFN = 32
R_FIN = 64
R_BPW1 = 72
R_WDW = 104
R_BDW = 600
R_LNG = 616
R_LNB = 632
R_BPW2 = 648
R_FWDW = 664
R_FBDW = 1192
R_TOT = 1408


class _Op:
    __slots__ = ("eng", "fn", "deps", "needs_inc", "semval", "grp", "gen", "pre")

    def __init__(self, eng, fn):
        self.eng = eng
        self.fn = fn
        self.deps = []
        self.needs_inc = False
        self.semval = None
        self.grp = None
        self.gen = 0
        self.pre = None


class _Grp:
    __slots__ = ("sem", "gens", "closed")

    def __init__(self, sem):
        self.sem = sem
        self.gens = [0]
        self.closed = False


class Sched:
    ENGS = ("pe", "act", "dve", "pool", "sp")

    def __init__(self, nc, es):
        self.nc = nc
        self.es = es
        self.q = {e: [] for e in self.ENGS}
        self.lastw = {}
        self.readers = {}
        self.esem = {e: es.enter_context(nc.semaphore("sem_" + e)) for e in ("pe", "act", "dve", "pool")}
        self.ecnt = {e: 0 for e in ("pe", "act", "dve", "pool")}
        self.seen = {e: {} for e in self.ENGS}
        self.groups = {}
        self.lastreal = {e: None for e in self.ENGS}
        self.nops = 0

    def _group(self, name):
        g = self.groups.get(name)
        if g is None:
            g = _Grp(self.es.enter_context(self.nc.semaphore("dg_" + name)))
            self.groups[name] = g
        return g

    def add(self, eng, fn, reads=(), writes=(), dma=None):
        op = _Op(eng, fn)
        deps = {}
        for k in reads:
            w = self.lastw.get(k)
            if w is not None:
                deps[id(w)] = w
            if isinstance(k, tuple) and k[0] == "ps":
                rd = self.readers.get(k)
                if rd:
                    for rk_, r in rd.items():
                        if rk_ != eng:
                            deps[id(r)] = r
        for k in writes:
            w = self.lastw.get(k)
            if w is not None:
                deps[id(w)] = w
            rd = self.readers.get(k)
            if rd:
                for r in rd.values():
                    deps[id(r)] = r
        if dma is not None:
            g = self._group(dma)
            op.grp = g
            if g.closed:
                op.pre = [(g, len(g.gens) - 1)]
                g.gens.append(g.gens[-1])
                g.closed = False
            g.gens[-1] += 16
            op.gen = len(g.gens) - 1
        for d in deps.values():
            if eng == "pe" and d.eng == "pe" and d.grp is None:
                continue
            if op.grp is not None and d.grp is op.grp:
                continue
            op.deps.append(d)
            d.needs_inc = True
            if d.grp is not None:
                d.grp.closed = True
        rk = eng if dma is None else ("dma", id(op))
        for k in reads:
            self.readers.setdefault(k, {})[rk] = op
        for k in writes:
            self.lastw[k] = op
            self.readers[k] = {}
        self.q[eng].append(op)
        if dma is None:
            self.lastreal[eng] = op
        self.nops += 1
        return op

    def pe(self, fn, reads=(), writes=()):
        return self.add("pe", fn, reads, writes)

    def act(self, fn, reads=(), writes=()):
        return self.add("act", fn, reads, writes)

    def dve(self, fn, reads=(), writes=()):
        return self.add("dve", fn, reads, writes)

    def pool(self, fn, reads=(), writes=()):
        return self.add("pool", fn, reads, writes)

    def dma(self, queue, group, fn, reads=(), writes=()):
        return self.add(queue, fn, reads, writes, dma=group)

    def barrier(self):
        lasts = [op for op in self.lastreal.values() if op is not None and op.grp is None]
        gl = [(g, len(g.gens) - 1) for g in self.groups.values() if g.gens[-1] > 0]
        for g, _ in gl:
            g.closed = True
        for e in self.ENGS:
            op = _Op(e, None)
            for d in lasts:
                if not (e == "pe" and d.eng == "pe"):
                    op.deps.append(d)
                    d.needs_inc = True
            op.pre = list(gl)
            self.q[e].append(op)
        self.lastw = {}
        self.readers = {}

    def _assign(self):
        for e in ("pe", "act", "dve", "pool"):
            c = self.ecnt[e]
            for op in self.q[e]:
                if op.grp is None and op.fn is not None and op.needs_inc and op.semval is None:
                    c += 1
                    op.semval = c
            self.ecnt[e] = c

    def _emit_engine(self, e, h):
        seen = self.seen[e]

        def wait(sem, val):
            key = id(sem)
            if seen.get(key, 0) < val:
                h.wait_ge(sem, val)
                seen[key] = val

        for op in self.q[e]:
            if op.pre is not None:
                for g, gi in op.pre:
                    wait(g.sem, g.gens[gi])
            for d in op.deps:
                if d.grp is not None:
                    wait(d.grp.sem, d.grp.gens[d.gen])
                else:
                    wait(self.esem[d.eng], d.semval)
            if op.fn is not None:
                ins = op.fn(h)
                if op.grp is not None:
                    ins.then_inc(op.grp.sem, 16)
                elif op.needs_inc:
                    ins.then_inc(self.esem[e], 1)
        self.q[e] = []
        self.lastreal[e] = None

    def flush(self):
        self._assign()
        with self.nc.Block() as block:
            @block.tensor
            def _(h):
                self._emit_engine("pe", h)

            @block.scalar
            def _(h):
                self._emit_engine("act", h)

            @block.vector
            def _(h):
                self._emit_engine("dve", h)

            @block.gpsimd
            def _(h):
                self._emit_engine("pool", h)

            @block.sync
            def _(h):
                self._emit_engine("sp", h)


class Builder:
    def __init__(self, stop_after=None):
        self.stop_after = stop_after
        self.nc = bass.Bass("TRN2", target_bir_lowering=False)
        nc = self.nc
        dt = nc.dram_tensor
        self.x = dt("x", [SEQ, D], F32, kind="ExternalInput").ap()
        self.ptab_in = dt("ptab_in", [R_TOT, 128], F32, kind="ExternalInput").ap()
        self.w_qkv = dt("w_qkv", [2, D, 3 * D], F32, kind="ExternalInput").ap()
        self.w_o = dt("w_o", [2, D, D], F32, kind="ExternalInput").ap()
        self.w_pw1 = dt("w_pw1", [2, D, 2 * D], F32, kind="ExternalInput").ap()
        self.w_pw2 = dt("w_pw2", [2, D, D], F32, kind="ExternalInput").ap()
        self.w_up = dt("w_up", [DEPTH, D, 2 * DFF], F32, kind="ExternalInput").ap()
        self.w_down = dt("w_down", [DEPTH, DFF, D], F32, kind="ExternalInput").ap()
        self.out = dt("out", [SEQ, D], F32, kind="ExternalOutput").ap()
        self._uid = 0

    def sb(self, es, shape, dtype, name=None):
        self._uid += 1
        return es.enter_context(self.nc.sbuf_tensor("%s_%d" % (name or "t", self._uid), shape, dtype))

    def mm(self, out, lhsT, rhs, start, stop, reads, writes):
        self.S.pe(lambda h: h.matmul(out, lhsT=lhsT, rhs=rhs, start=start, stop=stop), reads, writes)

    def prow(self, r):
        return self.ptab[:, r:r + 1]

    def build(self):
        nc = self.nc
        with ExitStack() as es:
            self.S = S = Sched(nc, es)
            self.ps = [es.enter_context(nc.psum_tensor("ps%d" % i, [128, TT], F32)) for i in range(8)]
            self.h = self.sb(es, [128, NCH, SEQ], F32, "h")
            self.hn = self.sb(es, [128, NCH, SEQ], BF16, "hn")
            self.cosT = self.sb(es, [128, SEQ], BF16, "cosT")
            self.sinT = self.sb(es, [128, SEQ], BF16, "sinT")
            self.ptab = self.sb(es, [128, R_TOT], F32, "ptab")
            self.identf = self.sb(es, [128, 128], F32, "identf")
            self.identb = self.sb(es, [128, 128], BF16, "identb")
            self.onesb = self.sb(es, [128, 128], BF16, "onesb")
            self.maskT = self.sb(es, [128, 128], BF16, "maskT")
            self.prot = self.sb(es, [128, 128], BF16, "prot")
            self.ind = [self.sb(es, [128, 8 * 128], BF16, "ind") for _ in range(2)]
            self.nsq = self.sb(es, [128, 2, TT], BF16, "nsq")
            self.nstd = self.sb(es, [128, 2, TT], F32, "nstd")

            self.phase_setup()
            stop = self.stop_after
            done = stop == "load"
            for i in range(DEPTH):
                if done:
                    break
                j = i // 2
                self.rmsnorm(R_MIX + i * 8)
                if i % 2 == 0:
                    self.phase_attention(j)
                else:
                    self.phase_conformer(j)
                if stop is not None and stop.split(":")[0] == "mix%d" % i:
                    done = True
                    break
                self.rmsnorm(R_FFN + i * 8)
                fsub = stop.split(":")[1] if (stop and ":" in stop and stop.startswith("ffn")) else None
                if fsub != "norm":
                    self.phase_ffn(i)
                if stop is not None and stop.split(":")[0] == "ffn%d" % i:
                    done = True
                    break
            self.phase_final(raw=(stop is not None))
        return nc

    def phase_setup(self):
        nc, S = self.nc, self.S
        ps = self.ps
        with ExitStack() as es:
            xs = [self.sb(es, [128, D], F32, "xs") for _ in range(3)]
            pst = [self.sb(es, [128, 128], F32, "pst") for _ in range(2)]
            pi_i = self.sb(es, [128, 1], I32, "pi_i")
            pm_i = self.sb(es, [128, 2], I32, "pm_i")
            sgn = self.sb(es, [128, 2], F32, "sgn")
            invrow = self.sb(es, [1, 128], F32, "invrow")
            invf = self.sb(es, [128, 1], F32, "invf")
            pos_i = self.sb(es, [128, SEQ], I32, "pos_i")
            ang = self.sb(es, [128, SEQ], F32, "ang")
            ta = self.sb(es, [128, SEQ], F32, "ta")
            tb = self.sb(es, [128, SEQ], F32, "tb")
            tc = self.sb(es, [128, SEQ], F32, "tc")

            identf, identb, onesb, maskT, prot, ind = self.identf, self.identb, self.onesb, self.maskT, self.prot, self.ind
            S.pool(lambda h: h.memset(identf[:], 1.0), writes=["identf"])
            S.pool(lambda h: h.affine_select(out=identf[:], in_=identf[:], pattern=[[-1, 128]], compare_op=ALU.is_equal,
                                             fill=0.0, base=0, channel_multiplier=1), reads=["identf"], writes=["identf"])
            S.dve(lambda h: h.tensor_copy(out=identb[:], in_=identf[:]), reads=["identf"], writes=["identb"])
            S.pool(lambda h: h.memset(onesb[:], 1.0), writes=["onesb"])
            S.pool(lambda h: h.memset(maskT[:], 0.0), writes=["maskT"])
            S.pool(lambda h: h.affine_select(out=maskT[:], in_=maskT[:], pattern=[[1, 128]], compare_op=ALU.is_ge,
                                             fill=NEG, base=0, channel_multiplier=-1), reads=["maskT"], writes=["maskT"])
            for (dst, src) in ((0, 32), (32, 0), (64, 96), (96, 64)):
                S.dve(lambda h, dst=dst, src=src: h.tensor_copy(out=prot[:, dst:dst + 32], in_=identb[:, src:src + 32]),
                      reads=["identb"], writes=["prot"])
            for jj in range(2):
                S.pool(lambda h, jj=jj: h.memset(ind[jj][:], 0.0), writes=["ind"])
                S.pool(lambda h, jj=jj: h.memset(ind[jj][64 * jj:64 * jj + 64, :], 1.0), reads=["ind"], writes=["ind"])
                S.pool(lambda h, jj=jj: h.affine_select(out=ind[jj][64 * jj:64 * jj + 64, :], in_=ind[jj][64 * jj:64 * jj + 64, :],
                                                        pattern=[[1, 8], [0, 128]], compare_op=ALU.is_equal, fill=0.0,
                                                        base=0, channel_multiplier=-1), reads=["ind"], writes=["ind"])
            for r in range(R_TOT // 128):
                st = pst[r % 2]
                S.dma("sp", "pst%d" % (r % 2), lambda h, r=r, st=st: h.dma_start(out=st[:], in_=self.ptab_in[r * 128:(r + 1) * 128, :]),
                      writes=[("pst", r % 2)])
                bank = ps[r % 2]
                S.pe(lambda h, st=st, bank=bank: h.transpose(out=bank[:, 0:128], in_=st[:], identity=identf[:]),
                     reads=[("pst", r % 2), "identf"], writes=[("ps", r % 2)])
                S.act(lambda h, r=r, bank=bank: h.activation(out=self.ptab[:, r * 128:(r + 1) * 128], in_=bank[:, 0:128], func=AF.Copy),
                      reads=[("ps", r % 2)], writes=["ptab"])
            inv = (np.float32(1.0) / np.power(np.float32(10000.0), np.arange(0, DH, 2, dtype=np.float32) / np.float32(DH))).astype(np.float32)
            irv = invrow[:].rearrange("o (a b) -> o a b", b=32)
            for i in range(32):
                S.dve(lambda h, i=i: h.memset(irv[:, :, i:i + 1], float(inv[i])), writes=["invrow"])
            S.pe(lambda h: h.transpose(out=ps[2][:, 0:1], in_=invrow[:], identity=identf[0:1, 0:1]),
                 reads=["invrow", "identf"], writes=[("ps", 2)])
            S.act(lambda h: h.activation(out=invf[:], in_=ps[2][:, 0:1], func=AF.Copy), reads=[("ps", 2)], writes=["invf"])
            S.pool(lambda h: h.iota(pi_i[:], pattern=[[0, 1]], base=0, channel_multiplier=1), writes=["pi_i"])
            S.dve(lambda h: h.tensor_scalar(out=pm_i[:, 0:1], in0=pi_i[:], scalar1=32, scalar2=None, op0=ALU.bitwise_and),
                  reads=["pi_i"], writes=["pm_i"])
            S.dve(lambda h: h.tensor_copy(out=sgn[:, 0:1], in_=pm_i[:, 0:1]), reads=["pm_i"], writes=["sgn0"])
            S.dve(lambda h: h.tensor_scalar(out=sgn[:, 1:2], in0=sgn[:, 0:1], scalar1=1.0 / 16.0, scalar2=-1.0, op0=ALU.mult, op1=ALU.add),
                  reads=["sgn0"], writes=["sgn1"])
            S.pool(lambda h: h.iota(pos_i[:], pattern=[[1, SEQ]], base=0, channel_multiplier=0), writes=["pos_i"])
            S.dve(lambda h: h.tensor_copy(out=ta[:], in_=pos_i[:]), reads=["pos_i"], writes=["ta"])
            S.dve(lambda h: h.tensor_scalar(out=ang[:], in0=ta[:], scalar1=invf[:, 0:1], scalar2=None, op0=ALU.mult),
                  reads=["ta", "invf"], writes=["ang"])
            TWO_PI = 2.0 * math.pi
            C1 = 6.28125
            C2 = TWO_PI - C1
            MAGIC = 12582912.0
            LIM = 3.1415925
            S.dve(lambda h: h.tensor_scalar(out=ta[:], in0=ang[:], scalar1=1.0 / TWO_PI, scalar2=None, op0=ALU.mult),
                  reads=["ang", "ta"], writes=["ta"])
            S.dve(lambda h: h.tensor_scalar(out=tb[:], in0=ta[:], scalar1=MAGIC, scalar2=MAGIC, op0=ALU.add, op1=ALU.subtract),
                  reads=["ta"], writes=["tb"])
            S.dve(lambda h: h.scalar_tensor_tensor(out=ta[:], in0=tb[:], scalar=-C1, in1=ang[:], op0=ALU.mult, op1=ALU.add),
                  reads=["tb", "ang", "ta"], writes=["ta"])
            S.dve(lambda h: h.scalar_tensor_tensor(out=tc[:], in0=tb[:], scalar=-C2, in1=ta[:], op0=ALU.mult, op1=ALU.add),
                  reads=["tb", "ta"], writes=["tc"])
            S.dve(lambda h: h.tensor_scalar(out=ta[:], in0=tc[:], scalar1=LIM, scalar2=-LIM, op0=ALU.min, op1=ALU.max),
                  reads=["tc", "ta"], writes=["ta"])
            S.act(lambda h: h.activation(out=tb[:], in_=ta[:], func=AF.Sin), reads=["ta", "tb"], writes=["tb"])
            S.dve(lambda h: h.tensor_scalar(out=self.sinT[:], in0=tb[:], scalar1=sgn[:, 1:2], scalar2=None, op0=ALU.mult),
                  reads=["tb", "sgn1"], writes=["sinT"])
            S.dve(lambda h: h.tensor_scalar(out=ang[:], in0=tc[:], scalar1=math.pi / 2, scalar2=None, op0=ALU.add),
                  reads=["tc", "ang"], writes=["ang"])
            S.dve(lambda h: h.tensor_scalar(out=ta[:], in0=ang[:], scalar1=math.pi, scalar2=None, op0=ALU.is_gt),
                  reads=["ang", "ta"], writes=["ta"])
            S.dve(lambda h: h.scalar_tensor_tensor(out=tc[:], in0=ta[:], scalar=-TWO_PI, in1=ang[:], op0=ALU.mult, op1=ALU.add),
                  reads=["ta", "ang", "tc"], writes=["tc"])
            S.dve(lambda h: h.tensor_scalar(out=ta[:], in0=tc[:], scalar1=LIM, scalar2=-LIM, op0=ALU.min, op1=ALU.max),
                  reads=["tc", "ta"], writes=["ta"])
            S.act(lambda h: h.activation(out=self.cosT[:], in_=ta[:], func=AF.Sin), reads=["ta"], writes=["cosT"])
            for tt in range(16):
                sl = tt % 3
                S.dma("sp", "xs%d" % sl, lambda h, tt=tt, sl=sl: h.dma_start(out=xs[sl][:], in_=self.x[tt * 128:(tt + 1) * 128, :]),
                      writes=[("xs", sl)])
                T = tt // 4
                for half in range(2):
                    bi = 4 + (2 * tt + half) % 4
                    bank = ps[bi]
                    for jj in range(4):
                        c = half * 4 + jj
                        S.pe(lambda h, bank=bank, jj=jj, c=c, sl=sl: h.transpose(out=bank[:, jj * 128:(jj + 1) * 128],
                                                                               in_=xs[sl][:, c * 128:(c + 1) * 128], identity=identf[:]),
                             reads=[("xs", sl), "identf"], writes=[("ps", bi)])
                    dst = self.h[:, half * 4:half * 4 + 4, tt * 128:(tt + 1) * 128]
                    src = bank[:, :].rearrange("p (a b) -> p a b", b=128)
                    wr = [("h", half * 4 + jj, T) for jj in range(4)]
                    if (2 * tt + half) % 2 == 0:
                        S.act(lambda h, dst=dst, src=src: h.activation(out=dst, in_=src, func=AF.Copy), reads=[("ps", bi)], writes=wr)
                    else:
                        S.dve(lambda h, dst=dst, src=src: h.tensor_copy(out=dst, in_=src), reads=[("ps", bi)], writes=wr)
            S.barrier()
            S.flush()

    def rmsnorm(self, grow, out_fn=None):
        S, ps = self.S, self.ps
        for T in range(NT):
            bi = 6 + T % 2
            bank = ps[bi]
            tsl = slice(T * TT, (T + 1) * TT)
            for c in range(NCH):
                sq = self.nsq[:, c % 2, :]
                S.act(lambda h, sq=sq, c=c, tsl=tsl: h.activation(out=sq, in_=self.h[:, c, tsl], func=AF.Square),
                      reads=[("h", c, T)], writes=[("nsq", c % 2)])
                self.mm(bank[:, :], self.onesb[:], sq, c == 0, c == NCH - 1, [("nsq", c % 2), "onesb"], [("ps", bi)])
            sd = self.nstd[:, T % 2, :]
            S.act(lambda h, sd=sd, bank=bank: h.activation(out=sd, in_=bank[:, :], func=AF.Sqrt, scale=1.0 / D, bias=NORM_EPS),
                  reads=[("ps", bi)], writes=[("nstd", T % 2)])
            S.dve(lambda h, sd=sd, bank=bank: h.reciprocal(bank[:, :], sd), reads=[("nstd", T % 2)], writes=[("ps", bi)])
            for c in range(NCH):
                if out_fn is None:
                    dst = self.hn[:, c, tsl]
                    wr = [("hn", c, T)]
                else:
                    dst, wr = out_fn(c, T)
                S.dve(lambda h, dst=dst, c=c, tsl=tsl, bank=bank: h.scalar_tensor_tensor(
                    out=dst, in0=self.h[:, c, tsl], scalar=self.prow(grow + c), in1=bank[:, :], op0=ALU.mult, op1=ALU.mult),
                    reads=[("h", c, T), ("ps", bi), "ptab"], writes=wr)

    def load_w(self, slot_ap, src_ap, key, group):
        self.S.dma("pool", group, lambda h: h.dma_start(out=slot_ap, in_=src_ap), writes=[key])

    def phase_attention(self, j):
        nc, S, ps = self.nc, self.S, self.ps
        hn, h = self.hn, self.h
        with ExitStack() as es:
            qT = self.sb(es, [128, 2, SEQ], BF16, "qT")
            kT = self.sb(es, [128, 2, 2, SEQ], BF16, "kT")
            Vt = self.sb(es, [128, 16, 4, 128], BF16, "Vt")
            NW = 5
            wsl = [self.sb(es, [128, 2048], BF16, "wsl") for _ in range(NW)]
            biasT = self.sb(es, [128, 2, 1024], BF16, "biasT")
            kms = self.sb(es, [128, 4, 8], F32, "kms")
            kmT = self.sb(es, [128, 4, 8], BF16, "kmT")
            gsb = self.sb(es, [128, 256], F32, "gsb")
            cmp_ = self.sb(es, [128, 2, 4 * 49], BF16, "cmp")
            rank = self.sb(es, [128, 2, 32], F32, "rank")
            btok = self.sb(es, [128, 4, 256], BF16, "btok")
            pT = [self.sb(es, [128, TT], BF16, "pT") for _ in range(4)]
            rec = [self.sb(es, [128, TT], F32, "rec") for _ in range(2)]
            qs = [self.sb(es, [128, TT], BF16, "qs") for _ in range(2)]
            t1 = [self.sb(es, [128, TT], F32, "t1") for _ in range(2)]
            t2 = [self.sb(es, [128, TT], F32, "t2") for _ in range(2)]

            wcount = [0]

            def next_slot():
                i = wcount[0] % NW
                wcount[0] += 1
                return i

            S.pool(lambda h_: h_.memset(Vt[:, :, 0:4:2, 64:128], 1.0), writes=[("Vt1", 0)])
            S.pool(lambda h_: h_.memset(Vt[:, :, 1:4:2, 0:64], 1.0), writes=[("Vt1", 1)])
            S.pool(lambda h_: h_.memset(biasT[:], 0.0), writes=[("biasT", a, u) for a in range(2) for u in range(2)])
            S.pool(lambda h_: h_.memset(kT[64:128, :, 0, :], 0.0), writes=[("kz", 0)])
            S.pool(lambda h_: h_.memset(kT[0:64, :, 1, :], 0.0), writes=[("kz", 1)])

            wq_src = self.w_qkv[j].rearrange("(c p) f -> p c f", p=128)
            wo_src = self.w_o[j].rearrange("(c p) f -> p c f", p=128)

            def load_group(g):
                sl = {}
                for nm, off in (("q", 0), ("k", D), ("v", 2 * D)):
                    i = next_slot()
                    sl[nm] = i
                    self.load_w(wsl[i][:, :].rearrange("p (c f) -> p c f", f=256), wq_src[:, :, off + g * 256: off + (g + 1) * 256],
                                ("wsl", i), "aw%d" % i)
                i = next_slot()
                sl["o"] = i
                self.load_w(wsl[i][:, :].rearrange("p (c f) -> p c f", f=1024), wo_src[:, 2 * g:2 * g + 2, :], ("wsl", i), "aw%d" % i)
                return sl

            pending = load_group(0)
            rope_i = [0]
            sub = self.stop_after.split(":")[1] if (self.stop_after and ":" in self.stop_after) else None
            for g in range(4):
                if sub is not None and g > 0:
                    break
                sl = pending
                wq = wsl[sl["q"]][:, :].rearrange("p (c f) -> p c f", f=256)
                wk = wsl[sl["k"]][:, :].rearrange("p (c f) -> p c f", f=256)
                wv = wsl[sl["v"]][:, :].rearrange("p (c f) -> p c f", f=256)
                wo = wsl[sl["o"]][:, :].rearrange("p (c f) -> p c f", f=1024)
                pend = []

                def rope_a(which, cc, T, wsrc, wkey):
                    scale = DH ** -0.5 if which == "q" else 1.0
                    ri = rope_i[0]
                    rope_i[0] += 1
                    ba = ri % 2
                    A = ps[ba]
                    tsl = slice(T * TT, (T + 1) * TT)
                    for c in range(NCH):
                        self.mm(A[:, :], wsrc[:, c, cc * 128:(cc + 1) * 128], hn[:, c, tsl], c == 0, c == NCH - 1,
                                [wkey, ("hn", c, T)], [("ps", ba)])
                    q_s = qs[ri % 2]
                    S.act(lambda h_, q_s=q_s, A=A, scale=scale: h_.activation(out=q_s[:], in_=A[:, :], func=AF.Copy, scale=scale),
                          reads=[("ps", ba)], writes=[("qs", ri % 2)])
                    return (which, cc, T, ri, scale)

                def rope_b(which, cc, T, ri, scale):
                    ba, bb = ri % 2, 2 + ri % 2
                    A, B = ps[ba], ps[bb]
                    tsl = slice(T * TT, (T + 1) * TT)
                    q_s, t1_, t2_ = qs[ri % 2], t1[ri % 2], t2[ri % 2]
                    self.mm(B[:, :], self.prot[:], q_s[:], True, True, [("qs", ri % 2), "prot"], [("ps", bb)])
                    S.dve(lambda h_, t1_=t1_, A=A, scale=scale, tsl=tsl: h_.scalar_tensor_tensor(
                        out=t1_[:], in0=A[:, :], scalar=scale, in1=self.cosT[:, tsl], op0=ALU.mult, op1=ALU.mult),
                        reads=[("ps", ba), "cosT"], writes=[("t1", ri % 2)])
                    S.dve(lambda h_, t2_=t2_, B=B, tsl=tsl: h_.tensor_tensor(out=t2_[:], in0=B[:, :], in1=self.sinT[:, tsl], op=ALU.mult),
                          reads=[("ps", bb), "sinT"], writes=[("t2", ri % 2)])
                    if which == "q":
                        dst = qT[:, cc, tsl]
                        wr = [("q", cc, T, 0), ("q", cc, T, 1)]
                        S.pool(lambda h_, dst=dst, t1_=t1_, t2_=t2_: h_.tensor_tensor(out=dst, in0=t1_[:], in1=t2_[:], op=ALU.add),
                               reads=[("t1", ri % 2), ("t2", ri % 2)], writes=wr)
                    else:
                        tk_ = qs[ri % 2]
                        S.pool(lambda h_, tk_=tk_, t1_=t1_, t2_=t2_: h_.tensor_tensor(out=tk_[:], in0=t1_[:], in1=t2_[:], op=ALU.add),
                               reads=[("t1", ri % 2), ("t2", ri % 2), ("qs", ri % 2)], writes=[("qs", ri % 2)])
                        for hp in range(2):
                            prt = slice(hp * 64, (hp + 1) * 64)
                            S.act(lambda h_, prt=prt, hp=hp, cc=cc, tsl=tsl, tk_=tk_: h_.activation(
                                out=kT[prt, cc, hp, tsl], in_=tk_[prt, :], func=AF.Copy),
                                reads=[("qs", ri % 2)], writes=[("k", cc, T, hp)])

                def kmean(ch):
                    cc, hp = ch // 2, ch % 2
                    S.dve(lambda h_: h_.tensor_reduce(out=kms[:, ch, :], in_=kT[:, cc, hp, :].rearrange("p (n k) -> p n k", k=256),
                                                      axis=AX.X, op=ALU.add),
                          reads=[("k", cc, T, hp) for T in range(NT)] + [("kz", hp)], writes=[("kms", ch)])
                    S.dve(lambda h_: h_.tensor_scalar(out=kmT[:, ch, :], in0=kms[:, ch, :], scalar1=1.0 / 256.0, scalar2=None, op0=ALU.mult),
                          reads=[("kms", ch)], writes=[("kmT", ch)])

                def rope_units(which, wsrc, wkey, between=None):
                    ulist = [(which, cc, T) for cc in range(2) for T in range(NT)]
                    for ui_, u_ in enumerate(ulist):
                        st_ = rope_a(u_[0], u_[1], u_[2], wsrc, wkey)
                        if pend:
                            rope_b(*pend.pop(0))
                        pend.append(st_)
                        if which == "k" and ui_ == NT and sub is None:
                            kmean(0)
                            kmean(1)
                        if between is not None:
                            between(ui_)
                    while pend:
                        rope_b(*pend.pop(0))

                if g == 0 or sub is not None:
                    rope_units("k", wk, ("wsl", sl["k"]))
                rope_units("q", wq, ("wsl", sl["q"]))
                if sub == "qk":
                    break
                if sub == "v":
                    break
                def gate_matmuls():
                    gbank = ps[6]
                    for qt in range(8, 16):
                        T = qt // 4
                        for hl in range(4):
                            cc, hp = hl // 2, hl % 2
                            col = (qt - 8) * 32 + hl * 8
                            self.mm(gbank[:, col:col + 8], qT[:, cc, qt * 128:(qt + 1) * 128],
                                    kmT[:, hl, :], True, True,
                                    [("q", cc, T, 0), ("q", cc, T, 1), ("kmT", hl)], [("ps", 6)])
                    S.act(lambda h_: h_.activation(out=gsb[:, :], in_=gbank[:, 0:256], func=AF.Copy),
                          reads=[("ps", 6)], writes=["gsb"])

                def gate_chain(qt):
                    qb = qt // 2
                    sI = qt % 4
                    c2 = qt % 2
                    g3 = gsb[:, (qt - 8) * 32:(qt - 7) * 32].rearrange("p (a n) -> p a n", n=8)[:, :, 0:qb]
                    in0 = g3.unsqueeze(2).broadcast_to([128, 4, qb, qb])
                    in1 = g3.unsqueeze(3).broadcast_to([128, 4, qb, qb])
                    cm = cmp_[:, c2, 0:4 * qb * qb].rearrange("p (a n m) -> p a n m", a=4, n=qb)
                    S.dve(lambda h_, cm=cm, in0=in0, in1=in1: h_.tensor_tensor(out=cm, in0=in0, in1=in1, op=ALU.is_gt),
                          reads=["gsb"], writes=[("cmp", c2)])
                    rk = rank[:, c2, :].rearrange("p (a n) -> p a n", n=8)[:, :, 0:qb]
                    S.dve(lambda h_, rk=rk, cm=cm: h_.tensor_reduce(out=rk, in_=cm, axis=AX.X, op=ALU.add),
                          reads=[("cmp", c2)], writes=[("rank", c2)])
                    S.pool(lambda h_, sI=sI: h_.memset(btok[:, sI, :], 0.0), writes=[("btok", sI)])
                    bo = btok[:, sI, :].rearrange("p (a b n) -> p a b n", a=2, b=2)[:, :, :, 0:qb]
                    rk4 = rank[:, c2, :].rearrange("p (a b n) -> p a b n", a=2, b=2)[:, :, :, 0:qb]
                    S.dve(lambda h_, bo=bo, rk4=rk4: h_.tensor_scalar(out=bo, in0=rk4, scalar1=2.5, scalar2=NEG, op0=ALU.is_gt, op1=ALU.mult),
                          reads=[("rank", c2)], writes=[("btok", sI)])

                def gate_transposes(q4):
                    for a in range(2):
                        bti = 2 + a
                        for qq in range(4):
                            qt = 8 + 4 * q4 + qq
                            sI = qt % 4
                            self.mm(ps[bti][:, qq * 128:(qq + 1) * 128], btok[:, sI, a * 128:(a + 1) * 128], self.identb[:], True, True,
                                    [("btok", sI), "identb"], [("ps", bti)])
                        S.act(lambda h_, a=a, q4=q4, bti=bti: h_.activation(out=biasT[:, a, q4 * TT:(q4 + 1) * TT], in_=ps[bti][:, :], func=AF.Copy),
                              reads=[("ps", bti)], writes=[("biasT", a, q4)])
                def v_proj(tp_lo, tp_hi):
                  for tp in range(tp_lo, tp_hi):
                      bi = 4 + tp % 2
                      bank = ps[bi]
                      for u in range(2):
                          tt = 2 * tp + u
                          T = tt // 4
                          for c in range(NCH):
                              self.mm(bank[:, u * 256:(u + 1) * 256], hn[:, c, tt * 128:(tt + 1) * 128], wv[:, c, :], c == 0, c == NCH - 1,
                                      [("wsl", sl["v"]), ("hn", c, T)], [("ps", bi)])
                      src = bank[:, :].rearrange("p (u a b e) -> p u a b e", u=2, a=2, b=2)
                      S.act(lambda h_, src=src, tp=tp: h_.activation(out=Vt[:, 2 * tp:2 * tp + 2, 0:4:2, 0:64], in_=src[:, :, :, 0, :], func=AF.Copy),
                            reads=[("ps", bi)], writes=[("Vt", 2 * tp, 0), ("Vt", 2 * tp + 1, 0)])
                      S.dve(lambda h_, src=src, tp=tp: h_.tensor_copy(out=Vt[:, 2 * tp:2 * tp + 2, 1:4:2, 64:128], in_=src[:, :, :, 1, :]),
                            reads=[("ps", bi)], writes=[("Vt", 2 * tp, 1), ("Vt", 2 * tp + 1, 1)])
                v_proj(0, 1)
                kmean(2)
                v_proj(1, 2)
                kmean(3)
                v_proj(2, 4)
                gate_matmuls()
                for qt in range(8, 12):
                    gate_chain(qt)
                v_proj(4, 6)
                gate_transposes(0)
                for qt in range(12, 16):
                    gate_chain(qt)
                v_proj(6, 8)
                gate_transposes(1)
                if g + 1 < 4 and sub is None:
                    pending = load_group(g + 1)
                if sub == "gate":
                    break
                iters = []
                itc = 0
                for cc in range(2):
                    for T in range(NT):
                        nk = 4 * T + 4
                        ob = [4 + 2 * (itc % 2), 5 + 2 * (itc % 2)]
                        itc += 1
                        for kt in range(nk):
                            for hp in range(2):
                                iters.append((cc, T, kt, hp, nk, ob[hp]))
                LAG = 3

                def stage_a(n_, cc, T, kt, hp, nk, obk):
                    nb = kt // 2
                    q_lo = max(0, kt - 4 * T) * 128
                    qsl = slice(q_lo, TT)
                    Tk = kt // 4
                    sbi = n_ % 4
                    sbk = ps[sbi]
                    need_bias = (T >= 2 and kt < 4 * T + 2)
                    need_mask = kt >= 4 * T
                    self.mm(sbk[:, qsl], kT[:, cc, hp, kt * 128:(kt + 1) * 128], qT[:, cc, T * TT + q_lo:(T + 1) * TT],
                            True, not (need_bias or need_mask),
                            [("k", cc, Tk, hp), ("kz", hp), ("q", cc, T, 0), ("q", cc, T, 1)], [("ps", sbi)])
                    if need_bias:
                        self.mm(sbk[:, qsl], self.ind[hp][:, nb * 128:(nb + 1) * 128],
                                biasT[:, cc, (T - 2) * TT + q_lo:(T - 1) * TT],
                                False, not need_mask, ["ind", ("biasT", cc, T - 2)], [("ps", sbi)])
                    if need_mask:
                        self.mm(sbk[:, q_lo:q_lo + 128], self.identb[:], self.maskT[:], False, True,
                                ["identb", "maskT"], [("ps", sbi)])

                def stage_bc(n_, cc, T, kt, hp, nk, obk):
                    hl = 2 * cc + hp
                    q_lo = max(0, kt - 4 * T) * 128
                    qsl = slice(q_lo, TT)
                    sbi = n_ % 4
                    sbk = ps[sbi]
                    pi = n_ % 4
                    S.act(lambda h_, pi=pi, sbk=sbk, qsl=qsl: h_.activation(out=pT[pi][:, qsl], in_=sbk[:, qsl], func=AF.Exp),
                          reads=[("ps", sbi)], writes=[("pT", pi)])
                    self.mm(ps[obk][:, qsl], Vt[:, kt, hl, :], pT[pi][:, qsl], kt == 0, kt == nk - 1,
                            [("pT", pi), ("Vt", kt, hp), ("Vt1", hp)], [("ps", obk)])
                    if kt == nk - 1:
                        o = ps[obk]
                        num = slice(hp * 64, (hp + 1) * 64)
                        den = slice((1 - hp) * 64, (2 - hp) * 64)
                        rc = rec[hp]
                        S.dve(lambda h_, rc=rc, o=o, den=den: h_.reciprocal(rc[den, :], o[den, :]),
                              reads=[("ps", obk)], writes=[("rec", hp)])
                        S.dve(lambda h_, rc=rc, o=o, den=den, num=num, cc=cc, T=T: h_.tensor_tensor(
                            out=qT[num, cc, T * TT:(T + 1) * TT], in0=o[num, :], in1=rc[den, :], op=ALU.mult),
                            reads=[("ps", obk), ("rec", hp)], writes=[("q", cc, T, hp)])

                NI = len(iters)
                for n_ in range(NI + LAG):
                    if n_ < NI:
                        stage_a(n_, *iters[n_])
                    m_ = n_ - LAG
                    if m_ >= 0:
                        stage_bc(m_, *iters[m_])
                if sub == "core":
                    break
                wo_units = [(T, dc) for T in range(NT) for dc in range(NCH)]

                def wo_unit(T, dc):
                    tsl = slice(T * TT, (T + 1) * TT)
                    bi = 4 + (T * NCH + dc) % 4
                    for cc in range(2):
                        self.mm(ps[bi][:, :], wo[:, cc, dc * 128:(dc + 1) * 128], qT[:, cc, tsl], cc == 0, cc == 1,
                                [("wsl", sl["o"]), ("q", cc, T, 0), ("q", cc, T, 1)], [("ps", bi)])
                    S.dve(lambda h_: h_.tensor_tensor(out=h[:, dc, tsl], in0=ps[bi][:, :], in1=h[:, dc, tsl], op=ALU.add),
                          reads=[("ps", bi), ("h", dc, T)], writes=[("h", dc, T)])

                if g + 1 < 4 and sub is None:
                    nsl = pending
                    nwk = wsl[nsl["k"]][:, :].rearrange("p (c f) -> p c f", f=256)

                    def between(ui_):
                        for (T_, dc_) in wo_units[4 * ui_:4 * ui_ + 4]:
                            wo_unit(T_, dc_)

                    rope_units("k", nwk, ("wsl", nsl["k"]), between)
                else:
                    for (T_, dc_) in wo_units:
                        wo_unit(T_, dc_)
            S.barrier()
            S.flush()

    def phase_conformer(self, j):
        nc, S, ps = self.nc, self.S, self.ps
        hn, h = self.hn, self.h
        PADL = 32
        with ExitStack() as es:
            glu = self.sb(es, [128, NCH, PADL + SEQ], BF16, "glu")
            ybf = [self.sb(es, [128, NCH, TT], BF16, "ybf") for _ in range(2)]
            ysq = self.sb(es, [128, NCH, TT], BF16, "ysq")
            dg = self.sb(es, [128, CW, 128], BF16, "dg")
            NW = 3
            wsl = [self.sb(es, [128, NCH, 256], BF16, "cw") for _ in range(NW)]
            tA = [self.sb(es, [128, TT], F32, "tA") for _ in range(2)]
            sgm = tA
            mean_t = self.sb(es, [128, TT], F32, "mean_t")
            m2_t = self.sb(es, [128, TT], F32, "m2_t")
            wcount = [0]

            def next_slot():
                i = wcount[0] % NW
                wcount[0] += 1
                return i

            S.pool(lambda h_: h_.memset(glu[:, :, 0:PADL], 0.0), writes=[("glupad",)])
            w1 = self.w_pw1[j].rearrange("(c p) f -> p c f", p=128)
            w2 = self.w_pw2[j].rearrange("(c p) f -> p c f", p=128)

            def load_pw1(cb):
                ia = next_slot()
                self.load_w(wsl[ia][:], w1[:, :, cb * 256:(cb + 1) * 256], ("cw", ia), "cw%d" % ia)
                ig = next_slot()
                self.load_w(wsl[ig][:], w1[:, :, D + cb * 256:D + (cb + 1) * 256], ("cw", ig), "cw%d" % ig)
                return ia, ig

            it = 0
            for cb in range(4):
                ia, ig = load_pw1(cb)
                for ci in range(2):
                    cc = 2 * cb + ci
                    for T in range(NT):
                        ba, bg = (it % 2) * 2, (it % 2) * 2 + 1
                        it += 1
                        tsl = slice(T * TT, (T + 1) * TT)
                        for c in range(NCH):
                            self.mm(ps[ba][:, :], wsl[ia][:, c, ci * 128:(ci + 1) * 128], hn[:, c, tsl], c == 0, c == NCH - 1,
                                    [("cw", ia), ("hn", c, T)], [("ps", ba)])
                        for c in range(NCH):
                            self.mm(ps[bg][:, :], wsl[ig][:, c, ci * 128:(ci + 1) * 128], hn[:, c, tsl], c == 0, c == NCH - 1,
                                    [("cw", ig), ("hn", c, T)], [("ps", bg)])
                        sg_ = sgm[it % 2]
                        S.act(lambda h_, sg_=sg_, bg=bg, cc=cc: h_.activation(out=sg_[:], in_=ps[bg][:, :], func=AF.Sigmoid,
                                                                               bias=self.prow(R_BPW1 + j * 16 + 8 + cc)),
                              reads=[("ps", bg), "ptab"], writes=[("tA", it % 2)])
                        S.dve(lambda h_, sg_=sg_, ba=ba, cc=cc, T=T: h_.scalar_tensor_tensor(
                            out=glu[:, cc, PADL + T * TT:PADL + (T + 1) * TT], in0=ps[ba][:, :], scalar=self.prow(R_BPW1 + j * 16 + cc),
                            in1=sg_[:], op0=ALU.add, op1=ALU.mult),
                            reads=[("ps", ba), ("tA", it % 2), "ptab"], writes=[("glu", cc, T)])
            pw2_slots = {}

            def load_pw2(db):
                i = next_slot()
                self.load_w(wsl[i][:], w2[:, :, db * 256:(db + 1) * 256], ("cw", i), "cw%d" % i)
                pw2_slots[db] = i

            load_pw2(0)
            load_pw2(1)
            dcount = [0]

            YB = (0, 1, 6, 7)

            def conv_unit(T, cc):
                yb = YB[cc % 4]
                yT = ybf[T % 2]
                for tap in range(CW):
                    if dcount[0] % 2 == 0:
                        S.dve(lambda h_, tap=tap, cc=cc: h_.tensor_scalar(out=dg[:, tap, :], in0=self.identb[:],
                                                                          scalar1=self.prow(R_WDW + (j * CW + tap) * 8 + cc), scalar2=None, op0=ALU.mult),
                              reads=["identb", "ptab"], writes=[("dg", tap)])
                    else:
                        S.pool(lambda h_, tap=tap, cc=cc: h_.tensor_scalar(out=dg[:, tap, :], in0=self.identb[:],
                                                                           scalar1=self.prow(R_WDW + (j * CW + tap) * 8 + cc), scalar2=1.0,
                                                                           op0=ALU.mult, op1=ALU.mult),
                               reads=["identb", "ptab"], writes=[("dg", tap)])
                    dcount[0] += 1
                for tap in range(CW):
                    o0 = PADL + T * TT - (CW - 1) + tap
                    rd = [("dg", tap), ("glu", cc, T)]
                    rd.append(("glu", cc, T - 1) if T > 0 else ("glupad",))
                    self.mm(ps[yb][:, :], dg[:, tap, :], glu[:, cc, o0:o0 + TT], tap == 0, tap == CW - 1, rd, [("ps", yb)])

            def conv_evac(T, cc):
                yb = YB[cc % 4]
                yT = ybf[T % 2]
                S.act(lambda h_: h_.activation(out=yT[:, cc, :], in_=ps[yb][:, :], func=AF.Identity,
                                               bias=self.prow(R_BDW + j * 8 + cc)),
                      reads=[("ps", yb), "ptab"], writes=[("ybf", T % 2, cc)])
                S.act(lambda h_: h_.activation(out=ysq[:, cc, :], in_=yT[:, cc, :], func=AF.Square),
                      reads=[("ybf", T % 2, cc)], writes=[("ysq", cc)])

            def ln_head(T):
                yT = ybf[T % 2]
                bm, bq = 2 + (T % 2) * 2, 3 + (T % 2) * 2
                for cc in range(NCH):
                    self.mm(ps[bm][:, :], self.onesb[:], yT[:, cc, :], cc == 0, cc == NCH - 1, ["onesb", ("ybf", T % 2, cc)], [("ps", bm)])
                for cc in range(NCH):
                    self.mm(ps[bq][:, :], self.onesb[:], ysq[:, cc, :], cc == 0, cc == NCH - 1, ["onesb", ("ysq", cc)], [("ps", bq)])

            def ln_head_b(T):
                bm, bq = 2 + (T % 2) * 2, 3 + (T % 2) * 2
                S.act(lambda h_: h_.activation(out=mean_t[:], in_=ps[bm][:, :], func=AF.Copy, scale=1.0 / D),
                      reads=[("ps", bm)], writes=["mean_t"])
                S.dve(lambda h_: h_.tensor_tensor(out=m2_t[:], in0=mean_t[:], in1=mean_t[:], op=ALU.mult), reads=["mean_t"], writes=["m2_t"])
                S.dve(lambda h_: h_.scalar_tensor_tensor(out=m2_t[:], in0=ps[bq][:, :], scalar=1.0 / D, in1=m2_t[:],
                                                         op0=ALU.mult, op1=ALU.subtract),
                      reads=[("ps", bq), "m2_t"], writes=["m2_t"])
                S.act(lambda h_: h_.activation(out=m2_t[:], in_=m2_t[:], func=AF.Sqrt, bias=LN_EPS), reads=["m2_t"], writes=["m2_t"])
                S.dve(lambda h_: h_.reciprocal(ps[bm][:, :], m2_t[:]), reads=["m2_t"], writes=[("ps", bm)])
                S.dve(lambda h_: h_.tensor_tensor(out=ps[bq][:, :], in0=ps[bm][:, :], in1=mean_t[:], op=ALU.mult),
                      reads=[("ps", bm), "mean_t"], writes=[("ps", bq)])

            def ln_norm(T, cc):
                yT = ybf[T % 2]
                bm, bq = 2 + (T % 2) * 2, 3 + (T % 2) * 2
                ta_ = tA[cc % 2]
                S.dve(lambda h_: h_.tensor_tensor(out=ta_[:], in0=ps[bm][:, :], in1=yT[:, cc, :], op=ALU.mult),
                      reads=[("ps", bm), ("ybf", T % 2, cc)], writes=[("tA", cc % 2)])
                S.dve(lambda h_: h_.tensor_tensor(out=ta_[:], in0=ta_[:], in1=ps[bq][:, :], op=ALU.subtract),
                      reads=[("ps", bq), ("tA", cc % 2)], writes=[("tA", cc % 2)])
                S.act(lambda h_: h_.activation(out=hn[:, cc, T * TT:(T + 1) * TT], in_=ta_[:], func=AF.Silu,
                                               scale=self.prow(R_LNG + j * 8 + cc), bias=self.prow(R_LNB + j * 8 + cc)),
                      reads=[("tA", cc % 2), "ptab"], writes=[("hn", cc, T)])

            for T in range(NT):
                for cc in range(NCH):
                    conv_unit(T, cc)
                    if T > 0 and cc == 0:
                        ln_head(T - 1)
                    if T > 0 and cc == 1:
                        ln_head_b(T - 1)
                    conv_evac(T, cc)
                    if T > 0 and cc >= 1:
                        ln_norm(T - 1, cc - 1)
                if T > 0:
                    ln_norm(T - 1, NCH - 1)
            ln_head(NT - 1)
            ln_head_b(NT - 1)
            for cc in range(NCH):
                ln_norm(NT - 1, cc)
            it = 0
            for db in range(4):
                if db + 2 < 4:
                    load_pw2(db + 2)
                i = pw2_slots[db]
                for T in range(NT):
                    for di in range(2):
                        dc = 2 * db + di
                        bi = (6, 7, 0, 1, 2, 3, 4, 5)[it % 8]
                        it += 1
                        tsl = slice(T * TT, (T + 1) * TT)
                        for cc in range(NCH):
                            self.mm(ps[bi][:, :], wsl[i][:, cc, di * 128:(di + 1) * 128], hn[:, cc, tsl], cc == 0, cc == NCH - 1,
                                    [("cw", i), ("hn", cc, T)], [("ps", bi)])
                        S.dve(lambda h_, bi=bi, dc=dc, tsl=tsl: h_.scalar_tensor_tensor(
                            out=h[:, dc, tsl], in0=ps[bi][:, :], scalar=self.prow(R_BPW2 + j * 8 + dc), in1=h[:, dc, tsl], op0=ALU.add, op1=ALU.add),
                            reads=[("ps", bi), ("h", dc, T), "ptab"], writes=[("h", dc, T)])
            S.barrier()
            S.flush()

    def phase_ffn(self, i):
        nc, S, ps = self.nc, self.S, self.ps
        hn, h = self.hn, self.h
        groups = [[0, 1, 2], [3, 4, 5], [6, 7, 8], [9, 10]]
        with ExitStack() as es:
            act = self.sb(es, [128, 6, SEQ], BF16, "act")
            NWU = 3
            wup = [self.sb(es, [128, NCH, 512], BF16, "wup") for _ in range(NWU)]
            wdn = [self.sb(es, [128, 6, D], BF16, "wdn") for _ in range(2)]
            Ag = [self.sb(es, [128, TT], F32, "Ag") for _ in range(2)]
            Av = [self.sb(es, [128, TT], F32, "Av") for _ in range(2)]
            sg = [self.sb(es, [128, TT], F32, "sg") for _ in range(2)]
            halo = self.sb(es, [128, 2, 2, 2], F32, "halo")
            wu_src = self.w_up[i].rearrange("(c p) f -> p c f", p=128)
            wd_src = self.w_down[i].rearrange("(c p) f -> p c f", p=128)
            ucount = [0]
            blocks = [b for g in groups for b in g]
            up_slot = {}

            def load_up(b):
                s = ucount[0] % NWU
                ucount[0] += 1
                up_slot[b] = s
                self.load_w(wup[s][:, :, 0:256], wu_src[:, :, b * 256:(b + 1) * 256], ("wup", s), "wu%d" % s)
                self.load_w(wup[s][:, :, 256:512], wu_src[:, :, DFF + b * 256:DFF + (b + 1) * 256], ("wup", s), "wu%d" % s)

            def load_dn(gi):
                import os
                if os.environ.get("FFN_NODN"):
                    return
                g = groups[gi]
                np_ = 2 * len(g)
                j0 = 2 * g[0]
                self.load_w(wdn[gi % 2][:, 0:np_, :], wd_src[:, j0:j0 + np_, :], ("wdn", gi % 2), "wd%d" % (gi % 2))

            pend3 = []
            load_up(blocks[0])
            load_up(blocks[1])
            load_dn(0)
            nxt = 2
            it = 0
            for gi, g in enumerate(groups):
                for bl, b in enumerate(g):
                    s = up_slot[b]
                    for pi in range(2):
                        jp = 2 * b + pi
                        jl = 2 * bl + pi
                        rows = {}
                        for kind, fc in (("g", jp), ("v", NFP + jp)):
                            rows[kind] = [R_FWDW + (i * 3 + tap) * 44 + fc for tap in range(3)] + [R_FBDW + i * 44 + fc]
                        for T in range(NT):
                            par = it % 2
                            it += 1
                            tsl = slice(T * TT, (T + 1) * TT)
                            s3 = (it - 1) % 3
                            bg_, bv_ = s3 * 2, s3 * 2 + 1
                            import os
                            for c in range(NCH if not os.environ.get("FFN_NOMM") else 0):
                                self.mm(ps[bg_][:, :], wup[s][:, c, pi * 128:(pi + 1) * 128], hn[:, c, tsl], c == 0, c == NCH - 1,
                                        [("wup", s), ("hn", c, T)], [("ps", bg_)])
                            for c in range(NCH if not os.environ.get("FFN_NOMM") else 0):
                                self.mm(ps[bv_][:, :], wup[s][:, c, 256 + pi * 128:256 + (pi + 1) * 128], hn[:, c, tsl], c == 0, c == NCH - 1,
                                        [("wup", s), ("hn", c, T)], [("ps", bv_)])
                            A = {"g": Ag[par], "v": Av[par]}
                            U = {"g": ps[bg_], "v": ps[bv_]}
                            UB = {"g": bg_, "v": bv_}
                            AK = {"g": ("Ag", par), "v": ("Av", par)}
                            KI = {"g": 0, "v": 1}
                            hp_prev = (T - 1) % 2
                            hp_cur = T % 2
                            import os
                            FL = int(os.environ.get("FFN_LEVEL", "9"))
                            for kind in ("g", "v"):
                                if FL < 2:
                                    break
                                r = rows[kind]
                                S.act(lambda h_, A_=A[kind], U_=U[kind], r=r: h_.activation(out=A_[:], in_=U_[:, :], func=AF.Identity,
                                                                                          scale=self.prow(r[2]), bias=self.prow(r[3])),
                                      reads=[("ps", UB[kind]), "ptab"], writes=[AK[kind]])
                                if T < NT - 1 and FL >= 3:
                                    S.act(lambda h_, U_=U[kind], kind=kind, hp_cur=hp_cur: h_.activation(
                                        out=halo[:, hp_cur, KI[kind], :], in_=U_[:, TT - 2:TT], func=AF.Copy),
                                        reads=[("ps", UB[kind])], writes=[("halo", hp_cur, kind)])
                            for kind in ("g", "v"):
                                if FL < 4:
                                    break
                                r = rows[kind]
                                S.dve(lambda h_, A_=A[kind], U_=U[kind], r=r: h_.scalar_tensor_tensor(
                                    out=A_[:, 1:TT], in0=U_[:, 0:TT - 1], scalar=self.prow(r[1]), in1=A_[:, 1:TT], op0=ALU.mult, op1=ALU.add),
                                    reads=[("ps", UB[kind]), AK[kind], "ptab"], writes=[AK[kind]])
                            for kind in ("g", "v"):
                                if FL < 4:
                                    break
                                r = rows[kind]
                                S.dve(lambda h_, A_=A[kind], U_=U[kind], r=r: h_.scalar_tensor_tensor(
                                    out=A_[:, 2:TT], in0=U_[:, 0:TT - 2], scalar=self.prow(r[0]), in1=A_[:, 2:TT], op0=ALU.mult, op1=ALU.add),
                                    reads=[("ps", UB[kind]), AK[kind], "ptab"], writes=[AK[kind]])
                            if T > 0 and FL >= 5:
                                for kind in ("g", "v"):
                                    r = rows[kind]
                                    hl_ = halo[:, hp_prev, KI[kind], :]
                                    S.dve(lambda h_, A_=A[kind], hl_=hl_, r=r: h_.scalar_tensor_tensor(
                                        out=A_[:, 0:1], in0=hl_[:, 1:2], scalar=self.prow(r[1]), in1=A_[:, 0:1], op0=ALU.mult, op1=ALU.add),
                                        reads=[("halo", hp_prev, kind), AK[kind], "ptab"], writes=[AK[kind]])
                                for kind in ("g", "v"):
                                    r = rows[kind]
                                    hl_ = halo[:, hp_prev, KI[kind], :]
                                    S.dve(lambda h_, A_=A[kind], hl_=hl_, r=r: h_.scalar_tensor_tensor(
                                        out=A_[:, 0:2], in0=hl_[:, 0:2], scalar=self.prow(r[0]), in1=A_[:, 0:2], op0=ALU.mult, op1=ALU.add),
                                        reads=[("halo", hp_prev, kind), AK[kind], "ptab"], writes=[AK[kind]])
                            if FL < 6:
                                continue
                            def stage3(par=par, Ag_=A["g"], Av_=A["v"], jl=jl, tsl=tsl, T=T):
                                sg_ = sg[par]
                                S.act(lambda h_: h_.activation(out=sg_[:], in_=Ag_[:], func=AF.Silu),
                                      reads=[("Ag", par)], writes=[("sg", par)])
                                S.pool(lambda h_: h_.tensor_tensor(out=act[:, jl, tsl], in0=sg_[:], in1=Av_[:], op=ALU.mult),
                                       reads=[("sg", par), ("Av", par)], writes=[("act", jl, T)])
                            if pend3:
                                pend3.pop(0)()
                            pend3.append(stage3)
                            if pi == 0 and T == 2:
                                if nxt < len(blocks):
                                    load_up(blocks[nxt])
                                    nxt += 1
                                if bl == 0 and gi + 1 < len(groups):
                                    load_dn(gi + 1)
                while pend3:
                    pend3.pop(0)()
                np_ = 2 * len(g)
                wd = wdn[gi % 2]
                dn = 0
                for T in range(NT if FL >= 7 else 0):
                    tsl = slice(T * TT, (T + 1) * TT)
                    for dc in range(NCH):
                        s_next = it % 3
                        bi = (6, 7, 2 * s_next, 2 * s_next + 1)[dn % 4]
                        dn += 1
                        for jl in range(np_):
                            self.mm(ps[bi][:, :], wd[:, jl, dc * 128:(dc + 1) * 128], act[:, jl, tsl], jl == 0, jl == np_ - 1,
                                    [("wdn", gi % 2), ("act", jl, T)], [("ps", bi)])
                        S.dve(lambda h_, bi=bi, dc=dc, tsl=tsl: h_.tensor_tensor(out=h[:, dc, tsl], in0=ps[bi][:, :], in1=h[:, dc, tsl], op=ALU.add),
                              reads=[("ps", bi), ("h", dc, T)], writes=[("h", dc, T)])
            S.barrier()
            S.flush()

    def phase_final(self, raw=False):
        nc, S, ps = self.nc, self.S, self.ps
        with ExitStack() as es:
            ofm = self.sb(es, [128, NCH, TT], F32, "ofm")
            ost = [self.sb(es, [128, D], F32, "ost") for _ in range(3)]
            oc = 0
            for T in range(NT):
                tsl = slice(T * TT, (T + 1) * TT)
                if raw:
                    for c in range(NCH):
                        eng = S.act if c % 2 == 0 else S.dve
                        if c % 2 == 0:
                            S.act(lambda h_, c=c, tsl=tsl: h_.activation(out=ofm[:, c, :], in_=self.h[:, c, tsl], func=AF.Copy),
                                  reads=[("h", c, T)], writes=[("ofm", c)])
                        else:
                            S.dve(lambda h_, c=c, tsl=tsl: h_.tensor_copy(out=ofm[:, c, :], in_=self.h[:, c, tsl]),
                                  reads=[("h", c, T)], writes=[("ofm", c)])
                else:
                    self.rmsnorm_tile(T, R_FIN, ofm)
                for ts in range(4):
                    tt = T * 4 + ts
                    sl = oc % 3
                    oc += 1
                    for half in range(2):
                        bi = (2 * tt + half) % 4
                        bank = ps[bi]
                        for jj in range(4):
                            c = half * 4 + jj
                            S.pe(lambda h_, bank=bank, jj=jj, c=c, ts=ts: h_.transpose(out=bank[:, jj * 128:(jj + 1) * 128],
                                                                                         in_=ofm[:, c, ts * 128:(ts + 1) * 128], identity=self.identf[:]),
                                 reads=[("ofm", c), "identf"], writes=[("ps", bi)])
                        dst = ost[sl][:, half * 512:(half + 1) * 512]
                        if half == 0:
                            S.act(lambda h_, dst=dst, bank=bank: h_.activation(out=dst, in_=bank[:, :], func=AF.Copy),
                                  reads=[("ps", bi)], writes=[("ost", sl, half)])
                        else:
                            S.dve(lambda h_, dst=dst, bank=bank: h_.tensor_copy(out=dst, in_=bank[:, :]),
                                  reads=[("ps", bi)], writes=[("ost", sl, half)])
                    S.dma("sp", "ost%d" % sl, lambda h_, sl=sl, tt=tt: h_.dma_start(out=self.out[tt * 128:(tt + 1) * 128, :], in_=ost[sl][:]),
                          reads=[("ost", sl, 0), ("ost", sl, 1)])
            S.barrier()
            S.flush()

    def rmsnorm_tile(self, T, grow, ofm):
        S, ps = self.S, self.ps
        bi = 6 + T % 2
        bank = ps[bi]
        tsl = slice(T * TT, (T + 1) * TT)
        for c in range(NCH):
            sq = self.nsq[:, c % 2, :]
            S.act(lambda h, sq=sq, c=c: h.activation(out=sq, in_=self.h[:, c, tsl], func=AF.Square),
                  reads=[("h", c, T)], writes=[("nsq", c % 2)])
            self.mm(bank[:, :], self.onesb[:], sq, c == 0, c == NCH - 1, [("nsq", c % 2), "onesb"], [("ps", bi)])
        sd = self.nstd[:, T % 2, :]
        S.act(lambda h: h.activation(out=sd, in_=bank[:, :], func=AF.Sqrt, scale=1.0 / D, bias=NORM_EPS),
              reads=[("ps", bi)], writes=[("nstd", T % 2)])
        S.dve(lambda h: h.reciprocal(bank[:, :], sd), reads=[("nstd", T % 2)], writes=[("ps", bi)])
        for c in range(NCH):
            S.dve(lambda h, c=c: h.scalar_tensor_tensor(out=ofm[:, c, :], in0=self.h[:, c, tsl], scalar=self.prow(grow + c), in1=bank[:, :],
                                                        op0=ALU.mult, op1=ALU.mult),
                  reads=[("h", c, T), ("ps", bi), "ptab"], writes=[("ofm", c)])


def _pack_ptab(inp):
    f = lambda a: np.ascontiguousarray(np.asarray(a, dtype=np.float32)).reshape(-1, 128)
    parts = [
        f(inp["norm_mix_g"]), f(inp["norm_ffn_g"]), f(inp["final_norm_g"]), f(inp["conv_b_pw1"]),
        f(inp["conv_w_dw"]), f(inp["conv_b_dw"]), f(inp["conv_ln_g"]), f(inp["conv_ln_b"]), f(inp["conv_b_pw2"]),
        f(inp["ffn_w_dw"]), f(inp["ffn_b_dw"]),
    ]
    tab = np.concatenate(parts, axis=0)
    assert tab.shape[0] == 1368
    pad = np.zeros((R_TOT - tab.shape[0], 128), np.float32)
    return np.ascontiguousarray(np.concatenate([tab, pad], axis=0))


_NC_CACHE = {}


def _run(inputs, stop_after=None, trace=False):
    x = np.ascontiguousarray(np.asarray(inputs["x"], dtype=np.float32))
    B = x.shape[0]
    key = stop_after
    if key not in _NC_CACHE:
        _NC_CACHE[key] = Builder(stop_after).build()
    nc = _NC_CACHE[key]
    ptab = _pack_ptab(inputs)
    c = lambda k: np.ascontiguousarray(np.asarray(inputs[k], dtype=np.float32))
    shared = {
        "ptab_in": ptab, "w_qkv": c("attn_w_qkv"), "w_o": c("attn_w_o"), "w_pw1": c("conv_w_pw1"), "w_pw2": c("conv_w_pw2"),
        "w_up": c("ffn_w_up"), "w_down": c("ffn_w_down"),
    }
    in_maps = [dict(shared, x=x[b]) for b in range(B)]
    res = run_bass_kernel_spmd(nc, in_maps, core_ids=list(range(B)), trace=trace)
    out = np.stack([np.asarray(r["out"]) for r in res.results], axis=0).astype(np.float32)
    return out, res


def kernel(**inputs):
    out, _ = _run(inputs)
    return out
```

```python
import math
import numpy as np
from contextlib import ExitStack
import concourse.bass as bass
import concourse.mybir as mybir
from concourse.bass_utils import run_bass_kernel_spmd

F32 = mybir.dt.float32
BF16 = mybir.dt.bfloat16
I32 = mybir.dt.int32
ALU = mybir.AluOpType
AF = mybir.ActivationFunctionType
AX = mybir.AxisListType

D = 1024
SEQ = 2048
NCH = 8
NT = 4
TT = 512
H = 16
DH = 64
DFF = 2816
NFP = 22
DEPTH = 4
NEG = -30000.0
NORM_EPS = 1e-6
LN_EPS = 1e-5
CW = 31

R_MIX = 0
R_FFN = 32
R_FIN = 64
R_BPW1 = 72
R_WDW = 104
R_BDW = 600
R_LNG = 616
R_LNB = 632
R_BPW2 = 648
R_FWDW = 664
R_FBDW = 1192
R_TOT = 1408


class _Op:
    __slots__ = ("eng", "fn", "deps", "needs_inc", "semval", "grp", "gen", "pre")

    def __init__(self, eng, fn):
        self.eng = eng
        self.fn = fn
        self.deps = []
        self.needs_inc = False
        self.semval = None
        self.grp = None
        self.gen = 0
        self.pre = None


class _Grp:
    __slots__ = ("sem", "gens", "closed")

    def __init__(self, sem):
        self.sem = sem
        self.gens = [0]
        self.closed = False


class Sched:
    ENGS = ("pe", "act", "dve", "pool", "sp")

    def __init__(self, nc, es):
        self.nc = nc
        self.es = es
        self.q = {e: [] for e in self.ENGS}
        self.lastw = {}
        self.readers = {}
        self.esem = {e: es.enter_context(nc.semaphore("sem_" + e)) for e in ("pe", "act", "dve", "pool")}
        self.ecnt = {e: 0 for e in ("pe", "act", "dve", "pool")}
        self.seen = {e: {} for e in self.ENGS}
        self.groups = {}
        self.lastreal = {e: None for e in self.ENGS}
        self.nops = 0

    def _group(self, name):
        g = self.groups.get(name)
        if g is None:
            g = _Grp(self.es.enter_context(self.nc.semaphore("dg_" + name)))
            self.groups[name] = g
        return g

    def add(self, eng, fn, reads=(), writes=(), dma=None):
        op = _Op(eng, fn)
        deps = {}
        for k in reads:
            w = self.lastw.get(k)
            if w is not None:
                deps[id(w)] = w
            if isinstance(k, tuple) and k[0] == "ps":
                rd = self.readers.get(k)
                if rd:
                    for rk_, r in rd.items():
                        if rk_ != eng:
                            deps[id(r)] = r
        for k in writes:
            w = self.lastw.get(k)
            if w is not None:
                deps[id(w)] = w
            rd = self.readers.get(k)
            if rd:
                for r in rd.values():
                    deps[id(r)] = r
        if dma is not None:
            g = self._group(dma)
            op.grp = g
            if g.closed:
                op.pre = [(g, len(g.gens) - 1)]
                g.gens.append(g.gens[-1])
                g.closed = False
            g.gens[-1] += 16
            op.gen = len(g.gens) - 1
        for d in deps.values():
            if eng == "pe" and d.eng == "pe" and d.grp is None:
                continue
            if op.grp is not None and d.grp is op.grp:
                continue
            op.deps.append(d)
            d.needs_inc = True
            if d.grp is not None:
                d.grp.closed = True
        rk = eng if dma is None else ("dma", id(op))
        for k in reads:
            self.readers.setdefault(k, {})[rk] = op
        for k in writes:
            self.lastw[k] = op
            self.readers[k] = {}
        self.q[eng].append(op)
        if dma is None:
            self.lastreal[eng] = op
        self.nops += 1
        return op

    def pe(self, fn, reads=(), writes=()):
        return self.add("pe", fn, reads, writes)

    def act(self, fn, reads=(), writes=()):
        return self.add("act", fn, reads, writes)

    def dve(self, fn, reads=(), writes=()):
        return self.add("dve", fn, reads, writes)

    def pool(self, fn, reads=(), writes=()):
        return self.add("pool", fn, reads, writes)

    def dma(self, queue, group, fn, reads=(), writes=()):
        return self.add(queue, fn, reads, writes, dma=group)

    def barrier(self):
        lasts = [op for op in self.lastreal.values() if op is not None and op.grp is None]
        gl = [(g, len(g.gens) - 1) for g in self.groups.values() if g.gens[-1] > 0]
        for g, _ in gl:
            g.closed = True
        for e in self.ENGS:
            op = _Op(e, None)
            for d in lasts:
                if not (e == "pe" and d.eng == "pe"):
                    op.deps.append(d)
                    d.needs_inc = True
            op.pre = list(gl)
            self.q[e].append(op)
        self.lastw = {}
        self.readers = {}

    def _assign(self):
        for e in ("pe", "act", "dve", "pool"):
            c = self.ecnt[e]
            for op in self.q[e]:
                if op.grp is None and op.fn is not None and op.needs_inc and op.semval is None:
                    c += 1
                    op.semval = c
            self.ecnt[e] = c

    def _emit_engine(self, e, h):
        seen = self.seen[e]

        def wait(sem, val):
            key = id(sem)
            if seen.get(key, 0) < val:
                h.wait_ge(sem, val)
                seen[key] = val

        for op in self.q[e]:
            if op.pre is not None:
                for g, gi in op.pre:
                    wait(g.sem, g.gens[gi])
            for d in op.deps:
                if d.grp is not None:
                    wait(d.grp.sem, d.grp.gens[d.gen])
                else:
                    wait(self.esem[d.eng], d.semval)
            if op.fn is not None:
                ins = op.fn(h)
                if op.grp is not None:
                    ins.then_inc(op.grp.sem, 16)
                elif op.needs_inc:
                    ins.then_inc(self.esem[e], 1)
        self.q[e] = []
        self.lastreal[e] = None

    def flush(self):
        self._assign()
        with self.nc.Block() as block:
            @block.tensor
            def _(h):
                self._emit_engine("pe", h)

            @block.scalar
            def _(h):
                self._emit_engine("act", h)

            @block.vector
            def _(h):
                self._emit_engine("dve", h)

            @block.gpsimd
            def _(h):
                self._emit_engine("pool", h)

            @block.sync
            def _(h):
                self._emit_engine("sp", h)


class Builder:
    def __init__(self, stop_after=None):
        self.stop_after = stop_after
        self.nc = bass.Bass("TRN2", target_bir_lowering=False)
        nc = self.nc
        dt = nc.dram_tensor
        self.x = dt("x", [SEQ, D], F32, kind="ExternalInput").ap()
        self.ptab_in = dt("ptab_in", [R_TOT, 128], F32, kind="ExternalInput").ap()
        self.w_qkv = dt("w_qkv", [2, D, 3 * D], F32, kind="ExternalInput").ap()
        self.w_o = dt("w_o", [2, D, D], F32, kind="ExternalInput").ap()
        self.w_pw1 = dt("w_pw1", [2, D, 2 * D], F32, kind="ExternalInput").ap()
        self.w_pw2 = dt("w_pw2", [2, D, D], F32, kind="ExternalInput").ap()
        self.w_up = dt("w_up", [DEPTH, D, 2 * DFF], F32, kind="ExternalInput").ap()
        self.w_down = dt("w_down", [DEPTH, DFF, D], F32, kind="ExternalInput").ap()
        self.out = dt("out", [SEQ, D], F32, kind="ExternalOutput").ap()
        self._uid = 0

    def sb(self, es, shape, dtype, name=None):
        self._uid += 1
        return es.enter_context(self.nc.sbuf_tensor("%s_%d" % (name or "t", self._uid), shape, dtype))

    def mm(self, out, lhsT, rhs, start, stop, reads, writes):
        self.S.pe(lambda h: h.matmul(out, lhsT=lhsT, rhs=rhs, start=start, stop=stop), reads, writes)

    def prow(self, r):
        return self.ptab[:, r:r + 1]

    def build(self):
        nc = self.nc
        with ExitStack() as es:
            self.S = S = Sched(nc, es)
            self.ps = [es.enter_context(nc.psum_tensor("ps%d" % i, [128, TT], F32)) for i in range(8)]
            self.h = self.sb(es, [128, NCH, SEQ], F32, "h")
            self.hn = self.sb(es, [128, NCH, SEQ], BF16, "hn")
            self.cosT = self.sb(es, [128, SEQ], BF16, "cosT")
            self.sinT = self.sb(es, [128, SEQ], BF16, "sinT")
            self.ptab = self.sb(es, [128, R_TOT], F32, "ptab")
            self.identf = self.sb(es, [128, 128], F32, "identf")
            self.identb = self.sb(es, [128, 128], BF16, "identb")
            self.onesb = self.sb(es, [128, 128], BF16, "onesb")
            self.maskT = self.sb(es, [128, 128], BF16, "maskT")
            self.prot = self.sb(es, [128, 128], BF16, "prot")
            self.ind = [self.sb(es, [128, 8 * 128], BF16, "ind") for _ in range(2)]
            self.nsq = self.sb(es, [128, 2, TT], BF16, "nsq")
            self.nstd = self.sb(es, [128, 2, TT], F32, "nstd")

            self.phase_setup()
            stop = self.stop_after
            done = stop == "load"
            for i in range(DEPTH):
                if done:
                    break
                j = i // 2
                self.rmsnorm(R_MIX + i * 8)
                if i % 2 == 0:
                    self.phase_attention(j)
                else:
                    self.phase_conformer(j)
                if stop is not None and stop.split(":")[0] == "mix%d" % i:
                    done = True
                    break
                self.rmsnorm(R_FFN + i * 8)
                fsub = stop.split(":")[1] if (stop and ":" in stop and stop.startswith("ffn")) else None
                if fsub != "norm":
                    self.phase_ffn(i)
                if stop is not None and stop.split(":")[0] == "ffn%d" % i:
                    done = True
                    break
            self.phase_final(raw=(stop is not None))
        return nc

    def phase_setup(self):
        nc, S = self.nc, self.S
        ps = self.ps
        with ExitStack() as es:
            xs = [self.sb(es, [128, D], F32, "xs") for _ in range(3)]
            pst = [self.sb(es, [128, 128], F32, "pst") for _ in range(2)]
            pi_i = self.sb(es, [128, 1], I32, "pi_i")
            pm_i = self.sb(es, [128, 2], I32, "pm_i")
            sgn = self.sb(es, [128, 2], F32, "sgn")
            invrow = self.sb(es, [1, 128], F32, "invrow")
            invf = self.sb(es, [128, 1], F32, "invf")
            pos_i = self.sb(es, [128, SEQ], I32, "pos_i")
            ang = self.sb(es, [128, SEQ], F32, "ang")
            ta = self.sb(es, [128, SEQ], F32, "ta")
            tb = self.sb(es, [128, SEQ], F32, "tb")
            tc = self.sb(es, [128, SEQ], F32, "tc")

            identf, identb, onesb, maskT, prot, ind = self.identf, self.identb, self.onesb, self.maskT, self.prot, self.ind
            S.pool(lambda h: h.memset(identf[:], 1.0), writes=["identf"])
            S.pool(lambda h: h.affine_select(out=identf[:], in_=identf[:], pattern=[[-1, 128]], compare_op=ALU.is_equal,
                                             fill=0.0, base=0, channel_multiplier=1), reads=["identf"], writes=["identf"])
            S.dve(lambda h: h.tensor_copy(out=identb[:], in_=identf[:]), reads=["identf"], writes=["identb"])
            S.pool(lambda h: h.memset(onesb[:], 1.0), writes=["onesb"])
            S.pool(lambda h: h.memset(maskT[:], 0.0), writes=["maskT"])
            S.pool(lambda h: h.affine_select(out=maskT[:], in_=maskT[:], pattern=[[1, 128]], compare_op=ALU.is_ge,
                                             fill=NEG, base=0, channel_multiplier=-1), reads=["maskT"], writes=["maskT"])
            for (dst, src) in ((0, 32), (32, 0), (64, 96), (96, 64)):
                S.dve(lambda h, dst=dst, src=src: h.tensor_copy(out=prot[:, dst:dst + 32], in_=identb[:, src:src + 32]),
                      reads=["identb"], writes=["prot"])
            for jj in range(2):
                S.pool(lambda h, jj=jj: h.memset(ind[jj][:], 0.0), writes=["ind"])
                S.pool(lambda h, jj=jj: h.memset(ind[jj][64 * jj:64 * jj + 64, :], 1.0), reads=["ind"], writes=["ind"])
                S.pool(lambda h, jj=jj: h.affine_select(out=ind[jj][64 * jj:64 * jj + 64, :], in_=ind[jj][64 * jj:64 * jj + 64, :],
                                                        pattern=[[1, 8], [0, 128]], compare_op=ALU.is_equal, fill=0.0,
                                                        base=0, channel_multiplier=-1), reads=["ind"], writes=["ind"])
            for r in range(R_TOT // 128):
                st = pst[r % 2]
                S.dma("sp", "pst%d" % (r % 2), lambda h, r=r, st=st: h.dma_start(out=st[:], in_=self.ptab_in[r * 128:(r + 1) * 128, :]),
                      writes=[("pst", r % 2)])
                bank = ps[r % 2]
                S.pe(lambda h, st=st, bank=bank: h.transpose(out=bank[:, 0:128], in_=st[:], identity=identf[:]),
                     reads=[("pst", r % 2), "identf"], writes=[("ps", r % 2)])
                S.act(lambda h, r=r, bank=bank: h.activation(out=self.ptab[:, r * 128:(r + 1) * 128], in_=bank[:, 0:128], func=AF.Copy),
                      reads=[("ps", r % 2)], writes=["ptab"])
            inv = (np.float32(1.0) / np.power(np.float32(10000.0), np.arange(0, DH, 2, dtype=np.float32) / np.float32(DH))).astype(np.float32)
            irv = invrow[:].rearrange("o (a b) -> o a b", b=32)
            for i in range(32):
                S.dve(lambda h, i=i: h.memset(irv[:, :, i:i + 1], float(inv[i])), writes=["invrow"])
            S.pe(lambda h: h.transpose(out=ps[2][:, 0:1], in_=invrow[:], identity=identf[0:1, 0:1]),
                 reads=["invrow", "identf"], writes=[("ps", 2)])
            S.act(lambda h: h.activation(out=invf[:], in_=ps[2][:, 0:1], func=AF.Copy), reads=[("ps", 2)], writes=["invf"])
            S.pool(lambda h: h.iota(pi_i[:], pattern=[[0, 1]], base=0, channel_multiplier=1), writes=["pi_i"])
            S.dve(lambda h: h.tensor_scalar(out=pm_i[:, 0:1], in0=pi_i[:], scalar1=32, scalar2=None, op0=ALU.bitwise_and),
                  reads=["pi_i"], writes=["pm_i"])
            S.dve(lambda h: h.tensor_copy(out=sgn[:, 0:1], in_=pm_i[:, 0:1]), reads=["pm_i"], writes=["sgn0"])
            S.dve(lambda h: h.tensor_scalar(out=sgn[:, 1:2], in0=sgn[:, 0:1], scalar1=1.0 / 16.0, scalar2=-1.0, op0=ALU.mult, op1=ALU.add),
                  reads=["sgn0"], writes=["sgn1"])
            S.pool(lambda h: h.iota(pos_i[:], pattern=[[1, SEQ]], base=0, channel_multiplier=0), writes=["pos_i"])
            S.dve(lambda h: h.tensor_copy(out=ta[:], in_=pos_i[:]), reads=["pos_i"], writes=["ta"])
            S.dve(lambda h: h.tensor_scalar(out=ang[:], in0=ta[:], scalar1=invf[:, 0:1], scalar2=None, op0=ALU.mult),
                  reads=["ta", "invf"], writes=["ang"])
            TWO_PI = 2.0 * math.pi
            C1 = 6.28125
            C2 = TWO_PI - C1
            MAGIC = 12582912.0
            LIM = 3.1415925
            S.dve(lambda h: h.tensor_scalar(out=ta[:], in0=ang[:], scalar1=1.0 / TWO_PI, scalar2=None, op0=ALU.mult),
                  reads=["ang", "ta"], writes=["ta"])
            S.dve(lambda h: h.tensor_scalar(out=tb[:], in0=ta[:], scalar1=MAGIC, scalar2=MAGIC, op0=ALU.add, op1=ALU.subtract),
                  reads=["ta"], writes=["tb"])
            S.dve(lambda h: h.scalar_tensor_tensor(out=ta[:], in0=tb[:], scalar=-C1, in1=ang[:], op0=ALU.mult, op1=ALU.add),
                  reads=["tb", "ang", "ta"], writes=["ta"])
            S.dve(lambda h: h.scalar_tensor_tensor(out=tc[:], in0=tb[:], scalar=-C2, in1=ta[:], op0=ALU.mult, op1=ALU.add),
                  reads=["tb", "ta"], writes=["tc"])
            S.dve(lambda h: h.tensor_scalar(out=ta[:], in0=tc[:], scalar1=LIM, scalar2=-LIM, op0=ALU.min, op1=ALU.max),
                  reads=["tc", "ta"], writes=["ta"])
            S.act(lambda h: h.activation(out=tb[:], in_=ta[:], func=AF.Sin), reads=["ta", "tb"], writes=["tb"])
            S.dve(lambda h: h.tensor_scalar(out=self.sinT[:], in0=tb[:], scalar1=sgn[:, 1:2], scalar2=None, op0=ALU.mult),
                  reads=["tb", "sgn1"], writes=["sinT"])
            S.dve(lambda h: h.tensor_scalar(out=ang[:], in0=tc[:], scalar1=math.pi / 2, scalar2=None, op0=ALU.add),
                  reads=["tc", "ang"], writes=["ang"])
            S.dve(lambda h: h.tensor_scalar(out=ta[:], in0=ang[:], scalar1=math.pi, scalar2=None, op0=ALU.is_gt),
                  reads=["ang", "ta"], writes=["ta"])
            S.dve(lambda h: h.scalar_tensor_tensor(out=tc[:], in0=ta[:], scalar=-TWO_PI, in1=ang[:], op0=ALU.mult, op1=ALU.add),
                  reads=["ta", "ang", "tc"], writes=["tc"])
            S.dve(lambda h: h.tensor_scalar(out=ta[:], in0=tc[:], scalar1=LIM, scalar2=-LIM, op0=ALU.min, op1=ALU.max),
                  reads=["tc", "ta"], writes=["ta"])
            S.act(lambda h: h.activation(out=self.cosT[:], in_=ta[:], func=AF.Sin), reads=["ta"], writes=["cosT"])
            for tt in range(16):
                sl = tt % 3
                S.dma("sp", "xs%d" % sl, lambda h, tt=tt, sl=sl: h.dma_start(out=xs[sl][:], in_=self.x[tt * 128:(tt + 1) * 128, :]),
                      writes=[("xs", sl)])
                T = tt // 4
                for half in range(2):
                    bi = 4 + (2 * tt + half) % 4
                    bank = ps[bi]
                    for jj in range(4):
                        c = half * 4 + jj
                        S.pe(lambda h, bank=bank, jj=jj, c=c, sl=sl: h.transpose(out=bank[:, jj * 128:(jj + 1) * 128],
                                                                               in_=xs[sl][:, c * 128:(c + 1) * 128], identity=identf[:]),
                             reads=[("xs", sl), "identf"], writes=[("ps", bi)])
                    dst = self.h[:, half * 4:half * 4 + 4, tt * 128:(tt + 1) * 128]
                    src = bank[:, :].rearrange("p (a b) -> p a b", b=128)
                    wr = [("h", half * 4 + jj, T) for jj in range(4)]
                    if (2 * tt + half) % 2 == 0:
                        S.act(lambda h, dst=dst, src=src: h.activation(out=dst, in_=src, func=AF.Copy), reads=[("ps", bi)], writes=wr)
                    else:
                        S.dve(lambda h, dst=dst, src=src: h.tensor_copy(out=dst, in_=src), reads=[("ps", bi)], writes=wr)
            S.barrier()
            S.flush()

    def rmsnorm(self, grow, out_fn=None):
        S, ps = self.S, self.ps
        for T in range(NT):
            bi = 6 + T % 2
            bank = ps[bi]
            tsl = slice(T * TT, (T + 1) * TT)
            for c in range(NCH):
                sq = self.nsq[:, c % 2, :]
                S.act(lambda h, sq=sq, c=c, tsl=tsl: h.activation(out=sq, in_=self.h[:, c, tsl], func=AF.Square),
                      reads=[("h", c, T)], writes=[("nsq", c % 2)])
                self.mm(bank[:, :], self.onesb[:], sq, c == 0, c == NCH - 1, [("nsq", c % 2), "onesb"], [("ps", bi)])
            sd = self.nstd[:, T % 2, :]
            S.act(lambda h, sd=sd, bank=bank: h.activation(out=sd, in_=bank[:, :], func=AF.Sqrt, scale=1.0 / D, bias=NORM_EPS),
                  reads=[("ps", bi)], writes=[("nstd", T % 2)])
            S.dve(lambda h, sd=sd, bank=bank: h.reciprocal(bank[:, :], sd), reads=[("nstd", T % 2)], writes=[("ps", bi)])
            for c in range(NCH):
                if out_fn is None:
                    dst = self.hn[:, c, tsl]
                    wr = [("hn", c, T)]
                else:
                    dst, wr = out_fn(c, T)
                S.dve(lambda h, dst=dst, c=c, tsl=tsl, bank=bank: h.scalar_tensor_tensor(
                    out=dst, in0=self.h[:, c, tsl], scalar=self.prow(grow + c), in1=bank[:, :], op0=ALU.mult, op1=ALU.mult),
                    reads=[("h", c, T), ("ps", bi), "ptab"], writes=wr)

    def load_w(self, slot_ap, src_ap, key, group):
        self.S.dma("pool", group, lambda h: h.dma_start(out=slot_ap, in_=src_ap), writes=[key])

    def phase_attention(self, j):
        nc, S, ps = self.nc, self.S, self.ps
        hn, h = self.hn, self.h
        with ExitStack() as es:
            qT = self.sb(es, [128, 2, SEQ], BF16, "qT")
            kT = self.sb(es, [128, 2, 2, SEQ], BF16, "kT")
            Vt = self.sb(es, [128, 16, 4, 128], BF16, "Vt")
            NW = 5
            wsl = [self.sb(es, [128, 2048], BF16, "wsl") for _ in range(NW)]
            biasT = self.sb(es, [128, 2, 1024], BF16, "biasT")
            kms = self.sb(es, [128, 4, 8], F32, "kms")
            kmT = self.sb(es, [128, 4, 8], BF16, "kmT")
            gsb = self.sb(es, [128, 256], F32, "gsb")
            cmp_ = self.sb(es, [128, 2, 4 * 49], BF16, "cmp")
            rank = self.sb(es, [128, 2, 32], F32, "rank")
            btok = self.sb(es, [128, 4, 256], BF16, "btok")
            pT = [self.sb(es, [128, TT], BF16, "pT") for _ in range(4)]
            rec = [self.sb(es, [128, TT], F32, "rec") for _ in range(2)]
            qs = [self.sb(es, [128, TT], BF16, "qs") for _ in range(2)]
            t1 = [self.sb(es, [128, TT], F32, "t1") for _ in range(2)]
            t2 = [self.sb(es, [128, TT], F32, "t2") for _ in range(2)]

            wcount = [0]

            def next_slot():
                i = wcount[0] % NW
                wcount[0] += 1
                return i

            S.pool(lambda h_: h_.memset(Vt[:, :, 0:4:2, 64:128], 1.0), writes=[("Vt1", 0)])
            S.pool(lambda h_: h_.memset(Vt[:, :, 1:4:2, 0:64], 1.0), writes=[("Vt1", 1)])
            S.pool(lambda h_: h_.memset(biasT[:], 0.0), writes=[("biasT", a, u) for a in range(2) for u in range(2)])
            S.pool(lambda h_: h_.memset(kms[:], 0.0), writes=[("kmsz",)])
            S.pool(lambda h_: h_.memset(kT[64:128, :, 0, :], 0.0), writes=[("kz", 0)])
            S.pool(lambda h_: h_.memset(kT[0:64, :, 1, :], 0.0), writes=[("kz", 1)])

            wq_src = self.w_qkv[j].rearrange("(c p) f -> p c f", p=128)
            wo_src = self.w_o[j].rearrange("(c p) f -> p c f", p=128)

            def load_group(g):
                sl = {}
                for nm, off in (("q", 0), ("k", D), ("v", 2 * D)):
                    i = next_slot()
                    sl[nm] = i
                    self.load_w(wsl[i][:, :].rearrange("p (c f) -> p c f", f=256), wq_src[:, :, off + g * 256: off + (g + 1) * 256],
                                ("wsl", i), "aw%d" % i)
                i = next_slot()
                sl["o"] = i
                self.load_w(wsl[i][:, :].rearrange("p (c f) -> p c f", f=1024), wo_src[:, 2 * g:2 * g + 2, :], ("wsl", i), "aw%d" % i)
                return sl

            pending = load_group(0)
            rope_i = [0]
            sub = self.stop_after.split(":")[1] if (self.stop_after and ":" in self.stop_after) else None
            for g in range(4):
                if sub is not None and g > 0:
                    break
                sl = pending
                wq = wsl[sl["q"]][:, :].rearrange("p (c f) -> p c f", f=256)
                wk = wsl[sl["k"]][:, :].rearrange("p (c f) -> p c f", f=256)
                wv = wsl[sl["v"]][:, :].rearrange("p (c f) -> p c f", f=256)
                wo = wsl[sl["o"]][:, :].rearrange("p (c f) -> p c f", f=1024)
                pend = []

                def rope_a(which, cc, T, wsrc, wkey):
                    scale = DH ** -0.5 if which == "q" else 1.0
                    ri = rope_i[0]
                    rope_i[0] += 1
                    ba = ri % 2
                    A = ps[ba]
                    tsl = slice(T * TT, (T + 1) * TT)
                    for c in range(NCH):
                        self.mm(A[:, :], wsrc[:, c, cc * 128:(cc + 1) * 128], hn[:, c, tsl], c == 0, c == NCH - 1,
                                [wkey, ("hn", c, T)], [("ps", ba)])
                    q_s = qs[ri % 2]
                    S.act(lambda h_, q_s=q_s, A=A, scale=scale: h_.activation(out=q_s[:], in_=A[:, :], func=AF.Copy, scale=scale),
                          reads=[("ps", ba)], writes=[("qs", ri % 2)])
                    return (which, cc, T, ri, scale)

                def rope_b(which, cc, T, ri, scale):
                    ba, bb = ri % 2, 2 + ri % 2
                    A, B = ps[ba], ps[bb]
                    tsl = slice(T * TT, (T + 1) * TT)
                    q_s, t1_, t2_ = qs[ri % 2], t1[ri % 2], t2[ri % 2]
                    self.mm(B[:, :], self.prot[:], q_s[:], True, True, [("qs", ri % 2), "prot"], [("ps", bb)])
                    S.dve(lambda h_, t1_=t1_, A=A, scale=scale, tsl=tsl: h_.scalar_tensor_tensor(
                        out=t1_[:], in0=A[:, :], scalar=scale, in1=self.cosT[:, tsl], op0=ALU.mult, op1=ALU.mult),
                        reads=[("ps", ba), "cosT"], writes=[("t1", ri % 2)])
                    S.dve(lambda h_, t2_=t2_, B=B, tsl=tsl: h_.tensor_tensor(out=t2_[:], in0=B[:, :], in1=self.sinT[:, tsl], op=ALU.mult),
                          reads=[("ps", bb), "sinT"], writes=[("t2", ri % 2)])
                    if which == "q":
                        dst = qT[:, cc, tsl]
                        wr = [("q", cc, T, 0), ("q", cc, T, 1)]
                        S.pool(lambda h_, dst=dst, t1_=t1_, t2_=t2_: h_.tensor_tensor(out=dst, in0=t1_[:], in1=t2_[:], op=ALU.add),
                               reads=[("t1", ri % 2), ("t2", ri % 2)], writes=wr)
                    else:
                        tk_ = qs[ri % 2]
                        S.pool(lambda h_, tk_=tk_, t1_=t1_, t2_=t2_: h_.tensor_tensor(out=tk_[:], in0=t1_[:], in1=t2_[:], op=ALU.add),
                               reads=[("t1", ri % 2), ("t2", ri % 2), ("qs", ri % 2)], writes=[("qs", ri % 2)])
                        for hp in range(2):
                            prt = slice(hp * 64, (hp + 1) * 64)
                            ch = 2 * cc + hp
                            for hf in range(2):
                                blk = 2 * T + hf
                                S.act(lambda h_, prt=prt, hp=hp, cc=cc, T=T, hf=hf, ch=ch, blk=blk, tk_=tk_: h_.activation(
                                    out=kT[prt, cc, hp, T * TT + hf * 256:T * TT + (hf + 1) * 256], in_=tk_[prt, hf * 256:(hf + 1) * 256],
                                    func=AF.Copy, accum_out=kms[prt, ch, blk:blk + 1]),
                                    reads=[("qs", ri % 2), ("kmsz",)], writes=[("k", cc, T, hp), ("kms", ch, blk)])

                def kmean(ch):
                    S.dve(lambda h_: h_.tensor_scalar(out=kmT[:, ch, :], in0=kms[:, ch, :], scalar1=1.0 / 256.0, scalar2=None, op0=ALU.mult),
                          reads=[("kms", ch, b_) for b_ in range(8)] + [("kmsz",)], writes=[("kmT", ch)])

                def rope_units(which, wsrc, wkey, between=None):
                    ulist = [(which, cc, T) for cc in range(2) for T in range(NT)]
                    for ui_, u_ in enumerate(ulist):
                        st_ = rope_a(u_[0], u_[1], u_[2], wsrc, wkey)
                        if pend:
                            rope_b(*pend.pop(0))
                        pend.append(st_)
                        if which == "k" and ui_ == NT and sub is None:
                            kmean(0)
                            kmean(1)
                        if between is not None:
                            between(ui_)
                    while pend:
                        rope_b(*pend.pop(0))

                if g == 0 or sub is not None:
                    rope_units("k", wk, ("wsl", sl["k"]))
                rope_units("q", wq, ("wsl", sl["q"]))
                if sub == "qk":
                    break
                if sub == "v":
                    break
                def gate_matmuls():
                    gbank = ps[6]
                    for qt in range(8, 16):
                        T = qt // 4
                        for hl in range(4):
                            cc, hp = hl // 2, hl % 2
                            col = (qt - 8) * 32 + hl * 8
                            self.mm(gbank[:, col:col + 8], qT[:, cc, qt * 128:(qt + 1) * 128],
                                    kmT[:, hl, :], True, True,
                                    [("q", cc, T, 0), ("q", cc, T, 1), ("kmT", hl)], [("ps", 6)])
                    S.act(lambda h_: h_.activation(out=gsb[:, :], in_=gbank[:, 0:256], func=AF.Copy),
                          reads=[("ps", 6)], writes=["gsb"])

                def gate_chain(qt):
                    qb = qt // 2
                    sI = qt % 4
                    c2 = qt % 2
                    g3 = gsb[:, (qt - 8) * 32:(qt - 7) * 32].rearrange("p (a n) -> p a n", n=8)[:, :, 0:qb]
                    in0 = g3.unsqueeze(2).broadcast_to([128, 4, qb, qb])
                    in1 = g3.unsqueeze(3).broadcast_to([128, 4, qb, qb])
                    cm = cmp_[:, c2, 0:4 * qb * qb].rearrange("p (a n m) -> p a n m", a=4, n=qb)
                    S.dve(lambda h_, cm=cm, in0=in0, in1=in1: h_.tensor_tensor(out=cm, in0=in0, in1=in1, op=ALU.is_gt),
                          reads=["gsb"], writes=[("cmp", c2)])
                    rk = rank[:, c2, :].rearrange("p (a n) -> p a n", n=8)[:, :, 0:qb]
                    S.dve(lambda h_, rk=rk, cm=cm: h_.tensor_reduce(out=rk, in_=cm, axis=AX.X, op=ALU.add),
                          reads=[("cmp", c2)], writes=[("rank", c2)])
                    S.pool(lambda h_, sI=sI: h_.memset(btok[:, sI, :], 0.0), writes=[("btok", sI)])
                    bo = btok[:, sI, :].rearrange("p (a b n) -> p a b n", a=2, b=2)[:, :, :, 0:qb]
                    rk4 = rank[:, c2, :].rearrange("p (a b n) -> p a b n", a=2, b=2)[:, :, :, 0:qb]
                    S.dve(lambda h_, bo=bo, rk4=rk4: h_.tensor_scalar(out=bo, in0=rk4, scalar1=2.5, scalar2=NEG, op0=ALU.is_gt, op1=ALU.mult),
                          reads=[("rank", c2)], writes=[("btok", sI)])

                def gate_transposes(q4):
                    for a in range(2):
                        bti = 2 + a
                        for qq in range(4):
                            qt = 8 + 4 * q4 + qq
                            sI = qt % 4
                            self.mm(ps[bti][:, qq * 128:(qq + 1) * 128], btok[:, sI, a * 128:(a + 1) * 128], self.identb[:], True, True,
                                    [("btok", sI), "identb"], [("ps", bti)])
                        S.act(lambda h_, a=a, q4=q4, bti=bti: h_.activation(out=biasT[:, a, q4 * TT:(q4 + 1) * TT], in_=ps[bti][:, :], func=AF.Copy),
                              reads=[("ps", bti)], writes=[("biasT", a, q4)])
                def v_proj(tp_lo, tp_hi):
                  for tp in range(tp_lo, tp_hi):
                      bi = 4 + tp % 2
                      bank = ps[bi]
                      for u in range(2):
                          tt = 2 * tp + u
                          T = tt // 4
                          for c in range(NCH):
                              self.mm(bank[:, u * 256:(u + 1) * 256], hn[:, c, tt * 128:(tt + 1) * 128], wv[:, c, :], c == 0, c == NCH - 1,
                                      [("wsl", sl["v"]), ("hn", c, T)], [("ps", bi)])
                      src = bank[:, :].rearrange("p (u a b e) -> p u a b e", u=2, a=2, b=2)
                      S.act(lambda h_, src=src, tp=tp: h_.activation(out=Vt[:, 2 * tp:2 * tp + 2, 0:4:2, 0:64], in_=src[:, :, :, 0, :], func=AF.Copy),
                            reads=[("ps", bi)], writes=[("Vt", 2 * tp, 0), ("Vt", 2 * tp + 1, 0)])
                      S.dve(lambda h_, src=src, tp=tp: h_.tensor_copy(out=Vt[:, 2 * tp:2 * tp + 2, 1:4:2, 64:128], in_=src[:, :, :, 1, :]),
                            reads=[("ps", bi)], writes=[("Vt", 2 * tp, 1), ("Vt", 2 * tp + 1, 1)])
                v_proj(0, 1)
                kmean(2)
                v_proj(1, 2)
                kmean(3)
                v_proj(2, 4)
                gate_matmuls()
                for qt in range(8, 12):
                    gate_chain(qt)
                v_proj(4, 6)
                gate_transposes(0)
                for qt in range(12, 16):
                    gate_chain(qt)
                v_proj(6, 8)
                gate_transposes(1)
                if g + 1 < 4 and sub is None:
                    pending = load_group(g + 1)
                if sub == "gate":
                    break
                iters = []
                itc = 0
                for cc in range(2):
                    for T in range(NT):
                        nk = 4 * T + 4
                        ob = [4 + 2 * (itc % 2), 5 + 2 * (itc % 2)]
                        itc += 1
                        for kt in range(nk):
                            for hp in range(2):
                                iters.append((cc, T, kt, hp, nk, ob[hp]))
                LAG = 3

                def stage_a(n_, cc, T, kt, hp, nk, obk):
                    nb = kt // 2
                    q_lo = max(0, kt - 4 * T) * 128
                    qsl = slice(q_lo, TT)
                    Tk = kt // 4
                    sbi = n_ % 4
                    sbk = ps[sbi]
                    need_bias = (T >= 2 and kt < 4 * T + 2)
                    need_mask = kt >= 4 * T
                    self.mm(sbk[:, qsl], kT[:, cc, hp, kt * 128:(kt + 1) * 128], qT[:, cc, T * TT + q_lo:(T + 1) * TT],
                            True, not (need_bias or need_mask),
                            [("k", cc, Tk, hp), ("kz", hp), ("q", cc, T, 0), ("q", cc, T, 1)], [("ps", sbi)])
                    if need_bias:
                        self.mm(sbk[:, qsl], self.ind[hp][:, nb * 128:(nb + 1) * 128],
                                biasT[:, cc, (T - 2) * TT + q_lo:(T - 1) * TT],
                                False, not need_mask, ["ind", ("biasT", cc, T - 2)], [("ps", sbi)])
                    if need_mask:
                        self.mm(sbk[:, q_lo:q_lo + 128], self.identb[:], self.maskT[:], False, True,
                                ["identb", "maskT"], [("ps", sbi)])

                def stage_bc(n_, cc, T, kt, hp, nk, obk):
                    hl = 2 * cc + hp
                    q_lo = max(0, kt - 4 * T) * 128
                    qsl = slice(q_lo, TT)
                    sbi = n_ % 4
                    sbk = ps[sbi]
                    pi = n_ % 4
                    S.act(lambda h_, pi=pi, sbk=sbk, qsl=qsl: h_.activation(out=pT[pi][:, qsl], in_=sbk[:, qsl], func=AF.Exp),
                          reads=[("ps", sbi)], writes=[("pT", pi)])
                    self.mm(ps[obk][:, qsl], Vt[:, kt, hl, :], pT[pi][:, qsl], kt == 0, kt == nk - 1,
                            [("pT", pi), ("Vt", kt, hp), ("Vt1", hp)], [("ps", obk)])
                    if kt == nk - 1:
                        o = ps[obk]
                        num = slice(hp * 64, (hp + 1) * 64)
                        den = slice((1 - hp) * 64, (2 - hp) * 64)
                        rc = rec[hp]
                        S.dve(lambda h_, rc=rc, o=o, den=den: h_.reciprocal(rc[den, :], o[den, :]),
                              reads=[("ps", obk)], writes=[("rec", hp)])
                        S.dve(lambda h_, rc=rc, o=o, den=den, num=num, cc=cc, T=T: h_.tensor_tensor(
                            out=qT[num, cc, T * TT:(T + 1) * TT], in0=o[num, :], in1=rc[den, :], op=ALU.mult),
                            reads=[("ps", obk), ("rec", hp)], writes=[("q", cc, T, hp)])

                NI = len(iters)
                for n_ in range(NI + LAG):
                    if n_ < NI:
                        stage_a(n_, *iters[n_])
                    m_ = n_ - LAG
                    if m_ >= 0:
                        stage_bc(m_, *iters[m_])
                if sub == "core":
                    break
                wo_units = [(T, dc) for T in range(NT) for dc in range(NCH)]

                def wo_unit(T, dc):
                    tsl = slice(T * TT, (T + 1) * TT)
                    bi = 4 + (T * NCH + dc) % 4
                    for cc in range(2):
                        self.mm(ps[bi][:, :], wo[:, cc, dc * 128:(dc + 1) * 128], qT[:, cc, tsl], cc == 0, cc == 1,
                                [("wsl", sl["o"]), ("q", cc, T, 0), ("q", cc, T, 1)], [("ps", bi)])
                    S.dve(lambda h_: h_.tensor_tensor(out=h[:, dc, tsl], in0=ps[bi][:, :], in1=h[:, dc, tsl], op=ALU.add),
                          reads=[("ps", bi), ("h", dc, T)], writes=[("h", dc, T)])

                if g + 1 < 4 and sub is None:
                    nsl = pending
                    nwk = wsl[nsl["k"]][:, :].rearrange("p (c f) -> p c f", f=256)

                    def between(ui_):
                        for (T_, dc_) in wo_units[4 * ui_:4 * ui_ + 4]:
                            wo_unit(T_, dc_)

                    rope_units("k", nwk, ("wsl", nsl["k"]), between)
                else:
                    for (T_, dc_) in wo_units:
                        wo_unit(T_, dc_)
            S.barrier()
            S.flush()

    def phase_conformer(self, j):
        nc, S, ps = self.nc, self.S, self.ps
        hn, h = self.hn, self.h
        PADL = 32
        with ExitStack() as es:
            glu = self.sb(es, [128, NCH, PADL + SEQ], BF16, "glu")
            ybf = [self.sb(es, [128, NCH, TT], BF16, "ybf") for _ in range(2)]
            ysq = self.sb(es, [128, NCH, TT], BF16, "ysq")
            dg = self.sb(es, [128, CW, 128], BF16, "dg")
            NW = 3
            wsl = [self.sb(es, [128, NCH, 256], BF16, "cw") for _ in range(NW)]
            tA = [self.sb(es, [128, TT], F32, "tA") for _ in range(2)]
            sgm = tA
            mean_t = self.sb(es, [128, TT], F32, "mean_t")
            m2_t = self.sb(es, [128, TT], F32, "m2_t")
            wcount = [0]

            def next_slot():
                i = wcount[0] % NW
                wcount[0] += 1
                return i

            S.pool(lambda h_: h_.memset(glu[:, :, 0:PADL], 0.0), writes=[("glupad",)])
            w1 = self.w_pw1[j].rearrange("(c p) f -> p c f", p=128)
            w2 = self.w_pw2[j].rearrange("(c p) f -> p c f", p=128)

            def load_pw1(cb):
                ia = next_slot()
                self.load_w(wsl[ia][:], w1[:, :, cb * 256:(cb + 1) * 256], ("cw", ia), "cw%d" % ia)
                ig = next_slot()
                self.load_w(wsl[ig][:], w1[:, :, D + cb * 256:D + (cb + 1) * 256], ("cw", ig), "cw%d" % ig)
                return ia, ig

            it = 0
            for cb in range(4):
                ia, ig = load_pw1(cb)
                for ci in range(2):
                    cc = 2 * cb + ci
                    for T in range(NT):
                        ba, bg = (it % 2) * 2, (it % 2) * 2 + 1
                        it += 1
                        tsl = slice(T * TT, (T + 1) * TT)
                        for c in range(NCH):
                            self.mm(ps[ba][:, :], wsl[ia][:, c, ci * 128:(ci + 1) * 128], hn[:, c, tsl], c == 0, c == NCH - 1,
                                    [("cw", ia), ("hn", c, T)], [("ps", ba)])
                        for c in range(NCH):
                            self.mm(ps[bg][:, :], wsl[ig][:, c, ci * 128:(ci + 1) * 128], hn[:, c, tsl], c == 0, c == NCH - 1,
                                    [("cw", ig), ("hn", c, T)], [("ps", bg)])
                        sg_ = sgm[it % 2]
                        S.act(lambda h_, sg_=sg_, bg=bg, cc=cc: h_.activation(out=sg_[:], in_=ps[bg][:, :], func=AF.Sigmoid,
                                                                               bias=self.prow(R_BPW1 + j * 16 + 8 + cc)),
                              reads=[("ps", bg), "ptab"], writes=[("tA", it % 2)])
                        S.dve(lambda h_, sg_=sg_, ba=ba, cc=cc, T=T: h_.scalar_tensor_tensor(
                            out=glu[:, cc, PADL + T * TT:PADL + (T + 1) * TT], in0=ps[ba][:, :], scalar=self.prow(R_BPW1 + j * 16 + cc),
                            in1=sg_[:], op0=ALU.add, op1=ALU.mult),
                            reads=[("ps", ba), ("tA", it % 2), "ptab"], writes=[("glu", cc, T)])
            pw2_slots = {}

            def load_pw2(db):
                i = next_slot()
                self.load_w(wsl[i][:], w2[:, :, db * 256:(db + 1) * 256], ("cw", i), "cw%d" % i)
                pw2_slots[db] = i

            load_pw2(0)
            load_pw2(1)
            dcount = [0]

            YB = (0, 1, 6, 7)

            def conv_unit(T, cc):
                yb = YB[cc % 4]
                yT = ybf[T % 2]
                for tap in range(CW):
                    if dcount[0] % 2 == 0:
                        S.dve(lambda h_, tap=tap, cc=cc: h_.tensor_scalar(out=dg[:, tap, :], in0=self.identb[:],
                                                                          scalar1=self.prow(R_WDW + (j * CW + tap) * 8 + cc), scalar2=None, op0=ALU.mult),
                              reads=["identb", "ptab"], writes=[("dg", tap)])
                    else:
                        S.pool(lambda h_, tap=tap, cc=cc: h_.tensor_scalar(out=dg[:, tap, :], in0=self.identb[:],
                                                                           scalar1=self.prow(R_WDW + (j * CW + tap) * 8 + cc), scalar2=1.0,
                                                                           op0=ALU.mult, op1=ALU.mult),
                               reads=["identb", "ptab"], writes=[("dg", tap)])
                    dcount[0] += 1
                for tap in range(CW):
                    o0 = PADL + T * TT - (CW - 1) + tap
                    rd = [("dg", tap), ("glu", cc, T)]
                    rd.append(("glu", cc, T - 1) if T > 0 else ("glupad",))
                    self.mm(ps[yb][:, :], dg[:, tap, :], glu[:, cc, o0:o0 + TT], tap == 0, tap == CW - 1, rd, [("ps", yb)])

            def conv_evac(T, cc):
                yb = YB[cc % 4]
                yT = ybf[T % 2]
                S.act(lambda h_: h_.activation(out=yT[:, cc, :], in_=ps[yb][:, :], func=AF.Identity,
                                               bias=self.prow(R_BDW + j * 8 + cc)),
                      reads=[("ps", yb), "ptab"], writes=[("ybf", T % 2, cc)])
                S.act(lambda h_: h_.activation(out=ysq[:, cc, :], in_=yT[:, cc, :], func=AF.Square),
                      reads=[("ybf", T % 2, cc)], writes=[("ysq", cc)])

            def ln_head(T):
                yT = ybf[T % 2]
                bm, bq = 2 + (T % 2) * 2, 3 + (T % 2) * 2
                for cc in range(NCH):
                    self.mm(ps[bm][:, :], self.onesb[:], yT[:, cc, :], cc == 0, cc == NCH - 1, ["onesb", ("ybf", T % 2, cc)], [("ps", bm)])
                for cc in range(NCH):
                    self.mm(ps[bq][:, :], self.onesb[:], ysq[:, cc, :], cc == 0, cc == NCH - 1, ["onesb", ("ysq", cc)], [("ps", bq)])

            def ln_head_b(T):
                bm, bq = 2 + (T % 2) * 2, 3 + (T % 2) * 2
                S.act(lambda h_: h_.activation(out=mean_t[:], in_=ps[bm][:, :], func=AF.Copy, scale=1.0 / D),
                      reads=[("ps", bm)], writes=["mean_t"])
                S.dve(lambda h_: h_.tensor_tensor(out=m2_t[:], in0=mean_t[:], in1=mean_t[:], op=ALU.mult), reads=["mean_t"], writes=["m2_t"])
                S.dve(lambda h_: h_.scalar_tensor_tensor(out=m2_t[:], in0=ps[bq][:, :], scalar=1.0 / D, in1=m2_t[:],
                                                         op0=ALU.mult, op1=ALU.subtract),
                      reads=[("ps", bq), "m2_t"], writes=["m2_t"])
                S.act(lambda h_: h_.activation(out=m2_t[:], in_=m2_t[:], func=AF.Sqrt, bias=LN_EPS), reads=["m2_t"], writes=["m2_t"])
                S.dve(lambda h_: h_.reciprocal(ps[bm][:, :], m2_t[:]), reads=["m2_t"], writes=[("ps", bm)])
                S.dve(lambda h_: h_.tensor_tensor(out=ps[bq][:, :], in0=ps[bm][:, :], in1=mean_t[:], op=ALU.mult),
                      reads=[("ps", bm), "mean_t"], writes=[("ps", bq)])

            def ln_norm(T, cc):
                yT = ybf[T % 2]
                bm, bq = 2 + (T % 2) * 2, 3 + (T % 2) * 2
                ta_ = tA[cc % 2]
                S.dve(lambda h_: h_.tensor_tensor(out=ta_[:], in0=ps[bm][:, :], in1=yT[:, cc, :], op=ALU.mult),
                      reads=[("ps", bm), ("ybf", T % 2, cc)], writes=[("tA", cc % 2)])
                S.dve(lambda h_: h_.tensor_tensor(out=ta_[:], in0=ta_[:], in1=ps[bq][:, :], op=ALU.subtract),
                      reads=[("ps", bq), ("tA", cc % 2)], writes=[("tA", cc % 2)])
                S.act(lambda h_: h_.activation(out=hn[:, cc, T * TT:(T + 1) * TT], in_=ta_[:], func=AF.Silu,
                                               scale=self.prow(R_LNG + j * 8 + cc), bias=self.prow(R_LNB + j * 8 + cc)),
                      reads=[("tA", cc % 2), "ptab"], writes=[("hn", cc, T)])

            for T in range(NT):
                for cc in range(NCH):
                    conv_unit(T, cc)
                    if T > 0 and cc == 0:
                        ln_head(T - 1)
                    if T > 0 and cc == 1:
                        ln_head_b(T - 1)
                    conv_evac(T, cc)
                    if T > 0 and cc >= 1:
                        ln_norm(T - 1, cc - 1)
                if T > 0:
                    ln_norm(T - 1, NCH - 1)
            ln_head(NT - 1)
            ln_head_b(NT - 1)
            for cc in range(NCH):
                ln_norm(NT - 1, cc)
            it = 0
            for db in range(4):
                if db + 2 < 4:
                    load_pw2(db + 2)
                i = pw2_slots[db]
                for T in range(NT):
                    for di in range(2):
                        dc = 2 * db + di
                        bi = (6, 7, 0, 1, 2, 3, 4, 5)[it % 8]
                        it += 1
                        tsl = slice(T * TT, (T + 1) * TT)
                        for cc in range(NCH):
                            self.mm(ps[bi][:, :], wsl[i][:, cc, di * 128:(di + 1) * 128], hn[:, cc, tsl], cc == 0, cc == NCH - 1,
                                    [("cw", i), ("hn", cc, T)], [("ps", bi)])
                        S.dve(lambda h_, bi=bi, dc=dc, tsl=tsl: h_.scalar_tensor_tensor(
                            out=h[:, dc, tsl], in0=ps[bi][:, :], scalar=self.prow(R_BPW2 + j * 8 + dc), in1=h[:, dc, tsl], op0=ALU.add, op1=ALU.add),
                            reads=[("ps", bi), ("h", dc, T), "ptab"], writes=[("h", dc, T)])
            S.barrier()
            S.flush()

    def phase_ffn(self, i):
        nc, S, ps = self.nc, self.S, self.ps
        hn, h = self.hn, self.h
        groups = [[0, 1, 2], [3, 4, 5], [6, 7, 8], [9, 10]]
        with ExitStack() as es:
            act = self.sb(es, [128, 6, SEQ], BF16, "act")
            NWU = 3
            wup = [self.sb(es, [128, NCH, 512], BF16, "wup") for _ in range(NWU)]
            wdn = [self.sb(es, [128, 6, D], BF16, "wdn") for _ in range(2)]
            Ag = [self.sb(es, [128, TT], F32, "Ag") for _ in range(2)]
            Av = [self.sb(es, [128, TT], F32, "Av") for _ in range(2)]
            sg = [self.sb(es, [128, TT], F32, "sg") for _ in range(2)]
            halo = self.sb(es, [128, 2, 2, 2], F32, "halo")
            wu_src = self.w_up[i].rearrange("(c p) f -> p c f", p=128)
            wd_src = self.w_down[i].rearrange("(c p) f -> p c f", p=128)
            ucount = [0]
            blocks = [b for g in groups for b in g]
            up_slot = {}

            def load_up(b):
                s = ucount[0] % NWU
                ucount[0] += 1
                up_slot[b] = s
                self.load_w(wup[s][:, :, 0:256], wu_src[:, :, b * 256:(b + 1) * 256], ("wup", s), "wu%d" % s)
                self.load_w(wup[s][:, :, 256:512], wu_src[:, :, DFF + b * 256:DFF + (b + 1) * 256], ("wup", s), "wu%d" % s)

            def load_dn(gi):
                import os
                if os.environ.get("FFN_NODN"):
                    return
                g = groups[gi]
                np_ = 2 * len(g)
                j0 = 2 * g[0]
                self.load_w(wdn[gi % 2][:, 0:np_, :], wd_src[:, j0:j0 + np_, :], ("wdn", gi % 2), "wd%d" % (gi % 2))

            pend3 = []
            load_up(blocks[0])
            load_up(blocks[1])
            load_dn(0)
            nxt = 2
            it = 0
            for gi, g in enumerate(groups):
                for bl, b in enumerate(g):
                    s = up_slot[b]
                    for pi in range(2):
                        jp = 2 * b + pi
                        jl = 2 * bl + pi
                        rows = {}
                        for kind, fc in (("g", jp), ("v", NFP + jp)):
                            rows[kind] = [R_FWDW + (i * 3 + tap) * 44 + fc for tap in range(3)] + [R_FBDW + i * 44 + fc]
                        for T in range(NT):
                            par = it % 2
                            it += 1
                            tsl = slice(T * TT, (T + 1) * TT)
                            s3 = (it - 1) % 3
                            bg_, bv_ = s3 * 2, s3 * 2 + 1
                            import os
                            for c in range(NCH if not os.environ.get("FFN_NOMM") else 0):
                                self.mm(ps[bg_][:, :], wup[s][:, c, pi * 128:(pi + 1) * 128], hn[:, c, tsl], c == 0, c == NCH - 1,
                                        [("wup", s), ("hn", c, T)], [("ps", bg_)])
                            for c in range(NCH if not os.environ.get("FFN_NOMM") else 0):
                                self.mm(ps[bv_][:, :], wup[s][:, c, 256 + pi * 128:256 + (pi + 1) * 128], hn[:, c, tsl], c == 0, c == NCH - 1,
                                        [("wup", s), ("hn", c, T)], [("ps", bv_)])
                            A = {"g": Ag[par], "v": Av[par]}
                            U = {"g": ps[bg_], "v": ps[bv_]}
                            UB = {"g": bg_, "v": bv_}
                            AK = {"g": ("Ag", par), "v": ("Av", par)}
                            KI = {"g": 0, "v": 1}
                            hp_prev = (T - 1) % 2
                            hp_cur = T % 2
                            import os
                            FL = int(os.environ.get("FFN_LEVEL", "9"))
                            for kind in ("g", "v"):
                                if FL < 2:
                                    break
                                r = rows[kind]
                                S.act(lambda h_, A_=A[kind], U_=U[kind], r=r: h_.activation(out=A_[:], in_=U_[:, :], func=AF.Identity,
                                                                                          scale=self.prow(r[2]), bias=self.prow(r[3])),
                                      reads=[("ps", UB[kind]), "ptab"], writes=[AK[kind]])
                                if T < NT - 1 and FL >= 3:
                                    S.act(lambda h_, U_=U[kind], kind=kind, hp_cur=hp_cur: h_.activation(
                                        out=halo[:, hp_cur, KI[kind], :], in_=U_[:, TT - 2:TT], func=AF.Copy),
                                        reads=[("ps", UB[kind])], writes=[("halo", hp_cur, kind)])
                            for kind in ("g", "v"):
                                if FL < 4:
                                    break
                                r = rows[kind]
                                S.dve(lambda h_, A_=A[kind], U_=U[kind], r=r: h_.scalar_tensor_tensor(
                                    out=A_[:, 1:TT], in0=U_[:, 0:TT - 1], scalar=self.prow(r[1]), in1=A_[:, 1:TT], op0=ALU.mult, op1=ALU.add),
                                    reads=[("ps", UB[kind]), AK[kind], "ptab"], writes=[AK[kind]])
                            for kind in ("g", "v"):
                                if FL < 4:
                                    break
                                r = rows[kind]
                                S.dve(lambda h_, A_=A[kind], U_=U[kind], r=r: h_.scalar_tensor_tensor(
                                    out=A_[:, 2:TT], in0=U_[:, 0:TT - 2], scalar=self.prow(r[0]), in1=A_[:, 2:TT], op0=ALU.mult, op1=ALU.add),
                                    reads=[("ps", UB[kind]), AK[kind], "ptab"], writes=[AK[kind]])
                            if T > 0 and FL >= 5:
                                for kind in ("g", "v"):
                                    r = rows[kind]
                                    hl_ = halo[:, hp_prev, KI[kind], :]
                                    S.dve(lambda h_, A_=A[kind], hl_=hl_, r=r: h_.scalar_tensor_tensor(
                                        out=A_[:, 0:1], in0=hl_[:, 1:2], scalar=self.prow(r[1]), in1=A_[:, 0:1], op0=ALU.mult, op1=ALU.add),
                                        reads=[("halo", hp_prev, kind), AK[kind], "ptab"], writes=[AK[kind]])
                                for kind in ("g", "v"):
                                    r = rows[kind]
                                    hl_ = halo[:, hp_prev, KI[kind], :]
                                    S.dve(lambda h_, A_=A[kind], hl_=hl_, r=r: h_.scalar_tensor_tensor(
                                        out=A_[:, 0:2], in0=hl_[:, 0:2], scalar=self.prow(r[0]), in1=A_[:, 0:2], op0=ALU.mult, op1=ALU.add),
                                        reads=[("halo", hp_prev, kind), AK[kind], "ptab"], writes=[AK[kind]])
                            if FL < 6:
                                continue
                            def stage3(par=par, Ag_=A["g"], Av_=A["v"], jl=jl, tsl=tsl, T=T):
                                sg_ = sg[par]
                                S.act(lambda h_: h_.activation(out=sg_[:], in_=Ag_[:], func=AF.Silu),
                                      reads=[("Ag", par)], writes=[("sg", par)])
                                S.pool(lambda h_: h_.tensor_tensor(out=act[:, jl, tsl], in0=sg_[:], in1=Av_[:], op=ALU.mult),
                                       reads=[("sg", par), ("Av", par)], writes=[("act", jl, T)])
                            if pend3:
                                pend3.pop(0)()
                            pend3.append(stage3)
                            if pi == 0 and T == 2:
                                if nxt < len(blocks):
                                    load_up(blocks[nxt])
                                    nxt += 1
                                if bl == 0 and gi + 1 < len(groups):
                                    load_dn(gi + 1)
                while pend3:
                    pend3.pop(0)()
                np_ = 2 * len(g)
                wd = wdn[gi % 2]
                dn = 0
                for T in range(NT if FL >= 7 else 0):
                    tsl = slice(T * TT, (T + 1) * TT)
                    for dc in range(NCH):
                        s_next = it % 3
                        bi = (6, 7, 2 * s_next, 2 * s_next + 1)[dn % 4]
                        dn += 1
                        for jl in range(np_):
                            self.mm(ps[bi][:, :], wd[:, jl, dc * 128:(dc + 1) * 128], act[:, jl, tsl], jl == 0, jl == np_ - 1,
                                    [("wdn", gi % 2), ("act", jl, T)], [("ps", bi)])
                        S.dve(lambda h_, bi=bi, dc=dc, tsl=tsl: h_.tensor_tensor(out=h[:, dc, tsl], in0=ps[bi][:, :], in1=h[:, dc, tsl], op=ALU.add),
                              reads=[("ps", bi), ("h", dc, T)], writes=[("h", dc, T)])
            S.barrier()
            S.flush()

    def phase_final(self, raw=False):
        nc, S, ps = self.nc, self.S, self.ps
        with ExitStack() as es:
            ofm = self.sb(es, [128, NCH, TT], F32, "ofm")
            ost = [self.sb(es, [128, D], F32, "ost") for _ in range(3)]
            oc = 0
            for T in range(NT):
                tsl = slice(T * TT, (T + 1) * TT)
                if raw:
                    for c in range(NCH):
                        eng = S.act if c % 2 == 0 else S.dve
                        if c % 2 == 0:
                            S.act(lambda h_, c=c, tsl=tsl: h_.activation(out=ofm[:, c, :], in_=self.h[:, c, tsl], func=AF.Copy),
                                  reads=[("h", c, T)], writes=[("ofm", c)])
                        else:
                            S.dve(lambda h_, c=c, tsl=tsl: h_.tensor_copy(out=ofm[:, c, :], in_=self.h[:, c, tsl]),
                                  reads=[("h", c, T)], writes=[("ofm", c)])
                else:
                    self.rmsnorm_tile(T, R_FIN, ofm)
                for ts in range(4):
                    tt = T * 4 + ts
                    sl = oc % 3
                    oc += 1
                    for half in range(2):
                        bi = (2 * tt + half) % 4
                        bank = ps[bi]
                        for jj in range(4):
                            c = half * 4 + jj
                            S.pe(lambda h_, bank=bank, jj=jj, c=c, ts=ts: h_.transpose(out=bank[:, jj * 128:(jj + 1) * 128],
                                                                                         in_=ofm[:, c, ts * 128:(ts + 1) * 128], identity=self.identf[:]),
                                 reads=[("ofm", c), "identf"], writes=[("ps", bi)])
                        dst = ost[sl][:, half * 512:(half + 1) * 512]
                        if half == 0:
                            S.act(lambda h_, dst=dst, bank=bank: h_.activation(out=dst, in_=bank[:, :], func=AF.Copy),
                                  reads=[("ps", bi)], writes=[("ost", sl, half)])
                        else:
                            S.dve(lambda h_, dst=dst, bank=bank: h_.tensor_copy(out=dst, in_=bank[:, :]),
                                  reads=[("ps", bi)], writes=[("ost", sl, half)])
                    S.dma("sp", "ost%d" % sl, lambda h_, sl=sl, tt=tt: h_.dma_start(out=self.out[tt * 128:(tt + 1) * 128, :], in_=ost[sl][:]),
                          reads=[("ost", sl, 0), ("ost", sl, 1)])
            S.barrier()
            S.flush()

    def rmsnorm_tile(self, T, grow, ofm):
        S, ps = self.S, self.ps
        bi = 6 + T % 2
        bank = ps[bi]
        tsl = slice(T * TT, (T + 1) * TT)
        for c in range(NCH):
            sq = self.nsq[:, c % 2, :]
            S.act(lambda h, sq=sq, c=c: h.activation(out=sq, in_=self.h[:, c, tsl], func=AF.Square),
                  reads=[("h", c, T)], writes=[("nsq", c % 2)])
            self.mm(bank[:, :], self.onesb[:], sq, c == 0, c == NCH - 1, [("nsq", c % 2), "onesb"], [("ps", bi)])
        sd = self.nstd[:, T % 2, :]
        S.act(lambda h: h.activation(out=sd, in_=bank[:, :], func=AF.Sqrt, scale=1.0 / D, bias=NORM_EPS),
              reads=[("ps", bi)], writes=[("nstd", T % 2)])
        S.dve(lambda h: h.reciprocal(bank[:, :], sd), reads=[("nstd", T % 2)], writes=[("ps", bi)])
        for c in range(NCH):
            S.dve(lambda h, c=c: h.scalar_tensor_tensor(out=ofm[:, c, :], in0=self.h[:, c, tsl], scalar=self.prow(grow + c), in1=bank[:, :],
                                                        op0=ALU.mult, op1=ALU.mult),
                  reads=[("h", c, T), ("ps", bi), "ptab"], writes=[("ofm", c)])


def _pack_ptab(inp):
    f = lambda a: np.ascontiguousarray(np.asarray(a, dtype=np.float32)).reshape(-1, 128)
    parts = [
        f(inp["norm_mix_g"]), f(inp["norm_ffn_g"]), f(inp["final_norm_g"]), f(inp["conv_b_pw1"]),
        f(inp["conv_w_dw"]), f(inp["conv_b_dw"]), f(inp["conv_ln_g"]), f(inp["conv_ln_b"]), f(inp["conv_b_pw2"]),
        f(inp["ffn_w_dw"]), f(inp["ffn_b_dw"]),
    ]
    tab = np.concatenate(parts, axis=0)
    assert tab.shape[0] == 1368
    pad = np.zeros((R_TOT - tab.shape[0], 128), np.float32)
    return np.ascontiguousarray(np.concatenate([tab, pad], axis=0))


_NC_CACHE = {}


def _run(inputs, stop_after=None, trace=False):
    x = np.ascontiguousarray(np.asarray(inputs["x"], dtype=np.float32))
    B = x.shape[0]
    key = stop_after
    if key not in _NC_CACHE:
        _NC_CACHE[key] = Builder(stop_after).build()
    nc = _NC_CACHE[key]
    ptab = _pack_ptab(inputs)
    c = lambda k: np.ascontiguousarray(np.asarray(inputs[k], dtype=np.float32))
    shared = {
        "ptab_in": ptab, "w_qkv": c("attn_w_qkv"), "w_o": c("attn_w_o"), "w_pw1": c("conv_w_pw1"), "w_pw2": c("conv_w_pw2"),
        "w_up": c("ffn_w_up"), "w_down": c("ffn_w_down"),
    }
    in_maps = [dict(shared, x=x[b]) for b in range(B)]
    res = run_bass_kernel_spmd(nc, in_maps, core_ids=list(range(B)), trace=trace)
    out = np.stack([np.asarray(r["out"]) for r in res.results], axis=0).astype(np.float32)
    return out, res


def kernel(**inputs):
    out, _ = _run(inputs)
    return out
```

```python
import math
import numpy as np
from contextlib import ExitStack
import concourse.bass as bass
import concourse.mybir as mybir
from concourse.bass_utils import run_bass_kernel_spmd

F32 = mybir.dt.float32
BF16 = mybir.dt.bfloat16
I32 = mybir.dt.int32
ALU = mybir.AluOpType
AF = mybir.ActivationFunctionType
AX = mybir.AxisListType

D = 1024
SEQ = 2048
NCH = 8
NT = 4
TT = 512
H = 16
DH = 64
DFF = 2816
NFP = 22
DEPTH = 4
NEG = -30000.0
NORM_EPS = 1e-6
LN_EPS = 1e-5
CW = 31

R_MIX = 0
R_FFN = 32
R_FIN = 64
R_BPW1 = 72
R_WDW = 104
R_BDW = 600
R_LNG = 616
R_LNB = 632
R_BPW2 = 648
R_FWDW = 664
R_FBDW = 1192
R_TOT = 1408


class _Op:
    __slots__ = ("eng", "fn", "deps", "needs_inc", "semval", "grp", "gen", "pre")

    def __init__(self, eng, fn):
        self.eng = eng
        self.fn = fn
        self.deps = []
        self.needs_inc = False
        self.semval = None
        self.grp = None
        self.gen = 0
        self.pre = None


class _Grp:
    __slots__ = ("sem", "gens", "closed")

    def __init__(self, sem):
        self.sem = sem
        self.gens = [0]
        self.closed = False


class Sched:
    ENGS = ("pe", "act", "dve", "pool", "sp")

    def __init__(self, nc, es):
        self.nc = nc
        self.es = es
        self.q = {e: [] for e in self.ENGS}
        self.lastw = {}
        self.readers = {}
        self.esem = {e: es.enter_context(nc.semaphore("sem_" + e)) for e in ("pe", "act", "dve", "pool")}
        self.ecnt = {e: 0 for e in ("pe", "act", "dve", "pool")}
        self.seen = {e: {} for e in self.ENGS}
        self.groups = {}
        self.lastreal = {e: None for e in self.ENGS}
        self.nops = 0

    def _group(self, name):
        g = self.groups.get(name)
        if g is None:
            g = _Grp(self.es.enter_context(self.nc.semaphore("dg_" + name)))
            self.groups[name] = g
        return g

    def add(self, eng, fn, reads=(), writes=(), dma=None):
        op = _Op(eng, fn)
        deps = {}
        for k in reads:
            w = self.lastw.get(k)
            if w is not None:
                deps[id(w)] = w
            if isinstance(k, tuple) and k[0] == "ps":
                rd = self.readers.get(k)
                if rd:
                    for rk_, r in rd.items():
                        if rk_ != eng:
                            deps[id(r)] = r
        for k in writes:
            w = self.lastw.get(k)
            if w is not None:
                deps[id(w)] = w
            rd = self.readers.get(k)
            if rd:
                for r in rd.values():
                    deps[id(r)] = r
        if dma is not None:
            g = self._group(dma)
            op.grp = g
            if g.closed:
                op.pre = [(g, len(g.gens) - 1)]
                g.gens.append(g.gens[-1])
                g.closed = False
            g.gens[-1] += 16
            op.gen = len(g.gens) - 1
        for d in deps.values():
            if eng == "pe" and d.eng == "pe" and d.grp is None:
                continue
            if op.grp is not None and d.grp is op.grp:
                continue
            op.deps.append(d)
            d.needs_inc = True
            if d.grp is not None:
                d.grp.closed = True
        rk = eng if dma is None else ("dma", id(op))
        for k in reads:
            self.readers.setdefault(k, {})[rk] = op
        for k in writes:
            self.lastw[k] = op
            self.readers[k] = {}
        self.q[eng].append(op)
        if dma is None:
            self.lastreal[eng] = op
        self.nops += 1
        return op

    def pe(self, fn, reads=(), writes=()):
        return self.add("pe", fn, reads, writes)

    def act(self, fn, reads=(), writes=()):
        return self.add("act", fn, reads, writes)

    def dve(self, fn, reads=(), writes=()):
        return self.add("dve", fn, reads, writes)

    def pool(self, fn, reads=(), writes=()):
        return self.add("pool", fn, reads, writes)

    def dma(self, queue, group, fn, reads=(), writes=()):
        return self.add(queue, fn, reads, writes, dma=group)

    def barrier(self):
        lasts = [op for op in self.lastreal.values() if op is not None and op.grp is None]
        gl = [(g, len(g.gens) - 1) for g in self.groups.values() if g.gens[-1] > 0]
        for g, _ in gl:
            g.closed = True
        for e in self.ENGS:
            op = _Op(e, None)
            for d in lasts:
                if not (e == "pe" and d.eng == "pe"):
                    op.deps.append(d)
                    d.needs_inc = True
            op.pre = list(gl)
            self.q[e].append(op)
        self.lastw = {}
        self.readers = {}

    def _assign(self):
        for e in ("pe", "act", "dve", "pool"):
            c = self.ecnt[e]
            for op in self.q[e]:
                if op.grp is None and op.fn is not None and op.needs_inc and op.semval is None:
                    c += 1
                    op.semval = c
            self.ecnt[e] = c

    def _emit_engine(self, e, h):
        seen = self.seen[e]

        def wait(sem, val):
            key = id(sem)
            if seen.get(key, 0) < val:
                h.wait_ge(sem, val)
                seen[key] = val

        for op in self.q[e]:
            if op.pre is not None:
                for g, gi in op.pre:
                    wait(g.sem, g.gens[gi])
            for d in op.deps:
                if d.grp is not None:
                    wait(d.grp.sem, d.grp.gens[d.gen])
                else:
                    wait(self.esem[d.eng], d.semval)
            if op.fn is not None:
                ins = op.fn(h)
                if op.grp is not None:
                    ins.then_inc(op.grp.sem, 16)
                elif op.needs_inc:
                    ins.then_inc(self.esem[e], 1)
        self.q[e] = []
        self.lastreal[e] = None

    def flush(self):
        self._assign()
        with self.nc.Block() as block:
            @block.tensor
            def _(h):
                self._emit_engine("pe", h)

            @block.scalar
            def _(h):
                self._emit_engine("act", h)

            @block.vector
            def _(h):
                self._emit_engine("dve", h)

            @block.gpsimd
            def _(h):
                self._emit_engine("pool", h)

            @block.sync
            def _(h):
                self._emit_engine("sp", h)


class Builder:
    def __init__(self, stop_after=None):
        self.stop_after = stop_after
        self.nc = bass.Bass("TRN2", target_bir_lowering=False)
        nc = self.nc
        dt = nc.dram_tensor
        self.x = dt("x", [SEQ, D], F32, kind="ExternalInput").ap()
        self.ptab_in = dt("ptab_in", [R_TOT, 128], F32, kind="ExternalInput").ap()
        self.w_qkv = dt("w_qkv", [2, D, 3 * D], F32, kind="ExternalInput").ap()
        self.w_o = dt("w_o", [2, D, D], F32, kind="ExternalInput").ap()
        self.w_pw1 = dt("w_pw1", [2, D, 2 * D], F32, kind="ExternalInput").ap()
        self.w_pw2 = dt("w_pw2", [2, D, D], F32, kind="ExternalInput").ap()
        self.w_up = dt("w_up", [DEPTH, D, 2 * DFF], F32, kind="ExternalInput").ap()
        self.w_down = dt("w_down", [DEPTH, DFF, D], F32, kind="ExternalInput").ap()
        self.out = dt("out", [SEQ, D], F32, kind="ExternalOutput").ap()
        self._uid = 0

    def sb(self, es, shape, dtype, name=None):
        self._uid += 1
        return es.enter_context(self.nc.sbuf_tensor("%s_%d" % (name or "t", self._uid), shape, dtype))

    def mm(self, out, lhsT, rhs, start, stop, reads, writes):
        self.S.pe(lambda h: h.matmul(out, lhsT=lhsT, rhs=rhs, start=start, stop=stop), reads, writes)

    def prow(self, r):
        return self.ptab[:, r:r + 1]

    def build(self):
        nc = self.nc
        with ExitStack() as es:
            self.S = S = Sched(nc, es)
            self.ps = [es.enter_context(nc.psum_tensor("ps%d" % i, [128, TT], F32)) for i in range(8)]
            self.h = self.sb(es, [128, NCH, SEQ], F32, "h")
            self.hn = self.sb(es, [128, NCH, SEQ], BF16, "hn")
            self.cosT = self.sb(es, [128, SEQ], BF16, "cosT")
            self.sinT = self.sb(es, [128, SEQ], BF16, "sinT")
            self.ptab = self.sb(es, [128, R_TOT], F32, "ptab")
            self.identf = self.sb(es, [128, 128], F32, "identf")
            self.identb = self.sb(es, [128, 128], BF16, "identb")
            self.onesb = self.sb(es, [128, 128], BF16, "onesb")
            self.maskT = self.sb(es, [128, 128], BF16, "maskT")
            self.prot = self.sb(es, [128, 128], BF16, "prot")
            self.ind = [self.sb(es, [128, 8 * 128], BF16, "ind") for _ in range(2)]
            self.nsq = self.sb(es, [128, 2, TT], BF16, "nsq")
            self.nstd = self.sb(es, [128, 2, TT], F32, "nstd")

            self.phase_setup()
            stop = self.stop_after
            done = stop == "load"
            for i in range(DEPTH):
                if done:
                    break
                j = i // 2
                self.rmsnorm(R_MIX + i * 8)
                if i % 2 == 0:
                    self.phase_attention(j)
                else:
                    self.phase_conformer(j)
                if stop is not None and stop.split(":")[0] == "mix%d" % i:
                    done = True
                    break
                self.rmsnorm(R_FFN + i * 8)
                fsub = stop.split(":")[1] if (stop and ":" in stop and stop.startswith("ffn")) else None
                if fsub != "norm":
                    self.phase_ffn(i)
                if stop is not None and stop.split(":")[0] == "ffn%d" % i:
                    done = True
                    break
            self.phase_final(raw=(stop is not None))
        return nc

    def phase_setup(self):
        nc, S = self.nc, self.S
        ps = self.ps
        with ExitStack() as es:
            xs = [self.sb(es, [128, D], F32, "xs") for _ in range(3)]
            pst = [self.sb(es, [128, 128], F32, "pst") for _ in range(2)]
            pi_i = self.sb(es, [128, 1], I32, "pi_i")
            pm_i = self.sb(es, [128, 2], I32, "pm_i")
            sgn = self.sb(es, [128, 2], F32, "sgn")
            invrow = self.sb(es, [1, 128], F32, "invrow")
            invf = self.sb(es, [128, 1], F32, "invf")
            pos_i = self.sb(es, [128, SEQ], I32, "pos_i")
            ang = self.sb(es, [128, SEQ], F32, "ang")
            ta = self.sb(es, [128, SEQ], F32, "ta")
            tb = self.sb(es, [128, SEQ], F32, "tb")
            tc = self.sb(es, [128, SEQ], F32, "tc")

            identf, identb, onesb, maskT, prot, ind = self.identf, self.identb, self.onesb, self.maskT, self.prot, self.ind
            S.pool(lambda h: h.memset(identf[:], 1.0), writes=["identf"])
            S.pool(lambda h: h.affine_select(out=identf[:], in_=identf[:], pattern=[[-1, 128]], compare_op=ALU.is_equal,
                                             fill=0.0, base=0, channel_multiplier=1), reads=["identf"], writes=["identf"])
            S.dve(lambda h: h.tensor_copy(out=identb[:], in_=identf[:]), reads=["identf"], writes=["identb"])
            S.pool(lambda h: h.memset(onesb[:], 1.0), writes=["onesb"])
            S.pool(lambda h: h.memset(maskT[:], 0.0), writes=["maskT"])
            S.pool(lambda h: h.affine_select(out=maskT[:], in_=maskT[:], pattern=[[1, 128]], compare_op=ALU.is_ge,
                                             fill=NEG, base=0, channel_multiplier=-1), reads=["maskT"], writes=["maskT"])
            for (dst, src) in ((0, 32), (32, 0), (64, 96), (96, 64)):
                S.dve(lambda h, dst=dst, src=src: h.tensor_copy(out=prot[:, dst:dst + 32], in_=identb[:, src:src + 32]),
                      reads=["identb"], writes=["prot"])
            for jj in range(2):
                S.pool(lambda h, jj=jj: h.memset(ind[jj][:], 0.0), writes=["ind"])
                S.pool(lambda h, jj=jj: h.memset(ind[jj][64 * jj:64 * jj + 64, :], 1.0), reads=["ind"], writes=["ind"])
                S.pool(lambda h, jj=jj: h.affine_select(out=ind[jj][64 * jj:64 * jj + 64, :], in_=ind[jj][64 * jj:64 * jj + 64, :],
                                                        pattern=[[1, 8], [0, 128]], compare_op=ALU.is_equal, fill=0.0,
                                                        base=0, channel_multiplier=-1), reads=["ind"], writes=["ind"])
            for r in range(R_TOT // 128):
                st = pst[r % 2]
                S.dma("sp", "pst%d" % (r % 2), lambda h, r=r, st=st: h.dma_start(out=st[:], in_=self.ptab_in[r * 128:(r + 1) * 128, :]),
                      writes=[("pst", r % 2)])
                bank = ps[r % 2]
                S.pe(lambda h, st=st, bank=bank: h.transpose(out=bank[:, 0:128], in_=st[:], identity=identf[:]),
                     reads=[("pst", r % 2), "identf"], writes=[("ps", r % 2)])
                S.act(lambda h, r=r, bank=bank: h.activation(out=self.ptab[:, r * 128:(r + 1) * 128], in_=bank[:, 0:128], func=AF.Copy),
                      reads=[("ps", r % 2)], writes=["ptab"])
            inv = (np.float32(1.0) / np.power(np.float32(10000.0), np.arange(0, DH, 2, dtype=np.float32) / np.float32(DH))).astype(np.float32)
            irv = invrow[:].rearrange("o (a b) -> o a b", b=32)
            for i in range(32):
                S.dve(lambda h, i=i: h.memset(irv[:, :, i:i + 1], float(inv[i])), writes=["invrow"])
            S.pe(lambda h: h.transpose(out=ps[2][:, 0:1], in_=invrow[:], identity=identf[0:1, 0:1]),
                 reads=["invrow", "identf"], writes=[("ps", 2)])
            S.act(lambda h: h.activation(out=invf[:], in_=ps[2][:, 0:1], func=AF.Copy), reads=[("ps", 2)], writes=["invf"])
            S.pool(lambda h: h.iota(pi_i[:], pattern=[[0, 1]], base=0, channel_multiplier=1), writes=["pi_i"])
            S.dve(lambda h: h.tensor_scalar(out=pm_i[:, 0:1], in0=pi_i[:], scalar1=32, scalar2=None, op0=ALU.bitwise_and),
                  reads=["pi_i"], writes=["pm_i"])
            S.dve(lambda h: h.tensor_copy(out=sgn[:, 0:1], in_=pm_i[:, 0:1]), reads=["pm_i"], writes=["sgn0"])
            S.dve(lambda h: h.tensor_scalar(out=sgn[:, 1:2], in0=sgn[:, 0:1], scalar1=1.0 / 16.0, scalar2=-1.0, op0=ALU.mult, op1=ALU.add),
                  reads=["sgn0"], writes=["sgn1"])
            S.pool(lambda h: h.iota(pos_i[:], pattern=[[1, SEQ]], base=0, channel_multiplier=0), writes=["pos_i"])
            S.dve(lambda h: h.tensor_copy(out=ta[:], in_=pos_i[:]), reads=["pos_i"], writes=["ta"])
            S.dve(lambda h: h.tensor_scalar(out=ang[:], in0=ta[:], scalar1=invf[:, 0:1], scalar2=None, op0=ALU.mult),
                  reads=["ta", "invf"], writes=["ang"])
            TWO_PI = 2.0 * math.pi
            C1 = 6.28125
            C2 = TWO_PI - C1
            MAGIC = 12582912.0
            LIM = 3.1415925
            S.dve(lambda h: h.tensor_scalar(out=ta[:], in0=ang[:], scalar1=1.0 / TWO_PI, scalar2=None, op0=ALU.mult),
                  reads=["ang", "ta"], writes=["ta"])
            S.dve(lambda h: h.tensor_scalar(out=tb[:], in0=ta[:], scalar1=MAGIC, scalar2=MAGIC, op0=ALU.add, op1=ALU.subtract),
                  reads=["ta"], writes=["tb"])
            S.dve(lambda h: h.scalar_tensor_tensor(out=ta[:], in0=tb[:], scalar=-C1, in1=ang[:], op0=ALU.mult, op1=ALU.add),
                  reads=["tb", "ang", "ta"], writes=["ta"])
            S.dve(lambda h: h.scalar_tensor_tensor(out=tc[:], in0=tb[:], scalar=-C2, in1=ta[:], op0=ALU.mult, op1=ALU.add),
                  reads=["tb", "ta"], writes=["tc"])
            S.dve(lambda h: h.tensor_scalar(out=ta[:], in0=tc[:], scalar1=LIM, scalar2=-LIM, op0=ALU.min, op1=ALU.max),
                  reads=["tc", "ta"], writes=["ta"])
            S.act(lambda h: h.activation(out=tb[:], in_=ta[:], func=AF.Sin), reads=["ta", "tb"], writes=["tb"])
            S.dve(lambda h: h.tensor_scalar(out=self.sinT[:], in0=tb[:], scalar1=sgn[:, 1:2], scalar2=None, op0=ALU.mult),
                  reads=["tb", "sgn1"], writes=["sinT"])
            S.dve(lambda h: h.tensor_scalar(out=ang[:], in0=tc[:], scalar1=math.pi / 2, scalar2=None, op0=ALU.add),
                  reads=["tc", "ang"], writes=["ang"])
            S.dve(lambda h: h.tensor_scalar(out=ta[:], in0=ang[:], scalar1=math.pi, scalar2=None, op0=ALU.is_gt),
                  reads=["ang", "ta"], writes=["ta"])
            S.dve(lambda h: h.scalar_tensor_tensor(out=tc[:], in0=ta[:], scalar=-TWO_PI, in1=ang[:], op0=ALU.mult, op1=ALU.add),
                  reads=["ta", "ang", "tc"], writes=["tc"])
            S.dve(lambda h: h.tensor_scalar(out=ta[:], in0=tc[:], scalar1=LIM, scalar2=-LIM, op0=ALU.min, op1=ALU.max),
                  reads=["tc", "ta"], writes=["ta"])
            S.act(lambda h: h.activation(out=self.cosT[:], in_=ta[:], func=AF.Sin), reads=["ta"], writes=["cosT"])
            for tt in range(16):
                sl = tt % 3
                S.dma("sp", "xs%d" % sl, lambda h, tt=tt, sl=sl: h.dma_start(out=xs[sl][:], in_=self.x[tt * 128:(tt + 1) * 128, :]),
                      writes=[("xs", sl)])
                T = tt // 4
                for half in range(2):
                    bi = 4 + (2 * tt + half) % 4
                    bank = ps[bi]
                    for jj in range(4):
                        c = half * 4 + jj
                        S.pe(lambda h, bank=bank, jj=jj, c=c, sl=sl: h.transpose(out=bank[:, jj * 128:(jj + 1) * 128],
                                                                               in_=xs[sl][:, c * 128:(c + 1) * 128], identity=identf[:]),
                             reads=[("xs", sl), "identf"], writes=[("ps", bi)])
                    dst = self.h[:, half * 4:half * 4 + 4, tt * 128:(tt + 1) * 128]
                    src = bank[:, :].rearrange("p (a b) -> p a b", b=128)
                    wr = [("h", half * 4 + jj, T) for jj in range(4)]
                    if (2 * tt + half) % 2 == 0:
                        S.act(lambda h, dst=dst, src=src: h.activation(out=dst, in_=src, func=AF.Copy), reads=[("ps", bi)], writes=wr)
                    else:
                        S.dve(lambda h, dst=dst, src=src: h.tensor_copy(out=dst, in_=src), reads=[("ps", bi)], writes=wr)
            S.barrier()
            S.flush()

    def rmsnorm(self, grow, out_fn=None):
        S, ps = self.S, self.ps
        for T in range(NT):
            bi = 6 + T % 2
            bank = ps[bi]
            tsl = slice(T * TT, (T + 1) * TT)
            for c in range(NCH):
                sq = self.nsq[:, c % 2, :]
                S.act(lambda h, sq=sq, c=c, tsl=tsl: h.activation(out=sq, in_=self.h[:, c, tsl], func=AF.Square),
                      reads=[("h", c, T)], writes=[("nsq", c % 2)])
                self.mm(bank[:, :], self.onesb[:], sq, c == 0, c == NCH - 1, [("nsq", c % 2), "onesb"], [("ps", bi)])
            sd = self.nstd[:, T % 2, :]
            S.act(lambda h, sd=sd, bank=bank: h.activation(out=sd, in_=bank[:, :], func=AF.Sqrt, scale=1.0 / D, bias=NORM_EPS),
                  reads=[("ps", bi)], writes=[("nstd", T % 2)])
            S.dve(lambda h, sd=sd, bank=bank: h.reciprocal(bank[:, :], sd), reads=[("nstd", T % 2)], writes=[("ps", bi)])
            for c in range(NCH):
                if out_fn is None:
                    dst = self.hn[:, c, tsl]
                    wr = [("hn", c, T)]
                else:
                    dst, wr = out_fn(c, T)
                S.dve(lambda h, dst=dst, c=c, tsl=tsl, bank=bank: h.scalar_tensor_tensor(
                    out=dst, in0=self.h[:, c, tsl], scalar=self.prow(grow + c), in1=bank[:, :], op0=ALU.mult, op1=ALU.mult),
                    reads=[("h", c, T), ("ps", bi), "ptab"], writes=wr)

    def load_w(self, slot_ap, src_ap, key, group):
        self.S.dma("pool", group, lambda h: h.dma_start(out=slot_ap, in_=src_ap), writes=[key])

    def phase_attention(self, j):
        nc, S, ps = self.nc, self.S, self.ps
        hn, h = self.hn, self.h
        with ExitStack() as es:
            qT = self.sb(es, [128, 2, SEQ], BF16, "qT")
            kT = self.sb(es, [128, 2, 2, SEQ], BF16, "kT")
            Vt = self.sb(es, [128, 16, 4, 128], BF16, "Vt")
            NW = 5
            wsl = [self.sb(es, [128, 2048], BF16, "wsl") for _ in range(NW)]
            biasT = self.sb(es, [128, 2, 1024], BF16, "biasT")
            kms = self.sb(es, [128, 4, 8], F32, "kms")
            kmT = self.sb(es, [128, 4, 8], BF16, "kmT")
            gsb = self.sb(es, [128, 256], F32, "gsb")
            cmp_ = self.sb(es, [128, 2, 4 * 49], BF16, "cmp")
            rank = self.sb(es, [128, 2, 32], F32, "rank")
            btok = self.sb(es, [128, 4, 256], BF16, "btok")
            pT = [self.sb(es, [128, TT], BF16, "pT") for _ in range(4)]
            rec = [self.sb(es, [128, TT], F32, "rec") for _ in range(2)]
            qs = [self.sb(es, [128, TT], BF16, "qs") for _ in range(2)]
            t1 = [self.sb(es, [128, TT], F32, "t1") for _ in range(2)]
            t2 = [self.sb(es, [128, TT], F32, "t2") for _ in range(2)]

            wcount = [0]

            def next_slot():
                i = wcount[0] % NW
                wcount[0] += 1
                return i

            S.pool(lambda h_: h_.memset(Vt[:, :, 0:4:2, 64:128], 1.0), writes=[("Vt1", 0)])
            S.pool(lambda h_: h_.memset(Vt[:, :, 1:4:2, 0:64], 1.0), writes=[("Vt1", 1)])
            S.pool(lambda h_: h_.memset(biasT[:], 0.0), writes=[("biasT", a, u) for a in range(2) for u in range(2)])
            S.pool(lambda h_: h_.memset(kms[:], 0.0), writes=[("kmsz",)])
            S.pool(lambda h_: h_.memset(kT[64:128, :, 0, :], 0.0), writes=[("kz", 0)])
            S.pool(lambda h_: h_.memset(kT[0:64, :, 1, :], 0.0), writes=[("kz", 1)])

            wq_src = self.w_qkv[j].rearrange("(c p) f -> p c f", p=128)
            wo_src = self.w_o[j].rearrange("(c p) f -> p c f", p=128)

            def load_group(g):
                sl = {}
                for nm, off in (("q", 0), ("k", D), ("v", 2 * D)):
                    i = next_slot()
                    sl[nm] = i
                    self.load_w(wsl[i][:, :].rearrange("p (c f) -> p c f", f=256), wq_src[:, :, off + g * 256: off + (g + 1) * 256],
                                ("wsl", i), "aw%d" % i)
                i = next_slot()
                sl["o"] = i
                self.load_w(wsl[i][:, :].rearrange("p (c f) -> p c f", f=1024), wo_src[:, 2 * g:2 * g + 2, :], ("wsl", i), "aw%d" % i)
                return sl

            pending = load_group(0)
            rope_i = [0]
            sub = self.stop_after.split(":")[1] if (self.stop_after and ":" in self.stop_after) else None
            for g in range(4):
                if sub is not None and g > 0:
                    break
                sl = pending
                wq = wsl[sl["q"]][:, :].rearrange("p (c f) -> p c f", f=256)
                wk = wsl[sl["k"]][:, :].rearrange("p (c f) -> p c f", f=256)
                wv = wsl[sl["v"]][:, :].rearrange("p (c f) -> p c f", f=256)
                wo = wsl[sl["o"]][:, :].rearrange("p (c f) -> p c f", f=1024)
                pend = []

                def rope_a(which, cc, T, wsrc, wkey):
                    scale = DH ** -0.5 if which == "q" else 1.0
                    ri = rope_i[0]
                    rope_i[0] += 1
                    ba = ri % 2
                    A = ps[ba]
                    tsl = slice(T * TT, (T + 1) * TT)
                    for c in range(NCH):
                        self.mm(A[:, :], wsrc[:, c, cc * 128:(cc + 1) * 128], hn[:, c, tsl], c == 0, c == NCH - 1,
                                [wkey, ("hn", c, T)], [("ps", ba)])
                    q_s = qs[ri % 2]
                    S.act(lambda h_, q_s=q_s, A=A, scale=scale: h_.activation(out=q_s[:], in_=A[:, :], func=AF.Copy, scale=scale),
                          reads=[("ps", ba)], writes=[("qs", ri % 2)])
                    return (which, cc, T, ri, scale)

                def rope_b(which, cc, T, ri, scale):
                    ba, bb = ri % 2, 2 + ri % 2
                    A, B = ps[ba], ps[bb]
                    tsl = slice(T * TT, (T + 1) * TT)
                    q_s, t1_, t2_ = qs[ri % 2], t1[ri % 2], t2[ri % 2]
                    self.mm(B[:, :], self.prot[:], q_s[:], True, True, [("qs", ri % 2), "prot"], [("ps", bb)])
                    S.dve(lambda h_, t1_=t1_, A=A, scale=scale, tsl=tsl: h_.scalar_tensor_tensor(
                        out=t1_[:], in0=A[:, :], scalar=scale, in1=self.cosT[:, tsl], op0=ALU.mult, op1=ALU.mult),
                        reads=[("ps", ba), "cosT"], writes=[("t1", ri % 2)])
                    S.dve(lambda h_, t2_=t2_, B=B, tsl=tsl: h_.tensor_tensor(out=t2_[:], in0=B[:, :], in1=self.sinT[:, tsl], op=ALU.mult),
                          reads=[("ps", bb), "sinT"], writes=[("t2", ri % 2)])
                    if which == "q":
                        dst = qT[:, cc, tsl]
                        wr = [("q", cc, T, 0), ("q", cc, T, 1)]
                        S.pool(lambda h_, dst=dst, t1_=t1_, t2_=t2_: h_.tensor_tensor(out=dst, in0=t1_[:], in1=t2_[:], op=ALU.add),
                               reads=[("t1", ri % 2), ("t2", ri % 2)], writes=wr)
                    else:
                        tk_ = qs[ri % 2]
                        S.pool(lambda h_, tk_=tk_, t1_=t1_, t2_=t2_: h_.tensor_tensor(out=tk_[:], in0=t1_[:], in1=t2_[:], op=ALU.add),
                               reads=[("t1", ri % 2), ("t2", ri % 2), ("qs", ri % 2)], writes=[("qs", ri % 2)])
                        for hp in range(2):
                            prt = slice(hp * 64, (hp + 1) * 64)
                            ch = 2 * cc + hp
                            for hf in range(2):
                                blk = 2 * T + hf
                                S.act(lambda h_, prt=prt, hp=hp, cc=cc, T=T, hf=hf, ch=ch, blk=blk, tk_=tk_: h_.activation(
                                    out=kT[prt, cc, hp, T * TT + hf * 256:T * TT + (hf + 1) * 256], in_=tk_[prt, hf * 256:(hf + 1) * 256],
                                    func=AF.Copy, accum_out=kms[prt, ch, blk:blk + 1]),
                                    reads=[("qs", ri % 2), ("kmsz",)], writes=[("k", cc, T, hp), ("kms", ch, blk)])

                def kmean(ch):
                    S.dve(lambda h_: h_.tensor_scalar(out=kmT[:, ch, :], in0=kms[:, ch, :], scalar1=1.0 / 256.0, scalar2=None, op0=ALU.mult),
                          reads=[("kms", ch, b_) for b_ in range(8)] + [("kmsz",)], writes=[("kmT", ch)])

                def rope_units(which, wsrc, wkey, between=None):
                    ulist = [(which, cc, T) for cc in range(2) for T in range(NT)]
                    for ui_, u_ in enumerate(ulist):
                        st_ = rope_a(u_[0], u_[1], u_[2], wsrc, wkey)
                        if pend:
                            rope_b(*pend.pop(0))
                        pend.append(st_)
                        if which == "k" and ui_ == NT and sub is None:
                            kmean(0)
                            kmean(1)
                        if between is not None:
                            between(ui_)
                    while pend:
                        rope_b(*pend.pop(0))

                if g == 0 or sub is not None:
                    rope_units("k", wk, ("wsl", sl["k"]))
                rope_units("q", wq, ("wsl", sl["q"]))
                if sub == "qk":
                    break
                if sub == "v":
                    break
                def gate_matmuls():
                    gbank = ps[6]
                    for qt in range(8, 16):
                        T = qt // 4
                        for hl in range(4):
                            cc, hp = hl // 2, hl % 2
                            col = (qt - 8) * 32 + hl * 8
                            self.mm(gbank[:, col:col + 8], qT[:, cc, qt * 128:(qt + 1) * 128],
                                    kmT[:, hl, :], True, True,
                                    [("q", cc, T, 0), ("q", cc, T, 1), ("kmT", hl)], [("ps", 6)])
                    S.act(lambda h_: h_.activation(out=gsb[:, :], in_=gbank[:, 0:256], func=AF.Copy),
                          reads=[("ps", 6)], writes=["gsb"])

                def gate_chain(qt):
                    qb = qt // 2
                    sI = qt % 4
                    c2 = qt % 2
                    g3 = gsb[:, (qt - 8) * 32:(qt - 7) * 32].rearrange("p (a n) -> p a n", n=8)[:, :, 0:qb]
                    in0 = g3.unsqueeze(2).broadcast_to([128, 4, qb, qb])
                    in1 = g3.unsqueeze(3).broadcast_to([128, 4, qb, qb])
                    cm = cmp_[:, c2, 0:4 * qb * qb].rearrange("p (a n m) -> p a n m", a=4, n=qb)
                    S.dve(lambda h_, cm=cm, in0=in0, in1=in1: h_.tensor_tensor(out=cm, in0=in0, in1=in1, op=ALU.is_gt),
                          reads=["gsb"], writes=[("cmp", c2)])
                    rk = rank[:, c2, :].rearrange("p (a n) -> p a n", n=8)[:, :, 0:qb]
                    S.dve(lambda h_, rk=rk, cm=cm: h_.tensor_reduce(out=rk, in_=cm, axis=AX.X, op=ALU.add),
                          reads=[("cmp", c2)], writes=[("rank", c2)])
                    S.pool(lambda h_, sI=sI: h_.memset(btok[:, sI, :], 0.0), writes=[("btok", sI)])
                    bo = btok[:, sI, :].rearrange("p (a b n) -> p a b n", a=2, b=2)[:, :, :, 0:qb]
                    rk4 = rank[:, c2, :].rearrange("p (a b n) -> p a b n", a=2, b=2)[:, :, :, 0:qb]
                    S.dve(lambda h_, bo=bo, rk4=rk4: h_.tensor_scalar(out=bo, in0=rk4, scalar1=2.5, scalar2=NEG, op0=ALU.is_gt, op1=ALU.mult),
                          reads=[("rank", c2)], writes=[("btok", sI)])

                def gate_transposes(q4):
                    for a in range(2):
                        bti = 2 + a
                        for qq in range(4):
                            qt = 8 + 4 * q4 + qq
                            sI = qt % 4
                            self.mm(ps[bti][:, qq * 128:(qq + 1) * 128], btok[:, sI, a * 128:(a + 1) * 128], self.identb[:], True, True,
                                    [("btok", sI), "identb"], [("ps", bti)])
                        S.act(lambda h_, a=a, q4=q4, bti=bti: h_.activation(out=biasT[:, a, q4 * TT:(q4 + 1) * TT], in_=ps[bti][:, :], func=AF.Copy),
                              reads=[("ps", bti)], writes=[("biasT", a, q4)])
                def v_proj(tp_lo, tp_hi):
                  for tp in range(tp_lo, tp_hi):
                      bi = 4 + tp % 2
                      bank = ps[bi]
                      for u in range(2):
                          tt = 2 * tp + u
                          T = tt // 4
                          for c in range(NCH):
                              self.mm(bank[:, u * 256:(u + 1) * 256], hn[:, c, tt * 128:(tt + 1) * 128], wv[:, c, :], c == 0, c == NCH - 1,
                                      [("wsl", sl["v"]), ("hn", c, T)], [("ps", bi)])
                      src = bank[:, :].rearrange("p (u a b e) -> p u a b e", u=2, a=2, b=2)
                      S.act(lambda h_, src=src, tp=tp: h_.activation(out=Vt[:, 2 * tp:2 * tp + 2, 0:4:2, 0:64], in_=src[:, :, :, 0, :], func=AF.Copy),
                            reads=[("ps", bi)], writes=[("Vt", 2 * tp, 0), ("Vt", 2 * tp + 1, 0)])
                      S.dve(lambda h_, src=src, tp=tp: h_.tensor_copy(out=Vt[:, 2 * tp:2 * tp + 2, 1:4:2, 64:128], in_=src[:, :, :, 1, :]),
                            reads=[("ps", bi)], writes=[("Vt", 2 * tp, 1), ("Vt", 2 * tp + 1, 1)])
                v_proj(0, 1)
                kmean(2)
                v_proj(1, 2)
                kmean(3)
                v_proj(2, 4)
                gate_matmuls()
                for qt in range(8, 12):
                    gate_chain(qt)
                v_proj(4, 6)
                gate_transposes(0)
                for qt in range(12, 16):
                    gate_chain(qt)
                v_proj(6, 8)
                gate_transposes(1)
                if g + 1 < 4 and sub is None:
                    pending = load_group(g + 1)
                if sub == "gate":
                    break
                iters = []
                itc = 0
                for cc in range(2):
                    for T in range(NT):
                        nk = 4 * T + 4
                        ob = [4 + 2 * (itc % 2), 5 + 2 * (itc % 2)]
                        itc += 1
                        for kt in range(nk):
                            for hp in range(2):
                                iters.append((cc, T, kt, hp, nk, ob[hp]))
                LAG = 3

                def stage_a(n_, cc, T, kt, hp, nk, obk):
                    nb = kt // 2
                    q_lo = max(0, kt - 4 * T) * 128
                    qsl = slice(q_lo, TT)
                    Tk = kt // 4
                    sbi = n_ % 4
                    sbk = ps[sbi]
                    need_bias = (T >= 2 and kt < 4 * T + 2)
                    need_mask = kt >= 4 * T
                    self.mm(sbk[:, qsl], kT[:, cc, hp, kt * 128:(kt + 1) * 128], qT[:, cc, T * TT + q_lo:(T + 1) * TT],
                            True, not (need_bias or need_mask),
                            [("k", cc, Tk, hp), ("kz", hp), ("q", cc, T, 0), ("q", cc, T, 1)], [("ps", sbi)])
                    if need_bias:
                        self.mm(sbk[:, qsl], self.ind[hp][:, nb * 128:(nb + 1) * 128],
                                biasT[:, cc, (T - 2) * TT + q_lo:(T - 1) * TT],
                                False, not need_mask, ["ind", ("biasT", cc, T - 2)], [("ps", sbi)])
                    if need_mask:
                        self.mm(sbk[:, q_lo:q_lo + 128], self.identb[:], self.maskT[:], False, True,
                                ["identb", "maskT"], [("ps", sbi)])

                def stage_bc(n_, cc, T, kt, hp, nk, obk):
                    hl = 2 * cc + hp
                    q_lo = max(0, kt - 4 * T) * 128
                    qsl = slice(q_lo, TT)
                    sbi = n_ % 4
                    sbk = ps[sbi]
                    pi = n_ % 4
                    S.act(lambda h_, pi=pi, sbk=sbk, qsl=qsl: h_.activation(out=pT[pi][:, qsl], in_=sbk[:, qsl], func=AF.Exp),
                          reads=[("ps", sbi)], writes=[("pT", pi)])
                    self.mm(ps[obk][:, qsl], Vt[:, kt, hl, :], pT[pi][:, qsl], kt == 0, kt == nk - 1,
                            [("pT", pi), ("Vt", kt, hp), ("Vt1", hp)], [("ps", obk)])
                    if kt == nk - 1:
                        o = ps[obk]
                        num = slice(hp * 64, (hp + 1) * 64)
                        den = slice((1 - hp) * 64, (2 - hp) * 64)
                        rc = rec[hp]
                        S.dve(lambda h_, rc=rc, o=o, den=den: h_.reciprocal(rc[den, :], o[den, :]),
                              reads=[("ps", obk)], writes=[("rec", hp)])
                        S.dve(lambda h_, rc=rc, o=o, den=den, num=num, cc=cc, T=T: h_.tensor_tensor(
                            out=qT[num, cc, T * TT:(T + 1) * TT], in0=o[num, :], in1=rc[den, :], op=ALU.mult),
                            reads=[("ps", obk), ("rec", hp)], writes=[("q", cc, T, hp)])

                NI = len(iters)
                for n_ in range(NI + LAG):
                    if n_ < NI:
                        stage_a(n_, *iters[n_])
                    m_ = n_ - LAG
                    if m_ >= 0:
                        stage_bc(m_, *iters[m_])
                if sub == "core":
                    break
                wo_units = [(T, dc) for T in range(NT) for dc in range(NCH)]

                def wo_unit(T, dc):
                    tsl = slice(T * TT, (T + 1) * TT)
                    bi = 4 + (T * NCH + dc) % 4
                    for cc in range(2):
                        self.mm(ps[bi][:, :], wo[:, cc, dc * 128:(dc + 1) * 128], qT[:, cc, tsl], cc == 0, cc == 1,
                                [("wsl", sl["o"]), ("q", cc, T, 0), ("q", cc, T, 1)], [("ps", bi)])
                    S.dve(lambda h_: h_.tensor_tensor(out=h[:, dc, tsl], in0=ps[bi][:, :], in1=h[:, dc, tsl], op=ALU.add),
                          reads=[("ps", bi), ("h", dc, T)], writes=[("h", dc, T)])

                if g + 1 < 4 and sub is None:
                    nsl = pending
                    nwk = wsl[nsl["k"]][:, :].rearrange("p (c f) -> p c f", f=256)

                    def between(ui_):
                        for (T_, dc_) in wo_units[4 * ui_:4 * ui_ + 4]:
                            wo_unit(T_, dc_)

                    rope_units("k", nwk, ("wsl", nsl["k"]), between)
                else:
                    for (T_, dc_) in wo_units:
                        wo_unit(T_, dc_)
            S.barrier()
            S.flush()

    def phase_conformer(self, j):
        nc, S, ps = self.nc, self.S, self.ps
        hn, h = self.hn, self.h
        PADL = 32
        with ExitStack() as es:
            glu = self.sb(es, [128, NCH, PADL + SEQ], BF16, "glu")
            ybf = [self.sb(es, [128, NCH, TT], BF16, "ybf") for _ in range(2)]
            ysq = self.sb(es, [128, NCH, TT], BF16, "ysq")
            dg = self.sb(es, [128, CW, 128], BF16, "dg")
            NW = 3
            wsl = [self.sb(es, [128, NCH, 256], BF16, "cw") for _ in range(NW)]
            tA = [self.sb(es, [128, TT], F32, "tA") for _ in range(2)]
            sgm = tA
            mean_t = self.sb(es, [128, TT], F32, "mean_t")
            m2_t = self.sb(es, [128, TT], F32, "m2_t")
            wcount = [0]

            def next_slot():
                i = wcount[0] % NW
                wcount[0] += 1
                return i

            S.pool(lambda h_: h_.memset(glu[:, :, 0:PADL], 0.0), writes=[("glupad",)])
            w1 = self.w_pw1[j].rearrange("(c p) f -> p c f", p=128)
            w2 = self.w_pw2[j].rearrange("(c p) f -> p c f", p=128)

            def half(i):
                return wsl[i // 2][:, :, (i % 2) * 128:(i % 2 + 1) * 128]

            def load_pw1_chunk(cc):
                ia, ig = (2 * cc) % 6, (2 * cc + 1) % 6
                self.load_w(half(ia), w1[:, :, cc * 128:(cc + 1) * 128], ("cwh", ia // 2, ia % 2), "cwh%d" % ia)
                self.load_w(half(ig), w1[:, :, D + cc * 128:D + (cc + 1) * 128], ("cwh", ig // 2, ig % 2), "cwh%d" % ig)

            for cc in range(3):
                load_pw1_chunk(cc)
            it = 0
            for cb in range(4):
                for ci in range(2):
                    cc = 2 * cb + ci
                    ia, ig = (2 * cc) % 6, (2 * cc + 1) % 6
                    for T in range(NT):
                        if T == 1 and cc >= 1 and cc + 2 < NCH:
                            load_pw1_chunk(cc + 2)
                        ba, bg = (it % 2) * 2, (it % 2) * 2 + 1
                        it += 1
                        tsl = slice(T * TT, (T + 1) * TT)
                        for c in range(NCH):
                            self.mm(ps[ba][:, :], half(ia)[:, c, :], hn[:, c, tsl], c == 0, c == NCH - 1,
                                    [("cwh", ia // 2, ia % 2), ("hn", c, T)], [("ps", ba)])
                        for c in range(NCH):
                            self.mm(ps[bg][:, :], half(ig)[:, c, :], hn[:, c, tsl], c == 0, c == NCH - 1,
                                    [("cwh", ig // 2, ig % 2), ("hn", c, T)], [("ps", bg)])
                        sg_ = sgm[it % 2]
                        S.act(lambda h_, sg_=sg_, bg=bg, cc=cc: h_.activation(out=sg_[:], in_=ps[bg][:, :], func=AF.Sigmoid,
                                                                               bias=self.prow(R_BPW1 + j * 16 + 8 + cc)),
                              reads=[("ps", bg), "ptab"], writes=[("tA", it % 2)])
                        S.dve(lambda h_, sg_=sg_, ba=ba, cc=cc, T=T: h_.scalar_tensor_tensor(
                            out=glu[:, cc, PADL + T * TT:PADL + (T + 1) * TT], in0=ps[ba][:, :], scalar=self.prow(R_BPW1 + j * 16 + cc),
                            in1=sg_[:], op0=ALU.add, op1=ALU.mult),
                            reads=[("ps", ba), ("tA", it % 2), "ptab"], writes=[("glu", cc, T)])
            pw2_slots = {}

            def load_pw2(db):
                i = next_slot()
                self.S.dma("pool", "cwh%d" % (2 * i), lambda h_: h_.dma_start(out=wsl[i][:], in_=w2[:, :, db * 256:(db + 1) * 256]),
                           writes=[("cwh", i, 0), ("cwh", i, 1)])
                pw2_slots[db] = i

            load_pw2(0)
            load_pw2(1)
            dcount = [0]

            YB = (0, 1, 6, 7)

            def conv_unit(T, cc):
                yb = YB[cc % 4]
                yT = ybf[T % 2]
                for tap in range(CW):
                    if dcount[0] % 2 == 0:
                        S.dve(lambda h_, tap=tap, cc=cc: h_.tensor_scalar(out=dg[:, tap, :], in0=self.identb[:],
                                                                          scalar1=self.prow(R_WDW + (j * CW + tap) * 8 + cc), scalar2=None, op0=ALU.mult),
                              reads=["identb", "ptab"], writes=[("dg", tap)])
                    else:
                        S.pool(lambda h_, tap=tap, cc=cc: h_.tensor_scalar(out=dg[:, tap, :], in0=self.identb[:],
                                                                           scalar1=self.prow(R_WDW + (j * CW + tap) * 8 + cc), scalar2=1.0,
                                                                           op0=ALU.mult, op1=ALU.mult),
                               reads=["identb", "ptab"], writes=[("dg", tap)])
                    dcount[0] += 1
                for tap in range(CW):
                    o0 = PADL + T * TT - (CW - 1) + tap
                    rd = [("dg", tap), ("glu", cc, T)]
                    rd.append(("glu", cc, T - 1) if T > 0 else ("glupad",))
                    self.mm(ps[yb][:, :], dg[:, tap, :], glu[:, cc, o0:o0 + TT], tap == 0, tap == CW - 1, rd, [("ps", yb)])

            def conv_evac(T, cc):
                yb = YB[cc % 4]
                yT = ybf[T % 2]
                S.act(lambda h_: h_.activation(out=yT[:, cc, :], in_=ps[yb][:, :], func=AF.Identity,
                                               bias=self.prow(R_BDW + j * 8 + cc)),
                      reads=[("ps", yb), "ptab"], writes=[("ybf", T % 2, cc)])
                S.act(lambda h_: h_.activation(out=ysq[:, cc, :], in_=yT[:, cc, :], func=AF.Square),
                      reads=[("ybf", T % 2, cc)], writes=[("ysq", cc)])

            def ln_head(T):
                yT = ybf[T % 2]
                bm, bq = 2 + (T % 2) * 2, 3 + (T % 2) * 2
                for cc in range(NCH):
                    self.mm(ps[bm][:, :], self.onesb[:], yT[:, cc, :], cc == 0, cc == NCH - 1, ["onesb", ("ybf", T % 2, cc)], [("ps", bm)])
                for cc in range(NCH):
                    self.mm(ps[bq][:, :], self.onesb[:], ysq[:, cc, :], cc == 0, cc == NCH - 1, ["onesb", ("ysq", cc)], [("ps", bq)])

            def ln_head_b(T):
                bm, bq = 2 + (T % 2) * 2, 3 + (T % 2) * 2
                S.act(lambda h_: h_.activation(out=mean_t[:], in_=ps[bm][:, :], func=AF.Copy, scale=1.0 / D),
                      reads=[("ps", bm)], writes=["mean_t"])
                S.dve(lambda h_: h_.tensor_tensor(out=m2_t[:], in0=mean_t[:], in1=mean_t[:], op=ALU.mult), reads=["mean_t"], writes=["m2_t"])
                S.dve(lambda h_: h_.scalar_tensor_tensor(out=m2_t[:], in0=ps[bq][:, :], scalar=1.0 / D, in1=m2_t[:],
                                                         op0=ALU.mult, op1=ALU.subtract),
                      reads=[("ps", bq), "m2_t"], writes=["m2_t"])
                S.act(lambda h_: h_.activation(out=m2_t[:], in_=m2_t[:], func=AF.Sqrt, bias=LN_EPS), reads=["m2_t"], writes=["m2_t"])
                S.dve(lambda h_: h_.reciprocal(ps[bm][:, :], m2_t[:]), reads=["m2_t"], writes=[("ps", bm)])
                S.dve(lambda h_: h_.tensor_tensor(out=ps[bq][:, :], in0=ps[bm][:, :], in1=mean_t[:], op=ALU.mult),
                      reads=[("ps", bm), "mean_t"], writes=[("ps", bq)])

            def ln_norm(T, cc):
                yT = ybf[T % 2]
                bm, bq = 2 + (T % 2) * 2, 3 + (T % 2) * 2
                ta_ = tA[cc % 2]
                S.dve(lambda h_: h_.tensor_tensor(out=ta_[:], in0=ps[bm][:, :], in1=yT[:, cc, :], op=ALU.mult),
                      reads=[("ps", bm), ("ybf", T % 2, cc)], writes=[("tA", cc % 2)])
                S.dve(lambda h_: h_.tensor_tensor(out=ta_[:], in0=ta_[:], in1=ps[bq][:, :], op=ALU.subtract),
                      reads=[("ps", bq), ("tA", cc % 2)], writes=[("tA", cc % 2)])
                S.act(lambda h_: h_.activation(out=hn[:, cc, T * TT:(T + 1) * TT], in_=ta_[:], func=AF.Silu,
                                               scale=self.prow(R_LNG + j * 8 + cc), bias=self.prow(R_LNB + j * 8 + cc)),
                      reads=[("tA", cc % 2), "ptab"], writes=[("hn", cc, T)])

            for T in range(NT):
                for cc in range(NCH):
                    conv_unit(T, cc)
                    if T > 0 and cc == 0:
                        ln_head(T - 1)
                    if T > 0 and cc == 1:
                        ln_head_b(T - 1)
                    conv_evac(T, cc)
                    if T > 0 and cc >= 1:
                        ln_norm(T - 1, cc - 1)
                if T > 0:
                    ln_norm(T - 1, NCH - 1)
            ln_head(NT - 1)
            ln_head_b(NT - 1)
            for cc in range(NCH):
                ln_norm(NT - 1, cc)
            it = 0
            for db in range(4):
                if db + 2 < 4:
                    load_pw2(db + 2)
                i = pw2_slots[db]
                for T in range(NT):
                    for di in range(2):
                        dc = 2 * db + di
                        bi = (6, 7, 0, 1, 2, 3, 4, 5)[it % 8]
                        it += 1
                        tsl = slice(T * TT, (T + 1) * TT)
                        for cc in range(NCH):
                            self.mm(ps[bi][:, :], wsl[i][:, cc, di * 128:(di + 1) * 128], hn[:, cc, tsl], cc == 0, cc == NCH - 1,
                                    [("cwh", i, di), ("hn", cc, T)], [("ps", bi)])
                        S.dve(lambda h_, bi=bi, dc=dc, tsl=tsl: h_.scalar_tensor_tensor(
                            out=h[:, dc, tsl], in0=ps[bi][:, :], scalar=self.prow(R_BPW2 + j * 8 + dc), in1=h[:, dc, tsl], op0=ALU.add, op1=ALU.add),
                            reads=[("ps", bi), ("h", dc, T), "ptab"], writes=[("h", dc, T)])
            S.barrier()
            S.flush()

    def phase_ffn(self, i):
        nc, S, ps = self.nc, self.S, self.ps
        hn, h = self.hn, self.h
        groups = [[0, 1, 2], [3, 4, 5], [6, 7, 8], [9, 10]]
        with ExitStack() as es:
            act = self.sb(es, [128, 6, SEQ], BF16, "act")
            NWU = 3
            wup = [self.sb(es, [128, NCH, 512], BF16, "wup") for _ in range(NWU)]
            wdn = [self.sb(es, [128, 6, D], BF16, "wdn") for _ in range(2)]
            Ag = [self.sb(es, [128, TT], F32, "Ag") for _ in range(2)]
            Av = [self.sb(es, [128, TT], F32, "Av") for _ in range(2)]
            sg = [self.sb(es, [128, TT], F32, "sg") for _ in range(2)]
            halo = self.sb(es, [128, 2, 2, 2], F32, "halo")
            wu_src = self.w_up[i].rearrange("(c p) f -> p c f", p=128)
            wd_src = self.w_down[i].rearrange("(c p) f -> p c f", p=128)
            ucount = [0]
            blocks = [b for g in groups for b in g]
            up_slot = {}

            def load_up(b):
                s = ucount[0] % NWU
                ucount[0] += 1
                up_slot[b] = s
                self.load_w(wup[s][:, :, 0:256], wu_src[:, :, b * 256:(b + 1) * 256], ("wup", s), "wu%d" % s)
                self.load_w(wup[s][:, :, 256:512], wu_src[:, :, DFF + b * 256:DFF + (b + 1) * 256], ("wup", s), "wu%d" % s)

            def load_dn(gi):
                import os
                if os.environ.get("FFN_NODN"):
                    return
                g = groups[gi]
                np_ = 2 * len(g)
                j0 = 2 * g[0]
                self.load_w(wdn[gi % 2][:, 0:np_, :], wd_src[:, j0:j0 + np_, :], ("wdn", gi % 2), "wd%d" % (gi % 2))

            pend3 = []
            load_up(blocks[0])
            load_up(blocks[1])
            load_dn(0)
            nxt = 2
            it = 0
            for gi, g in enumerate(groups):
                for bl, b in enumerate(g):
                    s = up_slot[b]
                    for pi in range(2):
                        jp = 2 * b + pi
                        jl = 2 * bl + pi
                        rows = {}
                        for kind, fc in (("g", jp), ("v", NFP + jp)):
                            rows[kind] = [R_FWDW + (i * 3 + tap) * 44 + fc for tap in range(3)] + [R_FBDW + i * 44 + fc]
                        for T in range(NT):
                            par = it % 2
                            it += 1
                            tsl = slice(T * TT, (T + 1) * TT)
                            s3 = (it - 1) % 3
                            bg_, bv_ = s3 * 2, s3 * 2 + 1
                            import os
                            for c in range(NCH if not os.environ.get("FFN_NOMM") else 0):
                                self.mm(ps[bg_][:, :], wup[s][:, c, pi * 128:(pi + 1) * 128], hn[:, c, tsl], c == 0, c == NCH - 1,
                                        [("wup", s), ("hn", c, T)], [("ps", bg_)])
                            for c in range(NCH if not os.environ.get("FFN_NOMM") else 0):
                                self.mm(ps[bv_][:, :], wup[s][:, c, 256 + pi * 128:256 + (pi + 1) * 128], hn[:, c, tsl], c == 0, c == NCH - 1,
                                        [("wup", s), ("hn", c, T)], [("ps", bv_)])
                            A = {"g": Ag[par], "v": Av[par]}
                            U = {"g": ps[bg_], "v": ps[bv_]}
                            UB = {"g": bg_, "v": bv_}
                            AK = {"g": ("Ag", par), "v": ("Av", par)}
                            KI = {"g": 0, "v": 1}
                            hp_prev = (T - 1) % 2
                            hp_cur = T % 2
                            import os
                            FL = int(os.environ.get("FFN_LEVEL", "9"))
                            for kind in ("g", "v"):
                                if FL < 2:
                                    break
                                r = rows[kind]
                                S.act(lambda h_, A_=A[kind], U_=U[kind], r=r: h_.activation(out=A_[:], in_=U_[:, :], func=AF.Identity,
                                                                                          scale=self.prow(r[2]), bias=self.prow(r[3])),
                                      reads=[("ps", UB[kind]), "ptab"], writes=[AK[kind]])
                                if T < NT - 1 and FL >= 3:
                                    S.act(lambda h_, U_=U[kind], kind=kind, hp_cur=hp_cur: h_.activation(
                                        out=halo[:, hp_cur, KI[kind], :], in_=U_[:, TT - 2:TT], func=AF.Copy),
                                        reads=[("ps", UB[kind])], writes=[("halo", hp_cur, kind)])
                            for kind in ("g", "v"):
                                if FL < 4:
                                    break
                                r = rows[kind]
                                S.dve(lambda h_, A_=A[kind], U_=U[kind], r=r: h_.scalar_tensor_tensor(
                                    out=A_[:, 1:TT], in0=U_[:, 0:TT - 1], scalar=self.prow(r[1]), in1=A_[:, 1:TT], op0=ALU.mult, op1=ALU.add),
                                    reads=[("ps", UB[kind]), AK[kind], "ptab"], writes=[AK[kind]])
                            for kind in ("g", "v"):
                                if FL < 4:
                                    break
                                r = rows[kind]
                                S.dve(lambda h_, A_=A[kind], U_=U[kind], r=r: h_.scalar_tensor_tensor(
                                    out=A_[:, 2:TT], in0=U_[:, 0:TT - 2], scalar=self.prow(r[0]), in1=A_[:, 2:TT], op0=ALU.mult, op1=ALU.add),
                                    reads=[("ps", UB[kind]), AK[kind], "ptab"], writes=[AK[kind]])
                            if T > 0 and FL >= 5:
                                for kind in ("g", "v"):
                                    r = rows[kind]
                                    hl_ = halo[:, hp_prev, KI[kind], :]
                                    S.dve(lambda h_, A_=A[kind], hl_=hl_, r=r: h_.scalar_tensor_tensor(
                                        out=A_[:, 0:1], in0=hl_[:, 1:2], scalar=self.prow(r[1]), in1=A_[:, 0:1], op0=ALU.mult, op1=ALU.add),
                                        reads=[("halo", hp_prev, kind), AK[kind], "ptab"], writes=[AK[kind]])
                                for kind in ("g", "v"):
                                    r = rows[kind]
                                    hl_ = halo[:, hp_prev, KI[kind], :]
                                    S.dve(lambda h_, A_=A[kind], hl_=hl_, r=r: h_.scalar_tensor_tensor(
                                        out=A_[:, 0:2], in0=hl_[:, 0:2], scalar=self.prow(r[0]), in1=A_[:, 0:2], op0=ALU.mult, op1=ALU.add),
                                        reads=[("halo", hp_prev, kind), AK[kind], "ptab"], writes=[AK[kind]])
                            if FL < 6:
                                continue
                            def stage3(par=par, Ag_=A["g"], Av_=A["v"], jl=jl, tsl=tsl, T=T):
                                sg_ = sg[par]
                                S.act(lambda h_: h_.activation(out=sg_[:], in_=Ag_[:], func=AF.Silu),
                                      reads=[("Ag", par)], writes=[("sg", par)])
                                S.pool(lambda h_: h_.tensor_tensor(out=act[:, jl, tsl], in0=sg_[:], in1=Av_[:], op=ALU.mult),
                                       reads=[("sg", par), ("Av", par)], writes=[("act", jl, T)])
                            if pend3:
                                pend3.pop(0)()
                            pend3.append(stage3)
                            if pi == 0 and T == 2:
                                if nxt < len(blocks):
                                    load_up(blocks[nxt])
                                    nxt += 1
                                if bl == 0 and gi + 1 < len(groups):
                                    load_dn(gi + 1)
                while pend3:
                    pend3.pop(0)()
                np_ = 2 * len(g)
                wd = wdn[gi % 2]
                dn = 0
                for T in range(NT if FL >= 7 else 0):
                    tsl = slice(T * TT, (T + 1) * TT)
                    for dc in range(NCH):
                        s_next = it % 3
                        bi = (6, 7, 2 * s_next, 2 * s_next + 1)[dn % 4]
                        dn += 1
                        for jl in range(np_):
                            self.mm(ps[bi][:, :], wd[:, jl, dc * 128:(dc + 1) * 128], act[:, jl, tsl], jl == 0, jl == np_ - 1,
                                    [("wdn", gi % 2), ("act", jl, T)], [("ps", bi)])
                        S.dve(lambda h_, bi=bi, dc=dc, tsl=tsl: h_.tensor_tensor(out=h[:, dc, tsl], in0=ps[bi][:, :], in1=h[:, dc, tsl], op=ALU.add),
                              reads=[("ps", bi), ("h", dc, T)], writes=[("h", dc, T)])
            S.barrier()
            S.flush()

    def phase_final(self, raw=False):
        nc, S, ps = self.nc, self.S, self.ps
        with ExitStack() as es:
            ofm = self.sb(es, [128, NCH, TT], F32, "ofm")
            ost = [self.sb(es, [128, D], F32, "ost") for _ in range(3)]
            oc = 0
            for T in range(NT):
                tsl = slice(T * TT, (T + 1) * TT)
                if raw:
                    for c in range(NCH):
                        eng = S.act if c % 2 == 0 else S.dve
                        if c % 2 == 0:
                            S.act(lambda h_, c=c, tsl=tsl: h_.activation(out=ofm[:, c, :], in_=self.h[:, c, tsl], func=AF.Copy),
                                  reads=[("h", c, T)], writes=[("ofm", c)])
                        else:
                            S.dve(lambda h_, c=c, tsl=tsl: h_.tensor_copy(out=ofm[:, c, :], in_=self.h[:, c, tsl]),
                                  reads=[("h", c, T)], writes=[("ofm", c)])
                else:
                    self.rmsnorm_tile(T, R_FIN, ofm)
                for ts in range(4):
                    tt = T * 4 + ts
                    sl = oc % 3
                    oc += 1
                    for half in range(2):
                        bi = (2 * tt + half) % 4
                        bank = ps[bi]
                        for jj in range(4):
                            c = half * 4 + jj
                            S.pe(lambda h_, bank=bank, jj=jj, c=c, ts=ts: h_.transpose(out=bank[:, jj * 128:(jj + 1) * 128],
                                                                                         in_=ofm[:, c, ts * 128:(ts + 1) * 128], identity=self.identf[:]),
                                 reads=[("ofm", c), "identf"], writes=[("ps", bi)])
                        dst = ost[sl][:, half * 512:(half + 1) * 512]
                        if half == 0:
                            S.act(lambda h_, dst=dst, bank=bank: h_.activation(out=dst, in_=bank[:, :], func=AF.Copy),
                                  reads=[("ps", bi)], writes=[("ost", sl, half)])
                        else:
                            S.dve(lambda h_, dst=dst, bank=bank: h_.tensor_copy(out=dst, in_=bank[:, :]),
                                  reads=[("ps", bi)], writes=[("ost", sl, half)])
                    S.dma("sp", "ost%d" % sl, lambda h_, sl=sl, tt=tt: h_.dma_start(out=self.out[tt * 128:(tt + 1) * 128, :], in_=ost[sl][:]),
                          reads=[("ost", sl, 0), ("ost", sl, 1)])
            S.barrier()
            S.flush()

    def rmsnorm_tile(self, T, grow, ofm):
        S, ps = self.S, self.ps
        bi = 6 + T % 2
        bank = ps[bi]
        tsl = slice(T * TT, (T + 1) * TT)
        for c in range(NCH):
            sq = self.nsq[:, c % 2, :]
            S.act(lambda h, sq=sq, c=c: h.activation(out=sq, in_=self.h[:, c, tsl], func=AF.Square),
                  reads=[("h", c, T)], writes=[("nsq", c % 2)])
            self.mm(bank[:, :], self.onesb[:], sq, c == 0, c == NCH - 1, [("nsq", c % 2), "onesb"], [("ps", bi)])
        sd = self.nstd[:, T % 2, :]
        S.act(lambda h: h.activation(out=sd, in_=bank[:, :], func=AF.Sqrt, scale=1.0 / D, bias=NORM_EPS),
              reads=[("ps", bi)], writes=[("nstd", T % 2)])
        S.dve(lambda h: h.reciprocal(bank[:, :], sd), reads=[("nstd", T % 2)], writes=[("ps", bi)])
        for c in range(NCH):
            S.dve(lambda h, c=c: h.scalar_tensor_tensor(out=ofm[:, c, :], in0=self.h[:, c, tsl], scalar=self.prow(grow + c), in1=bank[:, :],
                                                        op0=ALU.mult, op1=ALU.mult),
                  reads=[("h", c, T), ("ps", bi), "ptab"], writes=[("ofm", c)])


def _pack_ptab(inp):
    f = lambda a: np.ascontiguousarray(np.asarray(a, dtype=np.float32)).reshape(-1, 128)
    parts = [
        f(inp["norm_mix_g"]), f(inp["norm_ffn_g"]), f(inp["final_norm_g"]), f(inp["conv_b_pw1"]),
        f(inp["conv_w_dw"]), f(inp["conv_b_dw"]), f(inp["conv_ln_g"]), f(inp["conv_ln_b"]), f(inp["conv_b_pw2"]),
        f(inp["ffn_w_dw"]), f(inp["ffn_b_dw"]),
    ]
    tab = np.concatenate(parts, axis=0)
    assert tab.shape[0] == 1368
    pad = np.zeros((R_TOT - tab.shape[0], 128), np.float32)
    return np.ascontiguousarray(np.concatenate([tab, pad], axis=0))


_NC_CACHE = {}


def _run(inputs, stop_after=None, trace=False):
    x = np.ascontiguousarray(np.asarray(inputs["x"], dtype=np.float32))
    B = x.shape[0]
    key = stop_after
    if key not in _NC_CACHE:
        _NC_CACHE[key] = Builder(stop_after).build()
    nc = _NC_CACHE[key]
    ptab = _pack_ptab(inputs)
    c = lambda k: np.ascontiguousarray(np.asarray(inputs[k], dtype=np.float32))
    shared = {
        "ptab_in": ptab, "w_qkv": c("attn_w_qkv"), "w_o": c("attn_w_o"), "w_pw1": c("conv_w_pw1"), "w_pw2": c("conv_w_pw2"),
        "w_up": c("ffn_w_up"), "w_down": c("ffn_w_down"),
    }
    in_maps = [dict(shared, x=x[b]) for b in range(B)]
    res = run_bass_kernel_spmd(nc, in_maps, core_ids=list(range(B)), trace=trace)
    out = np.stack([np.asarray(r["out"]) for r in res.results], axis=0).astype(np.float32)
    return out, res


def kernel(**inputs):
    out, _ = _run(inputs)
    return out
```

```python
import math
import numpy as np
from contextlib import ExitStack
import concourse.bass as bass
import concourse.mybir as mybir
from concourse.bass_utils import run_bass_kernel_spmd

F32 = mybir.dt.float32
BF16 = mybir.dt.bfloat16
I32 = mybir.dt.int32
ALU = mybir.AluOpType
AF = mybir.ActivationFunctionType
AX = mybir.AxisListType

D = 1024
SEQ = 2048
NCH = 8
NT = 4
TT = 512
H = 16
DH = 64
DFF = 2816
NFP = 22
DEPTH = 4
NEG = -30000.0
NORM_EPS = 1e-6
LN_EPS = 1e-5
CW = 31

R_MIX = 0
R_FFN = 32
R_FIN = 64
R_BPW1 = 72
R_WDW = 104
R_BDW = 600
R_LNG = 616
R_LNB = 632
R_BPW2 = 648
R_FWDW = 664
R_FBDW = 1192
R_TOT = 1408


class _Op:
    __slots__ = ("eng", "fn", "deps", "needs_inc", "semval", "grp", "gen", "pre")

    def __init__(self, eng, fn):
        self.eng = eng
        self.fn = fn
        self.deps = []
        self.needs_inc = False
        self.semval = None
        self.grp = None
        self.gen = 0
        self.pre = None


class _Grp:
    __slots__ = ("sem", "gens", "closed")

    def __init__(self, sem):
        self.sem = sem
        self.gens = [0]
        self.closed = False


class Sched:
    ENGS = ("pe", "act", "dve", "pool", "sp")

    def __init__(self, nc, es):
        self.nc = nc
        self.es = es
        self.q = {e: [] for e in self.ENGS}
        self.lastw = {}
        self.readers = {}
        self.esem = {e: es.enter_context(nc.semaphore("sem_" + e)) for e in ("pe", "act", "dve", "pool")}
        self.ecnt = {e: 0 for e in ("pe", "act", "dve", "pool")}
        self.seen = {e: {} for e in self.ENGS}
        self.groups = {}
        self.lastreal = {e: None for e in self.ENGS}
        self.nops = 0

    def _group(self, name):
        g = self.groups.get(name)
        if g is None:
            g = _Grp(self.es.enter_context(self.nc.semaphore("dg_" + name)))
            self.groups[name] = g
        return g

    def add(self, eng, fn, reads=(), writes=(), dma=None):
        op = _Op(eng, fn)
        deps = {}
        for k in reads:
            w = self.lastw.get(k)
            if w is not None:
                deps[id(w)] = w
            if isinstance(k, tuple) and k[0] == "ps":
                rd = self.readers.get(k)
                if rd:
                    for rk_, r in rd.items():
                        if rk_ != eng:
                            deps[id(r)] = r
        for k in writes:
            w = self.lastw.get(k)
            if w is not None:
                deps[id(w)] = w
            rd = self.readers.get(k)
            if rd:
                for r in rd.values():
                    deps[id(r)] = r
        if dma is not None:
            g = self._group(dma)
            op.grp = g
            if g.closed:
                op.pre = [(g, len(g.gens) - 1)]
                g.gens.append(g.gens[-1])
                g.closed = False
            g.gens[-1] += 16
            op.gen = len(g.gens) - 1
        for d in deps.values():
            if eng == "pe" and d.eng == "pe" and d.grp is None:
                continue
            if op.grp is not None and d.grp is op.grp:
                continue
            op.deps.append(d)
            d.needs_inc = True
            if d.grp is not None:
                d.grp.closed = True
        rk = eng if dma is None else ("dma", id(op))
        for k in reads:
            self.readers.setdefault(k, {})[rk] = op
        for k in writes:
            self.lastw[k] = op
            self.readers[k] = {}
        self.q[eng].append(op)
        if dma is None:
            self.lastreal[eng] = op
        self.nops += 1
        return op

    def pe(self, fn, reads=(), writes=()):
        return self.add("pe", fn, reads, writes)

    def act(self, fn, reads=(), writes=()):
        return self.add("act", fn, reads, writes)

    def dve(self, fn, reads=(), writes=()):
        return self.add("dve", fn, reads, writes)

    def pool(self, fn, reads=(), writes=()):
        return self.add("pool", fn, reads, writes)

    def dma(self, queue, group, fn, reads=(), writes=()):
        return self.add(queue, fn, reads, writes, dma=group)

    def barrier(self):
        lasts = [op for op in self.lastreal.values() if op is not None and op.grp is None]
        gl = [(g, len(g.gens) - 1) for g in self.groups.values() if g.gens[-1] > 0]
        for g, _ in gl:
            g.closed = True
        for e in self.ENGS:
            op = _Op(e, None)
            for d in lasts:
                if not (e == "pe" and d.eng == "pe"):
                    op.deps.append(d)
                    d.needs_inc = True
            op.pre = list(gl)
            self.q[e].append(op)
        self.lastw = {}
        self.readers = {}

    def _assign(self):
        for e in ("pe", "act", "dve", "pool"):
            c = self.ecnt[e]
            for op in self.q[e]:
                if op.grp is None and op.fn is not None and op.needs_inc and op.semval is None:
                    c += 1
                    op.semval = c
            self.ecnt[e] = c

    def _emit_engine(self, e, h):
        seen = self.seen[e]

        def wait(sem, val):
            key = id(sem)
            if seen.get(key, 0) < val:
                h.wait_ge(sem, val)
                seen[key] = val

        for op in self.q[e]:
            if op.pre is not None:
                for g, gi in op.pre:
                    wait(g.sem, g.gens[gi])
            for d in op.deps:
                if d.grp is not None:
                    wait(d.grp.sem, d.grp.gens[d.gen])
                else:
                    wait(self.esem[d.eng], d.semval)
            if op.fn is not None:
                ins = op.fn(h)
                if op.grp is not None:
                    ins.then_inc(op.grp.sem, 16)
                elif op.needs_inc:
                    ins.then_inc(self.esem[e], 1)
        self.q[e] = []
        self.lastreal[e] = None

    def flush(self):
        self._assign()
        with self.nc.Block() as block:
            @block.tensor
            def _(h):
                self._emit_engine("pe", h)

            @block.scalar
            def _(h):
                self._emit_engine("act", h)

            @block.vector
            def _(h):
                self._emit_engine("dve", h)

            @block.gpsimd
            def _(h):
                self._emit_engine("pool", h)

            @block.sync
            def _(h):
                self._emit_engine("sp", h)


class Builder:
    def __init__(self, stop_after=None):
        self.stop_after = stop_after
        self.nc = bass.Bass("TRN2", target_bir_lowering=False)
        nc = self.nc
        dt = nc.dram_tensor
        self.x = dt("x", [SEQ, D], F32, kind="ExternalInput").ap()
        self.ptab_in = dt("ptab_in", [R_TOT, 128], F32, kind="ExternalInput").ap()
        self.w_qkv = dt("w_qkv", [2, D, 3 * D], F32, kind="ExternalInput").ap()
        self.w_o = dt("w_o", [2, D, D], F32, kind="ExternalInput").ap()
        self.w_pw1 = dt("w_pw1", [2, D, 2 * D], F32, kind="ExternalInput").ap()
        self.w_pw2 = dt("w_pw2", [2, D, D], F32, kind="ExternalInput").ap()
        self.w_up = dt("w_up", [DEPTH, D, 2 * DFF], F32, kind="ExternalInput").ap()
        self.w_down = dt("w_down", [DEPTH, DFF, D], F32, kind="ExternalInput").ap()
        self.out = dt("out", [SEQ, D], F32, kind="ExternalOutput").ap()
        self._uid = 0

    def sb(self, es, shape, dtype, name=None):
        self._uid += 1
        return es.enter_context(self.nc.sbuf_tensor("%s_%d" % (name or "t", self._uid), shape, dtype))

    def mm(self, out, lhsT, rhs, start, stop, reads, writes):
        self.S.pe(lambda h: h.matmul(out, lhsT=lhsT, rhs=rhs, start=start, stop=stop), reads, writes)

    def prow(self, r):
        return self.ptab[:, r:r + 1]

    def build(self):
        nc = self.nc
        with ExitStack() as es:
            self.S = S = Sched(nc, es)
            self.ps = [es.enter_context(nc.psum_tensor("ps%d" % i, [128, TT], F32)) for i in range(8)]
            self.h = self.sb(es, [128, NCH, SEQ], F32, "h")
            self.hn = self.sb(es, [128, NCH, SEQ], BF16, "hn")
            self.cosT = self.sb(es, [128, SEQ], BF16, "cosT")
            self.sinT = self.sb(es, [128, SEQ], BF16, "sinT")
            self.ptab = self.sb(es, [128, R_TOT], F32, "ptab")
            self.identf = self.sb(es, [128, 128], F32, "identf")
            self.identb = self.sb(es, [128, 128], BF16, "identb")
            self.onesb = self.sb(es, [128, 128], BF16, "onesb")
            self.maskT = self.sb(es, [128, 128], BF16, "maskT")
            self.prot = self.sb(es, [128, 128], BF16, "prot")
            self.ind = [self.sb(es, [128, 8 * 128], BF16, "ind") for _ in range(2)]
            self.nsq = self.sb(es, [128, 2, TT], BF16, "nsq")
            self.nstd = self.sb(es, [128, 2, TT], F32, "nstd")

            self.phase_setup()
            stop = self.stop_after
            done = stop == "load"
            for i in range(DEPTH):
                if done:
                    break
                j = i // 2
                self.rmsnorm(R_MIX + i * 8)
                if i % 2 == 0:
                    self.phase_attention(j)
                else:
                    self.phase_conformer(j)
                if stop is not None and stop.split(":")[0] == "mix%d" % i:
                    done = True
                    break
                self.rmsnorm(R_FFN + i * 8)
                fsub = stop.split(":")[1] if (stop and ":" in stop and stop.startswith("ffn")) else None
                if fsub != "norm":
                    self.phase_ffn(i)
                if stop is not None and stop.split(":")[0] == "ffn%d" % i:
                    done = True
                    break
            self.phase_final(raw=(stop is not None))
        return nc

    def phase_setup(self):
        nc, S = self.nc, self.S
        ps = self.ps
        with ExitStack() as es:
            xs = [self.sb(es, [128, D], F32, "xs") for _ in range(3)]
            pst = [self.sb(es, [128, 128], F32, "pst") for _ in range(2)]
            pi_i = self.sb(es, [128, 1], I32, "pi_i")
            pm_i = self.sb(es, [128, 2], I32, "pm_i")
            sgn = self.sb(es, [128, 2], F32, "sgn")
            invrow = self.sb(es, [1, 128], F32, "invrow")
            invf = self.sb(es, [128, 1], F32, "invf")
            pos_i = self.sb(es, [128, SEQ], I32, "pos_i")
            ang = self.sb(es, [128, SEQ], F32, "ang")
            ta = self.sb(es, [128, SEQ], F32, "ta")
            tb = self.sb(es, [128, SEQ], F32, "tb")
            tc = self.sb(es, [128, SEQ], F32, "tc")

            identf, identb, onesb, maskT, prot, ind = self.identf, self.identb, self.onesb, self.maskT, self.prot, self.ind
            S.pool(lambda h: h.memset(identf[:], 1.0), writes=["identf"])
            S.pool(lambda h: h.affine_select(out=identf[:], in_=identf[:], pattern=[[-1, 128]], compare_op=ALU.is_equal,
                                             fill=0.0, base=0, channel_multiplier=1), reads=["identf"], writes=["identf"])
            S.dve(lambda h: h.tensor_copy(out=identb[:], in_=identf[:]), reads=["identf"], writes=["identb"])
            S.pool(lambda h: h.memset(onesb[:], 1.0), writes=["onesb"])
            S.pool(lambda h: h.memset(maskT[:], 0.0), writes=["maskT"])
            S.pool(lambda h: h.affine_select(out=maskT[:], in_=maskT[:], pattern=[[1, 128]], compare_op=ALU.is_ge,
                                             fill=NEG, base=0, channel_multiplier=-1), reads=["maskT"], writes=["maskT"])
            for (dst, src) in ((0, 32), (32, 0), (64, 96), (96, 64)):
                S.dve(lambda h, dst=dst, src=src: h.tensor_copy(out=prot[:, dst:dst + 32], in_=identb[:, src:src + 32]),
                      reads=["identb"], writes=["prot"])
            for jj in range(2):
                S.pool(lambda h, jj=jj: h.memset(ind[jj][:], 0.0), writes=["ind"])
                S.pool(lambda h, jj=jj: h.memset(ind[jj][64 * jj:64 * jj + 64, :], 1.0), reads=["ind"], writes=["ind"])
                S.pool(lambda h, jj=jj: h.affine_select(out=ind[jj][64 * jj:64 * jj + 64, :], in_=ind[jj][64 * jj:64 * jj + 64, :],
                                                        pattern=[[1, 8], [0, 128]], compare_op=ALU.is_equal, fill=0.0,
                                                        base=0, channel_multiplier=-1), reads=["ind"], writes=["ind"])
            for r in range(R_TOT // 128):
                st = pst[r % 2]
                S.dma("sp", "pst%d" % (r % 2), lambda h, r=r, st=st: h.dma_start(out=st[:], in_=self.ptab_in[r * 128:(r + 1) * 128, :]),
                      writes=[("pst", r % 2)])
                bank = ps[r % 2]
                S.pe(lambda h, st=st, bank=bank: h.transpose(out=bank[:, 0:128], in_=st[:], identity=identf[:]),
                     reads=[("pst", r % 2), "identf"], writes=[("ps", r % 2)])
                S.act(lambda h, r=r, bank=bank: h.activation(out=self.ptab[:, r * 128:(r + 1) * 128], in_=bank[:, 0:128], func=AF.Copy),
                      reads=[("ps", r % 2)], writes=["ptab"])
            inv = (np.float32(1.0) / np.power(np.float32(10000.0), np.arange(0, DH, 2, dtype=np.float32) / np.float32(DH))).astype(np.float32)
            irv = invrow[:].rearrange("o (a b) -> o a b", b=32)
            for i in range(32):
                S.dve(lambda h, i=i: h.memset(irv[:, :, i:i + 1], float(inv[i])), writes=["invrow"])
            S.pe(lambda h: h.transpose(out=ps[2][:, 0:1], in_=invrow[:], identity=identf[0:1, 0:1]),
                 reads=["invrow", "identf"], writes=[("ps", 2)])
            S.act(lambda h: h.activation(out=invf[:], in_=ps[2][:, 0:1], func=AF.Copy), reads=[("ps", 2)], writes=["invf"])
            S.pool(lambda h: h.iota(pi_i[:], pattern=[[0, 1]], base=0, channel_multiplier=1), writes=["pi_i"])
            S.dve(lambda h: h.tensor_scalar(out=pm_i[:, 0:1], in0=pi_i[:], scalar1=32, scalar2=None, op0=ALU.bitwise_and),
                  reads=["pi_i"], writes=["pm_i"])
            S.dve(lambda h: h.tensor_copy(out=sgn[:, 0:1], in_=pm_i[:, 0:1]), reads=["pm_i"], writes=["sgn0"])
            S.dve(lambda h: h.tensor_scalar(out=sgn[:, 1:2], in0=sgn[:, 0:1], scalar1=1.0 / 16.0, scalar2=-1.0, op0=ALU.mult, op1=ALU.add),
                  reads=["sgn0"], writes=["sgn1"])
            S.pool(lambda h: h.iota(pos_i[:], pattern=[[1, SEQ]], base=0, channel_multiplier=0), writes=["pos_i"])
            S.dve(lambda h: h.tensor_copy(out=ta[:], in_=pos_i[:]), reads=["pos_i"], writes=["ta"])
            S.dve(lambda h: h.tensor_scalar(out=ang[:], in0=ta[:], scalar1=invf[:, 0:1], scalar2=None, op0=ALU.mult),
                  reads=["ta", "invf"], writes=["ang"])
            TWO_PI = 2.0 * math.pi
            C1 = 6.28125
            C2 = TWO_PI - C1
            MAGIC = 12582912.0
            LIM = 3.1415925
            S.dve(lambda h: h.tensor_scalar(out=ta[:], in0=ang[:], scalar1=1.0 / TWO_PI, scalar2=None, op0=ALU.mult),
                  reads=["ang", "ta"], writes=["ta"])
            S.dve(lambda h: h.tensor_scalar(out=tb[:], in0=ta[:], scalar1=MAGIC, scalar2=MAGIC, op0=ALU.add, op1=ALU.subtract),
                  reads=["ta"], writes=["tb"])
            S.dve(lambda h: h.scalar_tensor_tensor(out=ta[:], in0=tb[:], scalar=-C1, in1=ang[:], op0=ALU.mult, op1=ALU.add),
                  reads=["tb", "ang", "ta"], writes=["ta"])
            S.dve(lambda h: h.scalar_tensor_tensor(out=tc[:], in0=tb[:], scalar=-C2, in1=ta[:], op0=ALU.mult, op1=ALU.add),
                  reads=["tb", "ta"], writes=["tc"])
            S.dve(lambda h: h.tensor_scalar(out=ta[:], in0=tc[:], scalar1=LIM, scalar2=-LIM, op0=ALU.min, op1=ALU.max),
                  reads=["tc", "ta"], writes=["ta"])
            S.act(lambda h: h.activation(out=tb[:], in_=ta[:], func=AF.Sin), reads=["ta", "tb"], writes=["tb"])
            S.dve(lambda h: h.tensor_scalar(out=self.sinT[:], in0=tb[:], scalar1=sgn[:, 1:2], scalar2=None, op0=ALU.mult),
                  reads=["tb", "sgn1"], writes=["sinT"])
            S.dve(lambda h: h.tensor_scalar(out=ang[:], in0=tc[:], scalar1=math.pi / 2, scalar2=None, op0=ALU.add),
                  reads=["tc", "ang"], writes=["ang"])
            S.dve(lambda h: h.tensor_scalar(out=ta[:], in0=ang[:], scalar1=math.pi, scalar2=None, op0=ALU.is_gt),
                  reads=["ang", "ta"], writes=["ta"])
            S.dve(lambda h: h.scalar_tensor_tensor(out=tc[:], in0=ta[:], scalar=-TWO_PI, in1=ang[:], op0=ALU.mult, op1=ALU.add),
                  reads=["ta", "ang", "tc"], writes=["tc"])
            S.dve(lambda h: h.tensor_scalar(out=ta[:], in0=tc[:], scalar1=LIM, scalar2=-LIM, op0=ALU.min, op1=ALU.max),
                  reads=["tc", "ta"], writes=["ta"])
            S.act(lambda h: h.activation(out=self.cosT[:], in_=ta[:], func=AF.Sin), reads=["ta"], writes=["cosT"])
            for tt in range(16):
                sl = tt % 3
                S.dma("sp", "xs%d" % sl, lambda h, tt=tt, sl=sl: h.dma_start(out=xs[sl][:], in_=self.x[tt * 128:(tt + 1) * 128, :]),
                      writes=[("xs", sl)])
                T = tt // 4
                for half in range(2):
                    bi = 4 + (2 * tt + half) % 4
                    bank = ps[bi]
                    for jj in range(4):
                        c = half * 4 + jj
                        S.pe(lambda h, bank=bank, jj=jj, c=c, sl=sl: h.transpose(out=bank[:, jj * 128:(jj + 1) * 128],
                                                                               in_=xs[sl][:, c * 128:(c + 1) * 128], identity=identf[:]),
                             reads=[("xs", sl), "identf"], writes=[("ps", bi)])
                    dst = self.h[:, half * 4:half * 4 + 4, tt * 128:(tt + 1) * 128]
                    src = bank[:, :].rearrange("p (a b) -> p a b", b=128)
                    wr = [("h", half * 4 + jj, T) for jj in range(4)]
                    if (2 * tt + half) % 2 == 0:
                        S.act(lambda h, dst=dst, src=src: h.activation(out=dst, in_=src, func=AF.Copy), reads=[("ps", bi)], writes=wr)
                    else:
                        S.dve(lambda h, dst=dst, src=src: h.tensor_copy(out=dst, in_=src), reads=[("ps", bi)], writes=wr)
            S.barrier()
            S.flush()

    def rmsnorm(self, grow, out_fn=None):
        S, ps = self.S, self.ps
        for T in range(NT):
            bi = 6 + T % 2
            bank = ps[bi]
            tsl = slice(T * TT, (T + 1) * TT)
            for c in range(NCH):
                sq = self.nsq[:, c % 2, :]
                S.act(lambda h, sq=sq, c=c, tsl=tsl: h.activation(out=sq, in_=self.h[:, c, tsl], func=AF.Square),
                      reads=[("h", c, T)], writes=[("nsq", c % 2)])
                self.mm(bank[:, :], self.onesb[:], sq, c == 0, c == NCH - 1, [("nsq", c % 2), "onesb"], [("ps", bi)])
            sd = self.nstd[:, T % 2, :]
            S.act(lambda h, sd=sd, bank=bank: h.activation(out=sd, in_=bank[:, :], func=AF.Sqrt, scale=1.0 / D, bias=NORM_EPS),
                  reads=[("ps", bi)], writes=[("nstd", T % 2)])
            S.dve(lambda h, sd=sd, bank=bank: h.reciprocal(bank[:, :], sd), reads=[("nstd", T % 2)], writes=[("ps", bi)])
            for c in range(NCH):
                if out_fn is None:
                    dst = self.hn[:, c, tsl]
                    wr = [("hn", c, T)]
                else:
                    dst, wr = out_fn(c, T)
                S.dve(lambda h, dst=dst, c=c, tsl=tsl, bank=bank: h.scalar_tensor_tensor(
                    out=dst, in0=self.h[:, c, tsl], scalar=self.prow(grow + c), in1=bank[:, :], op0=ALU.mult, op1=ALU.mult),
                    reads=[("h", c, T), ("ps", bi), "ptab"], writes=wr)

    def load_w(self, slot_ap, src_ap, key, group):
        self.S.dma("pool", group, lambda h: h.dma_start(out=slot_ap, in_=src_ap), writes=[key])

    def phase_attention(self, j):
        nc, S, ps = self.nc, self.S, self.ps
        hn, h = self.hn, self.h
        with ExitStack() as es:
            qT = self.sb(es, [128, 2, SEQ], BF16, "qT")
            kT = self.sb(es, [128, 2, 2, SEQ], BF16, "kT")
            Vt = self.sb(es, [128, 16, 4, 128], BF16, "Vt")
            NW = 5
            wsl = [self.sb(es, [128, 2048], BF16, "wsl") for _ in range(NW)]
            biasT = self.sb(es, [128, 2, 1024], BF16, "biasT")
            kms = self.sb(es, [128, 4, 8], F32, "kms")
            kmT = self.sb(es, [128, 4, 8], BF16, "kmT")
            gsb = self.sb(es, [128, 256], F32, "gsb")
            cmp_ = self.sb(es, [128, 2, 4 * 49], BF16, "cmp")
            rank = self.sb(es, [128, 2, 32], F32, "rank")
            btok = self.sb(es, [128, 4, 256], BF16, "btok")
            pT = [self.sb(es, [128, TT], BF16, "pT") for _ in range(4)]
            rec = [self.sb(es, [128, TT], F32, "rec") for _ in range(2)]
            qs = [self.sb(es, [128, TT], BF16, "qs") for _ in range(2)]
            t1 = [self.sb(es, [128, TT], F32, "t1") for _ in range(2)]
            t2 = [self.sb(es, [128, TT], F32, "t2") for _ in range(2)]

            wcount = [0]

            def next_slot():
                i = wcount[0] % NW
                wcount[0] += 1
                return i

            S.pool(lambda h_: h_.memset(Vt[:, :, 0:4:2, 64:128], 1.0), writes=[("Vt1", 0)])
            S.pool(lambda h_: h_.memset(Vt[:, :, 1:4:2, 0:64], 1.0), writes=[("Vt1", 1)])
            S.pool(lambda h_: h_.memset(biasT[:], 0.0), writes=[("biasT", a, u) for a in range(2) for u in range(2)])
            S.pool(lambda h_: h_.memset(kms[:], 0.0), writes=[("kmsz",)])
            S.pool(lambda h_: h_.memset(kT[64:128, :, 0, :], 0.0), writes=[("kz", 0)])
            S.pool(lambda h_: h_.memset(kT[0:64, :, 1, :], 0.0), writes=[("kz", 1)])

            wq_src = self.w_qkv[j].rearrange("(c p) f -> p c f", p=128)
            wo_src = self.w_o[j].rearrange("(c p) f -> p c f", p=128)

            def load_group(g):
                sl = {}
                for nm, off in (("q", 0), ("k", D), ("v", 2 * D)):
                    i = next_slot()
                    sl[nm] = i
                    self.load_w(wsl[i][:, :].rearrange("p (c f) -> p c f", f=256), wq_src[:, :, off + g * 256: off + (g + 1) * 256],
                                ("wsl", i), "aw%d" % i)
                i = next_slot()
                sl["o"] = i
                self.load_w(wsl[i][:, :].rearrange("p (c f) -> p c f", f=1024), wo_src[:, 2 * g:2 * g + 2, :], ("wsl", i), "aw%d" % i)
                return sl

            pending = load_group(0)
            rope_i = [0]
            sub = self.stop_after.split(":")[1] if (self.stop_after and ":" in self.stop_after) else None
            for g in range(4):
                if sub is not None and g > 0:
                    break
                sl = pending
                wq = wsl[sl["q"]][:, :].rearrange("p (c f) -> p c f", f=256)
                wk = wsl[sl["k"]][:, :].rearrange("p (c f) -> p c f", f=256)
                wv = wsl[sl["v"]][:, :].rearrange("p (c f) -> p c f", f=256)
                wo = wsl[sl["o"]][:, :].rearrange("p (c f) -> p c f", f=1024)
                pend = []

                def rope_a(which, cc, T, wsrc, wkey):
                    scale = DH ** -0.5 if which == "q" else 1.0
                    ri = rope_i[0]
                    rope_i[0] += 1
                    ba = ri % 2
                    A = ps[ba]
                    tsl = slice(T * TT, (T + 1) * TT)
                    for c in range(NCH):
                        self.mm(A[:, :], wsrc[:, c, cc * 128:(cc + 1) * 128], hn[:, c, tsl], c == 0, c == NCH - 1,
                                [wkey, ("hn", c, T)], [("ps", ba)])
                    q_s = qs[ri % 2]
                    S.act(lambda h_, q_s=q_s, A=A, scale=scale: h_.activation(out=q_s[:], in_=A[:, :], func=AF.Copy, scale=scale),
                          reads=[("ps", ba)], writes=[("qs", ri % 2)])
                    return (which, cc, T, ri, scale)

                def rope_b(which, cc, T, ri, scale):
                    ba, bb = ri % 2, 2 + ri % 2
                    A, B = ps[ba], ps[bb]
                    tsl = slice(T * TT, (T + 1) * TT)
                    q_s, t1_, t2_ = qs[ri % 2], t1[ri % 2], t2[ri % 2]
                    self.mm(B[:, :], self.prot[:], q_s[:], True, True, [("qs", ri % 2), "prot"], [("ps", bb)])
                    S.dve(lambda h_, t1_=t1_, A=A, scale=scale, tsl=tsl: h_.scalar_tensor_tensor(
                        out=t1_[:], in0=A[:, :], scalar=scale, in1=self.cosT[:, tsl], op0=ALU.mult, op1=ALU.mult),
                        reads=[("ps", ba), "cosT"], writes=[("t1", ri % 2)])
                    S.dve(lambda h_, t2_=t2_, B=B, tsl=tsl: h_.tensor_tensor(out=t2_[:], in0=B[:, :], in1=self.sinT[:, tsl], op=ALU.mult),
                          reads=[("ps", bb), "sinT"], writes=[("t2", ri % 2)])
                    if which == "q":
                        dst = qT[:, cc, tsl]
                        wr = [("q", cc, T, 0), ("q", cc, T, 1)]
                        S.pool(lambda h_, dst=dst, t1_=t1_, t2_=t2_: h_.tensor_tensor(out=dst, in0=t1_[:], in1=t2_[:], op=ALU.add),
                               reads=[("t1", ri % 2), ("t2", ri % 2)], writes=wr)
                    else:
                        tk_ = qs[ri % 2]
                        S.pool(lambda h_, tk_=tk_, t1_=t1_, t2_=t2_: h_.tensor_tensor(out=tk_[:], in0=t1_[:], in1=t2_[:], op=ALU.add),
                               reads=[("t1", ri % 2), ("t2", ri % 2), ("qs", ri % 2)], writes=[("qs", ri % 2)])
                        for hp in range(2):
                            prt = slice(hp * 64, (hp + 1) * 64)
                            ch = 2 * cc + hp
                            for hf in range(2):
                                blk = 2 * T + hf
                                S.act(lambda h_, prt=prt, hp=hp, cc=cc, T=T, hf=hf, ch=ch, blk=blk, tk_=tk_: h_.activation(
                                    out=kT[prt, cc, hp, T * TT + hf * 256:T * TT + (hf + 1) * 256], in_=tk_[prt, hf * 256:(hf + 1) * 256],
                                    func=AF.Copy, accum_out=kms[prt, ch, blk:blk + 1]),
                                    reads=[("qs", ri % 2), ("kmsz",)], writes=[("k", cc, T, hp), ("kms", ch, blk)])

                def kmean(ch):
                    S.dve(lambda h_: h_.tensor_scalar(out=kmT[:, ch, :], in0=kms[:, ch, :], scalar1=1.0 / 256.0, scalar2=None, op0=ALU.mult),
                          reads=[("kms", ch, b_) for b_ in range(8)] + [("kmsz",)], writes=[("kmT", ch)])

                def rope_units(which, wsrc, wkey, between=None):
                    ulist = [(which, cc, T) for cc in range(2) for T in range(NT)]
                    for ui_, u_ in enumerate(ulist):
                        st_ = rope_a(u_[0], u_[1], u_[2], wsrc, wkey)
                        if pend:
                            rope_b(*pend.pop(0))
                        pend.append(st_)
                        if which == "k" and ui_ == NT and sub is None:
                            kmean(0)
                            kmean(1)
                        if between is not None:
                            between(ui_)
                    while pend:
                        rope_b(*pend.pop(0))

                if g == 0 or sub is not None:
                    rope_units("k", wk, ("wsl", sl["k"]))
                rope_units("q", wq, ("wsl", sl["q"]))
                if sub == "qk":
                    break
                if sub == "v":
                    break
                def gate_matmuls():
                    gbank = ps[6]
                    for qt in range(8, 16):
                        T = qt // 4
                        for hl in range(4):
                            cc, hp = hl // 2, hl % 2
                            col = (qt - 8) * 32 + hl * 8
                            self.mm(gbank[:, col:col + 8], qT[:, cc, qt * 128:(qt + 1) * 128],
                                    kmT[:, hl, :], True, True,
                                    [("q", cc, T, 0), ("q", cc, T, 1), ("kmT", hl)], [("ps", 6)])
                    S.act(lambda h_: h_.activation(out=gsb[:, :], in_=gbank[:, 0:256], func=AF.Copy),
                          reads=[("ps", 6)], writes=["gsb"])

                def gate_chain(qt):
                    qb = qt // 2
                    sI = qt % 4
                    c2 = qt % 2
                    g3 = gsb[:, (qt - 8) * 32:(qt - 7) * 32].rearrange("p (a n) -> p a n", n=8)[:, :, 0:qb]
                    in0 = g3.unsqueeze(2).broadcast_to([128, 4, qb, qb])
                    in1 = g3.unsqueeze(3).broadcast_to([128, 4, qb, qb])
                    cm = cmp_[:, c2, 0:4 * qb * qb].rearrange("p (a n m) -> p a n m", a=4, n=qb)
                    S.dve(lambda h_, cm=cm, in0=in0, in1=in1: h_.tensor_tensor(out=cm, in0=in0, in1=in1, op=ALU.is_gt),
                          reads=["gsb"], writes=[("cmp", c2)])
                    rk = rank[:, c2, :].rearrange("p (a n) -> p a n", n=8)[:, :, 0:qb]
                    S.dve(lambda h_, rk=rk, cm=cm: h_.tensor_reduce(out=rk, in_=cm, axis=AX.X, op=ALU.add),
                          reads=[("cmp", c2)], writes=[("rank", c2)])
                    S.pool(lambda h_, sI=sI: h_.memset(btok[:, sI, :], 0.0), writes=[("btok", sI)])
                    bo = btok[:, sI, :].rearrange("p (a b n) -> p a b n", a=2, b=2)[:, :, :, 0:qb]
                    rk4 = rank[:, c2, :].rearrange("p (a b n) -> p a b n", a=2, b=2)[:, :, :, 0:qb]
                    S.dve(lambda h_, bo=bo, rk4=rk4: h_.tensor_scalar(out=bo, in0=rk4, scalar1=2.5, scalar2=NEG, op0=ALU.is_gt, op1=ALU.mult),
                          reads=[("rank", c2)], writes=[("btok", sI)])

                def gate_transposes(q4):
                    for a in range(2):
                        bti = 2 + a
                        for qq in range(4):
                            qt = 8 + 4 * q4 + qq
                            sI = qt % 4
                            self.mm(ps[bti][:, qq * 128:(qq + 1) * 128], btok[:, sI, a * 128:(a + 1) * 128], self.identb[:], True, True,
                                    [("btok", sI), "identb"], [("ps", bti)])
                        S.act(lambda h_, a=a, q4=q4, bti=bti: h_.activation(out=biasT[:, a, q4 * TT:(q4 + 1) * TT], in_=ps[bti][:, :], func=AF.Copy),
                              reads=[("ps", bti)], writes=[("biasT", a, q4)])
                def v_proj(tp_lo, tp_hi):
                  for tp in range(tp_lo, tp_hi):
                      bi = 4 + tp % 2
                      bank = ps[bi]
                      for u in range(2):
                          tt = 2 * tp + u
                          T = tt // 4
                          for c in range(NCH):
                              self.mm(bank[:, u * 256:(u + 1) * 256], hn[:, c, tt * 128:(tt + 1) * 128], wv[:, c, :], c == 0, c == NCH - 1,
                                      [("wsl", sl["v"]), ("hn", c, T)], [("ps", bi)])
                      src = bank[:, :].rearrange("p (u a b e) -> p u a b e", u=2, a=2, b=2)
                      S.act(lambda h_, src=src, tp=tp: h_.activation(out=Vt[:, 2 * tp:2 * tp + 2, 0:4:2, 0:64], in_=src[:, :, :, 0, :], func=AF.Copy),
                            reads=[("ps", bi)], writes=[("Vt", 2 * tp, 0), ("Vt", 2 * tp + 1, 0)])
                      S.dve(lambda h_, src=src, tp=tp: h_.tensor_copy(out=Vt[:, 2 * tp:2 * tp + 2, 1:4:2, 64:128], in_=src[:, :, :, 1, :]),
                            reads=[("ps", bi)], writes=[("Vt", 2 * tp, 1), ("Vt", 2 * tp + 1, 1)])
                v_proj(0, 1)
                kmean(2)
                v_proj(1, 2)
                kmean(3)
                v_proj(2, 4)
                gate_matmuls()
                for qt in range(8, 12):
                    gate_chain(qt)
                v_proj(4, 6)
                gate_transposes(0)
                for qt in range(12, 16):
                    gate_chain(qt)
                v_proj(6, 8)
                gate_transposes(1)
                if g + 1 < 4 and sub is None:
                    pending = load_group(g + 1)
                if sub == "gate":
                    break
                iters = []
                itc = 0
                for cc in range(2):
                    for T in range(NT):
                        nk = 4 * T + 4
                        ob = [4 + 2 * (itc % 2), 5 + 2 * (itc % 2)]
                        itc += 1
                        for kt in range(nk):
                            for hp in range(2):
                                iters.append((cc, T, kt, hp, nk, ob[hp]))
                LAG = 3

                def stage_a(n_, cc, T, kt, hp, nk, obk):
                    nb = kt // 2
                    q_lo = max(0, kt - 4 * T) * 128
                    qsl = slice(q_lo, TT)
                    Tk = kt // 4
                    sbi = n_ % 4
                    sbk = ps[sbi]
                    need_bias = (T >= 2 and kt < 4 * T + 2)
                    need_mask = kt >= 4 * T
                    self.mm(sbk[:, qsl], kT[:, cc, hp, kt * 128:(kt + 1) * 128], qT[:, cc, T * TT + q_lo:(T + 1) * TT],
                            True, not (need_bias or need_mask),
                            [("k", cc, Tk, hp), ("kz", hp), ("q", cc, T, 0), ("q", cc, T, 1)], [("ps", sbi)])
                    if need_bias:
                        self.mm(sbk[:, qsl], self.ind[hp][:, nb * 128:(nb + 1) * 128],
                                biasT[:, cc, (T - 2) * TT + q_lo:(T - 1) * TT],
                                False, not need_mask, ["ind", ("biasT", cc, T - 2)], [("ps", sbi)])
                    if need_mask:
                        self.mm(sbk[:, q_lo:q_lo + 128], self.identb[:], self.maskT[:], False, True,
                                ["identb", "maskT"], [("ps", sbi)])

                def stage_bc(n_, cc, T, kt, hp, nk, obk):
                    hl = 2 * cc + hp
                    q_lo = max(0, kt - 4 * T) * 128
                    qsl = slice(q_lo, TT)
                    sbi = n_ % 4
                    sbk = ps[sbi]
                    pi = n_ % 4
                    S.act(lambda h_, pi=pi, sbk=sbk, qsl=qsl: h_.activation(out=pT[pi][:, qsl], in_=sbk[:, qsl], func=AF.Exp),
                          reads=[("ps", sbi)], writes=[("pT", pi)])
                    self.mm(ps[obk][:, qsl], Vt[:, kt, hl, :], pT[pi][:, qsl], kt == 0, kt == nk - 1,
                            [("pT", pi), ("Vt", kt, hp), ("Vt1", hp)], [("ps", obk)])
                    if kt == nk - 1:
                        o = ps[obk]
                        num = slice(hp * 64, (hp + 1) * 64)
                        den = slice((1 - hp) * 64, (2 - hp) * 64)
                        rc = rec[hp]
                        S.dve(lambda h_, rc=rc, o=o, den=den: h_.reciprocal(rc[den, :], o[den, :]),
                              reads=[("ps", obk)], writes=[("rec", hp)])
                        S.dve(lambda h_, rc=rc, o=o, den=den, num=num, cc=cc, T=T: h_.tensor_tensor(
                            out=qT[num, cc, T * TT:(T + 1) * TT], in0=o[num, :], in1=rc[den, :], op=ALU.mult),
                            reads=[("ps", obk), ("rec", hp)], writes=[("q", cc, T, hp)])

                NI = len(iters)
                for n_ in range(NI + LAG):
                    if n_ < NI:
                        stage_a(n_, *iters[n_])
                    m_ = n_ - LAG
                    if m_ >= 0:
                        stage_bc(m_, *iters[m_])
                if sub == "core":
                    break
                wo_units = [(T, dc) for T in range(NT) for dc in range(NCH)]

                def wo_unit(T, dc):
                    tsl = slice(T * TT, (T + 1) * TT)
                    bi = 4 + (T * NCH + dc) % 4
                    for cc in range(2):
                        self.mm(ps[bi][:, :], wo[:, cc, dc * 128:(dc + 1) * 128], qT[:, cc, tsl], cc == 0, cc == 1,
                                [("wsl", sl["o"]), ("q", cc, T, 0), ("q", cc, T, 1)], [("ps", bi)])
                    S.dve(lambda h_: h_.tensor_tensor(out=h[:, dc, tsl], in0=ps[bi][:, :], in1=h[:, dc, tsl], op=ALU.add),
                          reads=[("ps", bi), ("h", dc, T)], writes=[("h", dc, T)])

                if g + 1 < 4 and sub is None:
                    nsl = pending
                    nwk = wsl[nsl["k"]][:, :].rearrange("p (c f) -> p c f", f=256)

                    def between(ui_):
                        for (T_, dc_) in wo_units[4 * ui_:4 * ui_ + 4]:
                            wo_unit(T_, dc_)

                    rope_units("k", nwk, ("wsl", nsl["k"]), between)
                else:
                    for (T_, dc_) in wo_units:
                        wo_unit(T_, dc_)
            S.barrier()
            S.flush()

    def phase_conformer(self, j):
        nc, S, ps = self.nc, self.S, self.ps
        hn, h = self.hn, self.h
        PADL = 32
        with ExitStack() as es:
            glu = self.sb(es, [128, NCH, PADL + SEQ], BF16, "glu")
            ybf = [self.sb(es, [128, NCH, TT], BF16, "ybf") for _ in range(2)]
            ysq = self.sb(es, [128, NCH, TT], BF16, "ysq")
            dg = self.sb(es, [128, CW, 128], BF16, "dg")
            NW = 3
            wsl = [self.sb(es, [128, NCH, 256], BF16, "cw") for _ in range(NW)]
            tA = [self.sb(es, [128, TT], F32, "tA") for _ in range(2)]
            sgm = tA
            mean_t = self.sb(es, [128, TT], F32, "mean_t")
            m2_t = self.sb(es, [128, TT], F32, "m2_t")
            wcount = [0]

            def next_slot():
                i = wcount[0] % NW
                wcount[0] += 1
                return i

            S.pool(lambda h_: h_.memset(glu[:, :, 0:PADL], 0.0), writes=[("glupad",)])
            w1 = self.w_pw1[j].rearrange("(c p) f -> p c f", p=128)
            w2 = self.w_pw2[j].rearrange("(c p) f -> p c f", p=128)

            def half(i):
                return wsl[i // 2][:, :, (i % 2) * 128:(i % 2 + 1) * 128]

            def load_pw1_chunk(cc):
                ia, ig = (2 * cc) % 6, (2 * cc + 1) % 6
                self.load_w(half(ia), w1[:, :, cc * 128:(cc + 1) * 128], ("cwh", ia // 2, ia % 2), "cwh%d" % ia)
                self.load_w(half(ig), w1[:, :, D + cc * 128:D + (cc + 1) * 128], ("cwh", ig // 2, ig % 2), "cwh%d" % ig)

            for cc in range(3):
                load_pw1_chunk(cc)
            it = 0
            for cb in range(4):
                for ci in range(2):
                    cc = 2 * cb + ci
                    ia, ig = (2 * cc) % 6, (2 * cc + 1) % 6
                    for T in range(NT):
                        if T == 1 and cc >= 1 and cc + 2 < NCH:
                            load_pw1_chunk(cc + 2)
                        ba, bg = (it % 2) * 2, (it % 2) * 2 + 1
                        it += 1
                        tsl = slice(T * TT, (T + 1) * TT)
                        for c in range(NCH):
                            self.mm(ps[ba][:, :], half(ia)[:, c, :], hn[:, c, tsl], c == 0, c == NCH - 1,
                                    [("cwh", ia // 2, ia % 2), ("hn", c, T)], [("ps", ba)])
                        for c in range(NCH):
                            self.mm(ps[bg][:, :], half(ig)[:, c, :], hn[:, c, tsl], c == 0, c == NCH - 1,
                                    [("cwh", ig // 2, ig % 2), ("hn", c, T)], [("ps", bg)])
                        sg_ = sgm[it % 2]
                        S.act(lambda h_, sg_=sg_, bg=bg, cc=cc: h_.activation(out=sg_[:], in_=ps[bg][:, :], func=AF.Sigmoid,
                                                                               bias=self.prow(R_BPW1 + j * 16 + 8 + cc)),
                              reads=[("ps", bg), "ptab"], writes=[("tA", it % 2)])
                        S.dve(lambda h_, sg_=sg_, ba=ba, cc=cc, T=T: h_.scalar_tensor_tensor(
                            out=glu[:, cc, PADL + T * TT:PADL + (T + 1) * TT], in0=ps[ba][:, :], scalar=self.prow(R_BPW1 + j * 16 + cc),
                            in1=sg_[:], op0=ALU.add, op1=ALU.mult),
                            reads=[("ps", ba), ("tA", it % 2), "ptab"], writes=[("glu", cc, T)])
            pw2_slots = {}

            def load_pw2(db):
                i = next_slot()
                self.S.dma("pool", "cwh%d" % (2 * i), lambda h_: h_.dma_start(out=wsl[i][:], in_=w2[:, :, db * 256:(db + 1) * 256]),
                           writes=[("cwh", i, 0), ("cwh", i, 1)])
                pw2_slots[db] = i

            load_pw2(0)
            load_pw2(1)
            dcount = [0]

            YB = (0, 1, 6, 7)

            def conv_unit(T, cc):
                yb = YB[cc % 4]
                yT = ybf[T % 2]
                for tap in range(CW):
                    if dcount[0] % 2 == 0:
                        S.dve(lambda h_, tap=tap, cc=cc: h_.tensor_scalar(out=dg[:, tap, :], in0=self.identb[:],
                                                                          scalar1=self.prow(R_WDW + (j * CW + tap) * 8 + cc), scalar2=None, op0=ALU.mult),
                              reads=["identb", "ptab"], writes=[("dg", tap)])
                    else:
                        S.pool(lambda h_, tap=tap, cc=cc: h_.tensor_scalar(out=dg[:, tap, :], in0=self.identb[:],
                                                                           scalar1=self.prow(R_WDW + (j * CW + tap) * 8 + cc), scalar2=1.0,
                                                                           op0=ALU.mult, op1=ALU.mult),
                               reads=["identb", "ptab"], writes=[("dg", tap)])
                    dcount[0] += 1
                for tap in range(CW):
                    o0 = PADL + T * TT - (CW - 1) + tap
                    rd = [("dg", tap), ("glu", cc, T)]
                    rd.append(("glu", cc, T - 1) if T > 0 else ("glupad",))
                    self.mm(ps[yb][:, :], dg[:, tap, :], glu[:, cc, o0:o0 + TT], tap == 0, tap == CW - 1, rd, [("ps", yb)])

            def conv_evac(T, cc):
                yb = YB[cc % 4]
                yT = ybf[T % 2]
                S.act(lambda h_: h_.activation(out=yT[:, cc, :], in_=ps[yb][:, :], func=AF.Identity,
                                               bias=self.prow(R_BDW + j * 8 + cc)),
                      reads=[("ps", yb), "ptab"], writes=[("ybf", T % 2, cc)])
                S.act(lambda h_: h_.activation(out=ysq[:, cc, :], in_=yT[:, cc, :], func=AF.Square),
                      reads=[("ybf", T % 2, cc)], writes=[("ysq", cc)])

            def ln_head(T):
                yT = ybf[T % 2]
                bm, bq = 2 + (T % 2) * 2, 3 + (T % 2) * 2
                for cc in range(NCH):
                    self.mm(ps[bm][:, :], self.onesb[:], yT[:, cc, :], cc == 0, cc == NCH - 1, ["onesb", ("ybf", T % 2, cc)], [("ps", bm)])
                for cc in range(NCH):
                    self.mm(ps[bq][:, :], self.onesb[:], ysq[:, cc, :], cc == 0, cc == NCH - 1, ["onesb", ("ysq", cc)], [("ps", bq)])

            def ln_head_b(T):
                bm, bq = 2 + (T % 2) * 2, 3 + (T % 2) * 2
                S.act(lambda h_: h_.activation(out=mean_t[:], in_=ps[bm][:, :], func=AF.Copy, scale=1.0 / D),
                      reads=[("ps", bm)], writes=["mean_t"])
                S.dve(lambda h_: h_.tensor_tensor(out=m2_t[:], in0=mean_t[:], in1=mean_t[:], op=ALU.mult), reads=["mean_t"], writes=["m2_t"])
                S.dve(lambda h_: h_.scalar_tensor_tensor(out=m2_t[:], in0=ps[bq][:, :], scalar=1.0 / D, in1=m2_t[:],
                                                         op0=ALU.mult, op1=ALU.subtract),
                      reads=[("ps", bq), "m2_t"], writes=["m2_t"])
                S.act(lambda h_: h_.activation(out=m2_t[:], in_=m2_t[:], func=AF.Sqrt, bias=LN_EPS), reads=["m2_t"], writes=["m2_t"])
                S.dve(lambda h_: h_.reciprocal(ps[bm][:, :], m2_t[:]), reads=["m2_t"], writes=[("ps", bm)])
                S.dve(lambda h_: h_.tensor_tensor(out=ps[bq][:, :], in0=ps[bm][:, :], in1=mean_t[:], op=ALU.mult),
                      reads=[("ps", bm), "mean_t"], writes=[("ps", bq)])

            def ln_norm(T, cc):
                yT = ybf[T % 2]
                bm, bq = 2 + (T % 2) * 2, 3 + (T % 2) * 2
                ta_ = tA[cc % 2]
                S.dve(lambda h_: h_.tensor_tensor(out=ta_[:], in0=ps[bm][:, :], in1=yT[:, cc, :], op=ALU.mult),
                      reads=[("ps", bm), ("ybf", T % 2, cc)], writes=[("tA", cc % 2)])
                S.dve(lambda h_: h_.tensor_tensor(out=ta_[:], in0=ta_[:], in1=ps[bq][:, :], op=ALU.subtract),
                      reads=[("ps", bq), ("tA", cc % 2)], writes=[("tA", cc % 2)])
                S.act(lambda h_: h_.activation(out=hn[:, cc, T * TT:(T + 1) * TT], in_=ta_[:], func=AF.Silu,
                                               scale=self.prow(R_LNG + j * 8 + cc), bias=self.prow(R_LNB + j * 8 + cc)),
                      reads=[("tA", cc % 2), "ptab"], writes=[("hn", cc, T)])

            for T in range(NT):
                for cc in range(NCH):
                    conv_unit(T, cc)
                    if T > 0 and cc == 0:
                        ln_head(T - 1)
                    if T > 0 and cc == 1:
                        ln_head_b(T - 1)
                    conv_evac(T, cc)
                    if T > 0 and cc >= 1:
                        ln_norm(T - 1, cc - 1)
                if T > 0:
                    ln_norm(T - 1, NCH - 1)
            ln_head(NT - 1)
            ln_head_b(NT - 1)
            for cc in range(NCH):
                ln_norm(NT - 1, cc)
            it = 0
            for db in range(4):
                if db + 2 < 4:
                    load_pw2(db + 2)
                i = pw2_slots[db]
                for T in range(NT):
                    for di in range(2):
                        dc = 2 * db + di
                        bi = (6, 7, 0, 1, 2, 3, 4, 5)[it % 8]
                        it += 1
                        tsl = slice(T * TT, (T + 1) * TT)
                        for cc in range(NCH):
                            self.mm(ps[bi][:, :], wsl[i][:, cc, di * 128:(di + 1) * 128], hn[:, cc, tsl], cc == 0, cc == NCH - 1,
                                    [("cwh", i, di), ("hn", cc, T)], [("ps", bi)])
                        S.dve(lambda h_, bi=bi, dc=dc, tsl=tsl: h_.scalar_tensor_tensor(
                            out=h[:, dc, tsl], in0=ps[bi][:, :], scalar=self.prow(R_BPW2 + j * 8 + dc), in1=h[:, dc, tsl], op0=ALU.add, op1=ALU.add),
                            reads=[("ps", bi), ("h", dc, T), "ptab"], writes=[("h", dc, T)])
            S.barrier()
            S.flush()

    def phase_ffn(self, i):
        nc, S, ps = self.nc, self.S, self.ps
        hn, h = self.hn, self.h
        groups = [[0, 1, 2], [3, 4, 5], [6, 7, 8], [9, 10]]
        with ExitStack() as es:
            act = self.sb(es, [128, 6, SEQ], BF16, "act")
            NWU = 3
            wup = [self.sb(es, [128, NCH, 512], BF16, "wup") for _ in range(NWU)]
            wdn = [self.sb(es, [128, 6, D], BF16, "wdn") for _ in range(2)]
            Ag = [self.sb(es, [128, TT], F32, "Ag") for _ in range(2)]
            Av = [self.sb(es, [128, TT], F32, "Av") for _ in range(2)]
            sg = [self.sb(es, [128, TT], F32, "sg") for _ in range(2)]
            halo = self.sb(es, [128, 2, 2, 2], F32, "halo")
            wu_src = self.w_up[i].rearrange("(c p) f -> p c f", p=128)
            wd_src = self.w_down[i].rearrange("(c p) f -> p c f", p=128)
            ucount = [0]
            blocks = [b for g in groups for b in g]
            up_slot = {}

            def load_up(b):
                s = ucount[0] % NWU
                ucount[0] += 1
                up_slot[b] = s
                self.load_w(wup[s][:, :, 0:256], wu_src[:, :, b * 256:(b + 1) * 256], ("wup", s), "wu%d" % s)
                self.load_w(wup[s][:, :, 256:512], wu_src[:, :, DFF + b * 256:DFF + (b + 1) * 256], ("wup", s), "wu%d" % s)

            def load_dn(gi):
                g = groups[gi]
                np_ = 2 * len(g)
                j0 = 2 * g[0]
                self.load_w(wdn[gi % 2][:, 0:np_, :], wd_src[:, j0:j0 + np_, :], ("wdn", gi % 2), "wd%d" % (gi % 2))

            pend3 = []
            load_up(blocks[0])
            load_up(blocks[1])
            load_dn(0)
            nxt = 2
            it = 0
            for gi, g in enumerate(groups):
                for bl, b in enumerate(g):
                    s = up_slot[b]
                    for pi in range(2):
                        jp = 2 * b + pi
                        jl = 2 * bl + pi
                        rows = {}
                        for kind, fc in (("g", jp), ("v", NFP + jp)):
                            rows[kind] = [R_FWDW + (i * 3 + tap) * 44 + fc for tap in range(3)] + [R_FBDW + i * 44 + fc]
                        for T in range(NT):
                            par = it % 2
                            it += 1
                            tsl = slice(T * TT, (T + 1) * TT)
                            s3 = (it - 1) % 3
                            bg_, bv_ = s3 * 2, s3 * 2 + 1
                            for c in range(NCH):
                                self.mm(ps[bg_][:, :], wup[s][:, c, pi * 128:(pi + 1) * 128], hn[:, c, tsl], c == 0, c == NCH - 1,
                                        [("wup", s), ("hn", c, T)], [("ps", bg_)])
                            for c in range(NCH):
                                self.mm(ps[bv_][:, :], wup[s][:, c, 256 + pi * 128:256 + (pi + 1) * 128], hn[:, c, tsl], c == 0, c == NCH - 1,
                                        [("wup", s), ("hn", c, T)], [("ps", bv_)])
                            A = {"g": Ag[par], "v": Av[par]}
                            U = {"g": ps[bg_], "v": ps[bv_]}
                            UB = {"g": bg_, "v": bv_}
                            AK = {"g": ("Ag", par), "v": ("Av", par)}
                            KI = {"g": 0, "v": 1}
                            hp_prev = (T - 1) % 2
                            hp_cur = T % 2
                            for kind in ("g", "v"):
                                r = rows[kind]
                                S.act(lambda h_, A_=A[kind], U_=U[kind], r=r: h_.activation(out=A_[:], in_=U_[:, :], func=AF.Identity,
                                                                                          scale=self.prow(r[2]), bias=self.prow(r[3])),
                                      reads=[("ps", UB[kind]), "ptab"], writes=[AK[kind]])
                                if T < NT - 1:
                                    S.act(lambda h_, U_=U[kind], kind=kind, hp_cur=hp_cur: h_.activation(
                                        out=halo[:, hp_cur, KI[kind], :], in_=U_[:, TT - 2:TT], func=AF.Copy),
                                        reads=[("ps", UB[kind])], writes=[("halo", hp_cur, kind)])
                            for kind in ("g", "v"):
                                r = rows[kind]
                                S.dve(lambda h_, A_=A[kind], U_=U[kind], r=r: h_.scalar_tensor_tensor(
                                    out=A_[:, 1:TT], in0=U_[:, 0:TT - 1], scalar=self.prow(r[1]), in1=A_[:, 1:TT], op0=ALU.mult, op1=ALU.add),
                                    reads=[("ps", UB[kind]), AK[kind], "ptab"], writes=[AK[kind]])
                            for kind in ("g", "v"):
                                r = rows[kind]
                                S.dve(lambda h_, A_=A[kind], U_=U[kind], r=r: h_.scalar_tensor_tensor(
                                    out=A_[:, 2:TT], in0=U_[:, 0:TT - 2], scalar=self.prow(r[0]), in1=A_[:, 2:TT], op0=ALU.mult, op1=ALU.add),
                                    reads=[("ps", UB[kind]), AK[kind], "ptab"], writes=[AK[kind]])
                            if T > 0:
                                for kind in ("g", "v"):
                                    r = rows[kind]
                                    hl_ = halo[:, hp_prev, KI[kind], :]
                                    S.dve(lambda h_, A_=A[kind], hl_=hl_, r=r: h_.scalar_tensor_tensor(
                                        out=A_[:, 0:1], in0=hl_[:, 1:2], scalar=self.prow(r[1]), in1=A_[:, 0:1], op0=ALU.mult, op1=ALU.add),
                                        reads=[("halo", hp_prev, kind), AK[kind], "ptab"], writes=[AK[kind]])
                                for kind in ("g", "v"):
                                    r = rows[kind]
                                    hl_ = halo[:, hp_prev, KI[kind], :]
                                    S.dve(lambda h_, A_=A[kind], hl_=hl_, r=r: h_.scalar_tensor_tensor(
                                        out=A_[:, 0:2], in0=hl_[:, 0:2], scalar=self.prow(r[0]), in1=A_[:, 0:2], op0=ALU.mult, op1=ALU.add),
                                        reads=[("halo", hp_prev, kind), AK[kind], "ptab"], writes=[AK[kind]])
                            def stage3(par=par, Ag_=A["g"], Av_=A["v"], jl=jl, tsl=tsl, T=T):
                                sg_ = sg[par]
                                S.act(lambda h_: h_.activation(out=sg_[:], in_=Ag_[:], func=AF.Silu),
                                      reads=[("Ag", par)], writes=[("sg", par)])
                                S.pool(lambda h_: h_.tensor_tensor(out=act[:, jl, tsl], in0=sg_[:], in1=Av_[:], op=ALU.mult),
                                       reads=[("sg", par), ("Av", par)], writes=[("act", jl, T)])
                            if pend3:
                                pend3.pop(0)()
                            pend3.append(stage3)
                            if pi == 0 and T == 2:
                                if nxt < len(blocks):
                                    load_up(blocks[nxt])
                                    nxt += 1
                                if bl == 0 and gi + 1 < len(groups):
                                    load_dn(gi + 1)
                while pend3:
                    pend3.pop(0)()
                np_ = 2 * len(g)
                wd = wdn[gi % 2]
                dn = 0
                for T in range(NT):
                    tsl = slice(T * TT, (T + 1) * TT)
                    for dc in range(NCH):
                        s_next = it % 3
                        bi = (6, 7, 2 * s_next, 2 * s_next + 1)[dn % 4]
                        dn += 1
                        for jl in range(np_):
                            self.mm(ps[bi][:, :], wd[:, jl, dc * 128:(dc + 1) * 128], act[:, jl, tsl], jl == 0, jl == np_ - 1,
                                    [("wdn", gi % 2), ("act", jl, T)], [("ps", bi)])
                        S.dve(lambda h_, bi=bi, dc=dc, tsl=tsl: h_.tensor_tensor(out=h[:, dc, tsl], in0=ps[bi][:, :], in1=h[:, dc, tsl], op=ALU.add),
                              reads=[("ps", bi), ("h", dc, T)], writes=[("h", dc, T)])
            S.barrier()
            S.flush()

    def phase_final(self, raw=False):
        nc, S, ps = self.nc, self.S, self.ps
        with ExitStack() as es:
            ofm = self.sb(es, [128, NCH, TT], F32, "ofm")
            ost = [self.sb(es, [128, D], F32, "ost") for _ in range(3)]
            oc = 0
            for T in range(NT):
                tsl = slice(T * TT, (T + 1) * TT)
                if raw:
                    for c in range(NCH):
                        eng = S.act if c % 2 == 0 else S.dve
                        if c % 2 == 0:
                            S.act(lambda h_, c=c, tsl=tsl: h_.activation(out=ofm[:, c, :], in_=self.h[:, c, tsl], func=AF.Copy),
                                  reads=[("h", c, T)], writes=[("ofm", c)])
                        else:
                            S.dve(lambda h_, c=c, tsl=tsl: h_.tensor_copy(out=ofm[:, c, :], in_=self.h[:, c, tsl]),
                                  reads=[("h", c, T)], writes=[("ofm", c)])
                else:
                    self.rmsnorm_tile(T, R_FIN, ofm)
                for ts in range(4):
                    tt = T * 4 + ts
                    sl = oc % 3
                    oc += 1
                    for half in range(2):
                        bi = (2 * tt + half) % 4
                        bank = ps[bi]
                        for jj in range(4):
                            c = half * 4 + jj
                            S.pe(lambda h_, bank=bank, jj=jj, c=c, ts=ts: h_.transpose(out=bank[:, jj * 128:(jj + 1) * 128],
                                                                                         in_=ofm[:, c, ts * 128:(ts + 1) * 128], identity=self.identf[:]),
                                 reads=[("ofm", c), "identf"], writes=[("ps", bi)])
                        dst = ost[sl][:, half * 512:(half + 1) * 512]
                        if half == 0:
                            S.act(lambda h_, dst=dst, bank=bank: h_.activation(out=dst, in_=bank[:, :], func=AF.Copy),
                                  reads=[("ps", bi)], writes=[("ost", sl, half)])
                        else:
                            S.dve(lambda h_, dst=dst, bank=bank: h_.tensor_copy(out=dst, in_=bank[:, :]),
                                  reads=[("ps", bi)], writes=[("ost", sl, half)])
                    S.dma("sp", "ost%d" % sl, lambda h_, sl=sl, tt=tt: h_.dma_start(out=self.out[tt * 128:(tt + 1) * 128, :], in_=ost[sl][:]),
                          reads=[("ost", sl, 0), ("ost", sl, 1)])
            S.barrier()
            S.flush()

    def rmsnorm_tile(self, T, grow, ofm):
        S, ps = self.S, self.ps
        bi = 6 + T % 2
        bank = ps[bi]
        tsl = slice(T * TT, (T + 1) * TT)
        for c in range(NCH):
            sq = self.nsq[:, c % 2, :]
            S.act(lambda h, sq=sq, c=c: h.activation(out=sq, in_=self.h[:, c, tsl], func=AF.Square),
                  reads=[("h", c, T)], writes=[("nsq", c % 2)])
            self.mm(bank[:, :], self.onesb[:], sq, c == 0, c == NCH - 1, [("nsq", c % 2), "onesb"], [("ps", bi)])
        sd = self.nstd[:, T % 2, :]
        S.act(lambda h: h.activation(out=sd, in_=bank[:, :], func=AF.Sqrt, scale=1.0 / D, bias=NORM_EPS),
              reads=[("ps", bi)], writes=[("nstd", T % 2)])
        S.dve(lambda h: h.reciprocal(bank[:, :], sd), reads=[("nstd", T % 2)], writes=[("ps", bi)])
        for c in range(NCH):
            S.dve(lambda h, c=c: h.scalar_tensor_tensor(out=ofm[:, c, :], in0=self.h[:, c, tsl], scalar=self.prow(grow + c), in1=bank[:, :],
                                                        op0=ALU.mult, op1=ALU.mult),
                  reads=[("h", c, T), ("ps", bi), "ptab"], writes=[("ofm", c)])


def _pack_ptab(inp):
    f = lambda a: np.ascontiguousarray(np.asarray(a, dtype=np.float32)).reshape(-1, 128)
    parts = [
        f(inp["norm_mix_g"]), f(inp["norm_ffn_g"]), f(inp["final_norm_g"]), f(inp["conv_b_pw1"]),
        f(inp["conv_w_dw"]), f(inp["conv_b_dw"]), f(inp["conv_ln_g"]), f(inp["conv_ln_b"]), f(inp["conv_b_pw2"]),
        f(inp["ffn_w_dw"]), f(inp["ffn_b_dw"]),
    ]
    tab = np.concatenate(parts, axis=0)
    assert tab.shape[0] == 1368
    pad = np.zeros((R_TOT - tab.shape[0], 128), np.float32)
    return np.ascontiguousarray(np.concatenate([tab, pad], axis=0))


_NC_CACHE = {}


def _run(inputs, stop_after=None, trace=False):
    x = np.ascontiguousarray(np.asarray(inputs["x"], dtype=np.float32))
    B = x.shape[0]
    key = stop_after
    if key not in _NC_CACHE:
        _NC_CACHE[key] = Builder(stop_after).build()
    nc = _NC_CACHE[key]
    ptab = _pack_ptab(inputs)
    c = lambda k: np.ascontiguousarray(np.asarray(inputs[k], dtype=np.float32))
    shared = {
        "ptab_in": ptab, "w_qkv": c("attn_w_qkv"), "w_o": c("attn_w_o"), "w_pw1": c("conv_w_pw1"), "w_pw2": c("conv_w_pw2"),
        "w_up": c("ffn_w_up"), "w_down": c("ffn_w_down"),
    }
    in_maps = [dict(shared, x=x[b]) for b in range(B)]
    res = run_bass_kernel_spmd(nc, in_maps, core_ids=list(range(B)), trace=trace)
    out = np.stack([np.asarray(r["out"]) for r in res.results], axis=0).astype(np.float32)
    return out, res


def kernel(**inputs):
    out, _ = _run(inputs)
    return out
```

```python
import math
import numpy as np
from contextlib import ExitStack
import concourse.bass as bass
import concourse.mybir as mybir
from concourse.bass_utils import run_bass_kernel_spmd

F32 = mybir.dt.float32
BF16 = mybir.dt.bfloat16
I32 = mybir.dt.int32
ALU = mybir.AluOpType
AF = mybir.ActivationFunctionType
AX = mybir.AxisListType

D = 1024
SEQ = 2048
NCH = 8
NT = 4
TT = 512
H = 16
DH = 64
DFF = 2816
NFP = 22
DEPTH = 4
NEG = -30000.0
NORM_EPS = 1e-6
LN_EPS = 1e-5
CW = 31

R_MIX = 0
R_FFN = 32
R_FIN = 64
R_BPW1 = 72
R_WDW = 104
R_BDW = 600
R_LNG = 616
R_LNB = 632
R_BPW2 = 648
R_FWDW = 664
R_FBDW = 1192
R_TOT = 1408


class _Op:
    __slots__ = ("eng", "fn", "deps", "needs_inc", "semval", "grp", "gen", "pre")

    def __init__(self, eng, fn):
        self.eng = eng
        self.fn = fn
        self.deps = []
        self.needs_inc = False
        self.semval = None
        self.grp = None
        self.gen = 0
        self.pre = None


class _Grp:
    __slots__ = ("sem", "gens", "closed")

    def __init__(self, sem):
        self.sem = sem
        self.gens = [0]
        self.closed = False


class Sched:
    ENGS = ("pe", "act", "dve", "pool", "sp")

    def __init__(self, nc, es):
        self.nc = nc
        self.es = es
        self.q = {e: [] for e in self.ENGS}
        self.lastw = {}
        self.readers = {}
        self.esem = {e: es.enter_context(nc.semaphore("sem_" + e)) for e in ("pe", "act", "dve", "pool")}
        self.ecnt = {e: 0 for e in ("pe", "act", "dve", "pool")}
        self.seen = {e: {} for e in self.ENGS}
        self.groups = {}
        self.lastreal = {e: None for e in self.ENGS}
        self.nops = 0

    def _group(self, name):
        g = self.groups.get(name)
        if g is None:
            g = _Grp(self.es.enter_context(self.nc.semaphore("dg_" + name)))
            self.groups[name] = g
        return g

    def add(self, eng, fn, reads=(), writes=(), dma=None):
        op = _Op(eng, fn)
        deps = {}
        for k in reads:
            w = self.lastw.get(k)
            if w is not None:
                deps[id(w)] = w
            if isinstance(k, tuple) and k[0] == "ps":
                rd = self.readers.get(k)
                if rd:
                    for rk_, r in rd.items():
                        if rk_ != eng:
                            deps[id(r)] = r
        for k in writes:
            w = self.lastw.get(k)
            if w is not None:
                deps[id(w)] = w
            rd = self.readers.get(k)
            if rd:
                for r in rd.values():
                    deps[id(r)] = r
        if dma is not None:
            g = self._group(dma)
            op.grp = g
            if g.closed:
                op.pre = [(g, len(g.gens) - 1)]
                g.gens.append(g.gens[-1])
                g.closed = False
            g.gens[-1] += 16
            op.gen = len(g.gens) - 1
        for d in deps.values():
            if eng == "pe" and d.eng == "pe" and d.grp is None:
                continue
            if op.grp is not None and d.grp is op.grp:
                continue
            op.deps.append(d)
            d.needs_inc = True
            if d.grp is not None:
                d.grp.closed = True
        rk = eng if dma is None else ("dma", id(op))
        for k in reads:
            self.readers.setdefault(k, {})[rk] = op
        for k in writes:
            self.lastw[k] = op
            self.readers[k] = {}
        self.q[eng].append(op)
        if dma is None:
            self.lastreal[eng] = op
        self.nops += 1
        return op

    def pe(self, fn, reads=(), writes=()):
        return self.add("pe", fn, reads, writes)

    def act(self, fn, reads=(), writes=()):
        return self.add("act", fn, reads, writes)

    def dve(self, fn, reads=(), writes=()):
        return self.add("dve", fn, reads, writes)

    def pool(self, fn, reads=(), writes=()):
        return self.add("pool", fn, reads, writes)

    def dma(self, queue, group, fn, reads=(), writes=()):
        return self.add(queue, fn, reads, writes, dma=group)

    def barrier(self):
        lasts = [op for op in self.lastreal.values() if op is not None and op.grp is None]
        gl = [(g, len(g.gens) - 1) for g in self.groups.values() if g.gens[-1] > 0]
        for g, _ in gl:
            g.closed = True
        for e in self.ENGS:
            op = _Op(e, None)
            for d in lasts:
                if not (e == "pe" and d.eng == "pe"):
                    op.deps.append(d)
                    d.needs_inc = True
            op.pre = list(gl)
            self.q[e].append(op)
        self.lastw = {}
        self.readers = {}

    def _assign(self):
        for e in ("pe", "act", "dve", "pool"):
            c = self.ecnt[e]
            for op in self.q[e]:
                if op.grp is None and op.fn is not None and op.needs_inc and op.semval is None:
                    c += 1
                    op.semval = c
            self.ecnt[e] = c

    def _emit_engine(self, e, h):
        seen = self.seen[e]

        def wait(sem, val):
            key = id(sem)
            if seen.get(key, 0) < val:
                h.wait_ge(sem, val)
                seen[key] = val

        for op in self.q[e]:
            if op.pre is not None:
                for g, gi in op.pre:
                    wait(g.sem, g.gens[gi])
            for d in op.deps:
                if d.grp is not None:
                    wait(d.grp.sem, d.grp.gens[d.gen])
                else:
                    wait(self.esem[d.eng], d.semval)
            if op.fn is not None:
                ins = op.fn(h)
                if op.grp is not None:
                    ins.then_inc(op.grp.sem, 16)
                elif op.needs_inc:
                    ins.then_inc(self.esem[e], 1)
        self.q[e] = []
        self.lastreal[e] = None

    def flush(self):
        self._assign()
        with self.nc.Block() as block:
            @block.tensor
            def _(h):
                self._emit_engine("pe", h)

            @block.scalar
            def _(h):
                self._emit_engine("act", h)

            @block.vector
            def _(h):
                self._emit_engine("dve", h)

            @block.gpsimd
            def _(h):
                self._emit_engine("pool", h)

            @block.sync
            def _(h):
                self._emit_engine("sp", h)


class Builder:
    def __init__(self, stop_after=None):
        self.stop_after = stop_after
        self.nc = bass.Bass("TRN2", target_bir_lowering=False)
        nc = self.nc
        dt = nc.dram_tensor
        self.x = dt("x", [SEQ, D], F32, kind="ExternalInput").ap()
        self.ptab_in = dt("ptab_in", [R_TOT, 128], F32, kind="ExternalInput").ap()
        self.w_qkv = dt("w_qkv", [2, D, 3 * D], F32, kind="ExternalInput").ap()
        self.w_o = dt("w_o", [2, D, D], F32, kind="ExternalInput").ap()
        self.w_pw1 = dt("w_pw1", [2, D, 2 * D], F32, kind="ExternalInput").ap()
        self.w_pw2 = dt("w_pw2", [2, D, D], F32, kind="ExternalInput").ap()
        self.w_up = dt("w_up", [DEPTH, D, 2 * DFF], F32, kind="ExternalInput").ap()
        self.w_down = dt("w_down", [DEPTH, DFF, D], F32, kind="ExternalInput").ap()
        self.out = dt("out", [SEQ, D], F32, kind="ExternalOutput").ap()
        self._uid = 0

    def sb(self, es, shape, dtype, name=None):
        self._uid += 1
        return es.enter_context(self.nc.sbuf_tensor("%s_%d" % (name or "t", self._uid), shape, dtype))

    def mm(self, out, lhsT, rhs, start, stop, reads, writes):
        self.S.pe(lambda h: h.matmul(out, lhsT=lhsT, rhs=rhs, start=start, stop=stop), reads, writes)

    def prow(self, r):
        return self.ptab[:, r:r + 1]

    def build(self):
        nc = self.nc
        with ExitStack() as es:
            self.S = S = Sched(nc, es)
            self.ps = [es.enter_context(nc.psum_tensor("ps%d" % i, [128, TT], F32)) for i in range(8)]
            self.h = self.sb(es, [128, NCH, SEQ], F32, "h")
            self.hn = self.sb(es, [128, NCH, SEQ], BF16, "hn")
            self.cosT = self.sb(es, [128, SEQ], BF16, "cosT")
            self.sinT = self.sb(es, [128, SEQ], BF16, "sinT")
            self.ptab = self.sb(es, [128, R_TOT], F32, "ptab")
            self.identf = self.sb(es, [128, 128], F32, "identf")
            self.identb = self.sb(es, [128, 128], BF16, "identb")
            self.onesb = self.sb(es, [128, 128], BF16, "onesb")
            self.maskT = self.sb(es, [128, 128], BF16, "maskT")
            self.prot = self.sb(es, [128, 128], BF16, "prot")
            self.ind = [self.sb(es, [128, 8 * 128], BF16, "ind") for _ in range(2)]
            self.nsq = self.sb(es, [128, 2, TT], BF16, "nsq")
            self.nstd = self.sb(es, [128, 2, TT], F32, "nstd")

            self.phase_setup()
            stop = self.stop_after
            done = stop == "load"
            for i in range(DEPTH):
                if done:
                    break
                j = i // 2
                self.rmsnorm(R_MIX + i * 8)
                if i % 2 == 0:
                    self.phase_attention(j)
                else:
                    self.phase_conformer(j)
                if stop is not None and stop.split(":")[0] == "mix%d" % i:
                    done = True
                    break
                self.rmsnorm(R_FFN + i * 8)
                fsub = stop.split(":")[1] if (stop and ":" in stop and stop.startswith("ffn")) else None
                if fsub != "norm":
                    self.phase_ffn(i)
                if stop is not None and stop.split(":")[0] == "ffn%d" % i:
                    done = True
                    break
            self.phase_final(raw=(stop is not None))
        return nc

    def phase_setup(self):
        nc, S = self.nc, self.S
        ps = self.ps
        with ExitStack() as es:
            xs = [self.sb(es, [128, D], F32, "xs") for _ in range(8)]
            pst = [self.sb(es, [128, 128], F32, "pst") for _ in range(2)]
            pi_i = self.sb(es, [128, 1], I32, "pi_i")
            pm_i = self.sb(es, [128, 2], I32, "pm_i")
            sgn = self.sb(es, [128, 2], F32, "sgn")
            invrow = self.sb(es, [1, 128], F32, "invrow")
            invf = self.sb(es, [128, 1], F32, "invf")
            pos_i = self.sb(es, [128, SEQ], I32, "pos_i")
            ang = self.sb(es, [128, SEQ], F32, "ang")
            ta = self.sb(es, [128, SEQ], F32, "ta")
            tb = self.sb(es, [128, SEQ], F32, "tb")
            tc = self.sb(es, [128, SEQ], F32, "tc")

            identf, identb, onesb, maskT, prot, ind = self.identf, self.identb, self.onesb, self.maskT, self.prot, self.ind
            S.pool(lambda h: h.memset(identf[:], 1.0), writes=["identf"])
            S.pool(lambda h: h.affine_select(out=identf[:], in_=identf[:], pattern=[[-1, 128]], compare_op=ALU.is_equal,
                                             fill=0.0, base=0, channel_multiplier=1), reads=["identf"], writes=["identf"])
            S.dve(lambda h: h.tensor_copy(out=identb[:], in_=identf[:]), reads=["identf"], writes=["identb"])
            S.pool(lambda h: h.memset(onesb[:], 1.0), writes=["onesb"])
            S.pool(lambda h: h.memset(maskT[:], 0.0), writes=["maskT"])
            S.pool(lambda h: h.affine_select(out=maskT[:], in_=maskT[:], pattern=[[1, 128]], compare_op=ALU.is_ge,
                                             fill=NEG, base=0, channel_multiplier=-1), reads=["maskT"], writes=["maskT"])
            for (dst, src) in ((0, 32), (32, 0), (64, 96), (96, 64)):
                S.dve(lambda h, dst=dst, src=src: h.tensor_copy(out=prot[:, dst:dst + 32], in_=identb[:, src:src + 32]),
                      reads=["identb"], writes=["prot"])
            for jj in range(2):
                S.pool(lambda h, jj=jj: h.memset(ind[jj][:], 0.0), writes=["ind"])
                S.pool(lambda h, jj=jj: h.memset(ind[jj][64 * jj:64 * jj + 64, :], 1.0), reads=["ind"], writes=["ind"])
                S.pool(lambda h, jj=jj: h.affine_select(out=ind[jj][64 * jj:64 * jj + 64, :], in_=ind[jj][64 * jj:64 * jj + 64, :],
                                                        pattern=[[1, 8], [0, 128]], compare_op=ALU.is_equal, fill=0.0,
                                                        base=0, channel_multiplier=-1), reads=["ind"], writes=["ind"])
            for r in range(R_TOT // 128):
                st = pst[r % 2]
                S.dma("sp", "pst%d" % (r % 2), lambda h, r=r, st=st: h.dma_start(out=st[:], in_=self.ptab_in[r * 128:(r + 1) * 128, :]),
                      writes=[("pst", r % 2)])
                bank = ps[r % 2]
                S.pe(lambda h, st=st, bank=bank: h.transpose(out=bank[:, 0:128], in_=st[:], identity=identf[:]),
                     reads=[("pst", r % 2), "identf"], writes=[("ps", r % 2)])
                S.act(lambda h, r=r, bank=bank: h.activation(out=self.ptab[:, r * 128:(r + 1) * 128], in_=bank[:, 0:128], func=AF.Copy),
                      reads=[("ps", r % 2)], writes=["ptab"])
            inv = (np.float32(1.0) / np.power(np.float32(10000.0), np.arange(0, DH, 2, dtype=np.float32) / np.float32(DH))).astype(np.float32)
            irv = invrow[:].rearrange("o (a b) -> o a b", b=32)
            for i in range(32):
                S.dve(lambda h, i=i: h.memset(irv[:, :, i:i + 1], float(inv[i])), writes=["invrow"])
            S.pe(lambda h: h.transpose(out=ps[2][:, 0:1], in_=invrow[:], identity=identf[0:1, 0:1]),
                 reads=["invrow", "identf"], writes=[("ps", 2)])
            S.act(lambda h: h.activation(out=invf[:], in_=ps[2][:, 0:1], func=AF.Copy), reads=[("ps", 2)], writes=["invf"])
            S.pool(lambda h: h.iota(pi_i[:], pattern=[[0, 1]], base=0, channel_multiplier=1), writes=["pi_i"])
            S.dve(lambda h: h.tensor_scalar(out=pm_i[:, 0:1], in0=pi_i[:], scalar1=32, scalar2=None, op0=ALU.bitwise_and),
                  reads=["pi_i"], writes=["pm_i"])
            S.dve(lambda h: h.tensor_copy(out=sgn[:, 0:1], in_=pm_i[:, 0:1]), reads=["pm_i"], writes=["sgn0"])
            S.dve(lambda h: h.tensor_scalar(out=sgn[:, 1:2], in0=sgn[:, 0:1], scalar1=1.0 / 16.0, scalar2=-1.0, op0=ALU.mult, op1=ALU.add),
                  reads=["sgn0"], writes=["sgn1"])
            S.pool(lambda h: h.iota(pos_i[:], pattern=[[1, SEQ]], base=0, channel_multiplier=0), writes=["pos_i"])
            S.dve(lambda h: h.tensor_copy(out=ta[:], in_=pos_i[:]), reads=["pos_i"], writes=["ta"])
            S.dve(lambda h: h.tensor_scalar(out=ang[:], in0=ta[:], scalar1=invf[:, 0:1], scalar2=None, op0=ALU.mult),
                  reads=["ta", "invf"], writes=["ang"])
            TWO_PI = 2.0 * math.pi
            C1 = 6.28125
            C2 = TWO_PI - C1
            MAGIC = 12582912.0
            LIM = 3.1415925
            S.dve(lambda h: h.tensor_scalar(out=ta[:], in0=ang[:], scalar1=1.0 / TWO_PI, scalar2=None, op0=ALU.mult),
                  reads=["ang", "ta"], writes=["ta"])
            S.dve(lambda h: h.tensor_scalar(out=tb[:], in0=ta[:], scalar1=MAGIC, scalar2=MAGIC, op0=ALU.add, op1=ALU.subtract),
                  reads=["ta"], writes=["tb"])
            S.dve(lambda h: h.scalar_tensor_tensor(out=ta[:], in0=tb[:], scalar=-C1, in1=ang[:], op0=ALU.mult, op1=ALU.add),
                  reads=["tb", "ang", "ta"], writes=["ta"])
            S.dve(lambda h: h.scalar_tensor_tensor(out=tc[:], in0=tb[:], scalar=-C2, in1=ta[:], op0=ALU.mult, op1=ALU.add),
                  reads=["tb", "ta"], writes=["tc"])
            S.dve(lambda h: h.tensor_scalar(out=ta[:], in0=tc[:], scalar1=LIM, scalar2=-LIM, op0=ALU.min, op1=ALU.max),
                  reads=["tc", "ta"], writes=["ta"])
            S.act(lambda h: h.activation(out=tb[:], in_=ta[:], func=AF.Sin), reads=["ta", "tb"], writes=["tb"])
            S.dve(lambda h: h.tensor_scalar(out=self.sinT[:], in0=tb[:], scalar1=sgn[:, 1:2], scalar2=None, op0=ALU.mult),
                  reads=["tb", "sgn1"], writes=["sinT"])
            S.dve(lambda h: h.tensor_scalar(out=ang[:], in0=tc[:], scalar1=math.pi / 2, scalar2=None, op0=ALU.add),
                  reads=["tc", "ang"], writes=["ang"])
            S.dve(lambda h: h.tensor_scalar(out=ta[:], in0=ang[:], scalar1=math.pi, scalar2=None, op0=ALU.is_gt),
                  reads=["ang", "ta"], writes=["ta"])
            S.dve(lambda h: h.scalar_tensor_tensor(out=tc[:], in0=ta[:], scalar=-TWO_PI, in1=ang[:], op0=ALU.mult, op1=ALU.add),
                  reads=["ta", "ang", "tc"], writes=["tc"])
            S.dve(lambda h: h.tensor_scalar(out=ta[:], in0=tc[:], scalar1=LIM, scalar2=-LIM, op0=ALU.min, op1=ALU.max),
                  reads=["tc", "ta"], writes=["ta"])
            S.act(lambda h: h.activation(out=self.cosT[:], in_=ta[:], func=AF.Sin), reads=["ta"], writes=["cosT"])
            for tt in range(16):
                sl = tt % 8
                S.dma("sp", "xs%d" % sl, lambda h, tt=tt, sl=sl: h.dma_start(out=xs[sl][:], in_=self.x[tt * 128:(tt + 1) * 128, :]),
                      writes=[("xs", sl)])
                T = tt // 4
                for half in range(2):
                    bi = 4 + (2 * tt + half) % 4
                    bank = ps[bi]
                    for jj in range(4):
                        c = half * 4 + jj
                        S.pe(lambda h, bank=bank, jj=jj, c=c, sl=sl: h.transpose(out=bank[:, jj * 128:(jj + 1) * 128],
                                                                               in_=xs[sl][:, c * 128:(c + 1) * 128], identity=identf[:]),
                             reads=[("xs", sl), "identf"], writes=[("ps", bi)])
                    dst = self.h[:, half * 4:half * 4 + 4, tt * 128:(tt + 1) * 128]
                    src = bank[:, :].rearrange("p (a b) -> p a b", b=128)
                    wr = [("h", half * 4 + jj, T) for jj in range(4)]
                    if (2 * tt + half) % 2 == 0:
                        S.act(lambda h, dst=dst, src=src: h.activation(out=dst, in_=src, func=AF.Copy), reads=[("ps", bi)], writes=wr)
                    else:
                        S.dve(lambda h, dst=dst, src=src: h.tensor_copy(out=dst, in_=src), reads=[("ps", bi)], writes=wr)
            S.barrier()
            S.flush()

    def rmsnorm(self, grow, out_fn=None):
        S, ps = self.S, self.ps
        for T in range(NT):
            bi = 6 + T % 2
            bank = ps[bi]
            tsl = slice(T * TT, (T + 1) * TT)
            for c in range(NCH):
                sq = self.nsq[:, c % 2, :]
                S.act(lambda h, sq=sq, c=c, tsl=tsl: h.activation(out=sq, in_=self.h[:, c, tsl], func=AF.Square),
                      reads=[("h", c, T)], writes=[("nsq", c % 2)])
                self.mm(bank[:, :], self.onesb[:], sq, c == 0, c == NCH - 1, [("nsq", c % 2), "onesb"], [("ps", bi)])
            sd = self.nstd[:, T % 2, :]
            S.act(lambda h, sd=sd, bank=bank: h.activation(out=sd, in_=bank[:, :], func=AF.Sqrt, scale=1.0 / D, bias=NORM_EPS),
                  reads=[("ps", bi)], writes=[("nstd", T % 2)])
            S.dve(lambda h, sd=sd, bank=bank: h.reciprocal(bank[:, :], sd), reads=[("nstd", T % 2)], writes=[("ps", bi)])
            for c in range(NCH):
                if out_fn is None:
                    dst = self.hn[:, c, tsl]
                    wr = [("hn", c, T)]
                else:
                    dst, wr = out_fn(c, T)
                S.dve(lambda h, dst=dst, c=c, tsl=tsl, bank=bank: h.scalar_tensor_tensor(
                    out=dst, in0=self.h[:, c, tsl], scalar=self.prow(grow + c), in1=bank[:, :], op0=ALU.mult, op1=ALU.mult),
                    reads=[("h", c, T), ("ps", bi), "ptab"], writes=wr)

    def load_w(self, slot_ap, src_ap, key, group):
        self.S.dma("pool", group, lambda h: h.dma_start(out=slot_ap, in_=src_ap), writes=[key])

    def phase_attention(self, j):
        nc, S, ps = self.nc, self.S, self.ps
        hn, h = self.hn, self.h
        with ExitStack() as es:
            qT = self.sb(es, [128, 2, SEQ], BF16, "qT")
            kT = self.sb(es, [128, 2, 2, SEQ], BF16, "kT")
            Vt = self.sb(es, [128, 16, 4, 128], BF16, "Vt")
            NW = 5
            wsl = [self.sb(es, [128, 2048], BF16, "wsl") for _ in range(NW)]
            biasT = self.sb(es, [128, 2, 1024], BF16, "biasT")
            kms = self.sb(es, [128, 4, 8], F32, "kms")
            kmT = self.sb(es, [128, 4, 8], BF16, "kmT")
            gsb = self.sb(es, [128, 256], F32, "gsb")
            cmp_ = self.sb(es, [128, 2, 4 * 49], BF16, "cmp")
            rank = self.sb(es, [128, 2, 32], F32, "rank")
            btok = self.sb(es, [128, 4, 256], BF16, "btok")
            pT = [self.sb(es, [128, TT], BF16, "pT") for _ in range(4)]
            rec = [self.sb(es, [128, TT], F32, "rec") for _ in range(2)]
            qs = [self.sb(es, [128, TT], BF16, "qs") for _ in range(2)]
            t1 = [self.sb(es, [128, TT], F32, "t1") for _ in range(2)]
            t2 = [self.sb(es, [128, TT], F32, "t2") for _ in range(2)]

            wcount = [0]

            def next_slot():
                i = wcount[0] % NW
                wcount[0] += 1
                return i

            S.pool(lambda h_: h_.memset(Vt[:, :, 0:4:2, 64:128], 1.0), writes=[("Vt1", 0)])
            S.pool(lambda h_: h_.memset(Vt[:, :, 1:4:2, 0:64], 1.0), writes=[("Vt1", 1)])
            S.pool(lambda h_: h_.memset(biasT[:], 0.0), writes=[("biasT", a, u) for a in range(2) for u in range(2)])
            S.pool(lambda h_: h_.memset(kms[:], 0.0), writes=[("kmsz",)])
            S.pool(lambda h_: h_.memset(kT[64:128, :, 0, :], 0.0), writes=[("kz", 0)])
            S.pool(lambda h_: h_.memset(kT[0:64, :, 1, :], 0.0), writes=[("kz", 1)])

            wq_src = self.w_qkv[j].rearrange("(c p) f -> p c f", p=128)
            wo_src = self.w_o[j].rearrange("(c p) f -> p c f", p=128)

            def load_group(g):
                sl = {}
                for nm, off in (("q", 0), ("k", D), ("v", 2 * D)):
                    i = next_slot()
                    sl[nm] = i
                    self.load_w(wsl[i][:, :].rearrange("p (c f) -> p c f", f=256), wq_src[:, :, off + g * 256: off + (g + 1) * 256],
                                ("wsl", i), "aw%d" % i)
                i = next_slot()
                sl["o"] = i
                self.load_w(wsl[i][:, :].rearrange("p (c f) -> p c f", f=1024), wo_src[:, 2 * g:2 * g + 2, :], ("wsl", i), "aw%d" % i)
                return sl

            pending = load_group(0)
            rope_i = [0]
            sub = self.stop_after.split(":")[1] if (self.stop_after and ":" in self.stop_after) else None
            for g in range(4):
                if sub is not None and g > 0:
                    break
                sl = pending
                wq = wsl[sl["q"]][:, :].rearrange("p (c f) -> p c f", f=256)
                wk = wsl[sl["k"]][:, :].rearrange("p (c f) -> p c f", f=256)
                wv = wsl[sl["v"]][:, :].rearrange("p (c f) -> p c f", f=256)
                wo = wsl[sl["o"]][:, :].rearrange("p (c f) -> p c f", f=1024)
                pend = []

                def rope_a(which, cc, T, wsrc, wkey):
                    scale = DH ** -0.5 if which == "q" else 1.0
                    ri = rope_i[0]
                    rope_i[0] += 1
                    ba = ri % 2
                    A = ps[ba]
                    tsl = slice(T * TT, (T + 1) * TT)
                    for c in range(NCH):
                        self.mm(A[:, :], wsrc[:, c, cc * 128:(cc + 1) * 128], hn[:, c, tsl], c == 0, c == NCH - 1,
                                [wkey, ("hn", c, T)], [("ps", ba)])
                    q_s = qs[ri % 2]
                    S.act(lambda h_, q_s=q_s, A=A, scale=scale: h_.activation(out=q_s[:], in_=A[:, :], func=AF.Copy, scale=scale),
                          reads=[("ps", ba)], writes=[("qs", ri % 2)])
                    return (which, cc, T, ri, scale)

                def rope_b(which, cc, T, ri, scale):
                    ba, bb = ri % 2, 2 + ri % 2
                    A, B = ps[ba], ps[bb]
                    tsl = slice(T * TT, (T + 1) * TT)
                    q_s, t1_, t2_ = qs[ri % 2], t1[ri % 2], t2[ri % 2]
                    self.mm(B[:, :], self.prot[:], q_s[:], True, True, [("qs", ri % 2), "prot"], [("ps", bb)])
                    S.dve(lambda h_, t1_=t1_, A=A, scale=scale, tsl=tsl: h_.scalar_tensor_tensor(
                        out=t1_[:], in0=A[:, :], scalar=scale, in1=self.cosT[:, tsl], op0=ALU.mult, op1=ALU.mult),
                        reads=[("ps", ba), "cosT"], writes=[("t1", ri % 2)])
                    S.dve(lambda h_, t2_=t2_, B=B, tsl=tsl: h_.tensor_tensor(out=t2_[:], in0=B[:, :], in1=self.sinT[:, tsl], op=ALU.mult),
                          reads=[("ps", bb), "sinT"], writes=[("t2", ri % 2)])
                    if which == "q":
                        dst = qT[:, cc, tsl]
                        wr = [("q", cc, T, 0), ("q", cc, T, 1)]
                        S.pool(lambda h_, dst=dst, t1_=t1_, t2_=t2_: h_.tensor_tensor(out=dst, in0=t1_[:], in1=t2_[:], op=ALU.add),
                               reads=[("t1", ri % 2), ("t2", ri % 2)], writes=wr)
                    else:
                        tk_ = qs[ri % 2]
                        S.pool(lambda h_, tk_=tk_, t1_=t1_, t2_=t2_: h_.tensor_tensor(out=tk_[:], in0=t1_[:], in1=t2_[:], op=ALU.add),
                               reads=[("t1", ri % 2), ("t2", ri % 2), ("qs", ri % 2)], writes=[("qs", ri % 2)])
                        for hp in range(2):
                            prt = slice(hp * 64, (hp + 1) * 64)
                            ch = 2 * cc + hp
                            for hf in range(2):
                                blk = 2 * T + hf
                                S.act(lambda h_, prt=prt, hp=hp, cc=cc, T=T, hf=hf, ch=ch, blk=blk, tk_=tk_: h_.activation(
                                    out=kT[prt, cc, hp, T * TT + hf * 256:T * TT + (hf + 1) * 256], in_=tk_[prt, hf * 256:(hf + 1) * 256],
                                    func=AF.Copy, accum_out=kms[prt, ch, blk:blk + 1]),
                                    reads=[("qs", ri % 2), ("kmsz",)], writes=[("k", cc, T, hp), ("kms", ch, blk)])

                def kmean(ch):
                    S.dve(lambda h_: h_.tensor_scalar(out=kmT[:, ch, :], in0=kms[:, ch, :], scalar1=1.0 / 256.0, scalar2=None, op0=ALU.mult),
                          reads=[("kms", ch, b_) for b_ in range(8)] + [("kmsz",)], writes=[("kmT", ch)])

                def rope_units(which, wsrc, wkey, between=None):
                    ulist = [(which, cc, T) for cc in range(2) for T in range(NT)]
                    for ui_, u_ in enumerate(ulist):
                        st_ = rope_a(u_[0], u_[1], u_[2], wsrc, wkey)
                        if pend:
                            rope_b(*pend.pop(0))
                        pend.append(st_)
                        if which == "k" and ui_ == NT and sub is None:
                            kmean(0)
                            kmean(1)
                        if between is not None:
                            between(ui_)
                    while pend:
                        rope_b(*pend.pop(0))

                if g == 0 or sub is not None:
                    rope_units("k", wk, ("wsl", sl["k"]))
                rope_units("q", wq, ("wsl", sl["q"]))
                if sub == "qk":
                    break
                if sub == "v":
                    break
                def gate_matmuls():
                    gbank = ps[6]
                    for qt in range(8, 16):
                        T = qt // 4
                        for hl in range(4):
                            cc, hp = hl // 2, hl % 2
                            col = (qt - 8) * 32 + hl * 8
                            self.mm(gbank[:, col:col + 8], qT[:, cc, qt * 128:(qt + 1) * 128],
                                    kmT[:, hl, :], True, True,
                                    [("q", cc, T, 0), ("q", cc, T, 1), ("kmT", hl)], [("ps", 6)])
                    S.act(lambda h_: h_.activation(out=gsb[:, :], in_=gbank[:, 0:256], func=AF.Copy),
                          reads=[("ps", 6)], writes=["gsb"])

                def gate_chain(qt):
                    qb = qt // 2
                    sI = qt % 4
                    c2 = qt % 2
                    g3 = gsb[:, (qt - 8) * 32:(qt - 7) * 32].rearrange("p (a n) -> p a n", n=8)[:, :, 0:qb]
                    in0 = g3.unsqueeze(2).broadcast_to([128, 4, qb, qb])
                    in1 = g3.unsqueeze(3).broadcast_to([128, 4, qb, qb])
                    cm = cmp_[:, c2, 0:4 * qb * qb].rearrange("p (a n m) -> p a n m", a=4, n=qb)
                    S.dve(lambda h_, cm=cm, in0=in0, in1=in1: h_.tensor_tensor(out=cm, in0=in0, in1=in1, op=ALU.is_gt),
                          reads=["gsb"], writes=[("cmp", c2)])
                    rk = rank[:, c2, :].rearrange("p (a n) -> p a n", n=8)[:, :, 0:qb]
                    S.dve(lambda h_, rk=rk, cm=cm: h_.tensor_reduce(out=rk, in_=cm, axis=AX.X, op=ALU.add),
                          reads=[("cmp", c2)], writes=[("rank", c2)])
                    S.pool(lambda h_, sI=sI: h_.memset(btok[:, sI, :], 0.0), writes=[("btok", sI)])
                    bo = btok[:, sI, :].rearrange("p (a b n) -> p a b n", a=2, b=2)[:, :, :, 0:qb]
                    rk4 = rank[:, c2, :].rearrange("p (a b n) -> p a b n", a=2, b=2)[:, :, :, 0:qb]
                    S.dve(lambda h_, bo=bo, rk4=rk4: h_.tensor_scalar(out=bo, in0=rk4, scalar1=2.5, scalar2=NEG, op0=ALU.is_gt, op1=ALU.mult),
                          reads=[("rank", c2)], writes=[("btok", sI)])

                def gate_transposes(q4):
                    for a in range(2):
                        bti = 2 + a
                        for qq in range(4):
                            qt = 8 + 4 * q4 + qq
                            sI = qt % 4
                            self.mm(ps[bti][:, qq * 128:(qq + 1) * 128], btok[:, sI, a * 128:(a + 1) * 128], self.identb[:], True, True,
                                    [("btok", sI), "identb"], [("ps", bti)])
                        S.act(lambda h_, a=a, q4=q4, bti=bti: h_.activation(out=biasT[:, a, q4 * TT:(q4 + 1) * TT], in_=ps[bti][:, :], func=AF.Copy),
                              reads=[("ps", bti)], writes=[("biasT", a, q4)])
                def v_proj(tp_lo, tp_hi):
                  for tp in range(tp_lo, tp_hi):
                      bi = 4 + tp % 2
                      bank = ps[bi]
                      for u in range(2):
                          tt = 2 * tp + u
                          T = tt // 4
                          for c in range(NCH):
                              self.mm(bank[:, u * 256:(u + 1) * 256], hn[:, c, tt * 128:(tt + 1) * 128], wv[:, c, :], c == 0, c == NCH - 1,
                                      [("wsl", sl["v"]), ("hn", c, T)], [("ps", bi)])
                      src = bank[:, :].rearrange("p (u a b e) -> p u a b e", u=2, a=2, b=2)
                      S.act(lambda h_, src=src, tp=tp: h_.activation(out=Vt[:, 2 * tp:2 * tp + 2, 0:4:2, 0:64], in_=src[:, :, :, 0, :], func=AF.Copy),
                            reads=[("ps", bi)], writes=[("Vt", 2 * tp, 0), ("Vt", 2 * tp + 1, 0)])
                      S.dve(lambda h_, src=src, tp=tp: h_.tensor_copy(out=Vt[:, 2 * tp:2 * tp + 2, 1:4:2, 64:128], in_=src[:, :, :, 1, :]),
                            reads=[("ps", bi)], writes=[("Vt", 2 * tp, 1), ("Vt", 2 * tp + 1, 1)])
                v_proj(0, 1)
                kmean(2)
                v_proj(1, 2)
                kmean(3)
                v_proj(2, 4)
                gate_matmuls()
                for qt in range(8, 12):
                    gate_chain(qt)
                v_proj(4, 6)
                gate_transposes(0)
                for qt in range(12, 16):
                    gate_chain(qt)
                v_proj(6, 8)
                gate_transposes(1)
                if g + 1 < 4 and sub is None:
                    pending = load_group(g + 1)
                if sub == "gate":
                    break
                iters = []
                itc = 0
                for cc in range(2):
                    for T in range(NT):
                        nk = 4 * T + 4
                        ob = [4 + 2 * (itc % 2), 5 + 2 * (itc % 2)]
                        itc += 1
                        for kt in range(nk):
                            for hp in range(2):
                                iters.append((cc, T, kt, hp, nk, ob[hp]))
                LAG = 3

                def stage_a(n_, cc, T, kt, hp, nk, obk):
                    nb = kt // 2
                    q_lo = max(0, kt - 4 * T) * 128
                    qsl = slice(q_lo, TT)
                    Tk = kt // 4
                    sbi = n_ % 4
                    sbk = ps[sbi]
                    need_bias = (T >= 2 and kt < 4 * T + 2)
                    need_mask = kt >= 4 * T
                    self.mm(sbk[:, qsl], kT[:, cc, hp, kt * 128:(kt + 1) * 128], qT[:, cc, T * TT + q_lo:(T + 1) * TT],
                            True, not (need_bias or need_mask),
                            [("k", cc, Tk, hp), ("kz", hp), ("q", cc, T, 0), ("q", cc, T, 1)], [("ps", sbi)])
                    if need_bias:
                        self.mm(sbk[:, qsl], self.ind[hp][:, nb * 128:(nb + 1) * 128],
                                biasT[:, cc, (T - 2) * TT + q_lo:(T - 1) * TT],
                                False, not need_mask, ["ind", ("biasT", cc, T - 2)], [("ps", sbi)])
                    if need_mask:
                        self.mm(sbk[:, q_lo:q_lo + 128], self.identb[:], self.maskT[:], False, True,
                                ["identb", "maskT"], [("ps", sbi)])

                def stage_bc(n_, cc, T, kt, hp, nk, obk):
                    hl = 2 * cc + hp
                    q_lo = max(0, kt - 4 * T) * 128
                    qsl = slice(q_lo, TT)
                    sbi = n_ % 4
                    sbk = ps[sbi]
                    pi = n_ % 4
                    S.act(lambda h_, pi=pi, sbk=sbk, qsl=qsl: h_.activation(out=pT[pi][:, qsl], in_=sbk[:, qsl], func=AF.Exp),
                          reads=[("ps", sbi)], writes=[("pT", pi)])
                    self.mm(ps[obk][:, qsl], Vt[:, kt, hl, :], pT[pi][:, qsl], kt == 0, kt == nk - 1,
                            [("pT", pi), ("Vt", kt, hp), ("Vt1", hp)], [("ps", obk)])
                    if kt == nk - 1:
                        o = ps[obk]
                        num = slice(hp * 64, (hp + 1) * 64)
                        den = slice((1 - hp) * 64, (2 - hp) * 64)
                        rc = rec[hp]
                        S.dve(lambda h_, rc=rc, o=o, den=den: h_.reciprocal(rc[den, :], o[den, :]),
                              reads=[("ps", obk)], writes=[("rec", hp)])
                        S.dve(lambda h_, rc=rc, o=o, den=den, num=num, cc=cc, T=T: h_.tensor_tensor(
                            out=qT[num, cc, T * TT:(T + 1) * TT], in0=o[num, :], in1=rc[den, :], op=ALU.mult),
                            reads=[("ps", obk), ("rec", hp)], writes=[("q", cc, T, hp)])

                NI = len(iters)
                for n_ in range(NI + LAG):
                    if n_ < NI:
                        stage_a(n_, *iters[n_])
                    m_ = n_ - LAG
                    if m_ >= 0:
                        stage_bc(m_, *iters[m_])
                if sub == "core":
                    break
                wo_units = [(T, dc) for T in range(NT) for dc in range(NCH)]

                def wo_unit(T, dc):
                    tsl = slice(T * TT, (T + 1) * TT)
                    bi = 4 + (T * NCH + dc) % 4
                    for cc in range(2):
                        self.mm(ps[bi][:, :], wo[:, cc, dc * 128:(dc + 1) * 128], qT[:, cc, tsl], cc == 0, cc == 1,
                                [("wsl", sl["o"]), ("q", cc, T, 0), ("q", cc, T, 1)], [("ps", bi)])
                    S.dve(lambda h_: h_.tensor_tensor(out=h[:, dc, tsl], in0=ps[bi][:, :], in1=h[:, dc, tsl], op=ALU.add),
                          reads=[("ps", bi), ("h", dc, T)], writes=[("h", dc, T)])

                if g + 1 < 4 and sub is None:
                    nsl = pending
                    nwk = wsl[nsl["k"]][:, :].rearrange("p (c f) -> p c f", f=256)

                    def between(ui_):
                        for (T_, dc_) in wo_units[4 * ui_:4 * ui_ + 4]:
                            wo_unit(T_, dc_)

                    rope_units("k", nwk, ("wsl", nsl["k"]), between)
                else:
                    for (T_, dc_) in wo_units:
                        wo_unit(T_, dc_)
            S.barrier()
            S.flush()

    def phase_conformer(self, j):
        nc, S, ps = self.nc, self.S, self.ps
        hn, h = self.hn, self.h
        PADL = 32
        with ExitStack() as es:
            glu = self.sb(es, [128, NCH, PADL + SEQ], BF16, "glu")
            ybf = [self.sb(es, [128, NCH, TT], BF16, "ybf") for _ in range(2)]
            ysq = self.sb(es, [128, NCH, TT], BF16, "ysq")
            dg = self.sb(es, [128, CW, 128], BF16, "dg")
            NW = 3
            wsl = [self.sb(es, [128, NCH, 256], BF16, "cw") for _ in range(NW)]
            tA = [self.sb(es, [128, TT], F32, "tA") for _ in range(2)]
            sgm = tA
            mean_t = self.sb(es, [128, TT], F32, "mean_t")
            m2_t = self.sb(es, [128, TT], F32, "m2_t")
            wcount = [0]

            def next_slot():
                i = wcount[0] % NW
                wcount[0] += 1
                return i

            S.pool(lambda h_: h_.memset(glu[:, :, 0:PADL], 0.0), writes=[("glupad",)])
            w1 = self.w_pw1[j].rearrange("(c p) f -> p c f", p=128)
            w2 = self.w_pw2[j].rearrange("(c p) f -> p c f", p=128)

            def half(i):
                return wsl[i // 2][:, :, (i % 2) * 128:(i % 2 + 1) * 128]

            def load_pw1_chunk(cc):
                ia, ig = (2 * cc) % 6, (2 * cc + 1) % 6
                self.load_w(half(ia), w1[:, :, cc * 128:(cc + 1) * 128], ("cwh", ia // 2, ia % 2), "cwh%d" % ia)
                self.load_w(half(ig), w1[:, :, D + cc * 128:D + (cc + 1) * 128], ("cwh", ig // 2, ig % 2), "cwh%d" % ig)

            for cc in range(3):
                load_pw1_chunk(cc)
            it = 0
            for cb in range(4):
                for ci in range(2):
                    cc = 2 * cb + ci
                    ia, ig = (2 * cc) % 6, (2 * cc + 1) % 6
                    for T in range(NT):
                        if T == 1 and cc >= 1 and cc + 2 < NCH:
                            load_pw1_chunk(cc + 2)
                        ba, bg = (it % 2) * 2, (it % 2) * 2 + 1
                        it += 1
                        tsl = slice(T * TT, (T + 1) * TT)
                        for c in range(NCH):
                            self.mm(ps[ba][:, :], half(ia)[:, c, :], hn[:, c, tsl], c == 0, c == NCH - 1,
                                    [("cwh", ia // 2, ia % 2), ("hn", c, T)], [("ps", ba)])
                        for c in range(NCH):
                            self.mm(ps[bg][:, :], half(ig)[:, c, :], hn[:, c, tsl], c == 0, c == NCH - 1,
                                    [("cwh", ig // 2, ig % 2), ("hn", c, T)], [("ps", bg)])
                        sg_ = sgm[it % 2]
                        S.act(lambda h_, sg_=sg_, bg=bg, cc=cc: h_.activation(out=sg_[:], in_=ps[bg][:, :], func=AF.Sigmoid,
                                                                               bias=self.prow(R_BPW1 + j * 16 + 8 + cc)),
                              reads=[("ps", bg), "ptab"], writes=[("tA", it % 2)])
                        S.dve(lambda h_, sg_=sg_, ba=ba, cc=cc, T=T: h_.scalar_tensor_tensor(
                            out=glu[:, cc, PADL + T * TT:PADL + (T + 1) * TT], in0=ps[ba][:, :], scalar=self.prow(R_BPW1 + j * 16 + cc),
                            in1=sg_[:], op0=ALU.add, op1=ALU.mult),
                            reads=[("ps", ba), ("tA", it % 2), "ptab"], writes=[("glu", cc, T)])
            pw2_slots = {}

            def load_pw2(db):
                i = next_slot()
                self.S.dma("pool", "cwh%d" % (2 * i), lambda h_: h_.dma_start(out=wsl[i][:], in_=w2[:, :, db * 256:(db + 1) * 256]),
                           writes=[("cwh", i, 0), ("cwh", i, 1)])
                pw2_slots[db] = i

            load_pw2(0)
            load_pw2(1)
            dcount = [0]

            YB = (0, 1, 6, 7)

            def conv_unit(T, cc):
                yb = YB[cc % 4]
                yT = ybf[T % 2]
                for tap in range(CW):
                    if dcount[0] % 2 == 0:
                        S.dve(lambda h_, tap=tap, cc=cc: h_.tensor_scalar(out=dg[:, tap, :], in0=self.identb[:],
                                                                          scalar1=self.prow(R_WDW + (j * CW + tap) * 8 + cc), scalar2=None, op0=ALU.mult),
                              reads=["identb", "ptab"], writes=[("dg", tap)])
                    else:
                        S.pool(lambda h_, tap=tap, cc=cc: h_.tensor_scalar(out=dg[:, tap, :], in0=self.identb[:],
                                                                           scalar1=self.prow(R_WDW + (j * CW + tap) * 8 + cc), scalar2=1.0,
                                                                           op0=ALU.mult, op1=ALU.mult),
                               reads=["identb", "ptab"], writes=[("dg", tap)])
                    dcount[0] += 1
                for tap in range(CW):
                    o0 = PADL + T * TT - (CW - 1) + tap
                    rd = [("dg", tap), ("glu", cc, T)]
                    rd.append(("glu", cc, T - 1) if T > 0 else ("glupad",))
                    self.mm(ps[yb][:, :], dg[:, tap, :], glu[:, cc, o0:o0 + TT], tap == 0, tap == CW - 1, rd, [("ps", yb)])

            def conv_evac(T, cc):
                yb = YB[cc % 4]
                yT = ybf[T % 2]
                S.act(lambda h_: h_.activation(out=yT[:, cc, :], in_=ps[yb][:, :], func=AF.Identity,
                                               bias=self.prow(R_BDW + j * 8 + cc)),
                      reads=[("ps", yb), "ptab"], writes=[("ybf", T % 2, cc)])
                S.act(lambda h_: h_.activation(out=ysq[:, cc, :], in_=yT[:, cc, :], func=AF.Square),
                      reads=[("ybf", T % 2, cc)], writes=[("ysq", cc)])

            def ln_head(T):
                yT = ybf[T % 2]
                bm, bq = 2 + (T % 2) * 2, 3 + (T % 2) * 2
                for cc in range(NCH):
                    self.mm(ps[bm][:, :], self.onesb[:], yT[:, cc, :], cc == 0, cc == NCH - 1, ["onesb", ("ybf", T % 2, cc)], [("ps", bm)])
                for cc in range(NCH):
                    self.mm(ps[bq][:, :], self.onesb[:], ysq[:, cc, :], cc == 0, cc == NCH - 1, ["onesb", ("ysq", cc)], [("ps", bq)])

            def ln_head_b(T):
                bm, bq = 2 + (T % 2) * 2, 3 + (T % 2) * 2
                S.act(lambda h_: h_.activation(out=mean_t[:], in_=ps[bm][:, :], func=AF.Copy, scale=1.0 / D),
                      reads=[("ps", bm)], writes=["mean_t"])
                S.dve(lambda h_: h_.tensor_tensor(out=m2_t[:], in0=mean_t[:], in1=mean_t[:], op=ALU.mult), reads=["mean_t"], writes=["m2_t"])
                S.dve(lambda h_: h_.scalar_tensor_tensor(out=m2_t[:], in0=ps[bq][:, :], scalar=1.0 / D, in1=m2_t[:],
                                                         op0=ALU.mult, op1=ALU.subtract),
                      reads=[("ps", bq), "m2_t"], writes=["m2_t"])
                S.act(lambda h_: h_.activation(out=m2_t[:], in_=m2_t[:], func=AF.Sqrt, bias=LN_EPS), reads=["m2_t"], writes=["m2_t"])
                S.dve(lambda h_: h_.reciprocal(ps[bm][:, :], m2_t[:]), reads=["m2_t"], writes=[("ps", bm)])
                S.dve(lambda h_: h_.tensor_tensor(out=ps[bq][:, :], in0=ps[bm][:, :], in1=mean_t[:], op=ALU.mult),
                      reads=[("ps", bm), "mean_t"], writes=[("ps", bq)])

            def ln_norm(T, cc):
                yT = ybf[T % 2]
                bm, bq = 2 + (T % 2) * 2, 3 + (T % 2) * 2
                ta_ = tA[cc % 2]
                S.dve(lambda h_: h_.tensor_tensor(out=ta_[:], in0=ps[bm][:, :], in1=yT[:, cc, :], op=ALU.mult),
                      reads=[("ps", bm), ("ybf", T % 2, cc)], writes=[("tA", cc % 2)])
                S.dve(lambda h_: h_.tensor_tensor(out=ta_[:], in0=ta_[:], in1=ps[bq][:, :], op=ALU.subtract),
                      reads=[("ps", bq), ("tA", cc % 2)], writes=[("tA", cc % 2)])
                S.act(lambda h_: h_.activation(out=hn[:, cc, T * TT:(T + 1) * TT], in_=ta_[:], func=AF.Silu,
                                               scale=self.prow(R_LNG + j * 8 + cc), bias=self.prow(R_LNB + j * 8 + cc)),
                      reads=[("tA", cc % 2), "ptab"], writes=[("hn", cc, T)])

            for T in range(NT):
                for cc in range(NCH):
                    conv_unit(T, cc)
                    if T > 0 and cc == 0:
                        ln_head(T - 1)
                    if T > 0 and cc == 1:
                        ln_head_b(T - 1)
                    conv_evac(T, cc)
                    if T > 0 and cc >= 1:
                        ln_norm(T - 1, cc - 1)
                if T > 0:
                    ln_norm(T - 1, NCH - 1)
            ln_head(NT - 1)
            ln_head_b(NT - 1)
            for cc in range(NCH):
                ln_norm(NT - 1, cc)
            it = 0
            for db in range(4):
                if db + 2 < 4:
                    load_pw2(db + 2)
                i = pw2_slots[db]
                for T in range(NT):
                    for di in range(2):
                        dc = 2 * db + di
                        bi = (6, 7, 0, 1, 2, 3, 4, 5)[it % 8]
                        it += 1
                        tsl = slice(T * TT, (T + 1) * TT)
                        for cc in range(NCH):
                            self.mm(ps[bi][:, :], wsl[i][:, cc, di * 128:(di + 1) * 128], hn[:, cc, tsl], cc == 0, cc == NCH - 1,
                                    [("cwh", i, di), ("hn", cc, T)], [("ps", bi)])
                        S.dve(lambda h_, bi=bi, dc=dc, tsl=tsl: h_.scalar_tensor_tensor(
                            out=h[:, dc, tsl], in0=ps[bi][:, :], scalar=self.prow(R_BPW2 + j * 8 + dc), in1=h[:, dc, tsl], op0=ALU.add, op1=ALU.add),
                            reads=[("ps", bi), ("h", dc, T), "ptab"], writes=[("h", dc, T)])
            S.barrier()
            S.flush()

    def phase_ffn(self, i):
        nc, S, ps = self.nc, self.S, self.ps
        hn, h = self.hn, self.h
        groups = [[0, 1, 2], [3, 4, 5], [6, 7, 8], [9, 10]]
        with ExitStack() as es:
            act = self.sb(es, [128, 6, SEQ], BF16, "act")
            NWU = 3
            wup = [self.sb(es, [128, NCH, 512], BF16, "wup") for _ in range(NWU)]
            wdn = [self.sb(es, [128, 6, D], BF16, "wdn") for _ in range(2)]
            Ag = [self.sb(es, [128, TT], F32, "Ag") for _ in range(2)]
            Av = [self.sb(es, [128, TT], F32, "Av") for _ in range(2)]
            sg = [self.sb(es, [128, TT], F32, "sg") for _ in range(2)]
            halo = self.sb(es, [128, 2, 2, 2], F32, "halo")
            wu_src = self.w_up[i].rearrange("(c p) f -> p c f", p=128)
            wd_src = self.w_down[i].rearrange("(c p) f -> p c f", p=128)
            ucount = [0]
            blocks = [b for g in groups for b in g]
            up_slot = {}

            def load_up(b):
                s = ucount[0] % NWU
                ucount[0] += 1
                up_slot[b] = s
                self.load_w(wup[s][:, :, 0:256], wu_src[:, :, b * 256:(b + 1) * 256], ("wup", s), "wu%d" % s)
                self.load_w(wup[s][:, :, 256:512], wu_src[:, :, DFF + b * 256:DFF + (b + 1) * 256], ("wup", s), "wu%d" % s)

            def load_dn(gi):
                g = groups[gi]
                np_ = 2 * len(g)
                j0 = 2 * g[0]
                self.load_w(wdn[gi % 2][:, 0:np_, :], wd_src[:, j0:j0 + np_, :], ("wdn", gi % 2), "wd%d" % (gi % 2))

            pend3 = []
            load_up(blocks[0])
            load_up(blocks[1])
            load_dn(0)
            nxt = 2
            it = 0
            for gi, g in enumerate(groups):
                for bl, b in enumerate(g):
                    s = up_slot[b]
                    for pi in range(2):
                        jp = 2 * b + pi
                        jl = 2 * bl + pi
                        rows = {}
                        for kind, fc in (("g", jp), ("v", NFP + jp)):
                            rows[kind] = [R_FWDW + (i * 3 + tap) * 44 + fc for tap in range(3)] + [R_FBDW + i * 44 + fc]
                        for T in range(NT):
                            par = it % 2
                            it += 1
                            tsl = slice(T * TT, (T + 1) * TT)
                            s3 = (it - 1) % 3
                            bg_, bv_ = s3 * 2, s3 * 2 + 1
                            for c in range(NCH):
                                self.mm(ps[bg_][:, :], wup[s][:, c, pi * 128:(pi + 1) * 128], hn[:, c, tsl], c == 0, c == NCH - 1,
                                        [("wup", s), ("hn", c, T)], [("ps", bg_)])
                            for c in range(NCH):
                                self.mm(ps[bv_][:, :], wup[s][:, c, 256 + pi * 128:256 + (pi + 1) * 128], hn[:, c, tsl], c == 0, c == NCH - 1,
                                        [("wup", s), ("hn", c, T)], [("ps", bv_)])
                            A = {"g": Ag[par], "v": Av[par]}
                            U = {"g": ps[bg_], "v": ps[bv_]}
                            UB = {"g": bg_, "v": bv_}
                            AK = {"g": ("Ag", par), "v": ("Av", par)}
                            KI = {"g": 0, "v": 1}
                            hp_prev = (T - 1) % 2
                            hp_cur = T % 2
                            for kind in ("g", "v"):
                                r = rows[kind]
                                S.act(lambda h_, A_=A[kind], U_=U[kind], r=r: h_.activation(out=A_[:], in_=U_[:, :], func=AF.Identity,
                                                                                          scale=self.prow(r[2]), bias=self.prow(r[3])),
                                      reads=[("ps", UB[kind]), "ptab"], writes=[AK[kind]])
                                if T < NT - 1:
                                    S.act(lambda h_, U_=U[kind], kind=kind, hp_cur=hp_cur: h_.activation(
                                        out=halo[:, hp_cur, KI[kind], :], in_=U_[:, TT - 2:TT], func=AF.Copy),
                                        reads=[("ps", UB[kind])], writes=[("halo", hp_cur, kind)])
                            for kind in ("g", "v"):
                                r = rows[kind]
                                S.dve(lambda h_, A_=A[kind], U_=U[kind], r=r: h_.scalar_tensor_tensor(
                                    out=A_[:, 1:TT], in0=U_[:, 0:TT - 1], scalar=self.prow(r[1]), in1=A_[:, 1:TT], op0=ALU.mult, op1=ALU.add),
                                    reads=[("ps", UB[kind]), AK[kind], "ptab"], writes=[AK[kind]])
                            for kind in ("g", "v"):
                                r = rows[kind]
                                S.dve(lambda h_, A_=A[kind], U_=U[kind], r=r: h_.scalar_tensor_tensor(
                                    out=A_[:, 2:TT], in0=U_[:, 0:TT - 2], scalar=self.prow(r[0]), in1=A_[:, 2:TT], op0=ALU.mult, op1=ALU.add),
                                    reads=[("ps", UB[kind]), AK[kind], "ptab"], writes=[AK[kind]])
                            if T > 0:
                                for kind in ("g", "v"):
                                    r = rows[kind]
                                    hl_ = halo[:, hp_prev, KI[kind], :]
                                    S.dve(lambda h_, A_=A[kind], hl_=hl_, r=r: h_.scalar_tensor_tensor(
                                        out=A_[:, 0:1], in0=hl_[:, 1:2], scalar=self.prow(r[1]), in1=A_[:, 0:1], op0=ALU.mult, op1=ALU.add),
                                        reads=[("halo", hp_prev, kind), AK[kind], "ptab"], writes=[AK[kind]])
                                for kind in ("g", "v"):
                                    r = rows[kind]
                                    hl_ = halo[:, hp_prev, KI[kind], :]
                                    S.dve(lambda h_, A_=A[kind], hl_=hl_, r=r: h_.scalar_tensor_tensor(
                                        out=A_[:, 0:2], in0=hl_[:, 0:2], scalar=self.prow(r[0]), in1=A_[:, 0:2], op0=ALU.mult, op1=ALU.add),
                                        reads=[("halo", hp_prev, kind), AK[kind], "ptab"], writes=[AK[kind]])
                            def stage3(par=par, Ag_=A["g"], Av_=A["v"], jl=jl, tsl=tsl, T=T):
                                sg_ = sg[par]
                                S.act(lambda h_: h_.activation(out=sg_[:], in_=Ag_[:], func=AF.Silu),
                                      reads=[("Ag", par)], writes=[("sg", par)])
                                S.pool(lambda h_: h_.tensor_tensor(out=act[:, jl, tsl], in0=sg_[:], in1=Av_[:], op=ALU.mult),
                                       reads=[("sg", par), ("Av", par)], writes=[("act", jl, T)])
                            if pend3:
                                pend3.pop(0)()
                            pend3.append(stage3)
                            if pi == 0 and T == 2:
                                if nxt < len(blocks):
                                    load_up(blocks[nxt])
                                    nxt += 1
                                if bl == 0 and gi + 1 < len(groups):
                                    load_dn(gi + 1)
                while pend3:
                    pend3.pop(0)()
                np_ = 2 * len(g)
                wd = wdn[gi % 2]
                dn = 0
                for T in range(NT):
                    tsl = slice(T * TT, (T + 1) * TT)
                    for dc in range(NCH):
                        s_next = it % 3
                        bi = (6, 7, 2 * s_next, 2 * s_next + 1)[dn % 4]
                        dn += 1
                        for jl in range(np_):
                            self.mm(ps[bi][:, :], wd[:, jl, dc * 128:(dc + 1) * 128], act[:, jl, tsl], jl == 0, jl == np_ - 1,
                                    [("wdn", gi % 2), ("act", jl, T)], [("ps", bi)])
                        S.dve(lambda h_, bi=bi, dc=dc, tsl=tsl: h_.tensor_tensor(out=h[:, dc, tsl], in0=ps[bi][:, :], in1=h[:, dc, tsl], op=ALU.add),
                              reads=[("ps", bi), ("h", dc, T)], writes=[("h", dc, T)])
            S.barrier()
            S.flush()

    def phase_final(self, raw=False):
        nc, S, ps = self.nc, self.S, self.ps
        with ExitStack() as es:
            ofm = self.sb(es, [128, NCH, TT], F32, "ofm")
            ost = [self.sb(es, [128, D], F32, "ost") for _ in range(8)]
            oc = 0
            for T in range(NT):
                tsl = slice(T * TT, (T + 1) * TT)
                if raw:
                    for c in range(NCH):
                        eng = S.act if c % 2 == 0 else S.dve
                        if c % 2 == 0:
                            S.act(lambda h_, c=c, tsl=tsl: h_.activation(out=ofm[:, c, :], in_=self.h[:, c, tsl], func=AF.Copy),
                                  reads=[("h", c, T)], writes=[("ofm", c)])
                        else:
                            S.dve(lambda h_, c=c, tsl=tsl: h_.tensor_copy(out=ofm[:, c, :], in_=self.h[:, c, tsl]),
                                  reads=[("h", c, T)], writes=[("ofm", c)])
                else:
                    self.rmsnorm_tile(T, R_FIN, ofm)
                for ts in range(4):
                    tt = T * 4 + ts
                    sl = oc % 8
                    oc += 1
                    for half in range(2):
                        bi = (2 * tt + half) % 4
                        bank = ps[bi]
                        for jj in range(4):
                            c = half * 4 + jj
                            S.pe(lambda h_, bank=bank, jj=jj, c=c, ts=ts: h_.transpose(out=bank[:, jj * 128:(jj + 1) * 128],
                                                                                         in_=ofm[:, c, ts * 128:(ts + 1) * 128], identity=self.identf[:]),
                                 reads=[("ofm", c), "identf"], writes=[("ps", bi)])
                        dst = ost[sl][:, half * 512:(half + 1) * 512]
                        if half == 0:
                            S.act(lambda h_, dst=dst, bank=bank: h_.activation(out=dst, in_=bank[:, :], func=AF.Copy),
                                  reads=[("ps", bi)], writes=[("ost", sl, half)])
                        else:
                            S.dve(lambda h_, dst=dst, bank=bank: h_.tensor_copy(out=dst, in_=bank[:, :]),
                                  reads=[("ps", bi)], writes=[("ost", sl, half)])
                    S.dma("sp", "ost%d" % sl, lambda h_, sl=sl, tt=tt: h_.dma_start(out=self.out[tt * 128:(tt + 1) * 128, :], in_=ost[sl][:]),
                          reads=[("ost", sl, 0), ("ost", sl, 1)])
            S.barrier()
            S.flush()

    def rmsnorm_tile(self, T, grow, ofm):
        S, ps = self.S, self.ps
        bi = 6 + T % 2
        bank = ps[bi]
        tsl = slice(T * TT, (T + 1) * TT)
        for c in range(NCH):
            sq = self.nsq[:, c % 2, :]
            S.act(lambda h, sq=sq, c=c: h.activation(out=sq, in_=self.h[:, c, tsl], func=AF.Square),
                  reads=[("h", c, T)], writes=[("nsq", c % 2)])
            self.mm(bank[:, :], self.onesb[:], sq, c == 0, c == NCH - 1, [("nsq", c % 2), "onesb"], [("ps", bi)])
        sd = self.nstd[:, T % 2, :]
        S.act(lambda h: h.activation(out=sd, in_=bank[:, :], func=AF.Sqrt, scale=1.0 / D, bias=NORM_EPS),
              reads=[("ps", bi)], writes=[("nstd", T % 2)])
        S.dve(lambda h: h.reciprocal(bank[:, :], sd), reads=[("nstd", T % 2)], writes=[("ps", bi)])
        for c in range(NCH):
            S.dve(lambda h, c=c: h.scalar_tensor_tensor(out=ofm[:, c, :], in0=self.h[:, c, tsl], scalar=self.prow(grow + c), in1=bank[:, :],
                                                        op0=ALU.mult, op1=ALU.mult),
                  reads=[("h", c, T), ("ps", bi), "ptab"], writes=[("ofm", c)])


def _pack_ptab(inp):
    f = lambda a: np.ascontiguousarray(np.asarray(a, dtype=np.float32)).reshape(-1, 128)
    parts = [
        f(inp["norm_mix_g"]), f(inp["norm_ffn_g"]), f(inp["final_norm_g"]), f(inp["conv_b_pw1"]),
        f(inp["conv_w_dw"]), f(inp["conv_b_dw"]), f(inp["conv_ln_g"]), f(inp["conv_ln_b"]), f(inp["conv_b_pw2"]),
        f(inp["ffn_w_dw"]), f(inp["ffn_b_dw"]),
    ]
    tab = np.concatenate(parts, axis=0)
    assert tab.shape[0] == 1368
    pad = np.zeros((R_TOT - tab.shape[0], 128), np.float32)
    return np.ascontiguousarray(np.concatenate([tab, pad], axis=0))


_NC_CACHE = {}


def _run(inputs, stop_after=None, trace=False):
    x = np.ascontiguousarray(np.asarray(inputs["x"], dtype=np.float32))
    B = x.shape[0]
    key = stop_after
    if key not in _NC_CACHE:
        _NC_CACHE[key] = Builder(stop_after).build()
    nc = _NC_CACHE[key]
    ptab = _pack_ptab(inputs)
    c = lambda k: np.ascontiguousarray(np.asarray(inputs[k], dtype=np.float32))
    shared = {
        "ptab_in": ptab, "w_qkv": c("attn_w_qkv"), "w_o": c("attn_w_o"), "w_pw1": c("conv_w_pw1"), "w_pw2": c("conv_w_pw2"),
        "w_up": c("ffn_w_up"), "w_down": c("ffn_w_down"),
    }
    in_maps = [dict(shared, x=x[b]) for b in range(B)]
    res = run_bass_kernel_spmd(nc, in_maps, core_ids=list(range(B)), trace=trace)
    out = np.stack([np.asarray(r["out"]) for r in res.results], axis=0).astype(np.float32)
    return out, res


def kernel(**inputs):
    out, _ = _run(inputs)
    return out
```
